# Optimizing a Trainium2 kernel written in Bass

```python
import math
import jax, jax.numpy as jnp
from jax import lax
import numpy as np

D_MODEL = 1024
BATCH = 2
SEQ = 8192
DEPTH = 1
DEC_BATCH = 32
DEC_SEQ = 4
PAST_LEN = 8192
PAGE_SIZE = 128

N_HEADS_A = 8
HEAD_DIM_A = 64
WIDTH_A = N_HEADS_A * HEAD_DIM_A
N_HEADS_IDX = 4
HEAD_DIM_IDX = 64
TOPK_MAX = 256
Q_BLOCK = 128
N_HEADS_R = 8
DK_R = 64
DV_R = 64
WIDTH_R = N_HEADS_R * DV_R
CHUNK_R = 128
ROPE_BASE = 10000.0
EPS = 1e-6
IN_SIZES = (WIDTH_A, WIDTH_A, WIDTH_A, WIDTH_A,
            N_HEADS_IDX * HEAD_DIM_IDX, HEAD_DIM_IDX, N_HEADS_IDX,
            N_HEADS_R * DK_R, N_HEADS_R * DK_R, WIDTH_R, WIDTH_R,
            D_MODEL, D_MODEL)
N_IN = sum(IN_SIZES)

kernel_name = "dsa_retention_gated_hybrid_step"


def rms_norm(x, g):
    xf = x.astype(jnp.float32)
    r = lax.rsqrt(jnp.mean(xf * xf, axis=-1, keepdims=True) + EPS)
    return (xf * r).astype(x.dtype) * g


def modulated_input(x, c, w_ada, b_ada, g_norm):
    mod = jax.nn.silu(c) @ w_ada + b_ada
    shift, scale, gate = jnp.split(mod, 3, axis=-1)
    h = rms_norm(x, g_norm) * (1 + scale[:, None, :]) + shift[:, None, :]
    return h, gate


def split_projection(u):
    offs, o = [], 0
    for s in IN_SIZES[:-1]:
        o += s
        offs.append(o)
    return jnp.split(u, offs, axis=-1)


def rotary(x, pos):
    half = x.shape[-1] // 2
    inv = ROPE_BASE ** (-jnp.arange(half, dtype=jnp.float32) / half)
    ang = pos.astype(jnp.float32)[:, None] * inv[None, :]
    cos = jnp.cos(ang)[None, :, None, :]
    sin = jnp.sin(ang)[None, :, None, :]
    xf = x.astype(jnp.float32)
    x1, x2 = xf[..., :half], xf[..., half:]
    return jnp.concatenate([x1 * cos - x2 * sin, x1 * sin + x2 * cos], axis=-1)


def retention_log_decay():
    return jnp.log1p(-jnp.exp2(-5.0 - jnp.arange(N_HEADS_R, dtype=jnp.float32)))


def retention_chunk(state, q, k, v):
    C = q.shape[1]
    lg = retention_log_decay()
    i = jnp.arange(C, dtype=jnp.float32)
    diff = i[:, None] - i[None, :]
    dmat = jnp.where(diff >= 0, jnp.exp(lg[:, None, None] * jnp.maximum(diff, 0.0)), 0.0)
    scores = jnp.einsum('bihd,bjhd->bhij', q, k) * dmat
    out = jnp.einsum('bhij,bjhe->bihe', scores, v)
    q_decay = jnp.exp(lg[None, :] * (i[:, None] + 1.0))
    out = out + jnp.einsum('bihd,bhde->bihe', q, state) * q_decay[None, :, :, None]
    k_decay = jnp.exp(lg[None, :] * (C - 1.0 - i)[:, None])
    new_state = (jnp.exp(lg * C)[None, :, None, None] * state
                 + jnp.einsum('bjhd,bjhe->bhde', k * k_decay[None, :, :, None], v))
    return new_state, out


def retention_prompt(q, k, v):
    B, T = q.shape[:2]
    n = T // CHUNK_R
    chunks = lambda a: jnp.moveaxis(a.reshape(B, n, CHUNK_R, *a.shape[2:]), 1, 0)
    state0 = jnp.zeros((B, N_HEADS_R, DK_R, DV_R), jnp.float32)
    state, outs = lax.scan(lambda s, xs: retention_chunk(s, *xs), state0,
                           (chunks(q), chunks(k), chunks(v)))
    return jnp.moveaxis(outs, 0, 1).reshape(B, T, N_HEADS_R, DV_R), state


def group_norm_heads(o):
    mu = jnp.mean(o, axis=-1, keepdims=True)
    var = jnp.mean(jnp.square(o - mu), axis=-1, keepdims=True)
    y = (o - mu) * lax.rsqrt(var + EPS)
    return y.reshape(*o.shape[:2], -1)


def index_scores(qi, wi, ki):
    s = jnp.einsum('bqhd,bsd->bqhs', qi.astype(jnp.float32), ki.astype(jnp.float32))
    w = wi.astype(jnp.float32) * (N_HEADS_IDX ** -0.5 * HEAD_DIM_IDX ** -0.5)
    return jnp.einsum('bqhs,bqh->bqs', jax.nn.relu(s), w)


def select_keys(scores, q_pos, topk):
    S = scores.shape[-1]
    admissible = jnp.arange(S)[None, None, :] <= q_pos[None, :, None]
    _, idx = lax.top_k(jnp.where(admissible, scores, -jnp.inf), topk)
    valid = idx <= q_pos[None, :, None]
    return idx, valid


def sparse_attend(q, kg, vg, valid):
    logits = jnp.einsum('bqhd,bqkhd->bhqk', q.astype(jnp.float32), kg.astype(jnp.float32)) * HEAD_DIM_A ** -0.5
    logits = jnp.where(valid[:, None], logits, -jnp.inf)
    p = jax.nn.softmax(logits, axis=-1)
    return jnp.einsum('bhqk,bqkhd->bqhd', p, vg.astype(jnp.float32)).astype(q.dtype)


take_rows = jax.vmap(lambda a, i: a[i])


def dsa_prompt(q, k, v, qi, ki, wi):
    B, T = q.shape[:2]
    nb = T // Q_BLOCK
    topk = min(TOPK_MAX, T // 4)
    blocks = lambda a: jnp.moveaxis(a.reshape(B, nb, Q_BLOCK, *a.shape[2:]), 1, 0)

    def one_block(args):
        n, qb, qib, wib = args
        pos = n * Q_BLOCK + jnp.arange(Q_BLOCK)
        idx, valid = select_keys(index_scores(qib, wib, ki), pos, topk)
        return sparse_attend(qb, take_rows(k, idx), take_rows(v, idx), valid)

    out = lax.map(one_block, (jnp.arange(nb), blocks(q), blocks(qi), blocks(wi)))
    return jnp.moveaxis(out, 0, 1).reshape(B, T, WIDTH_A)


def dsa_sample(q, k_new, v_new, qi, ki_new, wi, cache_k, cache_v, cache_kidx, page_table):
    DB, T = q.shape[:2]
    topk = min(TOPK_MAX, (PAST_LEN + T) // 4)
    ki_past = cache_kidx[page_table].reshape(DB, PAST_LEN, HEAD_DIM_IDX)
    ki_all = jnp.concatenate([ki_past.astype(ki_new.dtype), ki_new], axis=1)
    pos = PAST_LEN + jnp.arange(T)
    idx, valid = select_keys(index_scores(qi, wi, ki_all), pos, topk)
    in_past = idx < PAST_LEN
    idx_p = jnp.minimum(idx, PAST_LEN - 1)
    phys = jnp.take_along_axis(page_table, (idx_p // PAGE_SIZE).reshape(DB, -1), axis=1).reshape(idx.shape)
    slot = idx_p % PAGE_SIZE
    idx_n = jnp.clip(idx - PAST_LEN, 0, T - 1)
    kg = jnp.where(in_past[..., None, None], cache_k[phys, slot].astype(k_new.dtype), take_rows(k_new, idx_n))
    vg = jnp.where(in_past[..., None, None], cache_v[phys, slot].astype(v_new.dtype), take_rows(v_new, idx_n))
    return sparse_attend(q, kg, vg, valid).reshape(DB, T, WIDTH_A)


def hybrid_layer(x, c, pos, attn_fn, ret_fn, w_ada, b_ada, g_norm, w_in, g_ret, w_pa, w_pr, w_out):
    B, T, _ = x.shape
    h, gate = modulated_input(x, c, w_ada, b_ada, g_norm)
    qa, ka, va, za, qi, ki, wi, qr, kr, vr, zr, ga, gr = split_projection(h @ w_in)
    heads = lambda a, n: a.reshape(B, T, n, -1)
    ka, va = heads(ka, N_HEADS_A), heads(va, N_HEADS_A)
    y_a = attn_fn(heads(qa, N_HEADS_A), ka, va, heads(qi, N_HEADS_IDX), ki, wi) * jax.nn.silu(za)
    q_r = rotary(heads(qr, N_HEADS_R), pos)
    k_r = rotary(heads(kr, N_HEADS_R), pos) * DK_R ** -0.5
    ret, ret_state = ret_fn(q_r, k_r, heads(vr, N_HEADS_R).astype(jnp.float32))
    y_r = group_norm_heads(ret).astype(x.dtype) * g_ret * jax.nn.silu(zr)
    merged = jax.nn.sigmoid(ga) * (y_a @ w_pa) + jax.nn.sigmoid(gr) * (y_r @ w_pr)
    x = x + gate[:, None, :] * (merged @ w_out)
    return x, (ka, va, ki, ret_state)


def setup_inputs(seed: int = 0) -> dict:
    key = jax.random.key(seed)
    ks = jax.random.split(key, 20)
    n_pages = PAST_LEN // PAGE_SIZE
    n_pool = (DEC_BATCH * n_pages * 5) // 4
    nrm = lambda k, shape, s=1.0: jax.random.normal(k, shape, jnp.float32) * s
    page_table = jax.random.permutation(ks[0], n_pool)[:DEC_BATCH * n_pages].reshape(DEC_BATCH, n_pages).astype(jnp.int32)
    return {
        "x_prompt": nrm(ks[1], (BATCH, SEQ, D_MODEL)),
        "x_sample": nrm(ks[2], (DEC_BATCH, DEC_SEQ, D_MODEL)),
        "cache_k": nrm(ks[3], (DEPTH, n_pool, PAGE_SIZE, N_HEADS_A, HEAD_DIM_A)),
        "cache_v": nrm(ks[4], (DEPTH, n_pool, PAGE_SIZE, N_HEADS_A, HEAD_DIM_A)),
        "cache_kidx": nrm(ks[5], (DEPTH, n_pool, PAGE_SIZE, HEAD_DIM_IDX)),
        "state_ret": nrm(ks[6], (DEPTH, DEC_BATCH, N_HEADS_R, DK_R, DV_R), 0.5),
        "page_table": page_table,
        "c_prompt": nrm(ks[7], (BATCH, D_MODEL)),
        "c_sample": nrm(ks[8], (DEC_BATCH, D_MODEL)),
        "w_ada": nrm(ks[9], (DEPTH, D_MODEL, 3 * D_MODEL), 0.2 * D_MODEL ** -0.5),
        "b_ada": nrm(ks[10], (DEPTH, 3 * D_MODEL), 0.01),
        "g_norm": 1.0 + nrm(ks[11], (DEPTH, D_MODEL), 0.02),
        "w_in": nrm(ks[12], (DEPTH, D_MODEL, N_IN), D_MODEL ** -0.5),
        "g_ret": 1.0 + nrm(ks[13], (DEPTH, WIDTH_R), 0.02),
        "w_pa": nrm(ks[14], (DEPTH, WIDTH_A, D_MODEL), WIDTH_A ** -0.5),
        "w_pr": nrm(ks[15], (DEPTH, WIDTH_R, D_MODEL), WIDTH_R ** -0.5),
        "w_out": nrm(ks[16], (DEPTH, D_MODEL, D_MODEL), D_MODEL ** -0.5),
        "g_final": 1.0 + nrm(ks[17], (D_MODEL,), 0.02),
    }


def reference(x_prompt, x_sample, cache_k, cache_v, cache_kidx, state_ret, page_table, c_prompt, c_sample,
              w_ada, b_ada, g_norm, w_in, g_ret, w_pa, w_pr, w_out, g_final):
    pos_p = jnp.arange(x_prompt.shape[1])
    pos_s = PAST_LEN + jnp.arange(x_sample.shape[1])
    hp, hs = x_prompt, x_sample
    kp, vp, kip, sp, kss, vss, kis, sss = [], [], [], [], [], [], [], []
    for l in range(DEPTH):
        lw = (w_ada[l], b_ada[l], g_norm[l], w_in[l], g_ret[l], w_pa[l], w_pr[l], w_out[l])
        hp, (k1, v1, ki1, s1) = hybrid_layer(hp, c_prompt, pos_p, dsa_prompt, retention_prompt, *lw)
        ck, cv, cki, st = cache_k[l], cache_v[l], cache_kidx[l], state_ret[l].astype(jnp.float32)
        attn_s = lambda q, k, v, qi, ki, wi: dsa_sample(q, k, v, qi, ki, wi, ck, cv, cki, page_table)
        def ret_s(q, k, v):
            new_state, out = retention_chunk(st, q, k, v)
            return out, new_state
        hs, (k2, v2, ki2, s2) = hybrid_layer(hs, c_sample, pos_s, attn_s, ret_s, *lw)
        kp.append(k1); vp.append(v1); kip.append(ki1); sp.append(s1)
        kss.append(k2); vss.append(v2); kis.append(ki2); sss.append(s2)
    y_prompt = rms_norm(hp, g_final)
    y_sample = rms_norm(hs, g_final)
    return (y_prompt, y_sample, jnp.stack(kp), jnp.stack(vp), jnp.stack(kip), jnp.stack(sp),
            jnp.stack(kss), jnp.stack(vss), jnp.stack(kis), jnp.stack(sss))
```

```python
import os
import numpy as np
from contextlib import ExitStack
import concourse.bass as bass
import concourse.mybir as mybir
from concourse.bass_utils import run_bass_kernel_spmd

F32 = mybir.dt.float32
BF16 = mybir.dt.bfloat16
AF = mybir.ActivationFunctionType
ALU = mybir.AluOpType
AX = mybir.AxisListType

D = 1024
SEQ = 8192
NSTEP = 16
NSTEPS_RUN = int(os.environ.get("MK_NSTEPS", "16"))
TOPK = 256
NEG_BIG = -3.0e38
C_KA, C_VA, C_KR, C_VR, C_KID = 0, 512, 1024, 1536, 2048
C_QA, C_QID, C_KIWI, C_ZA, C_QR, C_ZR, C_GA, C_GR = 2176, 2688, 3200, 3268, 3780, 4292, 4804, 5828
NEXT = 6852
ENGS = ("pe", "act", "dve", "pool", "sp")


class _Stop(Exception):
    pass


class _Stop2(Exception):
    pass


STOP = int(os.environ.get('MK_STOP', '99'))
PROJ_ONLY = os.environ.get('MK_FULL', '1') != '1'


def stop_at(k):
    if STOP == k:
        raise _Stop()


class Buf:
    __slots__ = ("w", "r")

    def __init__(self):
        self.w = None
        self.r = []


class Tl:
    def __init__(self, t):
        self.t = t
        self.b = Buf()

    def __getitem__(self, k):
        return self.t[k]


def _b(x):
    return x.b if hasattr(x, "b") else x


class Sched:
    def __init__(self, nc, stack):
        self.nc = nc
        self.stack = stack
        self.ops = {e: [] for e in ENGS}
        self.cnt = {e: 0 for e in ENGS}
        self.esem = {e: stack.enter_context(nc.semaphore("prog_" + e)) for e in ENGS if e != "sp"}
        self.dsem = {}
        self.dcnt = {}
        self.waited = {}
        self.same = {"pe": False, "act": True, "dve": True, "pool": True, "sp": False}

    def _collect(self, eng, reads, writes):
        deps = []
        for b in reads:
            if b.w is not None:
                deps.append(b.w)
        for b in writes:
            if b.w is not None:
                deps.append(b.w)
            deps.extend(b.r)
        waits = []
        for d in deps:
            if d[0] == "e":
                if d[1] == eng and not self.same[eng]:
                    continue
                key = (eng, "e", d[1])
                val = d[2]
                sem = self.esem[d[1]]
            else:
                key = (eng, "d", d[1])
                val = self.dcnt[d[1]]
                sem = self.dsem[d[1]]
            if self.waited.get(key, 0) >= val:
                continue
            self.waited[key] = val
            waits.append((sem, val))
        return waits

    def _update(self, d, reads, writes):
        for b in reads:
            b.r.append(d)
            if len(b.r) > 64:
                b.r = b.r[-64:] if False else b.r
        for b in writes:
            b.w = d
            b.r = []

    def op(self, eng, fn, reads=(), writes=()):
        reads = [_b(x) for x in reads]
        writes = [_b(x) for x in writes]
        waits = self._collect(eng, reads, writes)
        self.cnt[eng] += 1
        idx = self.cnt[eng]
        sem = self.esem[eng]

        def thunk(e, waits=waits, fn=fn, sem=sem):
            for ws, wv in waits:
                e.wait_ge(ws, wv)
            fn(e).then_inc(sem, 1)

        self.ops[eng].append(thunk)
        self._update(("e", eng, idx), reads, writes)

    def dma(self, semkey, fn, reads=(), writes=(), q="sp"):
        reads = [_b(x) for x in reads]
        writes = [_b(x) for x in writes]
        if semkey not in self.dsem:
            self.dsem[semkey] = self.stack.enter_context(self.nc.semaphore("d_" + semkey))
            self.dcnt[semkey] = 0
        waits = self._collect(q, reads, writes)
        self.dcnt[semkey] += 16
        sem = self.dsem[semkey]

        def thunk(e, waits=waits, fn=fn, sem=sem):
            for ws, wv in waits:
                e.wait_ge(ws, wv)
            fn(e).then_inc(sem, 16)

        self.ops[q].append(thunk)
        self._update(("d", semkey), reads, writes)

    def barrier(self):
        fin = [(sem, self.dcnt[k]) for k, sem in self.dsem.items() if self.dcnt[k] > 0]
        fin += [(sem, self.cnt[e]) for e, sem in self.esem.items() if self.cnt[e] > 0]
        for q in ENGS:
            def thunk(e, fin=fin):
                for ws, wv in fin:
                    e.wait_ge(ws, wv)
            self.ops[q].append(thunk)

    def finish(self, q="sp"):
        fin = [(sem, self.dcnt[k]) for k, sem in self.dsem.items()]
        fin += [(sem, self.cnt[e]) for e, sem in self.esem.items() if self.cnt[e] > 0]

        def thunk(e, fin=fin):
            for ws, wv in fin:
                e.wait_ge(ws, wv)

        self.ops[q].append(thunk)

    def emit(self):
        nc = self.nc
        ops = self.ops
        with nc.Block() as block:
            @block.tensor
            def _(e):
                for t in ops["pe"]:
                    t(e)

            @block.scalar
            def _(e):
                for t in ops["act"]:
                    t(e)

            @block.vector
            def _(e):
                for t in ops["dve"]:
                    t(e)

            @block.gpsimd
            def _(e):
                for t in ops["pool"]:
                    t(e)

            @block.sync
            def _(e):
                for t in ops["sp"]:
                    t(e)


def build_nc():
    nc = bass.Bass("TRN2", target_bir_lowering=False)
    din = lambda n, s, dt=F32: nc.dram_tensor(n, list(s), dt, kind="ExternalInput").ap()
    dout = lambda n, s, dt=F32: nc.dram_tensor(n, list(s), dt, kind="ExternalOutput").ap()
    dscr = lambda n, s, dt: nc.dram_tensor(n, list(s), dt, kind="Internal").ap()
    xall = din("xall", [SEQ, D]); xown = din("xown", [NSTEP * 128, D])
    wext = din("wext", [128, 8, NEXT]); wpo = din("wpo", [128, 16, 1024]); wada = din("wada", [128, 8, 3072])
    badaT = din("badaT", [128, 24]); bgate = din("bgate", [128, 1024]); gnT = din("gnT", [128, 8])
    gfin = din("gfin", [128, 1024]); gret = din("gret", [128, 512]); cT = din("cT", [128, 8, 5])
    ident_d = din("ident", [128, 128]); cs_all = din("cs_all", [SEQ, 64]); cs_own = din("cs_own", [NSTEP * 128, 64])
    kdec_d = din("kdec", [128, 8]); qdec_d = din("qdec", [128, 8]); dtab_d = din("dtab", [128, 8, 128])
    g128_d = din("g128", [128, 4, 128]); sel_d = din("sel", [128, 4]); iota_d = din("iota512", [128, 512])
    qrel_d = din("qrel", [128, 1])
    I32 = mybir.dt.int32
    xsp = din("xsp", [4, 128, D]); xzero = din("xzero", [128, D])
    cache_k_d = din("cache_k2", [2560 * 128, 512]); cache_v_d = din("cache_v2", [2560 * 128, 512]); cache_ki_d = din("cache_ki2", [2560 * 128, 64])
    ptab = din("ptab", [4, 64], I32); state_in = din("state_in", [4, 8, 64, 64])
    cs_s_d = din("cs_s", [128, 64]); kdec_s_d = din("kdec_s", [128, 8]); qdec_s_d = din("qdec_s", [128, 8])
    dtab_s_d = din("dtab_s", [128, 8, 128]); g4_d = din("g4", [128, 4, 128]); sel_s_d = din("sel_s", [128, 4])
    penS_s_d = din("penS_s", [128, 512]); penM_s_d = din("penM_s", [128, 512]); pidx_d = din("pidx", [128, 1])
    ys_o = dout("ys_o", [4, 4, D]); ks_o = dout("ks_o", [4, 4, 512]); vs_o = dout("vs_o", [4, 4, 512])
    kis_o = dout("kis_o", [4, 4, 64]); sts_o = dout("sts_o", [4, 8, 64, 64])
    gate5_d = dscr("gate5_d", [5, D], F32)
    y_o = dout("y_o", [NSTEP * 128, D]); k_o = dout("k_o", [NSTEP * 128, 512]); v_o = dout("v_o", [NSTEP * 128, 512])
    ki_o = dout("ki_o", [NSTEP * 128, 64]); st_o = dout("st_o", [8, 64, 64])
    wbf = dscr("wbf", [128, 8, NEXT], BF16); wpobf = dscr("wpobf", [128, 16, 1024], BF16)
    kscr = dscr("kscr", [128, 4, SEQ + 512], BF16); vscr = dscr("vscr", [SEQ // 128 + 4, 128, 520], BF16)

    with ExitStack() as st:
        S = Sched(nc, st)
        sb = lambda n, s, dt=F32: Tl(st.enter_context(nc.sbuf_tensor("s_" + n, list(s), dt)))
        psb = lambda n, s, dt=F32: Tl(st.enter_context(nc.psum_tensor("p_" + n, list(s), dt)))
        ACT = lambda fn, r=(), w=(): S.op("act", fn, r, w)
        POOL = lambda fn, r=(), w=(): S.op("pool", fn, r, w)
        DVE = lambda fn, r=(), w=(): S.op("dve", fn, r, w)
        PE = lambda fn, r=(), w=(): S.op("pe", fn, r, w)

        G = [psb("G%d" % i, [128, 512]) for i in range(2)]
        A = [psb("A%d" % i, [128, 512]) for i in range(4)]
        TP = psb("TP", [128, 8, 128], BF16)
        XP = psb("XP", [128, 512])
        gi = [0]

        def nextG():
            gi[0] ^= 1
            return G[gi[0]]

        kscr_b = [Buf() for _ in range(NSTEP + 1)]; vscr_b = [Buf() for _ in range(NSTEP + 1)]
        wbf_b = Buf(); wpobf_b = Buf()
        kiT = sb("kiT", [128, 4608], BF16)
        wsl = [sb("wsl%d" % i, [128, 4096], BF16) for i in range(2)]
        wi_ = [0]
        hT = sb("hT", [128, 8, 640], BF16)
        xo = [sb("xo%d" % i, [128, D]) for i in range(2)]
        xn = sb("xn", [128, D], BF16); sqj = xn
        ssq = sb("ssq", [128, 4]); epsT = sb("epsT", [128, 1])
        kst = sb("kst", [128, 4, 512], BF16); vst = sb("vst", [128, 4, 8, 65], BF16)
        ko = sb("ko", [128, 512]); vo = sb("vo", [128, 512]); kio = sb("kio", [128, 68])
        ev = [sb("ev%d" % i, [128, 512]) for i in range(2)]
        evi = [0]
        rt = [sb("rt%d" % i, [128, 8, 32]) for i in range(4)]
        krd = sb("krd", [128, 4, 512], BF16); vrb = sb("vrb", [128, 5, 512], BF16)
        krown = sb("krown", [128, 512], BF16); KrT = sb("KrT", [128, 4, 128], BF16)
        qp = sb("qp", [128, 512], BF16); QpT = sb("QpT", [128, 8, 128], BF16)
        Sst = sb("Sst", [128, 4, 128]); ownS = sb("ownS", [128, 4, 128])
        ownSb = sb("ownSb", [128, 8, 64], BF16)
        QT = [sb("QT%d" % i, [128, 8, 128], BF16) for i in range(2)]
        qiT2 = [sb("qiTlo", [128, 4, 128], BF16), sb("qiThi", [128, 4, 128], BF16)]
        wsc = [sb("wsc%d" % i, [128, 4]) for i in range(2)]
        siluza = [sb("siluza%d" % i, [128, 512], BF16) for i in range(2)]
        sga = [sb("sga%d" % i, [128, 1024], BF16) for i in range(2)]
        sgr = sb("sgr", [128, 1024], BF16)
        mr = [sb("mr%d" % i, [128, 1024], BF16) for i in range(2)]
        siluzr = sb("siluzr", [128, 512], BF16)
        rl = [sb("rl%d" % i, [128, 512]) for i in range(2)]
        rli = [0]
        eti = [0]
        dtab = sb("dtab", [128, 8, 128]); g128 = sb("g128", [128, 4, 128])
        cs5 = sb("cs5", [128, 5, 64])
        identf = sb("identf", [128, 128]); identb = sb("identb", [128, 128], BF16)
        ident4 = sb("ident4", [128, 4, 128], BF16); zl = sb("zl", [128, 128], BF16)
        ATb = sb("ATb", [128, 8, 128], BF16)
        oret = sb("oret", [128, 8, 64]); osq = sb("osq", [128, 8, 64])
        gst = sb("gst", [128, 8, 8])
        yrb = sb("yrb", [128, 512], BF16); yrT = sb("yrT", [128, 4, 128], BF16)
        acc = sb("acc", [128, 8, 65]); rden = sb("rden", [128, 8, 2])
        yaf = oret; yab = sb("yab", [128, 512], BF16); yaT = sb("yaT", [128, 4, 128], BF16)
        mg = sb("mg", [128, 1024], BF16); mT = sb("mT", [128, 8, 128], BF16)
        res = sb("res", [128, D]); yout = res; xs = [res] * 2
        gate_bc = sb("gate_bc", [128, D]); gfin_bc = sb("gfin_bc", [128, D]); gret_bc = sb("gret_bc", [128, 512])
        penS = sb("penS", [128, 512]); penM = sb("penM", [128, 512], BF16)
        kdec = sb("kdec", [128, 8]); qdec = sb("qdec", [128, 8]); sel = sb("sel", [128, 4])
        AcA = sb("AcA", [128, 5, 8]); BcA = sb("BcA", [128, 5, 8]); cur = {"j": 0}
        ptb = sb("ptb", [128, 64], I32); idxa = sb("idxa", [128, 64], I32); pidx = sb("pidx", [128, 1])
        m8 = sb("m8", [128, 8])

        def ld(key, dst_ap, src_ap, w, r=()):
            S.dma(key, lambda e: e.dma_start(out=dst_ap, in_=src_ap), reads=r, writes=w)

        def stq(key, dst_ap, src_ap, r, w=()):
            S.dma(key, lambda e: e.dma_start(out=dst_ap, in_=src_ap), reads=r, writes=w)

        def mm(out, lhsT, rhs, start, stop, r, w):
            PE(lambda e: e.matmul(out=out, lhsT=lhsT, rhs=rhs, start=start, stop=stop), r, w)

        def act(out, in_, func, r, w, **kw):
            ACT(lambda e: e.activation(out=out, in_=in_, func=func, **kw), r, w)

        def tt(eng, out, in0, in1, op, r, w):
            S.op(eng, lambda e: e.tensor_tensor(out=out, in0=in0, in1=in1, op=op), r, w)

        def ts(eng, out, in0, s1, s2, op0, op1, r, w):
            if op1 is None:
                S.op(eng, lambda e: e.tensor_scalar(out=out, in0=in0, scalar1=s1, scalar2=None, op0=op0), r, w)
            else:
                S.op(eng, lambda e: e.tensor_scalar(out=out, in0=in0, scalar1=s1, scalar2=s2, op0=op0, op1=op1), r, w)

        def stt(eng, out, in0, scalar, in1, op0, op1, r, w):
            S.op(eng, lambda e: e.scalar_tensor_tensor(out=out, in0=in0, scalar=scalar, in1=in1, op0=op0, op1=op1), r, w)

        def load_w(col0, ncols):
            sl = wsl[wi_[0] % 2]; wi_[0] += 1
            v = sl.t[:, 0:8 * ncols].rearrange("p (k c) -> p k c", k=8)
            ld("w%d" % ((wi_[0] - 1) % 2), v, wbf[:, :, col0:col0 + ncols], [sl], [wbf_b])
            return sl, v

        def load_wpo(kt0, nk, c0, ncols):
            sl = wsl[wi_[0] % 2]; wi_[0] += 1
            v = sl.t[:, 0:nk * ncols].rearrange("p (k c) -> p k c", k=nk)
            ld("w%d" % ((wi_[0] - 1) % 2), v, wpobf[:, kt0:kt0 + nk, c0:c0 + ncols], [sl], [wpobf_b])
            return sl, v

        def rstd_chain(src_tl, src_ap, n, col):
            act(sqj[:, 0:n], src_ap, AF.Square, [src_tl], [sqj, ssq], accum_out=ssq[:, 3:4])
            act(ssq[:, 2:3], ssq[:, 3:4], AF.Ln, [ssq], [ssq], scale=1.0 / n, bias=epsT[:, 0:1])
            act(ssq[:, col:col + 1], ssq[:, 2:3], AF.Exp, [ssq], [ssq], scale=-0.5)

        def make_hT(xt, col0):
            rstd_chain(xt, xt[:, :], D, 0)
            ts("pool", xn[:, :], xt[:, :], ssq[:, 0:1], None, ALU.mult, None, [xt, ssq], [xn])
            for kt in range(8):
                PE(lambda e, kt=kt: e.transpose(out=TP[:, kt, :], in_=xn[:, kt * 128:(kt + 1) * 128], identity=identb[:, :]),
                   [xn, identb], [TP])
            for kt in range(8):
                jj = cur["j"]
                act(hT[:, kt, col0:col0 + 128], TP[:, kt, :], AF.Identity, [TP, AcA, BcA], [hT],
                    scale=AcA[:, jj, kt:kt + 1], bias=BcA[:, jj, kt:kt + 1])

        def rotary(src, cs_ap, dst_ap, dst_tl):
            sv = src.t[:, :].rearrange("p (h t f) -> p h t f", h=8, t=2)
            dv = dst_ap.rearrange("p (h t f) -> p h t f", h=8, t=2)
            cosb = cs_ap[:, 0:32].unsqueeze(1).to_broadcast([128, 8, 32])
            sinb = cs_ap[:, 32:64].unsqueeze(1).to_broadcast([128, 8, 32])
            x1 = sv[:, :, 0, :]; x2 = sv[:, :, 1, :]
            tt("pool", rt[0][:, :, :], x1, cosb, ALU.mult, [src, cs5], [rt[0]])
            tt("pool", rt[1][:, :, :], x2, sinb, ALU.mult, [src, cs5], [rt[1]])
            tt("pool", dv[:, :, 0, :], rt[0][:, :, :], rt[1][:, :, :], ALU.subtract, [rt[0], rt[1]], [dst_tl])
            tt("pool", rt[2][:, :, :], x1, sinb, ALU.mult, [src, cs5], [rt[2]])
            tt("pool", rt[3][:, :, :], x2, cosb, ALU.mult, [src, cs5], [rt[3]])
            tt("pool", dv[:, :, 1, :], rt[2][:, :, :], rt[3][:, :, :], ALU.add, [rt[2], rt[3]], [dst_tl])

        def next_ev():
            evi[0] += 1
            return ev[evi[0] % 2]

        try:
            st2 = ExitStack()
            sb2 = lambda n, s_, dt=F32: Tl(st2.enter_context(nc.sbuf_tensor("s_" + n, list(s_), dt)))
            bgate_sb = sb2("bgate_sb", [128, D])
            iota = sb2("iota", [128, 512]); qrel = sb2("qrel", [128, 1])
            cTs = sb2("cTs", [128, 8, 5]); scT = sb2("scT", [128, 8, 5]); gate5 = sb2("gate5", [5, D])
            modT = sb2("modT", [128, 16, 5]); badaTs = sb2("badaTs", [128, 24]); gnTs = sb2("gnTs", [128, 8])
            wadas = [sb2("wadas%d" % i, [128, 8, 256]) for i in range(2)]
            stgf = [sb2("stgf%d" % i, [128, 1024]) for i in range(2)]
            stgb = [sb2("stgb%d" % i, [128, 1024], BF16) for i in range(2)]
            ld("c0", identf[:, :], ident_d, [identf]); ld("c0", dtab[:, :, :], dtab_d, [dtab]); ld("c0", g128[:, :, :], g128_d, [g128])
            ld("c0", kdec[:, :], kdec_d, [kdec]); ld("c0", qdec[:, :], qdec_d, [qdec]); ld("c0", sel[:, :], sel_d, [sel])
            ld("c0", iota[:, :], iota_d, [iota]); ld("c0", qrel[:, :], qrel_d, [qrel]); ld("c0", cTs[:, :, :], cT, [cTs])
            ld("c0", badaTs[:, :], badaT, [badaTs]); ld("c0", gnTs[:, :], gnT, [gnTs]); ld("c0", bgate_sb[:, :], bgate, [bgate_sb])
            ld("c0", gfin_bc[:, :], gfin, [gfin_bc]); ld("c0", gret_bc[:, :], gret, [gret_bc])
            POOL(lambda e: e.tensor_copy(out=identb[:, :], in_=identf[:, :]), [identf], [identb])
            for a4 in range(4):
                POOL(lambda e, a4=a4: e.tensor_copy(out=ident4[:, a4, :], in_=identf[:, :]), [identf], [ident4])
            POOL(lambda e: e.memset(zl[:, :], 0.0), [], [zl])
            POOL(lambda e: e.memset(epsT[:, :], 1e-6), [], [epsT])
            POOL(lambda e: e.memset(vst[:, :, :, 64:65], 1.0), [], [vst])
            POOL(lambda e: e.memset(Sst[:, :, :], 0.0), [], [Sst])
            POOL(lambda e: e.memset(ownSb[:, :, :], 0.0), [], [ownSb])
            POOL(lambda e: e.memset(QpT[:, :, :], 0.0), [], [QpT])
            POOL(lambda e: e.memset(kiT[:, :], 0.0), [], [kiT])
            for q_ in qiT2:
                POOL(lambda e, q_=q_: e.memset(q_[:, :, :], 0.0), [], [q_])
            for q_ in QT:
                POOL(lambda e, q_=q_: e.memset(q_[:, :, :], 0.0), [], [q_])
            ts("pool", penS[:, :], iota[:, :], qrel[:, 0:1], -1.0e30, ALU.is_gt, ALU.mult, [iota, qrel], [penS])
            ts("pool", penM[:, :], iota[:, :], qrel[:, 0:1], -30000.0, ALU.is_gt, ALU.mult, [iota, qrel], [penM])
            act(scT[:, :, :], cTs[:, :, :], AF.Silu, [cTs], [scT])
            for cc in range(12):
                wa = wadas[cc % 2]
                ld("wa%d" % (cc % 2), wa[:, :, :], wada[:, :, cc * 256:(cc + 1) * 256], [wa])
                if cc < 8:
                    for t2 in range(2):
                        ct = cc * 2 + t2
                        for kt in range(8):
                            mm(XP[:, ct * 5:ct * 5 + 5], wa[:, kt, t2 * 128:(t2 + 1) * 128], scT[:, kt, :], kt == 0, kt == 7,
                               [wa, scT], [XP])
                else:
                    gch = cc - 8
                    g_ = G[gch // 2]
                    for kt in range(8):
                        mm(g_[0:5, (gch % 2) * 256:(gch % 2) * 256 + 256], scT[:, kt, :], wa[:, kt, :], kt == 0, kt == 7,
                           [wa, scT], [g_])
            act(modT[:, :, :], XP[:, 0:80].rearrange("p (c j) -> p c j", j=5), AF.Identity, [XP], [modT])
            for j in range(5):
                tt("pool", BcA[:, j, :], modT[:, 0:8, j], badaTs[:, 0:8], ALU.add, [modT, badaTs], [BcA])
                ts("pool", AcA[:, j, :], modT[:, 8:16, j], 1.0, None, ALU.add, None, [modT], [AcA])
                tt("pool", AcA[:, j, :], AcA[:, j, :], badaTs[:, 8:16], ALU.add, [AcA, badaTs], [AcA])
                tt("pool", AcA[:, j, :], AcA[:, j, :], gnTs[:, :], ALU.mult, [AcA, gnTs], [AcA])
            for hf in range(2):
                act(gate5[:, hf * 512:(hf + 1) * 512], G[hf][0:5, :], AF.Identity, [G[hf]], [gate5])
            tt("pool", gate5[:, :], gate5[:, :], bgate_sb[0:5, :], ALU.add, [gate5, bgate_sb], [gate5])
            gate5_b = Buf()
            stq("g5", gate5_d, gate5[:, :], [gate5], [gate5_b])
            ld("g5l", gate_bc[:, :], gate5_d[0:1, :].partition_broadcast(128), [gate_bc], [gate5_b])
            ld("c0", pidx[:, :], pidx_d, [pidx])
            stop_at(1)
            nch = (NEXT + 127) // 128
            for c in range(nch):
                c0 = c * 128; n = min(128, NEXT - c0); i = c % 2
                fv = stgf[i].t[:, 0:8 * n].rearrange("p (k c) -> p k c", k=8)
                bv = stgb[i].t[:, 0:8 * n].rearrange("p (k c) -> p k c", k=8)
                ld("ps%d" % i, fv, wext[:, :, c0:c0 + n], [stgf[i]])
                if c % 2 == 0:
                    POOL(lambda e, fv=fv, bv=bv: e.tensor_copy(out=bv, in_=fv), [stgf[i]], [stgb[i]])
                else:
                    act(bv, fv, AF.Identity, [stgf[i]], [stgb[i]])
                stq("pt%d" % i, wbf[:, :, c0:c0 + n], bv, [stgb[i]], [wbf_b])
            for c in range(16):
                c0 = c * 64; i = c % 2
                fv = stgf[i].t[:, :].rearrange("p (k c) -> p k c", k=16)
                bv = stgb[i].t[:, :].rearrange("p (k c) -> p k c", k=16)
                ld("ps%d" % i, fv, wpo[:, :, c0:c0 + 64], [stgf[i]])
                if c % 2 == 0:
                    POOL(lambda e, fv=fv, bv=bv: e.tensor_copy(out=bv, in_=fv), [stgf[i]], [stgb[i]])
                else:
                    act(bv, fv, AF.Identity, [stgf[i]], [stgb[i]])
                stq("pt%d" % i, wpobf[:, :, c0:c0 + 64], bv, [stgb[i]], [wpobf_b])

            stop_at(2)
            S.barrier()
            st2.close()
            work = [sb("work0", [128, SEQ + 512])] * 2
            nm = [sb("nm%d" % i, [128, 512], BF16) for i in range(2)]
            kc = [sb("kc%d" % i, [128, 4, 512], BF16) for i in range(2)]
            vc = [sb("vc%d" % i, [128, 4, 520], BF16) for i in range(2)]
            ET = [sb("ET%d" % i, [128, 512], BF16) for i in range(2)]

            def projscores(g, smp=None):
                sl = g % 2
                L = (g + 1) * 512
                if smp is None:
                    ld("cs", cs5[:, 0:4, :], cs_all[g * 512:(g + 1) * 512, :].rearrange("(t p) c -> p t c", p=128), [cs5])
                    ld("cs", cs5[:, 4, :], cs_own[g * 128:(g + 1) * 128, :], [cs5])
                else:
                    for j5 in range(5):
                        ld("cs", cs5[:, j5, :], cs_s_d, [cs5])
                for j in range(4):
                    xt = xs[j % 2]
                    if smp is None:
                        ld("x0", xt[:, :], xall[(4 * g + j) * 128:(4 * g + j + 1) * 128, :], [xt])
                    else:
                        ld("x0", xt[:, :], xsp[smp, :, :] if j == 0 else xzero, [xt])
                    make_hT(xt, j * 128)
                ld("xo%d" % sl, xo[sl][:, :], xown[g * 128:(g + 1) * 128, :] if smp is None else xsp[smp, :, :], [xo[sl]])
                make_hT(xo[sl], 512)
                stop_at(31)
                wt, wv = load_w(C_KA, 512)
                for pr in range(4):
                    g_ = nextG()
                    for kt in range(8):
                        mm(g_[:, :], wv[:, kt, pr * 128:(pr + 1) * 128], hT[:, kt, 0:512], kt == 0, kt == 7, [wt, hT], [g_])
                    act(kst[:, pr, :], g_[:, :], AF.Identity, [g_], [kst])
                stq("kst", kscr[:, :, g * 512:(g + 1) * 512], kst[:, :, :], [kst], [kscr_b[g]])
                g_ = nextG()
                for kt in range(8):
                    mm(g_[:, :], hT[:, kt, 512:640], wv[:, kt, :], kt == 0, kt == 7, [wt, hT], [g_])
                act(ko[:, :], g_[:, :], AF.Identity, [g_], [ko])
                if smp is None:
                    stq("ko", k_o[g * 128:(g + 1) * 128, :], ko[:, :], [ko])
                else:
                    stq("ko", ks_o[smp, :, :], ko[0:4, :], [ko])
                stop_at(32)
                wt, wv = load_w(C_VA, 512)
                for j in range(5):
                    g_ = nextG()
                    for kt in range(8):
                        mm(g_[:, :], hT[:, kt, j * 128:(j + 1) * 128], wv[:, kt, :], kt == 0, kt == 7, [wt, hT], [g_])
                    if j < 4:
                        act(vst[:, j, :, 0:64], g_[:, :].rearrange("p (h d) -> p h d", h=8), AF.Identity, [g_], [vst])
                    else:
                        act(vo[:, :], g_[:, :], AF.Identity, [g_], [vo])
                stq("vst", vscr[4 * g:4 * g + 4, :, :].rearrange("t p c -> p t c"), vst[:, :, :, :].rearrange("p t h c -> p t (h c)"),
                    [vst], [vscr_b[g]])
                if smp is None:
                    stq("vo", v_o[g * 128:(g + 1) * 128, :], vo[:, :], [vo])
                else:
                    stq("vo", vs_o[smp, :, :], vo[0:4, :], [vo])
                stop_at(33)
                wt, wv = load_w(C_KR, 512)
                for j in range(5):
                    g_ = nextG()
                    for kt in range(8):
                        mm(g_[:, :], hT[:, kt, j * 128:(j + 1) * 128], wv[:, kt, :], kt == 0, kt == 7, [wt, hT], [g_])
                    e_ = next_ev()
                    if j < 4:
                        for h in range(8):
                            act(e_[:, h * 64:(h + 1) * 64], g_[:, h * 64:(h + 1) * 64], AF.Identity, [g_, kdec], [e_], scale=kdec[:, h:h + 1])
                        rotary(e_, cs5[:, j, :], krd[:, j, :], krd)
                    else:
                        act(e_[:, :], g_[:, :], AF.Identity, [g_], [e_])
                        rotary(e_, cs5[:, 4, :], krown[:, :], krown)
                stop_at(34)
                wt, wv = load_w(C_VR, 512)
                for j in range(5):
                    g_ = nextG()
                    for kt in range(8):
                        mm(g_[:, :], hT[:, kt, j * 128:(j + 1) * 128], wv[:, kt, :], kt == 0, kt == 7, [wt, hT], [g_])
                    act(vrb[:, j, :], g_[:, :], AF.Identity, [g_], [vrb])
                stop_at(35)
                for j in range(4 if smp is None else 1):
                    if j == 0:
                        ts("pool", ownS[:, :, :], Sst[:, :, :], sel[:, 0:1], None, ALU.mult, None, [Sst, sel], [ownS])
                    else:
                        tS = next_ev()
                        tSv = tS.t[:, :].rearrange("p (a b) -> p a b", a=4)
                        ts("pool", tSv, Sst[:, :, :], sel[:, j:j + 1], None, ALU.mult, None, [Sst, sel], [tS])
                        tt("pool", ownS[:, :, :], ownS[:, :, :], tSv, ALU.add, [ownS, tS], [ownS])
                    for pr in range(4):
                        mm(XP[:, pr * 128:(pr + 1) * 128], krd[:, j, pr * 128:(pr + 1) * 128], vrb[:, j, pr * 128:(pr + 1) * 128], True, True,
                           [krd, vrb], [XP])
                    tmpS = next_ev()
                    act(tmpS[:, :], XP[:, :], AF.Identity, [XP], [tmpS])
                    tt("pool", Sst[:, :, :], Sst[:, :, :], g128[:, :, :], ALU.mult, [Sst, g128], [Sst])
                    tt("pool", Sst[:, :, :], Sst[:, :, :], tmpS[:, :].rearrange("p (a b) -> p a b", a=4), ALU.add, [Sst, tmpS], [Sst])
                for hb in range(2):
                    POOL(lambda e, hb=hb: e.tensor_copy(out=ownSb[hb * 64:(hb + 1) * 64, hb::2, :],
                                                        in_=ownS[hb * 64:(hb + 1) * 64, :, hb * 64:(hb + 1) * 64]), [ownS], [ownSb])
                stop_at(36)
                wt, wv = load_w(C_KID, 512)
                g_ = nextG()
                for kt in range(8):
                    mm(g_[:, :], wv[:, kt, 0:128], hT[:, kt, 0:512], kt == 0, kt == 7, [wt, hT], [g_])
                pb = g // 9
                cc = g % 9
                if os.environ.get('MK_V', '') != 'noki':
                    act(kiT[pb * 64:(pb + 1) * 64, cc * 512:(cc + 1) * 512], g_[pb * 64:(pb + 1) * 64, :], AF.Identity, [g_], [kiT])
                stop_at(37)
                wt, wv = load_w(C_QA, 512)
                g_ = nextG()
                for pr in range(4):
                    for kt in range(8):
                        mm(g_[:, pr * 128:(pr + 1) * 128], wv[:, kt, pr * 128:(pr + 1) * 128], hT[:, kt, 512:640], kt == 0, kt == 7, [wt, hT], [g_])
                for hb in range(2):
                    act(QT[sl][hb * 64:(hb + 1) * 64, hb::2, :], g_[hb * 64:(hb + 1) * 64, :].rearrange("p (a q) -> p a q", a=4), AF.Identity,
                        [g_], [QT[sl]])
                wt, wv = load_w(C_QID, 512)
                g_ = nextG()
                for h in range(4):
                    for kt in range(8):
                        mm(g_[:, h * 128:(h + 1) * 128], wv[:, kt, h * 128:(h + 1) * 128], hT[:, kt, 512:640], kt == 0, kt == 7, [wt, hT], [g_])
                for hb in range(2):
                    act(qiT2[hb][hb * 64:(hb + 1) * 64, :, :], g_[hb * 64:(hb + 1) * 64, :].rearrange("p (a q) -> p a q", a=4), AF.Identity,
                        [g_], [qiT2[hb]])
                wt, wv = load_w(C_KIWI, 512)
                g_ = nextG()
                for kt in range(8):
                    mm(g_[:, 0:68], hT[:, kt, 512:640], wv[:, kt, 0:68], kt == 0, kt == 7, [wt, hT], [g_])
                act(kio[:, :], g_[:, 0:68], AF.Identity, [g_], [kio])
                if smp is None:
                    stq("kio", ki_o[g * 128:(g + 1) * 128, :], kio[:, 0:64], [kio])
                else:
                    stq("kio", kis_o[smp, :, :], kio[0:4, 0:64], [kio])
                ts("pool", wsc[sl][:, :], kio[:, 64:68], 1.0 / 16.0, None, ALU.mult, None, [kio], [wsc[sl]])
                if PROJ_ONLY:
                    return
                wt, wv = load_w(C_ZA, 512)
                g_ = nextG()
                for kt in range(8):
                    mm(g_[:, :], hT[:, kt, 512:640], wv[:, kt, :], kt == 0, kt == 7, [wt, hT], [g_])
                act(siluza[sl][:, :], g_[:, :], AF.Silu, [g_], [siluza[sl]])
                wt, wv = load_w(C_QR, 512)
                g_ = nextG()
                for kt in range(8):
                    mm(g_[:, :], hT[:, kt, 512:640], wv[:, kt, :], kt == 0, kt == 7, [wt, hT], [g_])
                e_ = next_ev()
                for h in range(8):
                    act(e_[:, h * 64:(h + 1) * 64], g_[:, h * 64:(h + 1) * 64], AF.Identity, [g_, qdec], [e_], scale=qdec[:, h:h + 1])
                rotary(e_, cs5[:, 4, :], qp[:, :], qp)
                for pr in range(4):
                    PE(lambda e, pr=pr: e.transpose(out=TP[:, pr, :], in_=qp[:, pr * 128:(pr + 1) * 128], identity=identb[:, :]), [qp, identb], [TP])
                for hb in range(2):
                    act(QpT[hb * 64:(hb + 1) * 64, hb::2, :], TP[hb * 64:(hb + 1) * 64, 0:4, :], AF.Identity, [TP], [QpT])
                for pr in range(4):
                    PE(lambda e, pr=pr: e.transpose(out=TP[:, 4 + pr, :], in_=krown[:, pr * 128:(pr + 1) * 128], identity=identb[:, :]), [krown, identb], [TP])
                act(KrT[:, :, :], TP[:, 4:8, :], AF.Identity, [TP], [KrT])
                wt, wv = load_w(C_ZR, 512)
                g_ = nextG()
                for kt in range(8):
                    mm(g_[:, :], hT[:, kt, 512:640], wv[:, kt, :], kt == 0, kt == 7, [wt, hT], [g_])
                act(siluzr[:, :], g_[:, :], AF.Silu, [g_], [siluzr])
                for ci, (c0, dst) in enumerate([(C_GA, sga[sl]), (C_GA + 512, sga[sl]), (C_GR, sgr), (C_GR + 512, sgr)]):
                    wt, wv = load_w(c0, 512)
                    g_ = nextG()
                    for kt in range(8):
                        mm(g_[:, :], hT[:, kt, 512:640], wv[:, kt, :], kt == 0, kt == 7, [wt, hT], [g_])
                    act(dst[:, (ci % 2) * 512:(ci % 2 + 1) * 512], g_[:, :], AF.Sigmoid, [g_], [dst])
                stop_at(38)
                for h in range(8):
                    pr, hb = h // 2, h % 2
                    a_ = A[h // 4]
                    mm(a_[:, (h % 4) * 128:(h % 4 + 1) * 128], KrT[:, pr, :], QpT[:, h, :], True, True,
                       [KrT, QpT], [a_])
                stop_at(381)
                for hf in range(2):
                    e_ = next_ev()
                    act(e_[:, :], A[hf][:, :], AF.Identity, [A[hf]], [e_])
                    tt("pool", ATb[:, hf * 4:(hf + 1) * 4, :], e_[:, :].rearrange("p (a q) -> p a q", a=4), dtab[:, hf * 4:(hf + 1) * 4, :], ALU.mult,
                       [e_, dtab], [ATb])
                stop_at(382)
                for h in range(8):
                    pr, hb = h // 2, h % 2
                    mm(A[2][:, h * 64:(h + 1) * 64], ATb[:, h, :], vrb[:, 4, h * 64:(h + 1) * 64], True, False, [ATb, vrb], [A[2]])
                    mm(A[2][:, h * 64:(h + 1) * 64], QpT[:, h, :], ownSb[:, h, :], False, True,
                       [QpT, ownSb], [A[2]])
                stop_at(383)
                act(oret[:, :, :], A[2][:, :].rearrange("p (h d) -> p h d", h=8), AF.Identity, [A[2]], [oret])
                stop_at(39)
                DVE(lambda e: e.tensor_reduce(out=gst[:, :, 0], in_=oret[:, :, :], axis=AX.X, op=ALU.add), [oret], [gst])
                tt("pool", osq[:, :, :], oret[:, :, :], oret[:, :, :], ALU.mult, [oret], [osq])
                DVE(lambda e: e.tensor_reduce(out=gst[:, :, 1], in_=osq[:, :, :], axis=AX.X, op=ALU.add), [osq], [gst])
                ts("pool", gst[:, :, 2], gst[:, :, 0], 1.0 / 64, None, ALU.mult, None, [gst], [gst])
                tt("pool", gst[:, :, 3], gst[:, :, 2], gst[:, :, 2], ALU.mult, [gst], [gst])
                ts("pool", gst[:, :, 7], gst[:, :, 1], 1.0 / 64, None, ALU.mult, None, [gst], [gst])
                tt("pool", gst[:, :, 4], gst[:, :, 7], gst[:, :, 3], ALU.subtract, [gst], [gst])
                act(gst[:, :, 5], gst[:, :, 4], AF.Ln, [gst], [gst], bias=epsT[:, 0:1])
                act(gst[:, :, 6], gst[:, :, 5], AF.Exp, [gst], [gst], scale=-0.5)
                tt("pool", osq[:, :, :], oret[:, :, :], gst[:, :, 2:3].to_broadcast([128, 8, 64]), ALU.subtract, [oret, gst], [osq])
                tt("pool", osq[:, :, :], osq[:, :, :], gst[:, :, 6:7].to_broadcast([128, 8, 64]), ALU.mult, [osq, gst], [osq])
                osq2 = osq.t[:, :, :].rearrange("p h d -> p (h d)")
                tt("pool", osq2, osq2, gret_bc[:, :], ALU.mult, [osq, gret_bc], [osq])
                tt("pool", yrb[:, :], osq2, siluzr[:, :], ALU.mult, [osq, siluzr], [yrb])
                for pr in range(4):
                    PE(lambda e, pr=pr: e.transpose(out=TP[:, pr, :], in_=yrb[:, pr * 128:(pr + 1) * 128], identity=identb[:, :]), [yrb, identb], [TP])
                act(yrT[:, :, :], TP[:, 0:4, :], AF.Identity, [TP], [yrT])
                for hf in range(2):
                    wt, wv = load_wpo(4, 4, hf * 512, 512)
                    g_ = nextG()
                    for kt in range(4):
                        mm(g_[:, :], yrT[:, kt, :], wv[:, kt, :], kt == 0, kt == 3, [wt, yrT], [g_])
                    e_ = next_ev()
                    act(e_[:, :], g_[:, :], AF.Identity, [g_], [e_])
                    tt("pool", mr[sl][:, hf * 512:(hf + 1) * 512], e_[:, :], sgr[:, hf * 512:(hf + 1) * 512], ALU.mult, [e_, sgr], [mr[sl]])
            def scores(g):
                sl = g % 2
                wk = work[sl]
                for c in range(g + 1):
                    pb = c // 9
                    cc = c % 9
                    for h in range(4):
                        g_ = nextG()
                        mm(g_[:, :], qiT2[pb][:, h, :], kiT[:, cc * 512:(cc + 1) * 512], True, True, [qiT2[pb], kiT], [g_])
                        rli[0] += 1
                        r_ = rl[rli[0] % 2]
                        act(r_[:, :], g_[:, :], AF.Relu, [g_], [r_])
                        dst = wk[:, c * 512:(c + 1) * 512]
                        if h == 0:
                            ts("pool", dst, r_[:, :], wsc[sl][:, 0:1], None, ALU.mult, None, [r_, wsc[sl]], [wk])
                            if c == g:
                                tt("pool", dst, dst, penS[:, :], ALU.add, [wk, penS], [wk])
                        else:
                            ts("pool", r_[:, :], r_[:, :], wsc[sl][:, h:h + 1], None, ALU.mult, None, [r_, wsc[sl]], [r_])
                            tt("pool", dst, dst, r_[:, :], ALU.add, [wk, r_], [wk])

            def topk(g):
                wk = work[g % 2]
                L = (g + 1) * 512
                for r in range(TOPK // 8):
                    DVE(lambda e: e.max(out=m8[:, :], in_=wk[:, 0:L]), [wk], [m8])
                    DVE(lambda e: e.match_replace(out=wk[:, 0:L], in_to_replace=m8[:, :], in_values=wk[:, 0:L], imm_value=NEG_BIG), [wk, m8], [wk])

            def attention(g, smp=None):
                sl = g % 2
                wk = work[sl]
                if g == 0 and os.environ.get('MK_DBG'):
                    stq("dbg", dbg[:, 0:512], wk[:, 0:512], [wk])
                nkt = 4 * (g + 1)
                for c in range(g + 1):
                    kc_ = kc[c % 2]; vc_ = vc[c % 2]; nm_ = nm[c % 2]
                    ld("kc%d" % (c % 2), kc_[:, :, :], kscr[:, :, c * 512:(c + 1) * 512], [kc_], [kscr_b[c]])
                    ld("vc%d" % (c % 2), vc_[:, :, :], vscr[4 * c:4 * c + 4, :, :].rearrange("t p c -> p t c"), [vc_], [vscr_b[c]])
                    ts("pool", nm_[:, :], wk[:, c * 512:(c + 1) * 512], -1.0e38, -30000.0, ALU.is_gt, ALU.mult, [wk], [nm_])
                    if c == g:
                        tt("pool", nm_[:, :], nm_[:, :], penM[:, :], ALU.add, [nm_, penM], [nm_])
                    if g == 0 and c == 0 and os.environ.get('MK_DBG'):
                        stq("dbg", dbgb[:, 0:512], nm_[:, :], [nm_])
                        stq("dbg", dbgb[:, 1024:1536], kc_[:, 0, :], [kc_])
                    for t in range(4):
                        kt_i = c * 4 + t
                        for hg in range(2):
                            a_ = A[(2 * (kt_i % 2)) + hg]
                            mm(a_[:, :], nm_[:, t * 128:(t + 1) * 128], ident4[:, :, :].rearrange("p a q -> p (a q)"), True, False,
                               [nm_, ident4], [a_])
                            for hh in range(4):
                                h = hg * 4 + hh
                                pr, hb = h // 2, h % 2
                                mm(a_[:, hh * 128:(hh + 1) * 128], kc_[:, pr, t * 128:(t + 1) * 128],
                                   QT[sl][:, h, :], False, hh == 3, [kc_, QT[sl]], [a_])
                            eti[0] += 1
                            et = ET[eti[0] % 2]
                            act(et[:, :], a_[:, :], AF.Exp, [a_], [et], scale=0.125)
                            if g == 0 and kt_i == 0 and hg == 0 and os.environ.get('MK_DBG'):
                                stq("dbg", dbgb[:, 512:1024], et[:, :], [et])
                                stq("dbg", dbgb[:, 1536:2048], QT[sl][:, 0:4, :].rearrange("p a q -> p (a q)"), [QT[sl]])
                            for hh in range(4):
                                h = hg * 4 + hh
                                if kt_i == 0 and hh == 0:
                                    mm(G[hg][:, 0:260], zl[:, :], vc_[:, t, 0:260], True, False, [zl, vc_], [G[hg]])
                                mm(G[hg][:, hh * 65:(hh + 1) * 65], et[:, hh * 128:(hh + 1) * 128], vc_[:, t, h * 65:(h + 1) * 65],
                                   False, (kt_i == nkt - 1 and hh == 3), [et, vc_], [G[hg]])
                for hg in range(2):
                    act(acc[:, hg * 4:(hg + 1) * 4, :], G[hg][:, 0:260].rearrange("p (h c) -> p h c", h=4), AF.Identity, [G[hg]], [acc])
                if g == 0 and os.environ.get('MK_DBG'):
                    stq("dbg", dbg[:, 512:1032], acc.t[:, :, :].rearrange("p h c -> p (h c)"), [acc])
                act(rden[:, :, 0], acc[:, :, 64], AF.Ln, [acc], [rden])
                act(rden[:, :, 1], rden[:, :, 0], AF.Exp, [rden], [rden], scale=-1.0)
                tt("pool", yaf[:, :, :], acc[:, :, 0:64], rden[:, :, 1:2].to_broadcast([128, 8, 64]), ALU.mult, [acc, rden], [yaf])
                tt("pool", yab[:, :], yaf.t[:, :, :].rearrange("p h d -> p (h d)"), siluza[sl][:, :], ALU.mult, [yaf, siluza[sl]], [yab])
                for pr in range(4):
                    PE(lambda e, pr=pr: e.transpose(out=TP[:, pr, :], in_=yab[:, pr * 128:(pr + 1) * 128], identity=identb[:, :]), [yab, identb], [TP])
                act(yaT[:, :, :], TP[:, 0:4, :], AF.Identity, [TP], [yaT])
                for hf in range(2):
                    wt, wv = load_wpo(0, 4, hf * 512, 512)
                    g_ = nextG()
                    for kt in range(4):
                        mm(g_[:, :], yaT[:, kt, :], wv[:, kt, :], kt == 0, kt == 3, [wt, yaT], [g_])
                    e_ = next_ev()
                    act(e_[:, :], g_[:, :], AF.Identity, [g_], [e_])
                    tt("pool", e_[:, :], e_[:, :], sga[sl][:, hf * 512:(hf + 1) * 512], ALU.mult, [e_, sga[sl]], [e_])
                    tt("pool", mg[:, hf * 512:(hf + 1) * 512], e_[:, :], mr[sl][:, hf * 512:(hf + 1) * 512], ALU.add, [e_, mr[sl]], [mg])
                for kt in range(8):
                    PE(lambda e, kt=kt: e.transpose(out=TP[:, kt, :], in_=mg[:, kt * 128:(kt + 1) * 128], identity=identb[:, :]), [mg, identb], [TP])
                act(mT[:, :, :], TP[:, :, :], AF.Identity, [TP], [mT])
                for hf in range(2):
                    wt, wv = load_wpo(8, 8, hf * 512, 512)
                    g_ = nextG()
                    for kt in range(8):
                        mm(g_[:, :], mT[:, kt, :], wv[:, kt, :], kt == 0, kt == 7, [wt, mT], [g_])
                    e_ = next_ev()
                    act(e_[:, :], g_[:, :], AF.Identity, [g_], [e_])
                    tt("pool", e_[:, :], e_[:, :], gate_bc[:, hf * 512:(hf + 1) * 512], ALU.mult, [e_, gate_bc], [e_])
                    tt("pool", res[:, hf * 512:(hf + 1) * 512], e_[:, :], xo[sl][:, hf * 512:(hf + 1) * 512], ALU.add, [e_, xo[sl]], [res])
                if g == 0 and os.environ.get('MK_DBG'):
                    stq("dbg", dbg[:, 2048:3072], res[:, :], [res])
                rstd_chain(res, res[:, :], D, 1)
                ts("pool", res[:, :], res[:, :], ssq[:, 1:2], None, ALU.mult, None, [res, ssq], [res])
                tt("pool", yout[:, :], res[:, :], gfin_bc[:, :], ALU.mult, [res, gfin_bc], [yout])
                if smp is None:
                    stq("yo", y_o[g * 128:(g + 1) * 128, :], yout[:, :], [yout])
                else:
                    stq("yo", ys_o[smp, :, :], yout[0:4, :], [yout])

            n = NSTEPS_RUN
            if PROJ_ONLY:
                for g in range(n):
                    projscores(g)
                raise _Stop2()
            projscores(0)
            stop_at(3)
            scores(0)
            stop_at(4)
            for g in range(n):
                topk(g)
                stop_at(5)
                if g + 1 < n:
                    projscores(g + 1)
                attention(g)
                stop_at(6)
                if g + 1 < n:
                    scores(g + 1)
            for h in range(8):
                pr, hb = h // 2, h % 2
                stq("sto", st_o[h, :, :], Sst[hb * 64:(hb + 1) * 64, pr, hb * 64:(hb + 1) * 64], [Sst])

            def ingest(s_):
                ld("pt", ptb[:, :], ptab[s_:s_ + 1, :].partition_broadcast(128), [ptb])
                ts("pool", idxa[:, :], ptb[:, :], 128.0, pidx[:, 0:1], ALU.mult, ALU.add, [ptb, pidx], [idxa])
                POOL(lambda e: e.memset(Sst[:, :, :], 0.0), [], [Sst])
                for hb in range(2):
                    ld("sti", Sst[hb * 64:(hb + 1) * 64, :, hb * 64:(hb + 1) * 64],
                       state_in[s_, hb::2, :, :].rearrange("h p e -> p h e"), [Sst])
                for pg in range(64):
                    c = pg // 4; t = pg % 4
                    kp = rl[pg % 2]; vp = ev[pg % 2]
                    S.dma("gk%d" % (pg % 2), lambda e, kp=kp, pg=pg: e.indirect_dma_start(
                        out=kp[:, :], out_offset=None, in_=cache_k_d,
                        in_offset=bass.IndirectOffsetOnAxis(ap=idxa[:, pg:pg + 1], axis=0)), reads=[_b(idxa)], writes=[_b(kp)], q="pool")
                    S.dma("gv%d" % (pg % 2), lambda e, vp=vp, pg=pg: e.indirect_dma_start(
                        out=vp[:, :], out_offset=None, in_=cache_v_d,
                        in_offset=bass.IndirectOffsetOnAxis(ap=idxa[:, pg:pg + 1], axis=0)), reads=[_b(idxa)], writes=[_b(vp)], q="pool")
                    S.dma("gi", lambda e, pg=pg: e.indirect_dma_start(
                        out=kio[:, 0:64], out_offset=None, in_=cache_ki_d,
                        in_offset=bass.IndirectOffsetOnAxis(ap=idxa[:, pg:pg + 1], axis=0)), reads=[_b(idxa)], writes=[_b(kio)], q="pool")
                    kb = qp if pg % 2 == 0 else krown
                    act(kb[:, :], kp[:, :], AF.Identity, [kp], [kb])
                    for pr in range(4):
                        PE(lambda e, pr=pr, kb=kb: e.transpose(out=TP[:, pr, :], in_=kb[:, pr * 128:(pr + 1) * 128], identity=identb[:, :]),
                           [kb, identb], [TP])
                    act(kst[:, :, t * 128:(t + 1) * 128], TP[:, 0:4, :], AF.Identity, [TP], [kst])
                    act(vst[:, t, :, 0:64], vp[:, :].rearrange("p (h d) -> p h d", h=8), AF.Identity, [vp], [vst])
                    for dd in range(2):
                        act(yab[:, dd * 64:(dd + 1) * 64], kio[:, 0:64], AF.Identity, [kio], [yab])
                    PE(lambda e: e.transpose(out=TP[:, 4, :], in_=yab[:, 0:128], identity=identb[:, :]), [yab, identb], [TP])
                    pb = c // 9; cc = c % 9
                    act(kiT[pb * 64:(pb + 1) * 64, cc * 512 + t * 128:cc * 512 + (t + 1) * 128], TP[pb * 64:(pb + 1) * 64, 4, :], AF.Identity,
                        [TP], [kiT])
                    if t == 3:
                        stq("kst", kscr[:, :, c * 512:(c + 1) * 512], kst[:, :, :], [kst], [kscr_b[c]])
                        stq("vst", vscr[4 * c:4 * c + 4, :, :].rearrange("t p c -> p t c"), vst[:, :, :, :].rearrange("p t h c -> p t (h c)"),
                            [vst], [vscr_b[c]])

            NSAMP = int(os.environ.get("MK_NSAMP", "4"))
            if NSAMP > 0:
                S.barrier()
                ld("c1", kdec[:, :], kdec_s_d, [kdec]); ld("c1", qdec[:, :], qdec_s_d, [qdec]); ld("c1", dtab[:, :, :], dtab_s_d, [dtab])
                ld("c1", g128[:, :, :], g4_d, [g128]); ld("c1", sel[:, :], sel_s_d, [sel]); ld("c1", penS[:, :], penS_s_d, [penS])
                pe_ = next_ev()
                ld("c1", pe_[:, :], penM_s_d, [pe_])
                POOL(lambda e, pe_=pe_: e.tensor_copy(out=penM[:, :], in_=pe_[:, :]), [pe_], [penM])
                for s_ in range(NSAMP):
                    ingest(s_)
                    stop_at(71)
                    cur["j"] = 1 + s_
                    ld("g5l", gate_bc[:, :], gate5_d[1 + s_:2 + s_, :].partition_broadcast(128), [gate_bc], [gate5_b])
                    projscores(16, smp=s_)
                    stop_at(72)
                    scores(16)
                    topk(16)
                    stop_at(73)
                    attention(16, smp=s_)
                    for h in range(8):
                        pr, hb = h // 2, h % 2
                        stq("sto", sts_o[s_, h, :, :], Sst[hb * 64:(hb + 1) * 64, pr, hb * 64:(hb + 1) * 64], [Sst])
        except _Stop:
            st2.close()
        except _Stop2:
            for h in range(8):
                pr, hb = h // 2, h % 2
                stq("sto", st_o[h, :, :], Sst[hb * 64:(hb + 1) * 64, pr, hb * 64:(hb + 1) * 64], [Sst])
        S.finish()
        S.emit()
    return nc


def _consts():
    half = 32
    inv = (10000.0 ** (-np.arange(half, dtype=np.float32) / half)).astype(np.float32)
    pos = np.arange(SEQ, dtype=np.float32)
    ang = (pos[:, None] * inv[None, :]).astype(np.float32)
    cs = np.concatenate([np.cos(ang), np.sin(ang)], axis=1).astype(np.float32)
    lg = np.log1p(-np.exp2(-5.0 - np.arange(8, dtype=np.float64)))
    i = np.arange(128, dtype=np.float64)
    kdec = (0.125 * np.exp(lg[None, :] * (127.0 - i)[:, None])).astype(np.float32)
    qdec = np.exp(lg[None, :] * (i[:, None] + 1.0)).astype(np.float32)
    dt = np.zeros((128, 8, 128), np.float64)
    for h in range(8):
        v = 0.125 * np.exp(-lg[h] * (i + 1.0))
        dt[:, h, :] = v[:, None] * (i[None, :] >= i[:, None])
    g128 = np.zeros((128, 4, 128), np.float32)
    for pr in range(4):
        for hb in range(2):
            g128[hb * 64:(hb + 1) * 64, pr, :] = np.exp(lg[2 * pr + hb] * 128.0)
    return cs, kdec, qdec, dt.astype(np.float32), g128


def _consts_sample():
    half = 32
    inv = (10000.0 ** (-np.arange(half, dtype=np.float32) / half)).astype(np.float32)
    tt_ = np.minimum(np.arange(128), 3)
    pos = (8192 + tt_).astype(np.float32)
    ang = (pos[:, None] * inv[None, :]).astype(np.float32)
    cs_s = np.concatenate([np.cos(ang), np.sin(ang)], axis=1).astype(np.float32)
    lg = np.log1p(-np.exp2(-5.0 - np.arange(8, dtype=np.float64)))
    p = np.arange(128, dtype=np.float64)
    real = (p < 4)
    kdec_s = (0.125 * np.exp(lg[None, :] * (3.0 - p)[:, None]) * real[:, None]).astype(np.float32)
    qdec_s = np.exp(lg[None, :] * (tt_.astype(np.float64)[:, None] + 1.0)).astype(np.float32)
    dt = np.zeros((128, 8, 128), np.float64)
    for h in range(8):
        v = 0.125 * np.exp(-lg[h] * (p + 1.0)) * real
        dt[:, h, :] = v[:, None] * (tt_[None, :] >= p[:, None])
    g4 = np.zeros((128, 4, 128), np.float32)
    for pr in range(4):
        for hb in range(2):
            g4[hb * 64:(hb + 1) * 64, pr, :] = np.exp(lg[2 * pr + hb] * 4.0)
    j = np.arange(512)[None, :]
    adm = j <= tt_[:, None]
    penS_s = np.where(adm, 0.0, -1.0e30).astype(np.float32)
    penM_s = np.where(adm, 0.0, -30000.0).astype(np.float32)
    return cs_s, kdec_s, qdec_s, dt.astype(np.float32), g4, penS_s, penM_s


def kernel(x_prompt, x_sample, cache_k, cache_v, cache_kidx, state_ret, page_table, c_prompt, c_sample,
           w_ada, b_ada, g_norm, w_in, g_ret, w_pa, w_pr, w_out, g_final):
    f = lambda a: np.ascontiguousarray(np.asarray(a, dtype=np.float32))
    x_prompt = f(x_prompt); w_in0 = f(w_in)[0]
    cols = np.concatenate([
        np.arange(512, 1024), np.arange(1024, 1536), np.arange(2884, 3396), np.arange(3396, 3908),
        np.arange(2304, 2368), np.arange(2304, 2368),
        np.arange(0, 512),
        np.concatenate([np.concatenate([np.arange(2048 + 64 * h, 2048 + 64 * (h + 1))] * 2) for h in range(4)]),
        np.arange(2304, 2372), np.arange(1536, 2048), np.arange(2372, 2884), np.arange(3908, 4420),
        np.arange(4420, 5444), np.arange(5444, 6468)])
    assert cols.shape[0] == NEXT
    ptile = lambda w: np.ascontiguousarray(w.reshape(w.shape[0] // 128, 128, w.shape[1]).transpose(1, 0, 2))
    wext = ptile(w_in0[:, cols])
    wpo = ptile(np.concatenate([f(w_pa)[0], f(w_pr)[0], f(w_out)[0]], axis=0))
    wada = ptile(f(w_ada)[0])
    badaT = np.ascontiguousarray(f(b_ada)[0].reshape(24, 128).T)
    bgate = np.ascontiguousarray(np.broadcast_to(f(b_ada)[0][2048:3072], (128, 1024)))
    gnT = np.ascontiguousarray(f(g_norm)[0].reshape(8, 128).T)
    gfin = np.ascontiguousarray(np.broadcast_to(f(g_final), (128, 1024)))
    gret = np.ascontiguousarray(np.broadcast_to(f(g_ret)[0], (128, 512)))
    cs, kdec, qdec, dtab, g128 = _consts()
    cs_s, kdec_s, qdec_s, dtab_s, g4, penS_s, penM_s = _consts_sample()
    x_sample = f(x_sample); state_ret = f(state_ret)
    ck2 = f(cache_k)[0].reshape(2560 * 128, 512); cv2 = f(cache_v)[0].reshape(2560 * 128, 512)
    cki2 = f(cache_kidx)[0].reshape(2560 * 128, 64)
    page_table = np.asarray(page_table).astype(np.int32)
    xzero = np.zeros((128, D), np.float32)
    sel_s = np.zeros((128, 4), np.float32); sel_s[:, 0] = 1.0
    pidx = np.arange(128, dtype=np.float32).reshape(128, 1)
    ident = np.eye(128, dtype=np.float32)
    iota = np.ascontiguousarray(np.broadcast_to(np.arange(512, dtype=np.float32), (128, 512)))
    c_prompt = f(c_prompt); c_sample = f(c_sample)
    in_maps = []
    for c in range(8):
        b, r = c // 4, c % 4
        own_tiles = np.array([4 * g + r for g in range(NSTEP)])
        rows = (own_tiles[:, None] * 128 + np.arange(128)[None, :]).reshape(-1)
        cvec = np.stack([c_prompt[b]] + [c_sample[4 * c + s] for s in range(4)], axis=1)
        cT = np.ascontiguousarray(cvec.reshape(8, 128, 5).transpose(1, 0, 2))
        sel = np.zeros((128, 4), np.float32); sel[:, r] = 1.0
        xsp = np.zeros((4, 128, D), np.float32)
        for s_ in range(4):
            xsp[s_, 0:4] = x_sample[4 * c + s_]
        qrel = (r * 128 + np.arange(128, dtype=np.float32)).reshape(128, 1)
        in_maps.append({
            "xall": x_prompt[b], "xown": np.ascontiguousarray(x_prompt[b][rows]),
            "wext": wext, "wpo": wpo, "wada": wada, "badaT": badaT, "bgate": bgate, "gnT": gnT, "gfin": gfin, "gret": gret,
            "cT": cT, "ident": ident, "cs_all": cs, "cs_own": np.ascontiguousarray(cs[rows]), "kdec": kdec, "qdec": qdec,
            "dtab": dtab, "g128": g128, "sel": sel, "iota512": iota, "qrel": qrel,
            "xsp": xsp, "xzero": xzero, "cache_k2": ck2, "cache_v2": cv2, "cache_ki2": cki2,
            "ptab": np.ascontiguousarray(page_table[4 * c:4 * c + 4]), "state_in": np.ascontiguousarray(state_ret[0, 4 * c:4 * c + 4]),
            "cs_s": cs_s, "kdec_s": kdec_s, "qdec_s": qdec_s, "dtab_s": dtab_s, "g4": g4, "sel_s": sel_s,
            "penS_s": penS_s, "penM_s": penM_s, "pidx": pidx,
        })
    nc = build_nc()
    resr = run_bass_kernel_spmd(nc, in_maps, core_ids=list(range(8)))
    R = resr.results
    y_prompt = np.zeros((2, SEQ, D), np.float32)
    k_rows = np.zeros((1, 2, SEQ, 8, 64), np.float32); v_rows = np.zeros((1, 2, SEQ, 8, 64), np.float32)
    ki_rows = np.zeros((1, 2, SEQ, 64), np.float32); st_p = np.zeros((1, 2, 8, 64, 64), np.float32)
    for c in range(8):
        b, r = c // 4, c % 4
        for g in range(NSTEP):
            t = 4 * g + r
            sl_ = slice(t * 128, (t + 1) * 128); so = slice(g * 128, (g + 1) * 128)
            y_prompt[b, sl_] = R[c]["y_o"][so]
            k_rows[0, b, sl_] = R[c]["k_o"][so].reshape(128, 8, 64)
            v_rows[0, b, sl_] = R[c]["v_o"][so].reshape(128, 8, 64)
            ki_rows[0, b, sl_] = R[c]["ki_o"][so]
        if r == 3:
            st_p[0, b] = R[c]["st_o"]
    y_sample = np.zeros((32, 4, D), np.float32)
    ks = np.zeros((1, 32, 4, 8, 64), np.float32); vs = np.zeros((1, 32, 4, 8, 64), np.float32)
    kis = np.zeros((1, 32, 4, 64), np.float32); sts = np.zeros((1, 32, 8, 64, 64), np.float32)
    for c in range(8):
        for s_ in range(4):
            bi = 4 * c + s_
            y_sample[bi] = R[c]["ys_o"][s_]
            ks[0, bi] = R[c]["ks_o"][s_].reshape(4, 8, 64); vs[0, bi] = R[c]["vs_o"][s_].reshape(4, 8, 64)
            kis[0, bi] = R[c]["kis_o"][s_]; sts[0, bi] = R[c]["sts_o"][s_]
    return (y_prompt, y_sample, k_rows, v_rows, ki_rows, st_p, ks, vs, kis, sts)
```

```python
import os
import numpy as np
from contextlib import ExitStack
import concourse.bass as bass
import concourse.mybir as mybir
from concourse.bass_utils import run_bass_kernel_spmd

F32 = mybir.dt.float32
BF16 = mybir.dt.bfloat16
AF = mybir.ActivationFunctionType
ALU = mybir.AluOpType
AX = mybir.AxisListType

D = 1024
SEQ = 8192
NSTEP = 16
NSTEPS_RUN = int(os.environ.get("MK_NSTEPS", "16"))
TOPK = 256
NEG_BIG = -3.0e38
C_KA, C_VA, C_KR, C_VR, C_KID = 0, 512, 1024, 1536, 2048
C_QA, C_QID, C_KIWI, C_ZA, C_QR, C_ZR, C_GA, C_GR = 2176, 2688, 3200, 3268, 3780, 4292, 4804, 5828
NEXT = 6852
ENGS = ("pe", "act", "dve", "pool", "sp")


class _Stop(Exception):
    pass


class _Stop2(Exception):
    pass


STOP = int(os.environ.get('MK_STOP', '99'))
PROJ_ONLY = os.environ.get('MK_FULL', '1') != '1'


def stop_at(k):
    if STOP == k:
        raise _Stop()


class Buf:
    __slots__ = ("w", "r")

    def __init__(self):
        self.w = None
        self.r = []


class Tl:
    def __init__(self, t):
        self.t = t
        self.b = Buf()

    def __getitem__(self, k):
        return self.t[k]


def _b(x):
    return x.b if hasattr(x, "b") else x


class Sched:
    def __init__(self, nc, stack):
        self.nc = nc
        self.stack = stack
        self.ops = {e: [] for e in ENGS}
        self.cnt = {e: 0 for e in ENGS}
        self.esem = {e: stack.enter_context(nc.semaphore("prog_" + e)) for e in ENGS if e != "sp"}
        self.dsem = {}
        self.dcnt = {}
        self.waited = {}
        self.same = {"pe": False, "act": True, "dve": True, "pool": True, "sp": False}

    def _collect(self, eng, reads, writes):
        deps = []
        for b in reads:
            if b.w is not None:
                deps.append(b.w)
        for b in writes:
            if b.w is not None:
                deps.append(b.w)
            deps.extend(b.r)
        waits = []
        for d in deps:
            if d[0] == "e":
                if d[1] == eng and not self.same[eng]:
                    continue
                key = (eng, "e", d[1])
                val = d[2]
                sem = self.esem[d[1]]
            else:
                key = (eng, "d", d[1])
                val = self.dcnt[d[1]]
                sem = self.dsem[d[1]]
            if self.waited.get(key, 0) >= val:
                continue
            self.waited[key] = val
            waits.append((sem, val))
        return waits

    def _update(self, d, reads, writes):
        for b in reads:
            b.r.append(d)
            if len(b.r) > 64:
                b.r = b.r[-64:] if False else b.r
        for b in writes:
            b.w = d
            b.r = []

    def op(self, eng, fn, reads=(), writes=()):
        reads = [_b(x) for x in reads]
        writes = [_b(x) for x in writes]
        waits = self._collect(eng, reads, writes)
        self.cnt[eng] += 1
        idx = self.cnt[eng]
        sem = self.esem[eng]

        def thunk(e, waits=waits, fn=fn, sem=sem):
            for ws, wv in waits:
                e.wait_ge(ws, wv)
            fn(e).then_inc(sem, 1)

        self.ops[eng].append(thunk)
        self._update(("e", eng, idx), reads, writes)

    def dma(self, semkey, fn, reads=(), writes=(), q="sp"):
        reads = [_b(x) for x in reads]
        writes = [_b(x) for x in writes]
        if semkey not in self.dsem:
            self.dsem[semkey] = self.stack.enter_context(self.nc.semaphore("d_" + semkey))
            self.dcnt[semkey] = 0
        waits = self._collect(q, reads, writes)
        self.dcnt[semkey] += 16
        sem = self.dsem[semkey]

        def thunk(e, waits=waits, fn=fn, sem=sem):
            for ws, wv in waits:
                e.wait_ge(ws, wv)
            fn(e).then_inc(sem, 16)

        self.ops[q].append(thunk)
        self._update(("d", semkey), reads, writes)

    def barrier(self):
        fin = [(sem, self.dcnt[k]) for k, sem in self.dsem.items() if self.dcnt[k] > 0]
        fin += [(sem, self.cnt[e]) for e, sem in self.esem.items() if self.cnt[e] > 0]
        for q in ENGS:
            def thunk(e, fin=fin):
                for ws, wv in fin:
                    e.wait_ge(ws, wv)
            self.ops[q].append(thunk)

    def finish(self, q="sp"):
        fin = [(sem, self.dcnt[k]) for k, sem in self.dsem.items()]
        fin += [(sem, self.cnt[e]) for e, sem in self.esem.items() if self.cnt[e] > 0]

        def thunk(e, fin=fin):
            for ws, wv in fin:
                e.wait_ge(ws, wv)

        self.ops[q].append(thunk)

    def emit(self):
        nc = self.nc
        ops = self.ops
        with nc.Block() as block:
            @block.tensor
            def _(e):
                for t in ops["pe"]:
                    t(e)

            @block.scalar
            def _(e):
                for t in ops["act"]:
                    t(e)

            @block.vector
            def _(e):
                for t in ops["dve"]:
                    t(e)

            @block.gpsimd
            def _(e):
                for t in ops["pool"]:
                    t(e)

            @block.sync
            def _(e):
                for t in ops["sp"]:
                    t(e)


def build_nc():
    nc = bass.Bass("TRN2", target_bir_lowering=False)
    din = lambda n, s, dt=F32: nc.dram_tensor(n, list(s), dt, kind="ExternalInput").ap()
    dout = lambda n, s, dt=F32: nc.dram_tensor(n, list(s), dt, kind="ExternalOutput").ap()
    dscr = lambda n, s, dt: nc.dram_tensor(n, list(s), dt, kind="Internal").ap()
    xall = din("xall", [SEQ, D]); xown = din("xown", [NSTEP * 128, D])
    wext = din("wext", [128, 8, NEXT]); wpo = din("wpo", [128, 16, 1024]); wada = din("wada", [128, 8, 3072])
    badaT = din("badaT", [128, 24]); bgate = din("bgate", [128, 1024]); gnT = din("gnT", [128, 8])
    gfin = din("gfin", [128, 1024]); gret = din("gret", [128, 512]); cT = din("cT", [128, 8, 5])
    ident_d = din("ident", [128, 128]); cs_all = din("cs_all", [SEQ, 64]); cs_own = din("cs_own", [NSTEP * 128, 64])
    kdec_d = din("kdec", [128, 8]); qdec_d = din("qdec", [128, 8]); dtab_d = din("dtab", [128, 8, 128])
    g128_d = din("g128", [128, 4, 128]); sel_d = din("sel", [128, 4]); iota_d = din("iota512", [128, 512])
    qrel_d = din("qrel", [128, 1])
    I32 = mybir.dt.int32
    xsp = din("xsp", [4, 128, D]); xzero = din("xzero", [128, D])
    cache_k_d = din("cache_k2", [2560 * 128, 512]); cache_v_d = din("cache_v2", [2560 * 128, 512]); cache_ki_d = din("cache_ki2", [2560 * 128, 64])
    ptab = din("ptab", [4, 64], I32); state_in = din("state_in", [4, 8, 64, 64])
    cs_s_d = din("cs_s", [128, 64]); kdec_s_d = din("kdec_s", [128, 8]); qdec_s_d = din("qdec_s", [128, 8])
    dtab_s_d = din("dtab_s", [128, 8, 128]); g4_d = din("g4", [128, 4, 128]); sel_s_d = din("sel_s", [128, 4])
    penS_s_d = din("penS_s", [128, 512]); penM_s_d = din("penM_s", [128, 512]); pidx_d = din("pidx", [128, 1])
    ys_o = dout("ys_o", [4, 4, D]); ks_o = dout("ks_o", [4, 4, 512]); vs_o = dout("vs_o", [4, 4, 512])
    kis_o = dout("kis_o", [4, 4, 64]); sts_o = dout("sts_o", [4, 8, 64, 64])
    gate5_d = dscr("gate5_d", [5, D], F32)
    y_o = dout("y_o", [NSTEP * 128, D]); k_o = dout("k_o", [NSTEP * 128, 512]); v_o = dout("v_o", [NSTEP * 128, 512])
    ki_o = dout("ki_o", [NSTEP * 128, 64]); st_o = dout("st_o", [8, 64, 64])
    wbf = dscr("wbf", [128, 8, NEXT], BF16); wpobf = dscr("wpobf", [128, 16, 1024], BF16)
    kscr = dscr("kscr", [128, 4, SEQ + 512], BF16); vscr = dscr("vscr", [SEQ // 128 + 4, 128, 520], BF16)

    with ExitStack() as st:
        S = Sched(nc, st)
        sb = lambda n, s, dt=F32: Tl(st.enter_context(nc.sbuf_tensor("s_" + n, list(s), dt)))
        psb = lambda n, s, dt=F32: Tl(st.enter_context(nc.psum_tensor("p_" + n, list(s), dt)))
        ACT = lambda fn, r=(), w=(): S.op("act", fn, r, w)
        POOL = lambda fn, r=(), w=(): S.op("pool", fn, r, w)
        DVE = lambda fn, r=(), w=(): S.op("dve", fn, r, w)
        PE = lambda fn, r=(), w=(): S.op("pe", fn, r, w)

        G = [psb("G%d" % i, [128, 512]) for i in range(2)]
        A = [psb("A%d" % i, [128, 512]) for i in range(4)]
        TP = psb("TP", [128, 8, 128], BF16)
        XP = psb("XP", [128, 512])
        gi = [0]

        def nextG():
            gi[0] ^= 1
            return G[gi[0]]

        kscr_b = [Buf() for _ in range(NSTEP + 1)]; vscr_b = [Buf() for _ in range(NSTEP + 1)]
        wbf_b = Buf(); wpobf_b = Buf()
        kiT = sb("kiT", [128, 4608], BF16)
        wsl = [sb("wsl%d" % i, [128, 4096], BF16) for i in range(2)]
        wi_ = [0]
        hT = sb("hT", [128, 8, 640], BF16)
        xo = [sb("xo%d" % i, [128, D]) for i in range(2)]
        xn = sb("xn", [128, D], BF16); sqj = xn
        ssq = sb("ssq", [128, 4]); epsT = sb("epsT", [128, 1])
        kst = sb("kst", [128, 4, 512], BF16); vst = sb("vst", [128, 4, 8, 65], BF16)
        ko = sb("ko", [128, 512]); vo = sb("vo", [128, 512]); kio = sb("kio", [128, 68])
        ev = [sb("ev%d" % i, [128, 512]) for i in range(2)]
        evi = [0]
        rt = [sb("rt%d" % i, [128, 8, 32]) for i in range(4)]
        krd = sb("krd", [128, 4, 512], BF16); vrb = sb("vrb", [128, 5, 512], BF16)
        krown = sb("krown", [128, 512], BF16); KrT = sb("KrT", [128, 4, 128], BF16)
        qp = sb("qp", [128, 512], BF16); QpT = sb("QpT", [128, 8, 128], BF16)
        Sst = sb("Sst", [128, 4, 128]); ownS = sb("ownS", [128, 4, 128])
        ownSb = sb("ownSb", [128, 8, 64], BF16)
        QT = [sb("QT%d" % i, [128, 8, 128], BF16) for i in range(2)]
        qiT2 = [sb("qiTlo", [128, 4, 128], BF16), sb("qiThi", [128, 4, 128], BF16)]
        wsc = [sb("wsc%d" % i, [128, 4]) for i in range(2)]
        siluza = [sb("siluza%d" % i, [128, 512], BF16) for i in range(2)]
        sga = [sb("sga%d" % i, [128, 1024], BF16) for i in range(2)]
        sgr = sb("sgr", [128, 1024], BF16)
        mr = [sb("mr%d" % i, [128, 1024], BF16) for i in range(2)]
        siluzr = sb("siluzr", [128, 512], BF16)
        rl = [sb("rl%d" % i, [128, 512]) for i in range(2)]
        rli = [0]
        eti = [0]
        dtab = sb("dtab", [128, 8, 128]); g128 = sb("g128", [128, 4, 128])
        cs5 = sb("cs5", [128, 5, 64])
        identf = sb("identf", [128, 128]); identb = sb("identb", [128, 128], BF16)
        ident4 = sb("ident4", [128, 4, 128], BF16); zl = sb("zl", [128, 128], BF16)
        ATb = sb("ATb", [128, 8, 128], BF16)
        oret = sb("oret", [128, 8, 64]); osq = sb("osq", [128, 8, 64])
        gst = sb("gst", [128, 8, 8])
        yrb = sb("yrb", [128, 512], BF16); yrT = sb("yrT", [128, 4, 128], BF16)
        acc = sb("acc", [128, 8, 65]); rden = sb("rden", [128, 8, 2])
        yaf = oret; yab = sb("yab", [128, 512], BF16); yaT = sb("yaT", [128, 4, 128], BF16)
        mg = sb("mg", [128, 1024], BF16); mT = sb("mT", [128, 8, 128], BF16)
        res = sb("res", [128, D]); yout = res; xs = [res] * 2
        gate_bc = sb("gate_bc", [128, D]); gfin_bc = sb("gfin_bc", [128, D]); gret_bc = sb("gret_bc", [128, 512])
        penS = sb("penS", [128, 512]); penM = sb("penM", [128, 512], BF16)
        kdec = sb("kdec", [128, 8]); qdec = sb("qdec", [128, 8]); sel = sb("sel", [128, 4])
        AcA = sb("AcA", [128, 5, 8]); BcA = sb("BcA", [128, 5, 8]); cur = {"j": 0}
        ptb = sb("ptb", [128, 64], I32); idxa = sb("idxa", [128, 64], I32); pidx = sb("pidx", [128, 1])
        m8 = sb("m8", [128, 8])

        def ld(key, dst_ap, src_ap, w, r=()):
            S.dma(key, lambda e: e.dma_start(out=dst_ap, in_=src_ap), reads=r, writes=w)

        def stq(key, dst_ap, src_ap, r, w=()):
            S.dma(key, lambda e: e.dma_start(out=dst_ap, in_=src_ap), reads=r, writes=w)

        def mm(out, lhsT, rhs, start, stop, r, w):
            PE(lambda e: e.matmul(out=out, lhsT=lhsT, rhs=rhs, start=start, stop=stop), r, w)

        def act(out, in_, func, r, w, **kw):
            ACT(lambda e: e.activation(out=out, in_=in_, func=func, **kw), r, w)

        def tt(eng, out, in0, in1, op, r, w):
            S.op(eng, lambda e: e.tensor_tensor(out=out, in0=in0, in1=in1, op=op), r, w)

        def ts(eng, out, in0, s1, s2, op0, op1, r, w):
            if op1 is None:
                S.op(eng, lambda e: e.tensor_scalar(out=out, in0=in0, scalar1=s1, scalar2=None, op0=op0), r, w)
            else:
                S.op(eng, lambda e: e.tensor_scalar(out=out, in0=in0, scalar1=s1, scalar2=s2, op0=op0, op1=op1), r, w)

        def stt(eng, out, in0, scalar, in1, op0, op1, r, w):
            S.op(eng, lambda e: e.scalar_tensor_tensor(out=out, in0=in0, scalar=scalar, in1=in1, op0=op0, op1=op1), r, w)

        def load_w(col0, ncols):
            sl = wsl[wi_[0] % 2]; wi_[0] += 1
            v = sl.t[:, 0:8 * ncols].rearrange("p (k c) -> p k c", k=8)
            ld("w%d" % ((wi_[0] - 1) % 2), v, wbf[:, :, col0:col0 + ncols], [sl], [wbf_b])
            return sl, v

        def load_wpo(kt0, nk, c0, ncols):
            sl = wsl[wi_[0] % 2]; wi_[0] += 1
            v = sl.t[:, 0:nk * ncols].rearrange("p (k c) -> p k c", k=nk)
            ld("w%d" % ((wi_[0] - 1) % 2), v, wpobf[:, kt0:kt0 + nk, c0:c0 + ncols], [sl], [wpobf_b])
            return sl, v

        def rstd_chain(src_tl, src_ap, n, col):
            act(sqj[:, 0:n], src_ap, AF.Square, [src_tl], [sqj, ssq], accum_out=ssq[:, 3:4])
            act(ssq[:, 2:3], ssq[:, 3:4], AF.Ln, [ssq], [ssq], scale=1.0 / n, bias=epsT[:, 0:1])
            act(ssq[:, col:col + 1], ssq[:, 2:3], AF.Exp, [ssq], [ssq], scale=-0.5)

        def make_hT(xt, col0):
            rstd_chain(xt, xt[:, :], D, 0)
            act(xn[:, :], xt[:, :], AF.Identity, [xt, ssq], [xn], scale=ssq[:, 0:1])
            for kt in range(8):
                PE(lambda e, kt=kt: e.transpose(out=TP[:, kt, :], in_=xn[:, kt * 128:(kt + 1) * 128], identity=identb[:, :]),
                   [xn, identb], [TP])
            for kt in range(8):
                jj = cur["j"]
                act(hT[:, kt, col0:col0 + 128], TP[:, kt, :], AF.Identity, [TP, AcA, BcA], [hT],
                    scale=AcA[:, jj, kt:kt + 1], bias=BcA[:, jj, kt:kt + 1])

        def rotary(src, cs_ap, dst_ap, dst_tl):
            sv = src.t[:, :].rearrange("p (h t f) -> p h t f", h=8, t=2)
            dv = dst_ap.rearrange("p (h t f) -> p h t f", h=8, t=2)
            cosb = cs_ap[:, 0:32].unsqueeze(1).to_broadcast([128, 8, 32])
            sinb = cs_ap[:, 32:64].unsqueeze(1).to_broadcast([128, 8, 32])
            x1 = sv[:, :, 0, :]; x2 = sv[:, :, 1, :]
            tt("pool", rt[0][:, :, :], x1, cosb, ALU.mult, [src, cs5], [rt[0]])
            tt("pool", rt[1][:, :, :], x2, sinb, ALU.mult, [src, cs5], [rt[1]])
            tt("pool", dv[:, :, 0, :], rt[0][:, :, :], rt[1][:, :, :], ALU.subtract, [rt[0], rt[1]], [dst_tl])
            tt("pool", rt[2][:, :, :], x1, sinb, ALU.mult, [src, cs5], [rt[2]])
            tt("pool", rt[3][:, :, :], x2, cosb, ALU.mult, [src, cs5], [rt[3]])
            tt("pool", dv[:, :, 1, :], rt[2][:, :, :], rt[3][:, :, :], ALU.add, [rt[2], rt[3]], [dst_tl])

        def next_ev():
            evi[0] += 1
            return ev[evi[0] % 2]

        try:
            st2 = ExitStack()
            sb2 = lambda n, s_, dt=F32: Tl(st2.enter_context(nc.sbuf_tensor("s_" + n, list(s_), dt)))
            bgate_sb = sb2("bgate_sb", [128, D])
            iota = sb2("iota", [128, 512]); qrel = sb2("qrel", [128, 1])
            cTs = sb2("cTs", [128, 8, 5]); scT = sb2("scT", [128, 8, 5]); gate5 = sb2("gate5", [5, D])
            modT = sb2("modT", [128, 16, 5]); badaTs = sb2("badaTs", [128, 24]); gnTs = sb2("gnTs", [128, 8])
            wadas = [sb2("wadas%d" % i, [128, 8, 256]) for i in range(2)]
            stgf = [sb2("stgf%d" % i, [128, 1024]) for i in range(2)]
            stgb = [sb2("stgb%d" % i, [128, 1024], BF16) for i in range(2)]
            ld("c0", identf[:, :], ident_d, [identf]); ld("c0", dtab[:, :, :], dtab_d, [dtab]); ld("c0", g128[:, :, :], g128_d, [g128])
            ld("c0", kdec[:, :], kdec_d, [kdec]); ld("c0", qdec[:, :], qdec_d, [qdec]); ld("c0", sel[:, :], sel_d, [sel])
            ld("c0", iota[:, :], iota_d, [iota]); ld("c0", qrel[:, :], qrel_d, [qrel]); ld("c0", cTs[:, :, :], cT, [cTs])
            ld("c0", badaTs[:, :], badaT, [badaTs]); ld("c0", gnTs[:, :], gnT, [gnTs]); ld("c0", bgate_sb[:, :], bgate, [bgate_sb])
            ld("c0", gfin_bc[:, :], gfin, [gfin_bc]); ld("c0", gret_bc[:, :], gret, [gret_bc])
            POOL(lambda e: e.tensor_copy(out=identb[:, :], in_=identf[:, :]), [identf], [identb])
            for a4 in range(4):
                POOL(lambda e, a4=a4: e.tensor_copy(out=ident4[:, a4, :], in_=identf[:, :]), [identf], [ident4])
            POOL(lambda e: e.memset(zl[:, :], 0.0), [], [zl])
            POOL(lambda e: e.memset(epsT[:, :], 1e-6), [], [epsT])
            POOL(lambda e: e.memset(vst[:, :, :, 64:65], 1.0), [], [vst])
            POOL(lambda e: e.memset(Sst[:, :, :], 0.0), [], [Sst])
            POOL(lambda e: e.memset(ownSb[:, :, :], 0.0), [], [ownSb])
            POOL(lambda e: e.memset(QpT[:, :, :], 0.0), [], [QpT])
            POOL(lambda e: e.memset(kiT[:, :], 0.0), [], [kiT])
            for q_ in qiT2:
                POOL(lambda e, q_=q_: e.memset(q_[:, :, :], 0.0), [], [q_])
            for q_ in QT:
                POOL(lambda e, q_=q_: e.memset(q_[:, :, :], 0.0), [], [q_])
            ts("pool", penS[:, :], iota[:, :], qrel[:, 0:1], -1.0e30, ALU.is_gt, ALU.mult, [iota, qrel], [penS])
            ts("pool", penM[:, :], iota[:, :], qrel[:, 0:1], -30000.0, ALU.is_gt, ALU.mult, [iota, qrel], [penM])
            act(scT[:, :, :], cTs[:, :, :], AF.Silu, [cTs], [scT])
            for cc in range(12):
                wa = wadas[cc % 2]
                ld("wa%d" % (cc % 2), wa[:, :, :], wada[:, :, cc * 256:(cc + 1) * 256], [wa])
                if cc < 8:
                    for t2 in range(2):
                        ct = cc * 2 + t2
                        for kt in range(8):
                            mm(XP[:, ct * 5:ct * 5 + 5], wa[:, kt, t2 * 128:(t2 + 1) * 128], scT[:, kt, :], kt == 0, kt == 7,
                               [wa, scT], [XP])
                else:
                    gch = cc - 8
                    g_ = G[gch // 2]
                    for kt in range(8):
                        mm(g_[0:5, (gch % 2) * 256:(gch % 2) * 256 + 256], scT[:, kt, :], wa[:, kt, :], kt == 0, kt == 7,
                           [wa, scT], [g_])
            act(modT[:, :, :], XP[:, 0:80].rearrange("p (c j) -> p c j", j=5), AF.Identity, [XP], [modT])
            for j in range(5):
                tt("pool", BcA[:, j, :], modT[:, 0:8, j], badaTs[:, 0:8], ALU.add, [modT, badaTs], [BcA])
                ts("pool", AcA[:, j, :], modT[:, 8:16, j], 1.0, None, ALU.add, None, [modT], [AcA])
                tt("pool", AcA[:, j, :], AcA[:, j, :], badaTs[:, 8:16], ALU.add, [AcA, badaTs], [AcA])
                tt("pool", AcA[:, j, :], AcA[:, j, :], gnTs[:, :], ALU.mult, [AcA, gnTs], [AcA])
            for hf in range(2):
                act(gate5[:, hf * 512:(hf + 1) * 512], G[hf][0:5, :], AF.Identity, [G[hf]], [gate5])
            tt("pool", gate5[:, :], gate5[:, :], bgate_sb[0:5, :], ALU.add, [gate5, bgate_sb], [gate5])
            gate5_b = Buf()
            stq("g5", gate5_d, gate5[:, :], [gate5], [gate5_b])
            ld("g5l", gate_bc[:, :], gate5_d[0:1, :].partition_broadcast(128), [gate_bc], [gate5_b])
            ld("c0", pidx[:, :], pidx_d, [pidx])
            stop_at(1)
            nch = (NEXT + 127) // 128
            for c in range(nch):
                c0 = c * 128; n = min(128, NEXT - c0); i = c % 2
                fv = stgf[i].t[:, 0:8 * n].rearrange("p (k c) -> p k c", k=8)
                bv = stgb[i].t[:, 0:8 * n].rearrange("p (k c) -> p k c", k=8)
                ld("ps%d" % i, fv, wext[:, :, c0:c0 + n], [stgf[i]])
                if c % 2 == 0:
                    POOL(lambda e, fv=fv, bv=bv: e.tensor_copy(out=bv, in_=fv), [stgf[i]], [stgb[i]])
                else:
                    act(bv, fv, AF.Identity, [stgf[i]], [stgb[i]])
                stq("pt%d" % i, wbf[:, :, c0:c0 + n], bv, [stgb[i]], [wbf_b])
            for c in range(16):
                c0 = c * 64; i = c % 2
                fv = stgf[i].t[:, :].rearrange("p (k c) -> p k c", k=16)
                bv = stgb[i].t[:, :].rearrange("p (k c) -> p k c", k=16)
                ld("ps%d" % i, fv, wpo[:, :, c0:c0 + 64], [stgf[i]])
                if c % 2 == 0:
                    POOL(lambda e, fv=fv, bv=bv: e.tensor_copy(out=bv, in_=fv), [stgf[i]], [stgb[i]])
                else:
                    act(bv, fv, AF.Identity, [stgf[i]], [stgb[i]])
                stq("pt%d" % i, wpobf[:, :, c0:c0 + 64], bv, [stgb[i]], [wpobf_b])

            stop_at(2)
            S.barrier()
            st2.close()
            work = [sb("work0", [128, SEQ + 512])] * 2
            nm = [sb("nm%d" % i, [128, 512], BF16) for i in range(2)]
            kc = [sb("kc%d" % i, [128, 4, 512], BF16) for i in range(2)]
            vc = [sb("vc%d" % i, [128, 4, 520], BF16) for i in range(2)]
            ET = [sb("ET%d" % i, [128, 512], BF16) for i in range(2)]

            def projscores(g, smp=None):
                sl = g % 2
                L = (g + 1) * 512
                if smp is None:
                    ld("cs", cs5[:, 0:4, :], cs_all[g * 512:(g + 1) * 512, :].rearrange("(t p) c -> p t c", p=128), [cs5])
                    ld("cs", cs5[:, 4, :], cs_own[g * 128:(g + 1) * 128, :], [cs5])
                else:
                    for j5 in range(5):
                        ld("cs", cs5[:, j5, :], cs_s_d, [cs5])
                for j in range(4):
                    xt = xs[j % 2]
                    if smp is None:
                        ld("x0", xt[:, :], xall[(4 * g + j) * 128:(4 * g + j + 1) * 128, :], [xt])
                    else:
                        ld("x0", xt[:, :], xsp[smp, :, :] if j == 0 else xzero, [xt])
                    make_hT(xt, j * 128)
                ld("xo%d" % sl, xo[sl][:, :], xown[g * 128:(g + 1) * 128, :] if smp is None else xsp[smp, :, :], [xo[sl]])
                make_hT(xo[sl], 512)
                stop_at(31)
                wt, wv = load_w(C_KA, 512)
                for pr in range(4):
                    g_ = nextG()
                    for kt in range(8):
                        mm(g_[:, :], wv[:, kt, pr * 128:(pr + 1) * 128], hT[:, kt, 0:512], kt == 0, kt == 7, [wt, hT], [g_])
                    act(kst[:, pr, :], g_[:, :], AF.Identity, [g_], [kst])
                stq("kst", kscr[:, :, g * 512:(g + 1) * 512], kst[:, :, :], [kst], [kscr_b[g]])
                g_ = nextG()
                for kt in range(8):
                    mm(g_[:, :], hT[:, kt, 512:640], wv[:, kt, :], kt == 0, kt == 7, [wt, hT], [g_])
                act(ko[:, :], g_[:, :], AF.Identity, [g_], [ko])
                if smp is None:
                    stq("ko", k_o[g * 128:(g + 1) * 128, :], ko[:, :], [ko])
                else:
                    stq("ko", ks_o[smp, :, :], ko[0:4, :], [ko])
                stop_at(32)
                wt, wv = load_w(C_VA, 512)
                for j in range(5):
                    g_ = nextG()
                    for kt in range(8):
                        mm(g_[:, :], hT[:, kt, j * 128:(j + 1) * 128], wv[:, kt, :], kt == 0, kt == 7, [wt, hT], [g_])
                    if j < 4:
                        act(vst[:, j, :, 0:64], g_[:, :].rearrange("p (h d) -> p h d", h=8), AF.Identity, [g_], [vst])
                    else:
                        act(vo[:, :], g_[:, :], AF.Identity, [g_], [vo])
                stq("vst", vscr[4 * g:4 * g + 4, :, :].rearrange("t p c -> p t c"), vst[:, :, :, :].rearrange("p t h c -> p t (h c)"),
                    [vst], [vscr_b[g]])
                if smp is None:
                    stq("vo", v_o[g * 128:(g + 1) * 128, :], vo[:, :], [vo])
                else:
                    stq("vo", vs_o[smp, :, :], vo[0:4, :], [vo])
                stop_at(33)
                wt, wv = load_w(C_KR, 512)
                for j in range(5):
                    g_ = nextG()
                    for kt in range(8):
                        mm(g_[:, :], hT[:, kt, j * 128:(j + 1) * 128], wv[:, kt, :], kt == 0, kt == 7, [wt, hT], [g_])
                    e_ = next_ev()
                    if j < 4:
                        for h in range(8):
                            act(e_[:, h * 64:(h + 1) * 64], g_[:, h * 64:(h + 1) * 64], AF.Identity, [g_, kdec], [e_], scale=kdec[:, h:h + 1])
                        rotary(e_, cs5[:, j, :], krd[:, j, :], krd)
                    else:
                        act(e_[:, :], g_[:, :], AF.Identity, [g_], [e_])
                        rotary(e_, cs5[:, 4, :], krown[:, :], krown)
                stop_at(34)
                wt, wv = load_w(C_VR, 512)
                for j in range(5):
                    g_ = nextG()
                    for kt in range(8):
                        mm(g_[:, :], hT[:, kt, j * 128:(j + 1) * 128], wv[:, kt, :], kt == 0, kt == 7, [wt, hT], [g_])
                    act(vrb[:, j, :], g_[:, :], AF.Identity, [g_], [vrb])
                stop_at(35)
                for j in range(4 if smp is None else 1):
                    if j == 0:
                        ts("pool", ownS[:, :, :], Sst[:, :, :], sel[:, 0:1], None, ALU.mult, None, [Sst, sel], [ownS])
                    else:
                        tS = next_ev()
                        tSv = tS.t[:, :].rearrange("p (a b) -> p a b", a=4)
                        ts("pool", tSv, Sst[:, :, :], sel[:, j:j + 1], None, ALU.mult, None, [Sst, sel], [tS])
                        tt("pool", ownS[:, :, :], ownS[:, :, :], tSv, ALU.add, [ownS, tS], [ownS])
                    for pr in range(4):
                        mm(XP[:, pr * 128:(pr + 1) * 128], krd[:, j, pr * 128:(pr + 1) * 128], vrb[:, j, pr * 128:(pr + 1) * 128], True, True,
                           [krd, vrb], [XP])
                    tmpS = next_ev()
                    act(tmpS[:, :], XP[:, :], AF.Identity, [XP], [tmpS])
                    tt("pool", Sst[:, :, :], Sst[:, :, :], g128[:, :, :], ALU.mult, [Sst, g128], [Sst])
                    tt("pool", Sst[:, :, :], Sst[:, :, :], tmpS[:, :].rearrange("p (a b) -> p a b", a=4), ALU.add, [Sst, tmpS], [Sst])
                for hb in range(2):
                    POOL(lambda e, hb=hb: e.tensor_copy(out=ownSb[hb * 64:(hb + 1) * 64, hb::2, :],
                                                        in_=ownS[hb * 64:(hb + 1) * 64, :, hb * 64:(hb + 1) * 64]), [ownS], [ownSb])
                stop_at(36)
                wt, wv = load_w(C_KID, 512)
                g_ = nextG()
                for kt in range(8):
                    mm(g_[:, :], wv[:, kt, 0:128], hT[:, kt, 0:512], kt == 0, kt == 7, [wt, hT], [g_])
                pb = g // 9
                cc = g % 9
                if os.environ.get('MK_V', '') != 'noki':
                    act(kiT[pb * 64:(pb + 1) * 64, cc * 512:(cc + 1) * 512], g_[pb * 64:(pb + 1) * 64, :], AF.Identity, [g_], [kiT])
                stop_at(37)
                wt, wv = load_w(C_QA, 512)
                g_ = nextG()
                for pr in range(4):
                    for kt in range(8):
                        mm(g_[:, pr * 128:(pr + 1) * 128], wv[:, kt, pr * 128:(pr + 1) * 128], hT[:, kt, 512:640], kt == 0, kt == 7, [wt, hT], [g_])
                for hb in range(2):
                    act(QT[sl][hb * 64:(hb + 1) * 64, hb::2, :], g_[hb * 64:(hb + 1) * 64, :].rearrange("p (a q) -> p a q", a=4), AF.Identity,
                        [g_], [QT[sl]])
                wt, wv = load_w(C_QID, 512)
                g_ = nextG()
                for h in range(4):
                    for kt in range(8):
                        mm(g_[:, h * 128:(h + 1) * 128], wv[:, kt, h * 128:(h + 1) * 128], hT[:, kt, 512:640], kt == 0, kt == 7, [wt, hT], [g_])
                for hb in range(2):
                    act(qiT2[hb][hb * 64:(hb + 1) * 64, :, :], g_[hb * 64:(hb + 1) * 64, :].rearrange("p (a q) -> p a q", a=4), AF.Identity,
                        [g_], [qiT2[hb]])
                wt, wv = load_w(C_KIWI, 512)
                g_ = nextG()
                for kt in range(8):
                    mm(g_[:, 0:68], hT[:, kt, 512:640], wv[:, kt, 0:68], kt == 0, kt == 7, [wt, hT], [g_])
                act(kio[:, :], g_[:, 0:68], AF.Identity, [g_], [kio])
                if smp is None:
                    stq("kio", ki_o[g * 128:(g + 1) * 128, :], kio[:, 0:64], [kio])
                else:
                    stq("kio", kis_o[smp, :, :], kio[0:4, 0:64], [kio])
                ts("pool", wsc[sl][:, :], kio[:, 64:68], 1.0 / 16.0, None, ALU.mult, None, [kio], [wsc[sl]])
                if PROJ_ONLY:
                    return
                wt, wv = load_w(C_ZA, 512)
                g_ = nextG()
                for kt in range(8):
                    mm(g_[:, :], hT[:, kt, 512:640], wv[:, kt, :], kt == 0, kt == 7, [wt, hT], [g_])
                act(siluza[sl][:, :], g_[:, :], AF.Silu, [g_], [siluza[sl]])
                wt, wv = load_w(C_QR, 512)
                g_ = nextG()
                for kt in range(8):
                    mm(g_[:, :], hT[:, kt, 512:640], wv[:, kt, :], kt == 0, kt == 7, [wt, hT], [g_])
                e_ = next_ev()
                for h in range(8):
                    act(e_[:, h * 64:(h + 1) * 64], g_[:, h * 64:(h + 1) * 64], AF.Identity, [g_, qdec], [e_], scale=qdec[:, h:h + 1])
                rotary(e_, cs5[:, 4, :], qp[:, :], qp)
                for pr in range(4):
                    PE(lambda e, pr=pr: e.transpose(out=TP[:, pr, :], in_=qp[:, pr * 128:(pr + 1) * 128], identity=identb[:, :]), [qp, identb], [TP])
                for hb in range(2):
                    act(QpT[hb * 64:(hb + 1) * 64, hb::2, :], TP[hb * 64:(hb + 1) * 64, 0:4, :], AF.Identity, [TP], [QpT])
                for pr in range(4):
                    PE(lambda e, pr=pr: e.transpose(out=TP[:, 4 + pr, :], in_=krown[:, pr * 128:(pr + 1) * 128], identity=identb[:, :]), [krown, identb], [TP])
                act(KrT[:, :, :], TP[:, 4:8, :], AF.Identity, [TP], [KrT])
                wt, wv = load_w(C_ZR, 512)
                g_ = nextG()
                for kt in range(8):
                    mm(g_[:, :], hT[:, kt, 512:640], wv[:, kt, :], kt == 0, kt == 7, [wt, hT], [g_])
                act(siluzr[:, :], g_[:, :], AF.Silu, [g_], [siluzr])
                for ci, (c0, dst) in enumerate([(C_GA, sga[sl]), (C_GA + 512, sga[sl]), (C_GR, sgr), (C_GR + 512, sgr)]):
                    wt, wv = load_w(c0, 512)
                    g_ = nextG()
                    for kt in range(8):
                        mm(g_[:, :], hT[:, kt, 512:640], wv[:, kt, :], kt == 0, kt == 7, [wt, hT], [g_])
                    act(dst[:, (ci % 2) * 512:(ci % 2 + 1) * 512], g_[:, :], AF.Sigmoid, [g_], [dst])
                stop_at(38)
                for h in range(8):
                    pr, hb = h // 2, h % 2
                    a_ = A[h // 4]
                    mm(a_[:, (h % 4) * 128:(h % 4 + 1) * 128], KrT[:, pr, :], QpT[:, h, :], True, True,
                       [KrT, QpT], [a_])
                stop_at(381)
                for hf in range(2):
                    e_ = next_ev()
                    act(e_[:, :], A[hf][:, :], AF.Identity, [A[hf]], [e_])
                    tt("pool", ATb[:, hf * 4:(hf + 1) * 4, :], e_[:, :].rearrange("p (a q) -> p a q", a=4), dtab[:, hf * 4:(hf + 1) * 4, :], ALU.mult,
                       [e_, dtab], [ATb])
                stop_at(382)
                for h in range(8):
                    pr, hb = h // 2, h % 2
                    mm(A[2][:, h * 64:(h + 1) * 64], ATb[:, h, :], vrb[:, 4, h * 64:(h + 1) * 64], True, False, [ATb, vrb], [A[2]])
                    mm(A[2][:, h * 64:(h + 1) * 64], QpT[:, h, :], ownSb[:, h, :], False, True,
                       [QpT, ownSb], [A[2]])
                stop_at(383)
                act(oret[:, :, :], A[2][:, :].rearrange("p (h d) -> p h d", h=8), AF.Identity, [A[2]], [oret])
                stop_at(39)
                DVE(lambda e: e.tensor_reduce(out=gst[:, :, 0], in_=oret[:, :, :], axis=AX.X, op=ALU.add), [oret], [gst])
                tt("pool", osq[:, :, :], oret[:, :, :], oret[:, :, :], ALU.mult, [oret], [osq])
                DVE(lambda e: e.tensor_reduce(out=gst[:, :, 1], in_=osq[:, :, :], axis=AX.X, op=ALU.add), [osq], [gst])
                ts("pool", gst[:, :, 2], gst[:, :, 0], 1.0 / 64, None, ALU.mult, None, [gst], [gst])
                tt("pool", gst[:, :, 3], gst[:, :, 2], gst[:, :, 2], ALU.mult, [gst], [gst])
                ts("pool", gst[:, :, 7], gst[:, :, 1], 1.0 / 64, None, ALU.mult, None, [gst], [gst])
                tt("pool", gst[:, :, 4], gst[:, :, 7], gst[:, :, 3], ALU.subtract, [gst], [gst])
                act(gst[:, :, 5], gst[:, :, 4], AF.Ln, [gst], [gst], bias=epsT[:, 0:1])
                act(gst[:, :, 6], gst[:, :, 5], AF.Exp, [gst], [gst], scale=-0.5)
                tt("pool", osq[:, :, :], oret[:, :, :], gst[:, :, 2:3].to_broadcast([128, 8, 64]), ALU.subtract, [oret, gst], [osq])
                tt("pool", osq[:, :, :], osq[:, :, :], gst[:, :, 6:7].to_broadcast([128, 8, 64]), ALU.mult, [osq, gst], [osq])
                osq2 = osq.t[:, :, :].rearrange("p h d -> p (h d)")
                tt("pool", osq2, osq2, gret_bc[:, :], ALU.mult, [osq, gret_bc], [osq])
                tt("pool", yrb[:, :], osq2, siluzr[:, :], ALU.mult, [osq, siluzr], [yrb])
                for pr in range(4):
                    PE(lambda e, pr=pr: e.transpose(out=TP[:, pr, :], in_=yrb[:, pr * 128:(pr + 1) * 128], identity=identb[:, :]), [yrb, identb], [TP])
                act(yrT[:, :, :], TP[:, 0:4, :], AF.Identity, [TP], [yrT])
                for hf in range(2):
                    wt, wv = load_wpo(4, 4, hf * 512, 512)
                    g_ = nextG()
                    for kt in range(4):
                        mm(g_[:, :], yrT[:, kt, :], wv[:, kt, :], kt == 0, kt == 3, [wt, yrT], [g_])
                    e_ = next_ev()
                    act(e_[:, :], g_[:, :], AF.Identity, [g_], [e_])
                    tt("pool", mr[sl][:, hf * 512:(hf + 1) * 512], e_[:, :], sgr[:, hf * 512:(hf + 1) * 512], ALU.mult, [e_, sgr], [mr[sl]])
            def scores(g):
                sl = g % 2
                wk = work[sl]
                for c in range(g + 1):
                    pb = c // 9
                    cc = c % 9
                    for h in range(4):
                        g_ = nextG()
                        mm(g_[:, :], qiT2[pb][:, h, :], kiT[:, cc * 512:(cc + 1) * 512], True, True, [qiT2[pb], kiT], [g_])
                        rli[0] += 1
                        r_ = rl[rli[0] % 2]
                        act(r_[:, :], g_[:, :], AF.Relu, [g_], [r_])
                        dst = wk[:, c * 512:(c + 1) * 512]
                        if h == 0:
                            if c == g:
                                stt("dve", dst, r_[:, :], wsc[sl][:, 0:1], penS[:, :], ALU.mult, ALU.add, [r_, wsc[sl], penS], [wk])
                            else:
                                ts("dve", dst, r_[:, :], wsc[sl][:, 0:1], None, ALU.mult, None, [r_, wsc[sl]], [wk])
                        else:
                            stt("dve", dst, r_[:, :], wsc[sl][:, h:h + 1], dst, ALU.mult, ALU.add, [r_, wsc[sl], wk], [wk])

            def topk(g):
                wk = work[g % 2]
                L = (g + 1) * 512
                for r in range(TOPK // 8):
                    DVE(lambda e: e.max(out=m8[:, :], in_=wk[:, 0:L]), [wk], [m8])
                    DVE(lambda e: e.match_replace(out=wk[:, 0:L], in_to_replace=m8[:, :], in_values=wk[:, 0:L], imm_value=NEG_BIG), [wk, m8], [wk])

            def attention(g, smp=None):
                sl = g % 2
                wk = work[sl]
                if g == 0 and os.environ.get('MK_DBG'):
                    stq("dbg", dbg[:, 0:512], wk[:, 0:512], [wk])
                nkt = 4 * (g + 1)
                for c in range(g + 1):
                    kc_ = kc[c % 2]; vc_ = vc[c % 2]; nm_ = nm[c % 2]
                    ld("kc%d" % (c % 2), kc_[:, :, :], kscr[:, :, c * 512:(c + 1) * 512], [kc_], [kscr_b[c]])
                    ld("vc%d" % (c % 2), vc_[:, :, :], vscr[4 * c:4 * c + 4, :, :].rearrange("t p c -> p t c"), [vc_], [vscr_b[c]])
                    ts("dve", nm_[:, :], wk[:, c * 512:(c + 1) * 512], -1.0e38, -30000.0, ALU.is_gt, ALU.mult, [wk], [nm_])
                    if c == g:
                        tt("dve", nm_[:, :], nm_[:, :], penM[:, :], ALU.add, [nm_, penM], [nm_])
                    if g == 0 and c == 0 and os.environ.get('MK_DBG'):
                        stq("dbg", dbgb[:, 0:512], nm_[:, :], [nm_])
                        stq("dbg", dbgb[:, 1024:1536], kc_[:, 0, :], [kc_])
                    for t in range(4):
                        kt_i = c * 4 + t
                        for hg in range(2):
                            a_ = A[(2 * (kt_i % 2)) + hg]
                            mm(a_[:, :], nm_[:, t * 128:(t + 1) * 128], ident4[:, :, :].rearrange("p a q -> p (a q)"), True, False,
                               [nm_, ident4], [a_])
                            for hh in range(4):
                                h = hg * 4 + hh
                                pr, hb = h // 2, h % 2
                                mm(a_[:, hh * 128:(hh + 1) * 128], kc_[:, pr, t * 128:(t + 1) * 128],
                                   QT[sl][:, h, :], False, hh == 3, [kc_, QT[sl]], [a_])
                            eti[0] += 1
                            et = ET[eti[0] % 2]
                            act(et[:, :], a_[:, :], AF.Exp, [a_], [et], scale=0.125)
                            if g == 0 and kt_i == 0 and hg == 0 and os.environ.get('MK_DBG'):
                                stq("dbg", dbgb[:, 512:1024], et[:, :], [et])
                                stq("dbg", dbgb[:, 1536:2048], QT[sl][:, 0:4, :].rearrange("p a q -> p (a q)"), [QT[sl]])
                            for hh in range(4):
                                h = hg * 4 + hh
                                if kt_i == 0 and hh == 0:
                                    mm(G[hg][:, 0:260], zl[:, :], vc_[:, t, 0:260], True, False, [zl, vc_], [G[hg]])
                                mm(G[hg][:, hh * 65:(hh + 1) * 65], et[:, hh * 128:(hh + 1) * 128], vc_[:, t, h * 65:(h + 1) * 65],
                                   False, (kt_i == nkt - 1 and hh == 3), [et, vc_], [G[hg]])
                for hg in range(2):
                    act(acc[:, hg * 4:(hg + 1) * 4, :], G[hg][:, 0:260].rearrange("p (h c) -> p h c", h=4), AF.Identity, [G[hg]], [acc])
                if g == 0 and os.environ.get('MK_DBG'):
                    stq("dbg", dbg[:, 512:1032], acc.t[:, :, :].rearrange("p h c -> p (h c)"), [acc])
                act(rden[:, :, 0], acc[:, :, 64], AF.Ln, [acc], [rden])
                act(rden[:, :, 1], rden[:, :, 0], AF.Exp, [rden], [rden], scale=-1.0)
                tt("pool", yaf[:, :, :], acc[:, :, 0:64], rden[:, :, 1:2].to_broadcast([128, 8, 64]), ALU.mult, [acc, rden], [yaf])
                tt("pool", yab[:, :], yaf.t[:, :, :].rearrange("p h d -> p (h d)"), siluza[sl][:, :], ALU.mult, [yaf, siluza[sl]], [yab])
                for pr in range(4):
                    PE(lambda e, pr=pr: e.transpose(out=TP[:, pr, :], in_=yab[:, pr * 128:(pr + 1) * 128], identity=identb[:, :]), [yab, identb], [TP])
                act(yaT[:, :, :], TP[:, 0:4, :], AF.Identity, [TP], [yaT])
                for hf in range(2):
                    wt, wv = load_wpo(0, 4, hf * 512, 512)
                    g_ = nextG()
                    for kt in range(4):
                        mm(g_[:, :], yaT[:, kt, :], wv[:, kt, :], kt == 0, kt == 3, [wt, yaT], [g_])
                    e_ = next_ev()
                    act(e_[:, :], g_[:, :], AF.Identity, [g_], [e_])
                    tt("pool", e_[:, :], e_[:, :], sga[sl][:, hf * 512:(hf + 1) * 512], ALU.mult, [e_, sga[sl]], [e_])
                    tt("pool", mg[:, hf * 512:(hf + 1) * 512], e_[:, :], mr[sl][:, hf * 512:(hf + 1) * 512], ALU.add, [e_, mr[sl]], [mg])
                for kt in range(8):
                    PE(lambda e, kt=kt: e.transpose(out=TP[:, kt, :], in_=mg[:, kt * 128:(kt + 1) * 128], identity=identb[:, :]), [mg, identb], [TP])
                act(mT[:, :, :], TP[:, :, :], AF.Identity, [TP], [mT])
                for hf in range(2):
                    wt, wv = load_wpo(8, 8, hf * 512, 512)
                    g_ = nextG()
                    for kt in range(8):
                        mm(g_[:, :], mT[:, kt, :], wv[:, kt, :], kt == 0, kt == 7, [wt, mT], [g_])
                    e_ = next_ev()
                    act(e_[:, :], g_[:, :], AF.Identity, [g_], [e_])
                    tt("pool", e_[:, :], e_[:, :], gate_bc[:, hf * 512:(hf + 1) * 512], ALU.mult, [e_, gate_bc], [e_])
                    tt("pool", res[:, hf * 512:(hf + 1) * 512], e_[:, :], xo[sl][:, hf * 512:(hf + 1) * 512], ALU.add, [e_, xo[sl]], [res])
                if g == 0 and os.environ.get('MK_DBG'):
                    stq("dbg", dbg[:, 2048:3072], res[:, :], [res])
                rstd_chain(res, res[:, :], D, 1)
                act(res[:, :], res[:, :], AF.Identity, [res, ssq], [res], scale=ssq[:, 1:2])
                tt("pool", yout[:, :], res[:, :], gfin_bc[:, :], ALU.mult, [res, gfin_bc], [yout])
                if smp is None:
                    stq("yo", y_o[g * 128:(g + 1) * 128, :], yout[:, :], [yout])
                else:
                    stq("yo", ys_o[smp, :, :], yout[0:4, :], [yout])

            n = NSTEPS_RUN
            if PROJ_ONLY:
                for g in range(n):
                    projscores(g)
                raise _Stop2()
            projscores(0)
            stop_at(3)
            scores(0)
            stop_at(4)
            for g in range(n):
                topk(g)
                stop_at(5)
                if g + 1 < n:
                    projscores(g + 1)
                attention(g)
                stop_at(6)
                if g + 1 < n:
                    scores(g + 1)
            for h in range(8):
                pr, hb = h // 2, h % 2
                stq("sto", st_o[h, :, :], Sst[hb * 64:(hb + 1) * 64, pr, hb * 64:(hb + 1) * 64], [Sst])

            def ingest(s_):
                ld("pt", ptb[:, :], ptab[s_:s_ + 1, :].partition_broadcast(128), [ptb])
                ts("pool", idxa[:, :], ptb[:, :], 128.0, pidx[:, 0:1], ALU.mult, ALU.add, [ptb, pidx], [idxa])
                POOL(lambda e: e.memset(Sst[:, :, :], 0.0), [], [Sst])
                for hb in range(2):
                    ld("sti", Sst[hb * 64:(hb + 1) * 64, :, hb * 64:(hb + 1) * 64],
                       state_in[s_, hb::2, :, :].rearrange("h p e -> p h e"), [Sst])
                for pg in range(64):
                    c = pg // 4; t = pg % 4
                    kp = rl[pg % 2]; vp = ev[pg % 2]
                    S.dma("gk%d" % (pg % 2), lambda e, kp=kp, pg=pg: e.indirect_dma_start(
                        out=kp[:, :], out_offset=None, in_=cache_k_d,
                        in_offset=bass.IndirectOffsetOnAxis(ap=idxa[:, pg:pg + 1], axis=0)), reads=[_b(idxa)], writes=[_b(kp)], q="pool")
                    S.dma("gv%d" % (pg % 2), lambda e, vp=vp, pg=pg: e.indirect_dma_start(
                        out=vp[:, :], out_offset=None, in_=cache_v_d,
                        in_offset=bass.IndirectOffsetOnAxis(ap=idxa[:, pg:pg + 1], axis=0)), reads=[_b(idxa)], writes=[_b(vp)], q="pool")
                    S.dma("gi", lambda e, pg=pg: e.indirect_dma_start(
                        out=kio[:, 0:64], out_offset=None, in_=cache_ki_d,
                        in_offset=bass.IndirectOffsetOnAxis(ap=idxa[:, pg:pg + 1], axis=0)), reads=[_b(idxa)], writes=[_b(kio)], q="pool")
                    kb = qp if pg % 2 == 0 else krown
                    act(kb[:, :], kp[:, :], AF.Identity, [kp], [kb])
                    for pr in range(4):
                        PE(lambda e, pr=pr, kb=kb: e.transpose(out=TP[:, pr, :], in_=kb[:, pr * 128:(pr + 1) * 128], identity=identb[:, :]),
                           [kb, identb], [TP])
                    act(kst[:, :, t * 128:(t + 1) * 128], TP[:, 0:4, :], AF.Identity, [TP], [kst])
                    act(vst[:, t, :, 0:64], vp[:, :].rearrange("p (h d) -> p h d", h=8), AF.Identity, [vp], [vst])
                    for dd in range(2):
                        act(yab[:, dd * 64:(dd + 1) * 64], kio[:, 0:64], AF.Identity, [kio], [yab])
                    PE(lambda e: e.transpose(out=TP[:, 4, :], in_=yab[:, 0:128], identity=identb[:, :]), [yab, identb], [TP])
                    pb = c // 9; cc = c % 9
                    act(kiT[pb * 64:(pb + 1) * 64, cc * 512 + t * 128:cc * 512 + (t + 1) * 128], TP[pb * 64:(pb + 1) * 64, 4, :], AF.Identity,
                        [TP], [kiT])
                    if t == 3:
                        stq("kst", kscr[:, :, c * 512:(c + 1) * 512], kst[:, :, :], [kst], [kscr_b[c]])
                        stq("vst", vscr[4 * c:4 * c + 4, :, :].rearrange("t p c -> p t c"), vst[:, :, :, :].rearrange("p t h c -> p t (h c)"),
                            [vst], [vscr_b[c]])

            NSAMP = int(os.environ.get("MK_NSAMP", "4"))
            if NSAMP > 0:
                S.barrier()
                ld("c1", kdec[:, :], kdec_s_d, [kdec]); ld("c1", qdec[:, :], qdec_s_d, [qdec]); ld("c1", dtab[:, :, :], dtab_s_d, [dtab])
                ld("c1", g128[:, :, :], g4_d, [g128]); ld("c1", sel[:, :], sel_s_d, [sel]); ld("c1", penS[:, :], penS_s_d, [penS])
                pe_ = next_ev()
                ld("c1", pe_[:, :], penM_s_d, [pe_])
                POOL(lambda e, pe_=pe_: e.tensor_copy(out=penM[:, :], in_=pe_[:, :]), [pe_], [penM])
                for s_ in range(NSAMP):
                    ingest(s_)
                    stop_at(71)
                    cur["j"] = 1 + s_
                    ld("g5l", gate_bc[:, :], gate5_d[1 + s_:2 + s_, :].partition_broadcast(128), [gate_bc], [gate5_b])
                    projscores(16, smp=s_)
                    stop_at(72)
                    scores(16)
                    topk(16)
                    stop_at(73)
                    attention(16, smp=s_)
                    for h in range(8):
                        pr, hb = h // 2, h % 2
                        stq("sto", sts_o[s_, h, :, :], Sst[hb * 64:(hb + 1) * 64, pr, hb * 64:(hb + 1) * 64], [Sst])
        except _Stop:
            st2.close()
        except _Stop2:
            for h in range(8):
                pr, hb = h // 2, h % 2
                stq("sto", st_o[h, :, :], Sst[hb * 64:(hb + 1) * 64, pr, hb * 64:(hb + 1) * 64], [Sst])
        S.finish()
        S.emit()
    return nc


def _consts():
    half = 32
    inv = (10000.0 ** (-np.arange(half, dtype=np.float32) / half)).astype(np.float32)
    pos = np.arange(SEQ, dtype=np.float32)
    ang = (pos[:, None] * inv[None, :]).astype(np.float32)
    cs = np.concatenate([np.cos(ang), np.sin(ang)], axis=1).astype(np.float32)
    lg = np.log1p(-np.exp2(-5.0 - np.arange(8, dtype=np.float64)))
    i = np.arange(128, dtype=np.float64)
    kdec = (0.125 * np.exp(lg[None, :] * (127.0 - i)[:, None])).astype(np.float32)
    qdec = np.exp(lg[None, :] * (i[:, None] + 1.0)).astype(np.float32)
    dt = np.zeros((128, 8, 128), np.float64)
    for h in range(8):
        v = 0.125 * np.exp(-lg[h] * (i + 1.0))
        dt[:, h, :] = v[:, None] * (i[None, :] >= i[:, None])
    g128 = np.zeros((128, 4, 128), np.float32)
    for pr in range(4):
        for hb in range(2):
            g128[hb * 64:(hb + 1) * 64, pr, :] = np.exp(lg[2 * pr + hb] * 128.0)
    return cs, kdec, qdec, dt.astype(np.float32), g128


def _consts_sample():
    half = 32
    inv = (10000.0 ** (-np.arange(half, dtype=np.float32) / half)).astype(np.float32)
    tt_ = np.minimum(np.arange(128), 3)
    pos = (8192 + tt_).astype(np.float32)
    ang = (pos[:, None] * inv[None, :]).astype(np.float32)
    cs_s = np.concatenate([np.cos(ang), np.sin(ang)], axis=1).astype(np.float32)
    lg = np.log1p(-np.exp2(-5.0 - np.arange(8, dtype=np.float64)))
    p = np.arange(128, dtype=np.float64)
    real = (p < 4)
    kdec_s = (0.125 * np.exp(lg[None, :] * (3.0 - p)[:, None]) * real[:, None]).astype(np.float32)
    qdec_s = np.exp(lg[None, :] * (tt_.astype(np.float64)[:, None] + 1.0)).astype(np.float32)
    dt = np.zeros((128, 8, 128), np.float64)
    for h in range(8):
        v = 0.125 * np.exp(-lg[h] * (p + 1.0)) * real
        dt[:, h, :] = v[:, None] * (tt_[None, :] >= p[:, None])
    g4 = np.zeros((128, 4, 128), np.float32)
    for pr in range(4):
        for hb in range(2):
            g4[hb * 64:(hb + 1) * 64, pr, :] = np.exp(lg[2 * pr + hb] * 4.0)
    j = np.arange(512)[None, :]
    adm = j <= tt_[:, None]
    penS_s = np.where(adm, 0.0, -1.0e30).astype(np.float32)
    penM_s = np.where(adm, 0.0, -30000.0).astype(np.float32)
    return cs_s, kdec_s, qdec_s, dt.astype(np.float32), g4, penS_s, penM_s


def kernel(x_prompt, x_sample, cache_k, cache_v, cache_kidx, state_ret, page_table, c_prompt, c_sample,
           w_ada, b_ada, g_norm, w_in, g_ret, w_pa, w_pr, w_out, g_final):
    f = lambda a: np.ascontiguousarray(np.asarray(a, dtype=np.float32))
    x_prompt = f(x_prompt); w_in0 = f(w_in)[0]
    cols = np.concatenate([
        np.arange(512, 1024), np.arange(1024, 1536), np.arange(2884, 3396), np.arange(3396, 3908),
        np.arange(2304, 2368), np.arange(2304, 2368),
        np.arange(0, 512),
        np.concatenate([np.concatenate([np.arange(2048 + 64 * h, 2048 + 64 * (h + 1))] * 2) for h in range(4)]),
        np.arange(2304, 2372), np.arange(1536, 2048), np.arange(2372, 2884), np.arange(3908, 4420),
        np.arange(4420, 5444), np.arange(5444, 6468)])
    assert cols.shape[0] == NEXT
    ptile = lambda w: np.ascontiguousarray(w.reshape(w.shape[0] // 128, 128, w.shape[1]).transpose(1, 0, 2))
    wext = ptile(w_in0[:, cols])
    wpo = ptile(np.concatenate([f(w_pa)[0], f(w_pr)[0], f(w_out)[0]], axis=0))
    wada = ptile(f(w_ada)[0])
    badaT = np.ascontiguousarray(f(b_ada)[0].reshape(24, 128).T)
    bgate = np.ascontiguousarray(np.broadcast_to(f(b_ada)[0][2048:3072], (128, 1024)))
    gnT = np.ascontiguousarray(f(g_norm)[0].reshape(8, 128).T)
    gfin = np.ascontiguousarray(np.broadcast_to(f(g_final), (128, 1024)))
    gret = np.ascontiguousarray(np.broadcast_to(f(g_ret)[0], (128, 512)))
    cs, kdec, qdec, dtab, g128 = _consts()
    cs_s, kdec_s, qdec_s, dtab_s, g4, penS_s, penM_s = _consts_sample()
    x_sample = f(x_sample); state_ret = f(state_ret)
    ck2 = f(cache_k)[0].reshape(2560 * 128, 512); cv2 = f(cache_v)[0].reshape(2560 * 128, 512)
    cki2 = f(cache_kidx)[0].reshape(2560 * 128, 64)
    page_table = np.asarray(page_table).astype(np.int32)
    xzero = np.zeros((128, D), np.float32)
    sel_s = np.zeros((128, 4), np.float32); sel_s[:, 0] = 1.0
    pidx = np.arange(128, dtype=np.float32).reshape(128, 1)
    ident = np.eye(128, dtype=np.float32)
    iota = np.ascontiguousarray(np.broadcast_to(np.arange(512, dtype=np.float32), (128, 512)))
    c_prompt = f(c_prompt); c_sample = f(c_sample)
    in_maps = []
    for c in range(8):
        b, r = c // 4, c % 4
        own_tiles = np.array([4 * g + r for g in range(NSTEP)])
        rows = (own_tiles[:, None] * 128 + np.arange(128)[None, :]).reshape(-1)
        cvec = np.stack([c_prompt[b]] + [c_sample[4 * c + s] for s in range(4)], axis=1)
        cT = np.ascontiguousarray(cvec.reshape(8, 128, 5).transpose(1, 0, 2))
        sel = np.zeros((128, 4), np.float32); sel[:, r] = 1.0
        xsp = np.zeros((4, 128, D), np.float32)
        for s_ in range(4):
            xsp[s_, 0:4] = x_sample[4 * c + s_]
        qrel = (r * 128 + np.arange(128, dtype=np.float32)).reshape(128, 1)
        in_maps.append({
            "xall": x_prompt[b], "xown": np.ascontiguousarray(x_prompt[b][rows]),
            "wext": wext, "wpo": wpo, "wada": wada, "badaT": badaT, "bgate": bgate, "gnT": gnT, "gfin": gfin, "gret": gret,
            "cT": cT, "ident": ident, "cs_all": cs, "cs_own": np.ascontiguousarray(cs[rows]), "kdec": kdec, "qdec": qdec,
            "dtab": dtab, "g128": g128, "sel": sel, "iota512": iota, "qrel": qrel,
            "xsp": xsp, "xzero": xzero, "cache_k2": ck2, "cache_v2": cv2, "cache_ki2": cki2,
            "ptab": np.ascontiguousarray(page_table[4 * c:4 * c + 4]), "state_in": np.ascontiguousarray(state_ret[0, 4 * c:4 * c + 4]),
            "cs_s": cs_s, "kdec_s": kdec_s, "qdec_s": qdec_s, "dtab_s": dtab_s, "g4": g4, "sel_s": sel_s,
            "penS_s": penS_s, "penM_s": penM_s, "pidx": pidx,
        })
    nc = build_nc()
    resr = run_bass_kernel_spmd(nc, in_maps, core_ids=list(range(8)))
    R = resr.results
    y_prompt = np.zeros((2, SEQ, D), np.float32)
    k_rows = np.zeros((1, 2, SEQ, 8, 64), np.float32); v_rows = np.zeros((1, 2, SEQ, 8, 64), np.float32)
    ki_rows = np.zeros((1, 2, SEQ, 64), np.float32); st_p = np.zeros((1, 2, 8, 64, 64), np.float32)
    for c in range(8):
        b, r = c // 4, c % 4
        for g in range(NSTEP):
            t = 4 * g + r
            sl_ = slice(t * 128, (t + 1) * 128); so = slice(g * 128, (g + 1) * 128)
            y_prompt[b, sl_] = R[c]["y_o"][so]
            k_rows[0, b, sl_] = R[c]["k_o"][so].reshape(128, 8, 64)
            v_rows[0, b, sl_] = R[c]["v_o"][so].reshape(128, 8, 64)
            ki_rows[0, b, sl_] = R[c]["ki_o"][so]
        if r == 3:
            st_p[0, b] = R[c]["st_o"]
    y_sample = np.zeros((32, 4, D), np.float32)
    ks = np.zeros((1, 32, 4, 8, 64), np.float32); vs = np.zeros((1, 32, 4, 8, 64), np.float32)
    kis = np.zeros((1, 32, 4, 64), np.float32); sts = np.zeros((1, 32, 8, 64, 64), np.float32)
    for c in range(8):
        for s_ in range(4):
            bi = 4 * c + s_
            y_sample[bi] = R[c]["ys_o"][s_]
            ks[0, bi] = R[c]["ks_o"][s_].reshape(4, 8, 64); vs[0, bi] = R[c]["vs_o"][s_].reshape(4, 8, 64)
            kis[0, bi] = R[c]["kis_o"][s_]; sts[0, bi] = R[c]["sts_o"][s_]
    return (y_prompt, y_sample, k_rows, v_rows, ki_rows, st_p, ks, vs, kis, sts)
```

```python
import os
import numpy as np
from contextlib import ExitStack
import concourse.bass as bass
import concourse.mybir as mybir
from concourse.bass_utils import run_bass_kernel_spmd

F32 = mybir.dt.float32
BF16 = mybir.dt.bfloat16
AF = mybir.ActivationFunctionType
ALU = mybir.AluOpType
AX = mybir.AxisListType

D = 1024
SEQ = 8192
NSTEP = 16
NSTEPS_RUN = int(os.environ.get("MK_NSTEPS", "16"))
TOPK = 256
NEG_BIG = -3.0e38
C_KA, C_VA, C_KR, C_VR, C_KID = 0, 512, 1024, 1536, 2048
C_QA, C_QID, C_KIWI, C_ZA, C_QR, C_ZR, C_GA, C_GR = 2176, 2688, 3200, 3268, 3780, 4292, 4804, 5828
NEXT = 6852
ENGS = ("pe", "act", "dve", "pool", "sp")


class _Stop(Exception):
    pass


class _Stop2(Exception):
    pass


STOP = int(os.environ.get('MK_STOP', '99'))
PROJ_ONLY = os.environ.get('MK_FULL', '1') != '1'


def stop_at(k):
    if STOP == k:
        raise _Stop()


class Buf:
    __slots__ = ("w", "r")

    def __init__(self):
        self.w = None
        self.r = []


class Tl:
    def __init__(self, t):
        self.t = t
        self.b = Buf()

    def __getitem__(self, k):
        return self.t[k]


def _b(x):
    return x.b if hasattr(x, "b") else x


class Sched:
    def __init__(self, nc, stack):
        self.nc = nc
        self.stack = stack
        self.ops = {e: [] for e in ENGS}
        self.cnt = {e: 0 for e in ENGS}
        self.esem = {e: stack.enter_context(nc.semaphore("prog_" + e)) for e in ENGS if e != "sp"}
        self.dsem = {}
        self.dcnt = {}
        self.waited = {}
        self.same = {"pe": False, "act": True, "dve": True, "pool": True, "sp": False}

    def _collect(self, eng, reads, writes):
        deps = []
        for b in reads:
            if b.w is not None:
                deps.append(b.w)
        for b in writes:
            if b.w is not None:
                deps.append(b.w)
            deps.extend(b.r)
        waits = []
        for d in deps:
            if d[0] == "e":
                if d[1] == eng and not self.same[eng]:
                    continue
                key = (eng, "e", d[1])
                val = d[2]
                sem = self.esem[d[1]]
            else:
                key = (eng, "d", d[1])
                val = self.dcnt[d[1]]
                sem = self.dsem[d[1]]
            if self.waited.get(key, 0) >= val:
                continue
            self.waited[key] = val
            waits.append((sem, val))
        return waits

    def _update(self, d, reads, writes):
        for b in reads:
            b.r.append(d)
            if len(b.r) > 64:
                b.r = b.r[-64:] if False else b.r
        for b in writes:
            b.w = d
            b.r = []

    def op(self, eng, fn, reads=(), writes=()):
        reads = [_b(x) for x in reads]
        writes = [_b(x) for x in writes]
        waits = self._collect(eng, reads, writes)
        self.cnt[eng] += 1
        idx = self.cnt[eng]
        sem = self.esem[eng]

        def thunk(e, waits=waits, fn=fn, sem=sem):
            for ws, wv in waits:
                e.wait_ge(ws, wv)
            fn(e).then_inc(sem, 1)

        self.ops[eng].append(thunk)
        self._update(("e", eng, idx), reads, writes)

    def dma(self, semkey, fn, reads=(), writes=(), q="sp"):
        reads = [_b(x) for x in reads]
        writes = [_b(x) for x in writes]
        if semkey not in self.dsem:
            self.dsem[semkey] = self.stack.enter_context(self.nc.semaphore("d_" + semkey))
            self.dcnt[semkey] = 0
        waits = self._collect(q, reads, writes)
        self.dcnt[semkey] += 16
        sem = self.dsem[semkey]

        def thunk(e, waits=waits, fn=fn, sem=sem):
            for ws, wv in waits:
                e.wait_ge(ws, wv)
            fn(e).then_inc(sem, 16)

        self.ops[q].append(thunk)
        self._update(("d", semkey), reads, writes)

    def barrier(self):
        fin = [(sem, self.dcnt[k]) for k, sem in self.dsem.items() if self.dcnt[k] > 0]
        fin += [(sem, self.cnt[e]) for e, sem in self.esem.items() if self.cnt[e] > 0]
        for q in ENGS:
            def thunk(e, fin=fin):
                for ws, wv in fin:
                    e.wait_ge(ws, wv)
            self.ops[q].append(thunk)

    def finish(self, q="sp"):
        fin = [(sem, self.dcnt[k]) for k, sem in self.dsem.items()]
        fin += [(sem, self.cnt[e]) for e, sem in self.esem.items() if self.cnt[e] > 0]

        def thunk(e, fin=fin):
            for ws, wv in fin:
                e.wait_ge(ws, wv)

        self.ops[q].append(thunk)

    def emit(self):
        nc = self.nc
        ops = self.ops
        with nc.Block() as block:
            @block.tensor
            def _(e):
                for t in ops["pe"]:
                    t(e)

            @block.scalar
            def _(e):
                for t in ops["act"]:
                    t(e)

            @block.vector
            def _(e):
                for t in ops["dve"]:
                    t(e)

            @block.gpsimd
            def _(e):
                for t in ops["pool"]:
                    t(e)

            @block.sync
            def _(e):
                for t in ops["sp"]:
                    t(e)


def build_nc():
    nc = bass.Bass("TRN2", target_bir_lowering=False)
    din = lambda n, s, dt=F32: nc.dram_tensor(n, list(s), dt, kind="ExternalInput").ap()
    dout = lambda n, s, dt=F32: nc.dram_tensor(n, list(s), dt, kind="ExternalOutput").ap()
    dscr = lambda n, s, dt: nc.dram_tensor(n, list(s), dt, kind="Internal").ap()
    xall = din("xall", [SEQ, D]); xown = din("xown", [NSTEP * 128, D])
    wext = din("wext", [128, 8, NEXT]); wpo = din("wpo", [128, 16, 1024]); wada = din("wada", [128, 8, 3072])
    badaT = din("badaT", [128, 24]); bgate = din("bgate", [128, 1024]); gnT = din("gnT", [128, 8])
    gfin = din("gfin", [128, 1024]); gret = din("gret", [128, 512]); cT = din("cT", [128, 8, 5])
    ident_d = din("ident", [128, 128]); cs_all = din("cs_all", [SEQ, 64]); cs_own = din("cs_own", [NSTEP * 128, 64])
    kdec_d = din("kdec", [128, 8]); qdec_d = din("qdec", [128, 8]); dtab_d = din("dtab", [128, 8, 128])
    g128_d = din("g128", [128, 4, 128]); sel_d = din("sel", [128, 4]); iota_d = din("iota512", [128, 512])
    qrel_d = din("qrel", [128, 1])
    I32 = mybir.dt.int32
    xsp = din("xsp", [4, 128, D]); xzero = din("xzero", [128, D])
    cache_k_d = din("cache_k2", [2560 * 128, 512]); cache_v_d = din("cache_v2", [2560 * 128, 512]); cache_ki_d = din("cache_ki2", [2560 * 128, 64])
    ptab = din("ptab", [4, 64], I32); state_in = din("state_in", [4, 8, 64, 64])
    cs_s_d = din("cs_s", [128, 64]); kdec_s_d = din("kdec_s", [128, 8]); qdec_s_d = din("qdec_s", [128, 8])
    dtab_s_d = din("dtab_s", [128, 8, 128]); g4_d = din("g4", [128, 4, 128]); sel_s_d = din("sel_s", [128, 4])
    penS_s_d = din("penS_s", [128, 512]); penM_s_d = din("penM_s", [128, 512]); pidx_d = din("pidx", [128, 1])
    ys_o = dout("ys_o", [4, 4, D]); ks_o = dout("ks_o", [4, 4, 512]); vs_o = dout("vs_o", [4, 4, 512])
    kis_o = dout("kis_o", [4, 4, 64]); sts_o = dout("sts_o", [4, 8, 64, 64])
    gate5_d = dscr("gate5_d", [5, D], F32)
    y_o = dout("y_o", [NSTEP * 128, D]); k_o = dout("k_o", [NSTEP * 128, 512]); v_o = dout("v_o", [NSTEP * 128, 512])
    ki_o = dout("ki_o", [NSTEP * 128, 64]); st_o = dout("st_o", [8, 64, 64])
    wbf = dscr("wbf", [128, 8, NEXT], BF16); wpobf = dscr("wpobf", [128, 16, 1024], BF16)
    kscr = dscr("kscr", [128, 4, SEQ + 512], BF16); vscr = dscr("vscr", [SEQ // 128 + 4, 128, 520], BF16)

    with ExitStack() as st:
        S = Sched(nc, st)
        sb = lambda n, s, dt=F32: Tl(st.enter_context(nc.sbuf_tensor("s_" + n, list(s), dt)))
        psb = lambda n, s, dt=F32: Tl(st.enter_context(nc.psum_tensor("p_" + n, list(s), dt)))
        ACT = lambda fn, r=(), w=(): S.op("act", fn, r, w)
        POOL = lambda fn, r=(), w=(): S.op("pool", fn, r, w)
        DVE = lambda fn, r=(), w=(): S.op("dve", fn, r, w)
        PE = lambda fn, r=(), w=(): S.op("pe", fn, r, w)

        G = [psb("G%d" % i, [128, 512]) for i in range(2)]
        A = [psb("A%d" % i, [128, 512]) for i in range(4)]
        TP = psb("TP", [128, 8, 128], BF16)
        XP = psb("XP", [128, 512])
        gi = [0]

        def nextG():
            gi[0] ^= 1
            return G[gi[0]]

        kscr_b = [Buf() for _ in range(NSTEP + 1)]; vscr_b = [Buf() for _ in range(NSTEP + 1)]
        wbf_b = Buf(); wpobf_b = Buf()
        kiT = sb("kiT", [128, 4608], BF16)
        wsl = [sb("wsl%d" % i, [128, 4096], BF16) for i in range(2)]
        wi_ = [0]
        hT = sb("hT", [128, 8, 640], BF16)
        xo = [sb("xo%d" % i, [128, D]) for i in range(2)]
        xn = sb("xn", [128, D], BF16); sqj = xn
        ssq = sb("ssq", [128, 4]); epsT = sb("epsT", [128, 1])
        kst = sb("kst", [128, 4, 512], BF16); vst = sb("vst", [128, 4, 8, 65], BF16)
        ko = sb("ko", [128, 512]); vo = sb("vo", [128, 512]); kio = sb("kio", [128, 68])
        ev = [sb("ev%d" % i, [128, 512]) for i in range(2)]
        evi = [0]
        rt = [sb("rt%d" % i, [128, 8, 32]) for i in range(4)]
        krd = sb("krd", [128, 4, 512], BF16); vrb = sb("vrb", [128, 5, 512], BF16)
        krown = sb("krown", [128, 512], BF16); KrT = sb("KrT", [128, 4, 128], BF16)
        qp = sb("qp", [128, 512], BF16); QpT = sb("QpT", [128, 8, 128], BF16)
        Sst = sb("Sst", [128, 4, 128]); ownS = sb("ownS", [128, 4, 128])
        ownSb = sb("ownSb", [128, 8, 64], BF16)
        QT = [sb("QT%d" % i, [128, 8, 128], BF16) for i in range(2)]
        qiT2 = [sb("qiTlo", [128, 4, 128], BF16), sb("qiThi", [128, 4, 128], BF16)]
        wsc = [sb("wsc%d" % i, [128, 4]) for i in range(2)]
        siluza = [sb("siluza%d" % i, [128, 512], BF16) for i in range(2)]
        sga = [sb("sga%d" % i, [128, 1024], BF16) for i in range(2)]
        sgr = sb("sgr", [128, 1024], BF16)
        mr = [sb("mr%d" % i, [128, 1024], BF16) for i in range(2)]
        siluzr = sb("siluzr", [128, 512], BF16)
        rl = [sb("rl%d" % i, [128, 512]) for i in range(2)]
        rli = [0]
        eti = [0]
        dtab = sb("dtab", [128, 8, 128]); g128 = sb("g128", [128, 4, 128])
        cs5 = sb("cs5", [128, 5, 64])
        identf = sb("identf", [128, 128]); identb = sb("identb", [128, 128], BF16)
        ident4 = sb("ident4", [128, 4, 128], BF16); zl = sb("zl", [128, 128], BF16)
        ATb = sb("ATb", [128, 8, 128], BF16)
        oret = sb("oret", [128, 8, 64]); osq = sb("osq", [128, 8, 64])
        gst = sb("gst", [128, 8, 8])
        yrb = sb("yrb", [128, 512], BF16); yrT = sb("yrT", [128, 4, 128], BF16)
        acc = sb("acc", [128, 8, 65]); rden = sb("rden", [128, 8, 2])
        yaf = oret; yab = sb("yab", [128, 512], BF16); yaT = sb("yaT", [128, 4, 128], BF16)
        mg = sb("mg", [128, 1024], BF16); mT = sb("mT", [128, 8, 128], BF16)
        res = sb("res", [128, D]); yout = res; xs = [res] * 2
        gate_bc = sb("gate_bc", [128, D]); gfin_bc = sb("gfin_bc", [128, D]); gret_bc = sb("gret_bc", [128, 512])
        penS = sb("penS", [128, 512]); penM = sb("penM", [128, 512], BF16)
        kdec = sb("kdec", [128, 8]); qdec = sb("qdec", [128, 8]); sel = sb("sel", [128, 4])
        AcA = sb("AcA", [128, 5, 8]); BcA = sb("BcA", [128, 5, 8]); cur = {"j": 0}
        ptb = sb("ptb", [128, 64], I32); idxa = sb("idxa", [128, 64], I32); pidx = sb("pidx", [128, 1])
        m8 = sb("m8", [128, 8])

        def ld(key, dst_ap, src_ap, w, r=()):
            S.dma(key, lambda e: e.dma_start(out=dst_ap, in_=src_ap), reads=r, writes=w)

        def stq(key, dst_ap, src_ap, r, w=()):
            S.dma(key, lambda e: e.dma_start(out=dst_ap, in_=src_ap), reads=r, writes=w)

        def mm(out, lhsT, rhs, start, stop, r, w):
            PE(lambda e: e.matmul(out=out, lhsT=lhsT, rhs=rhs, start=start, stop=stop), r, w)

        def act(out, in_, func, r, w, **kw):
            ACT(lambda e: e.activation(out=out, in_=in_, func=func, **kw), r, w)

        def tt(eng, out, in0, in1, op, r, w):
            S.op(eng, lambda e: e.tensor_tensor(out=out, in0=in0, in1=in1, op=op), r, w)

        def ts(eng, out, in0, s1, s2, op0, op1, r, w):
            if op1 is None:
                S.op(eng, lambda e: e.tensor_scalar(out=out, in0=in0, scalar1=s1, scalar2=None, op0=op0), r, w)
            else:
                S.op(eng, lambda e: e.tensor_scalar(out=out, in0=in0, scalar1=s1, scalar2=s2, op0=op0, op1=op1), r, w)

        def stt(eng, out, in0, scalar, in1, op0, op1, r, w):
            S.op(eng, lambda e: e.scalar_tensor_tensor(out=out, in0=in0, scalar=scalar, in1=in1, op0=op0, op1=op1), r, w)

        def load_w(col0, ncols):
            sl = wsl[wi_[0] % 2]; wi_[0] += 1
            v = sl.t[:, 0:8 * ncols].rearrange("p (k c) -> p k c", k=8)
            ld("w%d" % ((wi_[0] - 1) % 2), v, wbf[:, :, col0:col0 + ncols], [sl], [wbf_b])
            return sl, v

        def load_wpo(kt0, nk, c0, ncols):
            sl = wsl[wi_[0] % 2]; wi_[0] += 1
            v = sl.t[:, 0:nk * ncols].rearrange("p (k c) -> p k c", k=nk)
            ld("w%d" % ((wi_[0] - 1) % 2), v, wpobf[:, kt0:kt0 + nk, c0:c0 + ncols], [sl], [wpobf_b])
            return sl, v

        def rstd_chain(src_tl, src_ap, n, col):
            act(sqj[:, 0:n], src_ap, AF.Square, [src_tl], [sqj, ssq], accum_out=ssq[:, 3:4])
            act(ssq[:, 2:3], ssq[:, 3:4], AF.Ln, [ssq], [ssq], scale=1.0 / n, bias=epsT[:, 0:1])
            act(ssq[:, col:col + 1], ssq[:, 2:3], AF.Exp, [ssq], [ssq], scale=-0.5)

        def make_hT(xt, col0):
            rstd_chain(xt, xt[:, :], D, 0)
            act(xn[:, :], xt[:, :], AF.Identity, [xt, ssq], [xn], scale=ssq[:, 0:1])
            for kt in range(8):
                PE(lambda e, kt=kt: e.transpose(out=TP[:, kt, :], in_=xn[:, kt * 128:(kt + 1) * 128], identity=identb[:, :]),
                   [xn, identb], [TP])
            for kt in range(8):
                jj = cur["j"]
                act(hT[:, kt, col0:col0 + 128], TP[:, kt, :], AF.Identity, [TP, AcA, BcA], [hT],
                    scale=AcA[:, jj, kt:kt + 1], bias=BcA[:, jj, kt:kt + 1])

        def rotary(src, cs_ap, dst_ap, dst_tl):
            sv = src.t[:, :].rearrange("p (h t f) -> p h t f", h=8, t=2)
            dv = dst_ap.rearrange("p (h t f) -> p h t f", h=8, t=2)
            cosb = cs_ap[:, 0:32].unsqueeze(1).to_broadcast([128, 8, 32])
            sinb = cs_ap[:, 32:64].unsqueeze(1).to_broadcast([128, 8, 32])
            x1 = sv[:, :, 0, :]; x2 = sv[:, :, 1, :]
            tt("pool", rt[0][:, :, :], x1, cosb, ALU.mult, [src, cs5], [rt[0]])
            tt("pool", rt[1][:, :, :], x2, sinb, ALU.mult, [src, cs5], [rt[1]])
            tt("pool", dv[:, :, 0, :], rt[0][:, :, :], rt[1][:, :, :], ALU.subtract, [rt[0], rt[1]], [dst_tl])
            tt("pool", rt[2][:, :, :], x1, sinb, ALU.mult, [src, cs5], [rt[2]])
            tt("pool", rt[3][:, :, :], x2, cosb, ALU.mult, [src, cs5], [rt[3]])
            tt("pool", dv[:, :, 1, :], rt[2][:, :, :], rt[3][:, :, :], ALU.add, [rt[2], rt[3]], [dst_tl])

        def next_ev():
            evi[0] += 1
            return ev[evi[0] % 2]

        try:
            st2 = ExitStack()
            sb2 = lambda n, s_, dt=F32: Tl(st2.enter_context(nc.sbuf_tensor("s_" + n, list(s_), dt)))
            bgate_sb = sb2("bgate_sb", [128, D])
            iota = sb2("iota", [128, 512]); qrel = sb2("qrel", [128, 1])
            cTs = sb2("cTs", [128, 8, 5]); scT = sb2("scT", [128, 8, 5]); gate5 = sb2("gate5", [5, D])
            modT = sb2("modT", [128, 16, 5]); badaTs = sb2("badaTs", [128, 24]); gnTs = sb2("gnTs", [128, 8])
            wadas = [sb2("wadas%d" % i, [128, 8, 256]) for i in range(2)]
            stgf = [sb2("stgf%d" % i, [128, 1024]) for i in range(2)]
            stgb = [sb2("stgb%d" % i, [128, 1024], BF16) for i in range(2)]
            ld("c0", identf[:, :], ident_d, [identf]); ld("c0", dtab[:, :, :], dtab_d, [dtab]); ld("c0", g128[:, :, :], g128_d, [g128])
            ld("c0", kdec[:, :], kdec_d, [kdec]); ld("c0", qdec[:, :], qdec_d, [qdec]); ld("c0", sel[:, :], sel_d, [sel])
            ld("c0", iota[:, :], iota_d, [iota]); ld("c0", qrel[:, :], qrel_d, [qrel]); ld("c0", cTs[:, :, :], cT, [cTs])
            ld("c0", badaTs[:, :], badaT, [badaTs]); ld("c0", gnTs[:, :], gnT, [gnTs]); ld("c0", bgate_sb[:, :], bgate, [bgate_sb])
            ld("c0", gfin_bc[:, :], gfin, [gfin_bc]); ld("c0", gret_bc[:, :], gret, [gret_bc])
            POOL(lambda e: e.tensor_copy(out=identb[:, :], in_=identf[:, :]), [identf], [identb])
            for a4 in range(4):
                POOL(lambda e, a4=a4: e.tensor_copy(out=ident4[:, a4, :], in_=identf[:, :]), [identf], [ident4])
            POOL(lambda e: e.memset(zl[:, :], 0.0), [], [zl])
            POOL(lambda e: e.memset(epsT[:, :], 1e-6), [], [epsT])
            POOL(lambda e: e.memset(vst[:, :, :, 64:65], 1.0), [], [vst])
            POOL(lambda e: e.memset(Sst[:, :, :], 0.0), [], [Sst])
            POOL(lambda e: e.memset(ownSb[:, :, :], 0.0), [], [ownSb])
            POOL(lambda e: e.memset(QpT[:, :, :], 0.0), [], [QpT])
            POOL(lambda e: e.memset(kiT[:, :], 0.0), [], [kiT])
            for q_ in qiT2:
                POOL(lambda e, q_=q_: e.memset(q_[:, :, :], 0.0), [], [q_])
            for q_ in QT:
                POOL(lambda e, q_=q_: e.memset(q_[:, :, :], 0.0), [], [q_])
            ts("pool", penS[:, :], iota[:, :], qrel[:, 0:1], -1.0e30, ALU.is_gt, ALU.mult, [iota, qrel], [penS])
            ts("pool", penM[:, :], iota[:, :], qrel[:, 0:1], -30000.0, ALU.is_gt, ALU.mult, [iota, qrel], [penM])
            act(scT[:, :, :], cTs[:, :, :], AF.Silu, [cTs], [scT])
            for cc in range(12):
                wa = wadas[cc % 2]
                ld("wa%d" % (cc % 2), wa[:, :, :], wada[:, :, cc * 256:(cc + 1) * 256], [wa])
                if cc < 8:
                    for t2 in range(2):
                        ct = cc * 2 + t2
                        for kt in range(8):
                            mm(XP[:, ct * 5:ct * 5 + 5], wa[:, kt, t2 * 128:(t2 + 1) * 128], scT[:, kt, :], kt == 0, kt == 7,
                               [wa, scT], [XP])
                else:
                    gch = cc - 8
                    g_ = G[gch // 2]
                    for kt in range(8):
                        mm(g_[0:5, (gch % 2) * 256:(gch % 2) * 256 + 256], scT[:, kt, :], wa[:, kt, :], kt == 0, kt == 7,
                           [wa, scT], [g_])
            act(modT[:, :, :], XP[:, 0:80].rearrange("p (c j) -> p c j", j=5), AF.Identity, [XP], [modT])
            for j in range(5):
                tt("pool", BcA[:, j, :], modT[:, 0:8, j], badaTs[:, 0:8], ALU.add, [modT, badaTs], [BcA])
                ts("pool", AcA[:, j, :], modT[:, 8:16, j], 1.0, None, ALU.add, None, [modT], [AcA])
                tt("pool", AcA[:, j, :], AcA[:, j, :], badaTs[:, 8:16], ALU.add, [AcA, badaTs], [AcA])
                tt("pool", AcA[:, j, :], AcA[:, j, :], gnTs[:, :], ALU.mult, [AcA, gnTs], [AcA])
            for hf in range(2):
                act(gate5[:, hf * 512:(hf + 1) * 512], G[hf][0:5, :], AF.Identity, [G[hf]], [gate5])
            tt("pool", gate5[:, :], gate5[:, :], bgate_sb[0:5, :], ALU.add, [gate5, bgate_sb], [gate5])
            gate5_b = Buf()
            stq("g5", gate5_d, gate5[:, :], [gate5], [gate5_b])
            ld("g5l", gate_bc[:, :], gate5_d[0:1, :].partition_broadcast(128), [gate_bc], [gate5_b])
            ld("c0", pidx[:, :], pidx_d, [pidx])
            stop_at(1)
            nch = (NEXT + 127) // 128
            for c in range(nch):
                c0 = c * 128; n = min(128, NEXT - c0); i = c % 2
                fv = stgf[i].t[:, 0:8 * n].rearrange("p (k c) -> p k c", k=8)
                bv = stgb[i].t[:, 0:8 * n].rearrange("p (k c) -> p k c", k=8)
                ld("ps%d" % i, fv, wext[:, :, c0:c0 + n], [stgf[i]])
                if c % 2 == 0:
                    POOL(lambda e, fv=fv, bv=bv: e.tensor_copy(out=bv, in_=fv), [stgf[i]], [stgb[i]])
                else:
                    act(bv, fv, AF.Identity, [stgf[i]], [stgb[i]])
                stq("pt%d" % i, wbf[:, :, c0:c0 + n], bv, [stgb[i]], [])
            for c in range(16):
                c0 = c * 64; i = c % 2
                fv = stgf[i].t[:, :].rearrange("p (k c) -> p k c", k=16)
                bv = stgb[i].t[:, :].rearrange("p (k c) -> p k c", k=16)
                ld("ps%d" % i, fv, wpo[:, :, c0:c0 + 64], [stgf[i]])
                if c % 2 == 0:
                    POOL(lambda e, fv=fv, bv=bv: e.tensor_copy(out=bv, in_=fv), [stgf[i]], [stgb[i]])
                else:
                    act(bv, fv, AF.Identity, [stgf[i]], [stgb[i]])
                stq("pt%d" % i, wpobf[:, :, c0:c0 + 64], bv, [stgb[i]], [])

            stop_at(2)
            S.barrier()
            st2.close()
            work0 = sb("work0", [128, SEQ + 512]); wbA = work0.b; wbB = Buf()

            def wkinfo(g):
                if g < 8:
                    return (g % 2) * 4352, ([wbA] if g % 2 == 0 else [wbB])
                return 0, [wbA, wbB]
            nm = [sb("nm%d" % i, [128, 512], BF16) for i in range(2)]
            kc = [sb("kc%d" % i, [128, 4, 512], BF16) for i in range(2)]
            vc = [sb("vc%d" % i, [128, 4, 520], BF16) for i in range(2)]
            ET = [sb("ET%d" % i, [128, 512], BF16) for i in range(2)]

            def projscores(g, smp=None):
                sl = g % 2
                L = (g + 1) * 512
                if smp is None:
                    ld("cs", cs5[:, 0:4, :], cs_all[g * 512:(g + 1) * 512, :].rearrange("(t p) c -> p t c", p=128), [cs5])
                    ld("cs", cs5[:, 4, :], cs_own[g * 128:(g + 1) * 128, :], [cs5])
                else:
                    for j5 in range(5):
                        ld("cs", cs5[:, j5, :], cs_s_d, [cs5])
                for j in range(4):
                    xt = xs[j % 2]
                    if smp is None:
                        ld("x0", xt[:, :], xall[(4 * g + j) * 128:(4 * g + j + 1) * 128, :], [xt])
                    else:
                        ld("x0", xt[:, :], xsp[smp, :, :] if j == 0 else xzero, [xt])
                    make_hT(xt, j * 128)
                ld("xo%d" % sl, xo[sl][:, :], xown[g * 128:(g + 1) * 128, :] if smp is None else xsp[smp, :, :], [xo[sl]])
                make_hT(xo[sl], 512)
                stop_at(31)
                wt, wv = load_w(C_KA, 512)
                for pr in range(4):
                    g_ = nextG()
                    for kt in range(8):
                        mm(g_[:, :], wv[:, kt, pr * 128:(pr + 1) * 128], hT[:, kt, 0:512], kt == 0, kt == 7, [wt, hT], [g_])
                    act(kst[:, pr, :], g_[:, :], AF.Identity, [g_], [kst])
                stq("kst", kscr[:, :, g * 512:(g + 1) * 512], kst[:, :, :], [kst], [kscr_b[g]])
                g_ = nextG()
                for kt in range(8):
                    mm(g_[:, :], hT[:, kt, 512:640], wv[:, kt, :], kt == 0, kt == 7, [wt, hT], [g_])
                act(ko[:, :], g_[:, :], AF.Identity, [g_], [ko])
                if smp is None:
                    stq("ko", k_o[g * 128:(g + 1) * 128, :], ko[:, :], [ko])
                else:
                    stq("ko", ks_o[smp, :, :], ko[0:4, :], [ko])
                stop_at(32)
                wt, wv = load_w(C_VA, 512)
                for j in range(5):
                    g_ = nextG()
                    for kt in range(8):
                        mm(g_[:, :], hT[:, kt, j * 128:(j + 1) * 128], wv[:, kt, :], kt == 0, kt == 7, [wt, hT], [g_])
                    if j < 4:
                        act(vst[:, j, :, 0:64], g_[:, :].rearrange("p (h d) -> p h d", h=8), AF.Identity, [g_], [vst])
                    else:
                        act(vo[:, :], g_[:, :], AF.Identity, [g_], [vo])
                stq("vst", vscr[4 * g:4 * g + 4, :, :].rearrange("t p c -> p t c"), vst[:, :, :, :].rearrange("p t h c -> p t (h c)"),
                    [vst], [vscr_b[g]])
                if smp is None:
                    stq("vo", v_o[g * 128:(g + 1) * 128, :], vo[:, :], [vo])
                else:
                    stq("vo", vs_o[smp, :, :], vo[0:4, :], [vo])
                stop_at(33)
                wt, wv = load_w(C_KR, 512)
                for j in range(5):
                    g_ = nextG()
                    for kt in range(8):
                        mm(g_[:, :], hT[:, kt, j * 128:(j + 1) * 128], wv[:, kt, :], kt == 0, kt == 7, [wt, hT], [g_])
                    e_ = next_ev()
                    if j < 4:
                        for h in range(8):
                            act(e_[:, h * 64:(h + 1) * 64], g_[:, h * 64:(h + 1) * 64], AF.Identity, [g_, kdec], [e_], scale=kdec[:, h:h + 1])
                        rotary(e_, cs5[:, j, :], krd[:, j, :], krd)
                    else:
                        act(e_[:, :], g_[:, :], AF.Identity, [g_], [e_])
                        rotary(e_, cs5[:, 4, :], krown[:, :], krown)
                stop_at(34)
                wt, wv = load_w(C_VR, 512)
                for j in range(5):
                    g_ = nextG()
                    for kt in range(8):
                        mm(g_[:, :], hT[:, kt, j * 128:(j + 1) * 128], wv[:, kt, :], kt == 0, kt == 7, [wt, hT], [g_])
                    act(vrb[:, j, :], g_[:, :], AF.Identity, [g_], [vrb])
                stop_at(35)
                for j in range(4 if smp is None else 1):
                    if j == 0:
                        ts("pool", ownS[:, :, :], Sst[:, :, :], sel[:, 0:1], None, ALU.mult, None, [Sst, sel], [ownS])
                    else:
                        tS = next_ev()
                        tSv = tS.t[:, :].rearrange("p (a b) -> p a b", a=4)
                        ts("pool", tSv, Sst[:, :, :], sel[:, j:j + 1], None, ALU.mult, None, [Sst, sel], [tS])
                        tt("pool", ownS[:, :, :], ownS[:, :, :], tSv, ALU.add, [ownS, tS], [ownS])
                    for pr in range(4):
                        mm(XP[:, pr * 128:(pr + 1) * 128], krd[:, j, pr * 128:(pr + 1) * 128], vrb[:, j, pr * 128:(pr + 1) * 128], True, True,
                           [krd, vrb], [XP])
                    tmpS = next_ev()
                    act(tmpS[:, :], XP[:, :], AF.Identity, [XP], [tmpS])
                    tt("pool", Sst[:, :, :], Sst[:, :, :], g128[:, :, :], ALU.mult, [Sst, g128], [Sst])
                    tt("pool", Sst[:, :, :], Sst[:, :, :], tmpS[:, :].rearrange("p (a b) -> p a b", a=4), ALU.add, [Sst, tmpS], [Sst])
                for hb in range(2):
                    POOL(lambda e, hb=hb: e.tensor_copy(out=ownSb[hb * 64:(hb + 1) * 64, hb::2, :],
                                                        in_=ownS[hb * 64:(hb + 1) * 64, :, hb * 64:(hb + 1) * 64]), [ownS], [ownSb])
                stop_at(36)
                wt, wv = load_w(C_KID, 512)
                g_ = nextG()
                for kt in range(8):
                    mm(g_[:, :], wv[:, kt, 0:128], hT[:, kt, 0:512], kt == 0, kt == 7, [wt, hT], [g_])
                pb = g // 9
                cc = g % 9
                if os.environ.get('MK_V', '') != 'noki':
                    act(kiT[pb * 64:(pb + 1) * 64, cc * 512:(cc + 1) * 512], g_[pb * 64:(pb + 1) * 64, :], AF.Identity, [g_], [kiT])
                stop_at(37)
                wt, wv = load_w(C_QA, 512)
                g_ = nextG()
                for pr in range(4):
                    for kt in range(8):
                        mm(g_[:, pr * 128:(pr + 1) * 128], wv[:, kt, pr * 128:(pr + 1) * 128], hT[:, kt, 512:640], kt == 0, kt == 7, [wt, hT], [g_])
                for hb in range(2):
                    act(QT[sl][hb * 64:(hb + 1) * 64, hb::2, :], g_[hb * 64:(hb + 1) * 64, :].rearrange("p (a q) -> p a q", a=4), AF.Identity,
                        [g_], [QT[sl]])
                wt, wv = load_w(C_QID, 512)
                g_ = nextG()
                for h in range(4):
                    for kt in range(8):
                        mm(g_[:, h * 128:(h + 1) * 128], wv[:, kt, h * 128:(h + 1) * 128], hT[:, kt, 512:640], kt == 0, kt == 7, [wt, hT], [g_])
                for hb in range(2):
                    act(qiT2[hb][hb * 64:(hb + 1) * 64, :, :], g_[hb * 64:(hb + 1) * 64, :].rearrange("p (a q) -> p a q", a=4), AF.Identity,
                        [g_], [qiT2[hb]])
                wt, wv = load_w(C_KIWI, 512)
                g_ = nextG()
                for kt in range(8):
                    mm(g_[:, 0:68], hT[:, kt, 512:640], wv[:, kt, 0:68], kt == 0, kt == 7, [wt, hT], [g_])
                act(kio[:, :], g_[:, 0:68], AF.Identity, [g_], [kio])
                if smp is None:
                    stq("kio", ki_o[g * 128:(g + 1) * 128, :], kio[:, 0:64], [kio])
                else:
                    stq("kio", kis_o[smp, :, :], kio[0:4, 0:64], [kio])
                ts("pool", wsc[sl][:, :], kio[:, 64:68], 1.0 / 16.0, None, ALU.mult, None, [kio], [wsc[sl]])
                if PROJ_ONLY:
                    return
                wt, wv = load_w(C_ZA, 512)
                g_ = nextG()
                for kt in range(8):
                    mm(g_[:, :], hT[:, kt, 512:640], wv[:, kt, :], kt == 0, kt == 7, [wt, hT], [g_])
                act(siluza[sl][:, :], g_[:, :], AF.Silu, [g_], [siluza[sl]])
                wt, wv = load_w(C_QR, 512)
                g_ = nextG()
                for kt in range(8):
                    mm(g_[:, :], hT[:, kt, 512:640], wv[:, kt, :], kt == 0, kt == 7, [wt, hT], [g_])
                e_ = next_ev()
                for h in range(8):
                    act(e_[:, h * 64:(h + 1) * 64], g_[:, h * 64:(h + 1) * 64], AF.Identity, [g_, qdec], [e_], scale=qdec[:, h:h + 1])
                rotary(e_, cs5[:, 4, :], qp[:, :], qp)
                for pr in range(4):
                    PE(lambda e, pr=pr: e.transpose(out=TP[:, pr, :], in_=qp[:, pr * 128:(pr + 1) * 128], identity=identb[:, :]), [qp, identb], [TP])
                for hb in range(2):
                    act(QpT[hb * 64:(hb + 1) * 64, hb::2, :], TP[hb * 64:(hb + 1) * 64, 0:4, :], AF.Identity, [TP], [QpT])
                for pr in range(4):
                    PE(lambda e, pr=pr: e.transpose(out=TP[:, 4 + pr, :], in_=krown[:, pr * 128:(pr + 1) * 128], identity=identb[:, :]), [krown, identb], [TP])
                act(KrT[:, :, :], TP[:, 4:8, :], AF.Identity, [TP], [KrT])
                wt, wv = load_w(C_ZR, 512)
                g_ = nextG()
                for kt in range(8):
                    mm(g_[:, :], hT[:, kt, 512:640], wv[:, kt, :], kt == 0, kt == 7, [wt, hT], [g_])
                act(siluzr[:, :], g_[:, :], AF.Silu, [g_], [siluzr])
                for ci, (c0, dst) in enumerate([(C_GA, sga[sl]), (C_GA + 512, sga[sl]), (C_GR, sgr), (C_GR + 512, sgr)]):
                    wt, wv = load_w(c0, 512)
                    g_ = nextG()
                    for kt in range(8):
                        mm(g_[:, :], hT[:, kt, 512:640], wv[:, kt, :], kt == 0, kt == 7, [wt, hT], [g_])
                    act(dst[:, (ci % 2) * 512:(ci % 2 + 1) * 512], g_[:, :], AF.Sigmoid, [g_], [dst])
                stop_at(38)
                for h in range(8):
                    pr, hb = h // 2, h % 2
                    a_ = A[h // 4]
                    mm(a_[:, (h % 4) * 128:(h % 4 + 1) * 128], KrT[:, pr, :], QpT[:, h, :], True, True,
                       [KrT, QpT], [a_])
                stop_at(381)
                for hf in range(2):
                    e_ = next_ev()
                    act(e_[:, :], A[hf][:, :], AF.Identity, [A[hf]], [e_])
                    tt("pool", ATb[:, hf * 4:(hf + 1) * 4, :], e_[:, :].rearrange("p (a q) -> p a q", a=4), dtab[:, hf * 4:(hf + 1) * 4, :], ALU.mult,
                       [e_, dtab], [ATb])
                stop_at(382)
                for h in range(8):
                    pr, hb = h // 2, h % 2
                    mm(A[2][:, h * 64:(h + 1) * 64], ATb[:, h, :], vrb[:, 4, h * 64:(h + 1) * 64], True, False, [ATb, vrb], [A[2]])
                    mm(A[2][:, h * 64:(h + 1) * 64], QpT[:, h, :], ownSb[:, h, :], False, True,
                       [QpT, ownSb], [A[2]])
                stop_at(383)
                act(oret[:, :, :], A[2][:, :].rearrange("p (h d) -> p h d", h=8), AF.Identity, [A[2]], [oret])
                stop_at(39)
                DVE(lambda e: e.tensor_reduce(out=gst[:, :, 0], in_=oret[:, :, :], axis=AX.X, op=ALU.add), [oret], [gst])
                tt("pool", osq[:, :, :], oret[:, :, :], oret[:, :, :], ALU.mult, [oret], [osq])
                DVE(lambda e: e.tensor_reduce(out=gst[:, :, 1], in_=osq[:, :, :], axis=AX.X, op=ALU.add), [osq], [gst])
                ts("pool", gst[:, :, 2], gst[:, :, 0], 1.0 / 64, None, ALU.mult, None, [gst], [gst])
                tt("pool", gst[:, :, 3], gst[:, :, 2], gst[:, :, 2], ALU.mult, [gst], [gst])
                ts("pool", gst[:, :, 7], gst[:, :, 1], 1.0 / 64, None, ALU.mult, None, [gst], [gst])
                tt("pool", gst[:, :, 4], gst[:, :, 7], gst[:, :, 3], ALU.subtract, [gst], [gst])
                act(gst[:, :, 5], gst[:, :, 4], AF.Ln, [gst], [gst], bias=epsT[:, 0:1])
                act(gst[:, :, 6], gst[:, :, 5], AF.Exp, [gst], [gst], scale=-0.5)
                tt("pool", osq[:, :, :], oret[:, :, :], gst[:, :, 2:3].to_broadcast([128, 8, 64]), ALU.subtract, [oret, gst], [osq])
                tt("pool", osq[:, :, :], osq[:, :, :], gst[:, :, 6:7].to_broadcast([128, 8, 64]), ALU.mult, [osq, gst], [osq])
                osq2 = osq.t[:, :, :].rearrange("p h d -> p (h d)")
                tt("pool", osq2, osq2, gret_bc[:, :], ALU.mult, [osq, gret_bc], [osq])
                tt("pool", yrb[:, :], osq2, siluzr[:, :], ALU.mult, [osq, siluzr], [yrb])
                for pr in range(4):
                    PE(lambda e, pr=pr: e.transpose(out=TP[:, pr, :], in_=yrb[:, pr * 128:(pr + 1) * 128], identity=identb[:, :]), [yrb, identb], [TP])
                act(yrT[:, :, :], TP[:, 0:4, :], AF.Identity, [TP], [yrT])
                for hf in range(2):
                    wt, wv = load_wpo(4, 4, hf * 512, 512)
                    g_ = nextG()
                    for kt in range(4):
                        mm(g_[:, :], yrT[:, kt, :], wv[:, kt, :], kt == 0, kt == 3, [wt, yrT], [g_])
                    e_ = next_ev()
                    act(e_[:, :], g_[:, :], AF.Identity, [g_], [e_])
                    tt("pool", mr[sl][:, hf * 512:(hf + 1) * 512], e_[:, :], sgr[:, hf * 512:(hf + 1) * 512], ALU.mult, [e_, sgr], [mr[sl]])
            def scores(g):
                sl = g % 2
                off, wkb = wkinfo(g)
                for c in range(g + 1):
                    pb = c // 9
                    cc = c % 9
                    for h in range(4):
                        g_ = nextG()
                        mm(g_[:, :], qiT2[pb][:, h, :], kiT[:, cc * 512:(cc + 1) * 512], True, True, [qiT2[pb], kiT], [g_])
                        rli[0] += 1
                        r_ = rl[rli[0] % 2]
                        act(r_[:, :], g_[:, :], AF.Relu, [g_], [r_])
                        dst = work0[:, off + c * 512:off + (c + 1) * 512]
                        if h == 0:
                            if c == g:
                                stt("dve", dst, r_[:, :], wsc[sl][:, 0:1], penS[:, :], ALU.mult, ALU.add, [r_, wsc[sl], penS], wkb)
                            else:
                                ts("dve", dst, r_[:, :], wsc[sl][:, 0:1], None, ALU.mult, None, [r_, wsc[sl]], wkb)
                        else:
                            stt("dve", dst, r_[:, :], wsc[sl][:, h:h + 1], dst, ALU.mult, ALU.add, [r_, wsc[sl]] + wkb, wkb)

            def topk(g):
                off, wkb = wkinfo(g)
                L = (g + 1) * 512
                for r in range(TOPK // 8):
                    DVE(lambda e: e.max(out=m8[:, :], in_=work0[:, off:off + L]), wkb, [m8])
                    DVE(lambda e: e.match_replace(out=work0[:, off:off + L], in_to_replace=m8[:, :], in_values=work0[:, off:off + L], imm_value=NEG_BIG), wkb + [m8], wkb)

            def attention(g, smp=None):
                sl = g % 2
                off, wkb = wkinfo(g)
                nkt = 4 * (g + 1)
                for c in range(g + 1):
                    kc_ = kc[c % 2]; vc_ = vc[c % 2]; nm_ = nm[c % 2]
                    ld("kc%d" % (c % 2), kc_[:, :, :], kscr[:, :, c * 512:(c + 1) * 512], [kc_], [kscr_b[c]])
                    ld("vc%d" % (c % 2), vc_[:, :, :], vscr[4 * c:4 * c + 4, :, :].rearrange("t p c -> p t c"), [vc_], [vscr_b[c]])
                    ts("dve", nm_[:, :], work0[:, off + c * 512:off + (c + 1) * 512], -1.0e38, -30000.0, ALU.is_gt, ALU.mult, wkb, [nm_])
                    if c == g:
                        tt("dve", nm_[:, :], nm_[:, :], penM[:, :], ALU.add, [nm_, penM], [nm_])
                    if g == 0 and c == 0 and os.environ.get('MK_DBG'):
                        stq("dbg", dbgb[:, 0:512], nm_[:, :], [nm_])
                        stq("dbg", dbgb[:, 1024:1536], kc_[:, 0, :], [kc_])
                    for t in range(4):
                        kt_i = c * 4 + t
                        for hg in range(2):
                            a_ = A[(2 * (kt_i % 2)) + hg]
                            mm(a_[:, :], nm_[:, t * 128:(t + 1) * 128], ident4[:, :, :].rearrange("p a q -> p (a q)"), True, False,
                               [nm_, ident4], [a_])
                            for hh in range(4):
                                h = hg * 4 + hh
                                pr, hb = h // 2, h % 2
                                mm(a_[:, hh * 128:(hh + 1) * 128], kc_[:, pr, t * 128:(t + 1) * 128],
                                   QT[sl][:, h, :], False, hh == 3, [kc_, QT[sl]], [a_])
                            eti[0] += 1
                            et = ET[eti[0] % 2]
                            act(et[:, :], a_[:, :], AF.Exp, [a_], [et], scale=0.125)
                            if g == 0 and kt_i == 0 and hg == 0 and os.environ.get('MK_DBG'):
                                stq("dbg", dbgb[:, 512:1024], et[:, :], [et])
                                stq("dbg", dbgb[:, 1536:2048], QT[sl][:, 0:4, :].rearrange("p a q -> p (a q)"), [QT[sl]])
                            for hh in range(4):
                                h = hg * 4 + hh
                                if kt_i == 0 and hh == 0:
                                    mm(G[hg][:, 0:260], zl[:, :], vc_[:, t, 0:260], True, False, [zl, vc_], [G[hg]])
                                mm(G[hg][:, hh * 65:(hh + 1) * 65], et[:, hh * 128:(hh + 1) * 128], vc_[:, t, h * 65:(h + 1) * 65],
                                   False, (kt_i == nkt - 1 and hh == 3), [et, vc_], [G[hg]])
                for hg in range(2):
                    act(acc[:, hg * 4:(hg + 1) * 4, :], G[hg][:, 0:260].rearrange("p (h c) -> p h c", h=4), AF.Identity, [G[hg]], [acc])
                if g == 0 and os.environ.get('MK_DBG'):
                    stq("dbg", dbg[:, 512:1032], acc.t[:, :, :].rearrange("p h c -> p (h c)"), [acc])
                act(rden[:, :, 0], acc[:, :, 64], AF.Ln, [acc], [rden])
                act(rden[:, :, 1], rden[:, :, 0], AF.Exp, [rden], [rden], scale=-1.0)
                tt("pool", yaf[:, :, :], acc[:, :, 0:64], rden[:, :, 1:2].to_broadcast([128, 8, 64]), ALU.mult, [acc, rden], [yaf])
                tt("pool", yab[:, :], yaf.t[:, :, :].rearrange("p h d -> p (h d)"), siluza[sl][:, :], ALU.mult, [yaf, siluza[sl]], [yab])
                for pr in range(4):
                    PE(lambda e, pr=pr: e.transpose(out=TP[:, pr, :], in_=yab[:, pr * 128:(pr + 1) * 128], identity=identb[:, :]), [yab, identb], [TP])
                act(yaT[:, :, :], TP[:, 0:4, :], AF.Identity, [TP], [yaT])
                for hf in range(2):
                    wt, wv = load_wpo(0, 4, hf * 512, 512)
                    g_ = nextG()
                    for kt in range(4):
                        mm(g_[:, :], yaT[:, kt, :], wv[:, kt, :], kt == 0, kt == 3, [wt, yaT], [g_])
                    e_ = next_ev()
                    act(e_[:, :], g_[:, :], AF.Identity, [g_], [e_])
                    tt("pool", e_[:, :], e_[:, :], sga[sl][:, hf * 512:(hf + 1) * 512], ALU.mult, [e_, sga[sl]], [e_])
                    tt("pool", mg[:, hf * 512:(hf + 1) * 512], e_[:, :], mr[sl][:, hf * 512:(hf + 1) * 512], ALU.add, [e_, mr[sl]], [mg])
                for kt in range(8):
                    PE(lambda e, kt=kt: e.transpose(out=TP[:, kt, :], in_=mg[:, kt * 128:(kt + 1) * 128], identity=identb[:, :]), [mg, identb], [TP])
                act(mT[:, :, :], TP[:, :, :], AF.Identity, [TP], [mT])
                for hf in range(2):
                    wt, wv = load_wpo(8, 8, hf * 512, 512)
                    g_ = nextG()
                    for kt in range(8):
                        mm(g_[:, :], mT[:, kt, :], wv[:, kt, :], kt == 0, kt == 7, [wt, mT], [g_])
                    e_ = next_ev()
                    act(e_[:, :], g_[:, :], AF.Identity, [g_], [e_])
                    tt("pool", e_[:, :], e_[:, :], gate_bc[:, hf * 512:(hf + 1) * 512], ALU.mult, [e_, gate_bc], [e_])
                    tt("pool", res[:, hf * 512:(hf + 1) * 512], e_[:, :], xo[sl][:, hf * 512:(hf + 1) * 512], ALU.add, [e_, xo[sl]], [res])
                if g == 0 and os.environ.get('MK_DBG'):
                    stq("dbg", dbg[:, 2048:3072], res[:, :], [res])
                rstd_chain(res, res[:, :], D, 1)
                act(res[:, :], res[:, :], AF.Identity, [res, ssq], [res], scale=ssq[:, 1:2])
                tt("pool", yout[:, :], res[:, :], gfin_bc[:, :], ALU.mult, [res, gfin_bc], [yout])
                if smp is None:
                    stq("yo", y_o[g * 128:(g + 1) * 128, :], yout[:, :], [yout])
                else:
                    stq("yo", ys_o[smp, :, :], yout[0:4, :], [yout])

            n = NSTEPS_RUN
            if PROJ_ONLY:
                for g in range(n):
                    projscores(g)
                raise _Stop2()
            projscores(0)
            stop_at(3)
            scores(0)
            stop_at(4)
            for g in range(n):
                topk(g)
                if g + 1 < n:
                    projscores(g + 1)
                    if g + 1 < 8:
                        scores(g + 1)
                        attention(g)
                    else:
                        attention(g)
                        scores(g + 1)
                else:
                    attention(g)
            for h in range(8):
                pr, hb = h // 2, h % 2
                stq("sto", st_o[h, :, :], Sst[hb * 64:(hb + 1) * 64, pr, hb * 64:(hb + 1) * 64], [Sst])

            def ingest(s_):
                ld("pt", ptb[:, :], ptab[s_:s_ + 1, :].partition_broadcast(128), [ptb])
                ts("pool", idxa[:, :], ptb[:, :], 128.0, pidx[:, 0:1], ALU.mult, ALU.add, [ptb, pidx], [idxa])
                POOL(lambda e: e.memset(Sst[:, :, :], 0.0), [], [Sst])
                for hb in range(2):
                    ld("sti", Sst[hb * 64:(hb + 1) * 64, :, hb * 64:(hb + 1) * 64],
                       state_in[s_, hb::2, :, :].rearrange("h p e -> p h e"), [Sst])
                for pg in range(64):
                    c = pg // 4; t = pg % 4
                    kp = rl[pg % 2]; vp = ev[pg % 2]
                    S.dma("gk%d" % (pg % 2), lambda e, kp=kp, pg=pg: e.indirect_dma_start(
                        out=kp[:, :], out_offset=None, in_=cache_k_d,
                        in_offset=bass.IndirectOffsetOnAxis(ap=idxa[:, pg:pg + 1], axis=0)), reads=[_b(idxa)], writes=[_b(kp)], q="pool")
                    S.dma("gv%d" % (pg % 2), lambda e, vp=vp, pg=pg: e.indirect_dma_start(
                        out=vp[:, :], out_offset=None, in_=cache_v_d,
                        in_offset=bass.IndirectOffsetOnAxis(ap=idxa[:, pg:pg + 1], axis=0)), reads=[_b(idxa)], writes=[_b(vp)], q="pool")
                    S.dma("gi", lambda e, pg=pg: e.indirect_dma_start(
                        out=kio[:, 0:64], out_offset=None, in_=cache_ki_d,
                        in_offset=bass.IndirectOffsetOnAxis(ap=idxa[:, pg:pg + 1], axis=0)), reads=[_b(idxa)], writes=[_b(kio)], q="pool")
                    kb = qp if pg % 2 == 0 else krown
                    act(kb[:, :], kp[:, :], AF.Identity, [kp], [kb])
                    for pr in range(4):
                        PE(lambda e, pr=pr, kb=kb: e.transpose(out=TP[:, pr, :], in_=kb[:, pr * 128:(pr + 1) * 128], identity=identb[:, :]),
                           [kb, identb], [TP])
                    act(kst[:, :, t * 128:(t + 1) * 128], TP[:, 0:4, :], AF.Identity, [TP], [kst])
                    act(vst[:, t, :, 0:64], vp[:, :].rearrange("p (h d) -> p h d", h=8), AF.Identity, [vp], [vst])
                    for dd in range(2):
                        act(yab[:, dd * 64:(dd + 1) * 64], kio[:, 0:64], AF.Identity, [kio], [yab])
                    PE(lambda e: e.transpose(out=TP[:, 4, :], in_=yab[:, 0:128], identity=identb[:, :]), [yab, identb], [TP])
                    pb = c // 9; cc = c % 9
                    act(kiT[pb * 64:(pb + 1) * 64, cc * 512 + t * 128:cc * 512 + (t + 1) * 128], TP[pb * 64:(pb + 1) * 64, 4, :], AF.Identity,
                        [TP], [kiT])
                    if t == 3:
                        stq("kst", kscr[:, :, c * 512:(c + 1) * 512], kst[:, :, :], [kst], [kscr_b[c]])
                        stq("vst", vscr[4 * c:4 * c + 4, :, :].rearrange("t p c -> p t c"), vst[:, :, :, :].rearrange("p t h c -> p t (h c)"),
                            [vst], [vscr_b[c]])

            NSAMP = int(os.environ.get("MK_NSAMP", "4"))
            if NSAMP > 0:
                S.barrier()
                ld("c1", kdec[:, :], kdec_s_d, [kdec]); ld("c1", qdec[:, :], qdec_s_d, [qdec]); ld("c1", dtab[:, :, :], dtab_s_d, [dtab])
                ld("c1", g128[:, :, :], g4_d, [g128]); ld("c1", sel[:, :], sel_s_d, [sel]); ld("c1", penS[:, :], penS_s_d, [penS])
                pe_ = next_ev()
                ld("c1", pe_[:, :], penM_s_d, [pe_])
                POOL(lambda e, pe_=pe_: e.tensor_copy(out=penM[:, :], in_=pe_[:, :]), [pe_], [penM])
                for s_ in range(NSAMP):
                    ingest(s_)
                    stop_at(71)
                    cur["j"] = 1 + s_
                    ld("g5l", gate_bc[:, :], gate5_d[1 + s_:2 + s_, :].partition_broadcast(128), [gate_bc], [gate5_b])
                    projscores(16, smp=s_)
                    stop_at(72)
                    scores(16)
                    topk(16)
                    stop_at(73)
                    attention(16, smp=s_)
                    for h in range(8):
                        pr, hb = h // 2, h % 2
                        stq("sto", sts_o[s_, h, :, :], Sst[hb * 64:(hb + 1) * 64, pr, hb * 64:(hb + 1) * 64], [Sst])
        except _Stop:
            st2.close()
        except _Stop2:
            for h in range(8):
                pr, hb = h // 2, h % 2
                stq("sto", st_o[h, :, :], Sst[hb * 64:(hb + 1) * 64, pr, hb * 64:(hb + 1) * 64], [Sst])
        S.finish()
        S.emit()
    return nc


def _consts():
    half = 32
    inv = (10000.0 ** (-np.arange(half, dtype=np.float32) / half)).astype(np.float32)
    pos = np.arange(SEQ, dtype=np.float32)
    ang = (pos[:, None] * inv[None, :]).astype(np.float32)
    cs = np.concatenate([np.cos(ang), np.sin(ang)], axis=1).astype(np.float32)
    lg = np.log1p(-np.exp2(-5.0 - np.arange(8, dtype=np.float64)))
    i = np.arange(128, dtype=np.float64)
    kdec = (0.125 * np.exp(lg[None, :] * (127.0 - i)[:, None])).astype(np.float32)
    qdec = np.exp(lg[None, :] * (i[:, None] + 1.0)).astype(np.float32)
    dt = np.zeros((128, 8, 128), np.float64)
    for h in range(8):
        v = 0.125 * np.exp(-lg[h] * (i + 1.0))
        dt[:, h, :] = v[:, None] * (i[None, :] >= i[:, None])
    g128 = np.zeros((128, 4, 128), np.float32)
    for pr in range(4):
        for hb in range(2):
            g128[hb * 64:(hb + 1) * 64, pr, :] = np.exp(lg[2 * pr + hb] * 128.0)
    return cs, kdec, qdec, dt.astype(np.float32), g128


def _consts_sample():
    half = 32
    inv = (10000.0 ** (-np.arange(half, dtype=np.float32) / half)).astype(np.float32)
    tt_ = np.minimum(np.arange(128), 3)
    pos = (8192 + tt_).astype(np.float32)
    ang = (pos[:, None] * inv[None, :]).astype(np.float32)
    cs_s = np.concatenate([np.cos(ang), np.sin(ang)], axis=1).astype(np.float32)
    lg = np.log1p(-np.exp2(-5.0 - np.arange(8, dtype=np.float64)))
    p = np.arange(128, dtype=np.float64)
    real = (p < 4)
    kdec_s = (0.125 * np.exp(lg[None, :] * (3.0 - p)[:, None]) * real[:, None]).astype(np.float32)
    qdec_s = np.exp(lg[None, :] * (tt_.astype(np.float64)[:, None] + 1.0)).astype(np.float32)
    dt = np.zeros((128, 8, 128), np.float64)
    for h in range(8):
        v = 0.125 * np.exp(-lg[h] * (p + 1.0)) * real
        dt[:, h, :] = v[:, None] * (tt_[None, :] >= p[:, None])
    g4 = np.zeros((128, 4, 128), np.float32)
    for pr in range(4):
        for hb in range(2):
            g4[hb * 64:(hb + 1) * 64, pr, :] = np.exp(lg[2 * pr + hb] * 4.0)
    j = np.arange(512)[None, :]
    adm = j <= tt_[:, None]
    penS_s = np.where(adm, 0.0, -1.0e30).astype(np.float32)
    penM_s = np.where(adm, 0.0, -30000.0).astype(np.float32)
    return cs_s, kdec_s, qdec_s, dt.astype(np.float32), g4, penS_s, penM_s


def kernel(x_prompt, x_sample, cache_k, cache_v, cache_kidx, state_ret, page_table, c_prompt, c_sample,
           w_ada, b_ada, g_norm, w_in, g_ret, w_pa, w_pr, w_out, g_final):
    f = lambda a: np.ascontiguousarray(np.asarray(a, dtype=np.float32))
    x_prompt = f(x_prompt); w_in0 = f(w_in)[0]
    cols = np.concatenate([
        np.arange(512, 1024), np.arange(1024, 1536), np.arange(2884, 3396), np.arange(3396, 3908),
        np.arange(2304, 2368), np.arange(2304, 2368),
        np.arange(0, 512),
        np.concatenate([np.concatenate([np.arange(2048 + 64 * h, 2048 + 64 * (h + 1))] * 2) for h in range(4)]),
        np.arange(2304, 2372), np.arange(1536, 2048), np.arange(2372, 2884), np.arange(3908, 4420),
        np.arange(4420, 5444), np.arange(5444, 6468)])
    assert cols.shape[0] == NEXT
    ptile = lambda w: np.ascontiguousarray(w.reshape(w.shape[0] // 128, 128, w.shape[1]).transpose(1, 0, 2))
    wext = ptile(w_in0[:, cols])
    wpo = ptile(np.concatenate([f(w_pa)[0], f(w_pr)[0], f(w_out)[0]], axis=0))
    wada = ptile(f(w_ada)[0])
    badaT = np.ascontiguousarray(f(b_ada)[0].reshape(24, 128).T)
    bgate = np.ascontiguousarray(np.broadcast_to(f(b_ada)[0][2048:3072], (128, 1024)))
    gnT = np.ascontiguousarray(f(g_norm)[0].reshape(8, 128).T)
    gfin = np.ascontiguousarray(np.broadcast_to(f(g_final), (128, 1024)))
    gret = np.ascontiguousarray(np.broadcast_to(f(g_ret)[0], (128, 512)))
    cs, kdec, qdec, dtab, g128 = _consts()
    cs_s, kdec_s, qdec_s, dtab_s, g4, penS_s, penM_s = _consts_sample()
    x_sample = f(x_sample); state_ret = f(state_ret)
    ck2 = f(cache_k)[0].reshape(2560 * 128, 512); cv2 = f(cache_v)[0].reshape(2560 * 128, 512)
    cki2 = f(cache_kidx)[0].reshape(2560 * 128, 64)
    page_table = np.asarray(page_table).astype(np.int32)
    xzero = np.zeros((128, D), np.float32)
    sel_s = np.zeros((128, 4), np.float32); sel_s[:, 0] = 1.0
    pidx = np.arange(128, dtype=np.float32).reshape(128, 1)
    ident = np.eye(128, dtype=np.float32)
    iota = np.ascontiguousarray(np.broadcast_to(np.arange(512, dtype=np.float32), (128, 512)))
    c_prompt = f(c_prompt); c_sample = f(c_sample)
    in_maps = []
    for c in range(8):
        b, r = c // 4, c % 4
        own_tiles = np.array([4 * g + r for g in range(NSTEP)])
        rows = (own_tiles[:, None] * 128 + np.arange(128)[None, :]).reshape(-1)
        cvec = np.stack([c_prompt[b]] + [c_sample[4 * c + s] for s in range(4)], axis=1)
        cT = np.ascontiguousarray(cvec.reshape(8, 128, 5).transpose(1, 0, 2))
        sel = np.zeros((128, 4), np.float32); sel[:, r] = 1.0
        xsp = np.zeros((4, 128, D), np.float32)
        for s_ in range(4):
            xsp[s_, 0:4] = x_sample[4 * c + s_]
        qrel = (r * 128 + np.arange(128, dtype=np.float32)).reshape(128, 1)
        in_maps.append({
            "xall": x_prompt[b], "xown": np.ascontiguousarray(x_prompt[b][rows]),
            "wext": wext, "wpo": wpo, "wada": wada, "badaT": badaT, "bgate": bgate, "gnT": gnT, "gfin": gfin, "gret": gret,
            "cT": cT, "ident": ident, "cs_all": cs, "cs_own": np.ascontiguousarray(cs[rows]), "kdec": kdec, "qdec": qdec,
            "dtab": dtab, "g128": g128, "sel": sel, "iota512": iota, "qrel": qrel,
            "xsp": xsp, "xzero": xzero, "cache_k2": ck2, "cache_v2": cv2, "cache_ki2": cki2,
            "ptab": np.ascontiguousarray(page_table[4 * c:4 * c + 4]), "state_in": np.ascontiguousarray(state_ret[0, 4 * c:4 * c + 4]),
            "cs_s": cs_s, "kdec_s": kdec_s, "qdec_s": qdec_s, "dtab_s": dtab_s, "g4": g4, "sel_s": sel_s,
            "penS_s": penS_s, "penM_s": penM_s, "pidx": pidx,
        })
    nc = build_nc()
    resr = run_bass_kernel_spmd(nc, in_maps, core_ids=list(range(8)))
    R = resr.results
    y_prompt = np.zeros((2, SEQ, D), np.float32)
    k_rows = np.zeros((1, 2, SEQ, 8, 64), np.float32); v_rows = np.zeros((1, 2, SEQ, 8, 64), np.float32)
    ki_rows = np.zeros((1, 2, SEQ, 64), np.float32); st_p = np.zeros((1, 2, 8, 64, 64), np.float32)
    for c in range(8):
        b, r = c // 4, c % 4
        for g in range(NSTEP):
            t = 4 * g + r
            sl_ = slice(t * 128, (t + 1) * 128); so = slice(g * 128, (g + 1) * 128)
            y_prompt[b, sl_] = R[c]["y_o"][so]
            k_rows[0, b, sl_] = R[c]["k_o"][so].reshape(128, 8, 64)
            v_rows[0, b, sl_] = R[c]["v_o"][so].reshape(128, 8, 64)
            ki_rows[0, b, sl_] = R[c]["ki_o"][so]
        if r == 3:
            st_p[0, b] = R[c]["st_o"]
    y_sample = np.zeros((32, 4, D), np.float32)
    ks = np.zeros((1, 32, 4, 8, 64), np.float32); vs = np.zeros((1, 32, 4, 8, 64), np.float32)
    kis = np.zeros((1, 32, 4, 64), np.float32); sts = np.zeros((1, 32, 8, 64, 64), np.float32)
    for c in range(8):
        for s_ in range(4):
            bi = 4 * c + s_
            y_sample[bi] = R[c]["ys_o"][s_]
            ks[0, bi] = R[c]["ks_o"][s_].reshape(4, 8, 64); vs[0, bi] = R[c]["vs_o"][s_].reshape(4, 8, 64)
            kis[0, bi] = R[c]["kis_o"][s_]; sts[0, bi] = R[c]["sts_o"][s_]
    return (y_prompt, y_sample, k_rows, v_rows, ki_rows, st_p, ks, vs, kis, sts)
```

```python
import os
import numpy as np
from contextlib import ExitStack
import concourse.bass as bass
import concourse.mybir as mybir
from concourse.bass_utils import run_bass_kernel_spmd

F32 = mybir.dt.float32
BF16 = mybir.dt.bfloat16
AF = mybir.ActivationFunctionType
ALU = mybir.AluOpType
AX = mybir.AxisListType

D = 1024
SEQ = 8192
NSTEP = 16
NSTEPS_RUN = int(os.environ.get("MK_NSTEPS", "16"))
TOPK = 256
NEG_BIG = -3.0e38
C_KA, C_VA, C_KR, C_VR, C_KID = 0, 512, 1024, 1536, 2048
C_QA, C_QID, C_KIWI, C_ZA, C_QR, C_ZR, C_GA, C_GR = 2176, 2688, 3200, 3268, 3780, 4292, 4804, 5828
NEXT = 6852
ENGS = ("pe", "act", "dve", "pool", "sp")


class _Stop(Exception):
    pass


class _Stop2(Exception):
    pass


STOP = int(os.environ.get('MK_STOP', '99'))
PROJ_ONLY = os.environ.get('MK_FULL', '1') != '1'


def stop_at(k):
    if STOP == k:
        raise _Stop()


class Buf:
    __slots__ = ("w", "r")

    def __init__(self):
        self.w = None
        self.r = []


class Tl:
    def __init__(self, t):
        self.t = t
        self.b = Buf()

    def __getitem__(self, k):
        return self.t[k]


def _b(x):
    return x.b if hasattr(x, "b") else x


class Sched:
    def __init__(self, nc, stack):
        self.nc = nc
        self.stack = stack
        self.ops = {e: [] for e in ENGS}
        self.cnt = {e: 0 for e in ENGS}
        self.esem = {e: stack.enter_context(nc.semaphore("prog_" + e)) for e in ENGS if e != "sp"}
        self.dsem = {}
        self.dcnt = {}
        self.waited = {}
        self.same = {"pe": False, "act": True, "dve": True, "pool": True, "sp": False}

    def _collect(self, eng, reads, writes):
        deps = []
        for b in reads:
            if b.w is not None:
                deps.append(b.w)
        for b in writes:
            if b.w is not None:
                deps.append(b.w)
            deps.extend(b.r)
        waits = []
        for d in deps:
            if d[0] == "e":
                if d[1] == eng and not self.same[eng]:
                    continue
                key = (eng, "e", d[1])
                val = d[2]
                sem = self.esem[d[1]]
            else:
                key = (eng, "d", d[1])
                val = self.dcnt[d[1]]
                sem = self.dsem[d[1]]
            if self.waited.get(key, 0) >= val:
                continue
            self.waited[key] = val
            waits.append((sem, val))
        return waits

    def _update(self, d, reads, writes):
        for b in reads:
            b.r.append(d)
            if len(b.r) > 64:
                b.r = b.r[-64:] if False else b.r
        for b in writes:
            b.w = d
            b.r = []

    def op(self, eng, fn, reads=(), writes=()):
        reads = [_b(x) for x in reads]
        writes = [_b(x) for x in writes]
        waits = self._collect(eng, reads, writes)
        self.cnt[eng] += 1
        idx = self.cnt[eng]
        sem = self.esem[eng]

        def thunk(e, waits=waits, fn=fn, sem=sem):
            for ws, wv in waits:
                e.wait_ge(ws, wv)
            fn(e).then_inc(sem, 1)

        self.ops[eng].append(thunk)
        self._update(("e", eng, idx), reads, writes)

    def dma(self, semkey, fn, reads=(), writes=(), q="sp"):
        reads = [_b(x) for x in reads]
        writes = [_b(x) for x in writes]
        if semkey not in self.dsem:
            self.dsem[semkey] = self.stack.enter_context(self.nc.semaphore("d_" + semkey))
            self.dcnt[semkey] = 0
        waits = self._collect(q, reads, writes)
        self.dcnt[semkey] += 16
        sem = self.dsem[semkey]

        def thunk(e, waits=waits, fn=fn, sem=sem):
            for ws, wv in waits:
                e.wait_ge(ws, wv)
            fn(e).then_inc(sem, 16)

        self.ops[q].append(thunk)
        self._update(("d", semkey), reads, writes)

    def barrier(self):
        fin = [(sem, self.dcnt[k]) for k, sem in self.dsem.items() if self.dcnt[k] > 0]
        fin += [(sem, self.cnt[e]) for e, sem in self.esem.items() if self.cnt[e] > 0]
        for q in ENGS:
            def thunk(e, fin=fin):
                for ws, wv in fin:
                    e.wait_ge(ws, wv)
            self.ops[q].append(thunk)

    def finish(self, q="sp"):
        fin = [(sem, self.dcnt[k]) for k, sem in self.dsem.items()]
        fin += [(sem, self.cnt[e]) for e, sem in self.esem.items() if self.cnt[e] > 0]

        def thunk(e, fin=fin):
            for ws, wv in fin:
                e.wait_ge(ws, wv)

        self.ops[q].append(thunk)

    def emit(self):
        nc = self.nc
        ops = self.ops
        with nc.Block() as block:
            @block.tensor
            def _(e):
                for t in ops["pe"]:
                    t(e)

            @block.scalar
            def _(e):
                for t in ops["act"]:
                    t(e)

            @block.vector
            def _(e):
                for t in ops["dve"]:
                    t(e)

            @block.gpsimd
            def _(e):
                for t in ops["pool"]:
                    t(e)

            @block.sync
            def _(e):
                for t in ops["sp"]:
                    t(e)


def build_nc():
    nc = bass.Bass("TRN2", target_bir_lowering=False)
    din = lambda n, s, dt=F32: nc.dram_tensor(n, list(s), dt, kind="ExternalInput").ap()
    dout = lambda n, s, dt=F32: nc.dram_tensor(n, list(s), dt, kind="ExternalOutput").ap()
    dscr = lambda n, s, dt: nc.dram_tensor(n, list(s), dt, kind="Internal").ap()
    xall = din("xall", [SEQ, D]); xown = din("xown", [NSTEP * 128, D])
    wext = din("wext", [128, 8, NEXT]); wpo = din("wpo", [128, 16, 1024]); wada = din("wada", [128, 8, 3072])
    badaT = din("badaT", [128, 24]); bgate = din("bgate", [128, 1024]); gnT = din("gnT", [128, 8])
    gfin = din("gfin", [128, 1024]); gret = din("gret", [128, 512]); cT = din("cT", [128, 8, 5])
    ident_d = din("ident", [128, 128]); cs_all = din("cs_all", [SEQ, 64]); cs_own = din("cs_own", [NSTEP * 128, 64])
    kdec_d = din("kdec", [128, 8]); qdec_d = din("qdec", [128, 8]); dtab_d = din("dtab", [128, 8, 128])
    g128_d = din("g128", [128, 4, 128]); sel_d = din("sel", [128, 4]); iota_d = din("iota512", [128, 512])
    qrel_d = din("qrel", [128, 1])
    I32 = mybir.dt.int32
    xsp = din("xsp", [4, 128, D]); xzero = din("xzero", [128, D])
    cache_k_d = din("cache_k2", [2560 * 128, 512]); cache_v_d = din("cache_v2", [2560 * 128, 512]); cache_ki_d = din("cache_ki2", [2560 * 128, 64])
    ptab = din("ptab", [4, 64], I32); state_in = din("state_in", [4, 8, 64, 64])
    cs_s_d = din("cs_s", [128, 64]); kdec_s_d = din("kdec_s", [128, 8]); qdec_s_d = din("qdec_s", [128, 8])
    dtab_s_d = din("dtab_s", [128, 8, 128]); g4_d = din("g4", [128, 4, 128]); sel_s_d = din("sel_s", [128, 4])
    penS_s_d = din("penS_s", [128, 512]); penM_s_d = din("penM_s", [128, 512]); pidx_d = din("pidx", [128, 1])
    ys_o = dout("ys_o", [4, 4, D]); ks_o = dout("ks_o", [4, 4, 512]); vs_o = dout("vs_o", [4, 4, 512])
    kis_o = dout("kis_o", [4, 4, 64]); sts_o = dout("sts_o", [4, 8, 64, 64])
    gate5_d = dscr("gate5_d", [5, D], F32)
    y_o = dout("y_o", [NSTEP * 128, D]); k_o = dout("k_o", [NSTEP * 128, 512]); v_o = dout("v_o", [NSTEP * 128, 512])
    ki_o = dout("ki_o", [NSTEP * 128, 64]); st_o = dout("st_o", [8, 64, 64])
    wbf = dscr("wbf", [128, 8, NEXT], BF16); wpobf = dscr("wpobf", [128, 16, 1024], BF16)
    kscrs = [dscr("kscr%d" % i, [128, 4, SEQ + 512], BF16) for i in range(2)]; vscrs = [dscr("vscr%d" % i, [SEQ // 128 + 4, 128, 520], BF16) for i in range(2)]

    with ExitStack() as st:
        S = Sched(nc, st)
        sb = lambda n, s, dt=F32: Tl(st.enter_context(nc.sbuf_tensor("s_" + n, list(s), dt)))
        psb = lambda n, s, dt=F32: Tl(st.enter_context(nc.psum_tensor("p_" + n, list(s), dt)))
        ACT = lambda fn, r=(), w=(): S.op("act", fn, r, w)
        POOL = lambda fn, r=(), w=(): S.op("pool", fn, r, w)
        DVE = lambda fn, r=(), w=(): S.op("dve", fn, r, w)
        PE = lambda fn, r=(), w=(): S.op("pe", fn, r, w)

        G = [psb("G%d" % i, [128, 512]) for i in range(2)]
        A = [psb("A%d" % i, [128, 512]) for i in range(4)]
        TP = psb("TP", [128, 8, 128], BF16)
        XP = psb("XP", [128, 512])
        gi = [0]

        def nextG():
            gi[0] ^= 1
            return G[gi[0]]

        kscr_bs = [[Buf() for _ in range(NSTEP + 1)] for _ in range(2)]; vscr_bs = [[Buf() for _ in range(NSTEP + 1)] for _ in range(2)]
        wbf_b = Buf(); wpobf_b = Buf()
        kiT = sb("kiT", [128, 4608], BF16)
        wsl = [sb("wsl%d" % i, [128, 4096], BF16) for i in range(2)]
        wi_ = [0]
        hT = sb("hT", [128, 8, 640], BF16)
        xo = [sb("xo%d" % i, [128, D]) for i in range(2)]
        xn = sb("xn", [128, D], BF16); sqj = xn
        ssq = sb("ssq", [128, 4]); epsT = sb("epsT", [128, 1])
        kst = sb("kst", [128, 4, 512], BF16); vst = sb("vst", [128, 4, 8, 65], BF16)
        ko = sb("ko", [128, 512]); vo = sb("vo", [128, 512]); kio = sb("kio", [128, 68])
        ev = [sb("ev%d" % i, [128, 512]) for i in range(2)]
        evi = [0]
        rt = [sb("rt%d" % i, [128, 8, 32]) for i in range(4)]
        krd = sb("krd", [128, 4, 512], BF16); vrb = sb("vrb", [128, 5, 512], BF16)
        krown = sb("krown", [128, 512], BF16); KrT = sb("KrT", [128, 4, 128], BF16)
        qp = sb("qp", [128, 512], BF16); QpT = sb("QpT", [128, 8, 128], BF16)
        Sst = sb("Sst", [128, 4, 128]); ownS = sb("ownS", [128, 4, 128])
        ownSb = sb("ownSb", [128, 8, 64], BF16)
        QT = [sb("QT%d" % i, [128, 8, 128], BF16) for i in range(2)]
        qiT2 = [sb("qiTlo", [128, 4, 128], BF16), sb("qiThi", [128, 4, 128], BF16)]
        wsc = [sb("wsc%d" % i, [128, 4]) for i in range(2)]
        siluza = [sb("siluza%d" % i, [128, 512], BF16) for i in range(2)]
        sga = [sb("sga%d" % i, [128, 1024], BF16) for i in range(2)]
        sgr = sb("sgr", [128, 1024], BF16)
        mr = [sb("mr%d" % i, [128, 1024], BF16) for i in range(2)]
        siluzr = sb("siluzr", [128, 512], BF16)
        rl = [sb("rl%d" % i, [128, 512]) for i in range(2)]
        rli = [0]
        eti = [0]
        dtab = sb("dtab", [128, 8, 128]); g128 = sb("g128", [128, 4, 128])
        cs5 = sb("cs5", [128, 5, 64])
        identf = sb("identf", [128, 128]); identb = sb("identb", [128, 128], BF16)
        ident4 = sb("ident4", [128, 4, 128], BF16); zl = sb("zl", [128, 128], BF16)
        ATb = sb("ATb", [128, 8, 128], BF16)
        oret = sb("oret", [128, 8, 64]); osq = sb("osq", [128, 8, 64])
        gst = sb("gst", [128, 8, 8])
        yrb = sb("yrb", [128, 512], BF16); yrT = sb("yrT", [128, 4, 128], BF16)
        acc = sb("acc", [128, 8, 65]); rden = sb("rden", [128, 8, 2])
        yaf = oret; yab = sb("yab", [128, 512], BF16); yaT = sb("yaT", [128, 4, 128], BF16)
        mg = sb("mg", [128, 1024], BF16); mT = sb("mT", [128, 8, 128], BF16)
        res = sb("res", [128, D]); yout = res; xs = [res] * 2
        gate_bc = sb("gate_bc", [128, D]); gfin_bc = sb("gfin_bc", [128, D]); gret_bc = sb("gret_bc", [128, 512])
        penS = sb("penS", [128, 512]); penM = sb("penM", [128, 512], BF16)
        kdec = sb("kdec", [128, 8]); qdec = sb("qdec", [128, 8]); sel = sb("sel", [128, 4])
        AcA = sb("AcA", [128, 5, 8]); BcA = sb("BcA", [128, 5, 8]); cur = {"j": 0, "set": 0}
        ptb = sb("ptb", [128, 64], I32); idxa = sb("idxa", [128, 64], I32); pidx = sb("pidx", [128, 1])
        m8 = sb("m8", [128, 8])

        def ld(key, dst_ap, src_ap, w, r=()):
            S.dma(key, lambda e: e.dma_start(out=dst_ap, in_=src_ap), reads=r, writes=w)

        def stq(key, dst_ap, src_ap, r, w=()):
            S.dma(key, lambda e: e.dma_start(out=dst_ap, in_=src_ap), reads=r, writes=w)

        def mm(out, lhsT, rhs, start, stop, r, w):
            PE(lambda e: e.matmul(out=out, lhsT=lhsT, rhs=rhs, start=start, stop=stop), r, w)

        def act(out, in_, func, r, w, **kw):
            ACT(lambda e: e.activation(out=out, in_=in_, func=func, **kw), r, w)

        def tt(eng, out, in0, in1, op, r, w):
            S.op(eng, lambda e: e.tensor_tensor(out=out, in0=in0, in1=in1, op=op), r, w)

        def ts(eng, out, in0, s1, s2, op0, op1, r, w):
            if op1 is None:
                S.op(eng, lambda e: e.tensor_scalar(out=out, in0=in0, scalar1=s1, scalar2=None, op0=op0), r, w)
            else:
                S.op(eng, lambda e: e.tensor_scalar(out=out, in0=in0, scalar1=s1, scalar2=s2, op0=op0, op1=op1), r, w)

        def stt(eng, out, in0, scalar, in1, op0, op1, r, w):
            S.op(eng, lambda e: e.scalar_tensor_tensor(out=out, in0=in0, scalar=scalar, in1=in1, op0=op0, op1=op1), r, w)

        def load_w(col0, ncols):
            sl = wsl[wi_[0] % 2]; wi_[0] += 1
            v = sl.t[:, 0:8 * ncols].rearrange("p (k c) -> p k c", k=8)
            ld("w%d" % ((wi_[0] - 1) % 2), v, wbf[:, :, col0:col0 + ncols], [sl], [wbf_b])
            return sl, v

        def load_wpo(kt0, nk, c0, ncols):
            sl = wsl[wi_[0] % 2]; wi_[0] += 1
            v = sl.t[:, 0:nk * ncols].rearrange("p (k c) -> p k c", k=nk)
            ld("w%d" % ((wi_[0] - 1) % 2), v, wpobf[:, kt0:kt0 + nk, c0:c0 + ncols], [sl], [wpobf_b])
            return sl, v

        def rstd_chain(src_tl, src_ap, n, col):
            act(sqj[:, 0:n], src_ap, AF.Square, [src_tl], [sqj, ssq], accum_out=ssq[:, 3:4])
            act(ssq[:, 2:3], ssq[:, 3:4], AF.Ln, [ssq], [ssq], scale=1.0 / n, bias=epsT[:, 0:1])
            act(ssq[:, col:col + 1], ssq[:, 2:3], AF.Exp, [ssq], [ssq], scale=-0.5)

        def make_hT(xt, col0):
            rstd_chain(xt, xt[:, :], D, 0)
            act(xn[:, :], xt[:, :], AF.Identity, [xt, ssq], [xn], scale=ssq[:, 0:1])
            for kt in range(8):
                PE(lambda e, kt=kt: e.transpose(out=TP[:, kt, :], in_=xn[:, kt * 128:(kt + 1) * 128], identity=identb[:, :]),
                   [xn, identb], [TP])
            for kt in range(8):
                jj = cur["j"]
                act(hT[:, kt, col0:col0 + 128], TP[:, kt, :], AF.Identity, [TP, AcA, BcA], [hT],
                    scale=AcA[:, jj, kt:kt + 1], bias=BcA[:, jj, kt:kt + 1])

        def rotary(src, cs_ap, dst_ap, dst_tl):
            sv = src.t[:, :].rearrange("p (h t f) -> p h t f", h=8, t=2)
            dv = dst_ap.rearrange("p (h t f) -> p h t f", h=8, t=2)
            cosb = cs_ap[:, 0:32].unsqueeze(1).to_broadcast([128, 8, 32])
            sinb = cs_ap[:, 32:64].unsqueeze(1).to_broadcast([128, 8, 32])
            x1 = sv[:, :, 0, :]; x2 = sv[:, :, 1, :]
            tt("pool", rt[0][:, :, :], x1, cosb, ALU.mult, [src, cs5], [rt[0]])
            tt("pool", rt[1][:, :, :], x2, sinb, ALU.mult, [src, cs5], [rt[1]])
            tt("pool", dv[:, :, 0, :], rt[0][:, :, :], rt[1][:, :, :], ALU.subtract, [rt[0], rt[1]], [dst_tl])
            tt("pool", rt[2][:, :, :], x1, sinb, ALU.mult, [src, cs5], [rt[2]])
            tt("pool", rt[3][:, :, :], x2, cosb, ALU.mult, [src, cs5], [rt[3]])
            tt("pool", dv[:, :, 1, :], rt[2][:, :, :], rt[3][:, :, :], ALU.add, [rt[2], rt[3]], [dst_tl])

        def next_ev():
            evi[0] += 1
            return ev[evi[0] % 2]

        try:
            st2 = ExitStack()
            sb2 = lambda n, s_, dt=F32: Tl(st2.enter_context(nc.sbuf_tensor("s_" + n, list(s_), dt)))
            bgate_sb = sb2("bgate_sb", [128, D])
            iota = sb2("iota", [128, 512]); qrel = sb2("qrel", [128, 1])
            cTs = sb2("cTs", [128, 8, 5]); scT = sb2("scT", [128, 8, 5]); gate5 = sb2("gate5", [5, D])
            modT = sb2("modT", [128, 16, 5]); badaTs = sb2("badaTs", [128, 24]); gnTs = sb2("gnTs", [128, 8])
            wadas = [sb2("wadas%d" % i, [128, 8, 256]) for i in range(2)]
            stgf = [sb2("stgf%d" % i, [128, 1024]) for i in range(2)]
            stgb = [sb2("stgb%d" % i, [128, 1024], BF16) for i in range(2)]
            ld("c0", identf[:, :], ident_d, [identf]); ld("c0", dtab[:, :, :], dtab_d, [dtab]); ld("c0", g128[:, :, :], g128_d, [g128])
            ld("c0", kdec[:, :], kdec_d, [kdec]); ld("c0", qdec[:, :], qdec_d, [qdec]); ld("c0", sel[:, :], sel_d, [sel])
            ld("c0", iota[:, :], iota_d, [iota]); ld("c0", qrel[:, :], qrel_d, [qrel]); ld("c0", cTs[:, :, :], cT, [cTs])
            ld("c0", badaTs[:, :], badaT, [badaTs]); ld("c0", gnTs[:, :], gnT, [gnTs]); ld("c0", bgate_sb[:, :], bgate, [bgate_sb])
            ld("c0", gfin_bc[:, :], gfin, [gfin_bc]); ld("c0", gret_bc[:, :], gret, [gret_bc])
            POOL(lambda e: e.tensor_copy(out=identb[:, :], in_=identf[:, :]), [identf], [identb])
            for a4 in range(4):
                POOL(lambda e, a4=a4: e.tensor_copy(out=ident4[:, a4, :], in_=identf[:, :]), [identf], [ident4])
            POOL(lambda e: e.memset(zl[:, :], 0.0), [], [zl])
            POOL(lambda e: e.memset(epsT[:, :], 1e-6), [], [epsT])
            POOL(lambda e: e.memset(vst[:, :, :, 64:65], 1.0), [], [vst])
            POOL(lambda e: e.memset(Sst[:, :, :], 0.0), [], [Sst])
            POOL(lambda e: e.memset(ownSb[:, :, :], 0.0), [], [ownSb])
            POOL(lambda e: e.memset(QpT[:, :, :], 0.0), [], [QpT])
            POOL(lambda e: e.memset(kiT[:, :], 0.0), [], [kiT])
            for q_ in qiT2:
                POOL(lambda e, q_=q_: e.memset(q_[:, :, :], 0.0), [], [q_])
            for q_ in QT:
                POOL(lambda e, q_=q_: e.memset(q_[:, :, :], 0.0), [], [q_])
            ts("pool", penS[:, :], iota[:, :], qrel[:, 0:1], -1.0e30, ALU.is_gt, ALU.mult, [iota, qrel], [penS])
            ts("pool", penM[:, :], iota[:, :], qrel[:, 0:1], -30000.0, ALU.is_gt, ALU.mult, [iota, qrel], [penM])
            act(scT[:, :, :], cTs[:, :, :], AF.Silu, [cTs], [scT])
            for cc in range(12):
                wa = wadas[cc % 2]
                ld("wa%d" % (cc % 2), wa[:, :, :], wada[:, :, cc * 256:(cc + 1) * 256], [wa])
                if cc < 8:
                    for t2 in range(2):
                        ct = cc * 2 + t2
                        for kt in range(8):
                            mm(XP[:, ct * 5:ct * 5 + 5], wa[:, kt, t2 * 128:(t2 + 1) * 128], scT[:, kt, :], kt == 0, kt == 7,
                               [wa, scT], [XP])
                else:
                    gch = cc - 8
                    g_ = G[gch // 2]
                    for kt in range(8):
                        mm(g_[0:5, (gch % 2) * 256:(gch % 2) * 256 + 256], scT[:, kt, :], wa[:, kt, :], kt == 0, kt == 7,
                           [wa, scT], [g_])
            act(modT[:, :, :], XP[:, 0:80].rearrange("p (c j) -> p c j", j=5), AF.Identity, [XP], [modT])
            for j in range(5):
                tt("pool", BcA[:, j, :], modT[:, 0:8, j], badaTs[:, 0:8], ALU.add, [modT, badaTs], [BcA])
                ts("pool", AcA[:, j, :], modT[:, 8:16, j], 1.0, None, ALU.add, None, [modT], [AcA])
                tt("pool", AcA[:, j, :], AcA[:, j, :], badaTs[:, 8:16], ALU.add, [AcA, badaTs], [AcA])
                tt("pool", AcA[:, j, :], AcA[:, j, :], gnTs[:, :], ALU.mult, [AcA, gnTs], [AcA])
            for hf in range(2):
                act(gate5[:, hf * 512:(hf + 1) * 512], G[hf][0:5, :], AF.Identity, [G[hf]], [gate5])
            tt("pool", gate5[:, :], gate5[:, :], bgate_sb[0:5, :], ALU.add, [gate5, bgate_sb], [gate5])
            gate5_b = Buf()
            stq("g5", gate5_d, gate5[:, :], [gate5], [gate5_b])
            ld("g5l", gate_bc[:, :], gate5_d[0:1, :].partition_broadcast(128), [gate_bc], [gate5_b])
            ld("c0", pidx[:, :], pidx_d, [pidx])
            stop_at(1)
            nch = (NEXT + 127) // 128
            for c in range(nch):
                c0 = c * 128; n = min(128, NEXT - c0); i = c % 2
                fv = stgf[i].t[:, 0:8 * n].rearrange("p (k c) -> p k c", k=8)
                bv = stgb[i].t[:, 0:8 * n].rearrange("p (k c) -> p k c", k=8)
                ld("ps%d" % i, fv, wext[:, :, c0:c0 + n], [stgf[i]])
                if c % 2 == 0:
                    POOL(lambda e, fv=fv, bv=bv: e.tensor_copy(out=bv, in_=fv), [stgf[i]], [stgb[i]])
                else:
                    act(bv, fv, AF.Identity, [stgf[i]], [stgb[i]])
                stq("pt%d" % i, wbf[:, :, c0:c0 + n], bv, [stgb[i]], [])
            for c in range(16):
                c0 = c * 64; i = c % 2
                fv = stgf[i].t[:, :].rearrange("p (k c) -> p k c", k=16)
                bv = stgb[i].t[:, :].rearrange("p (k c) -> p k c", k=16)
                ld("ps%d" % i, fv, wpo[:, :, c0:c0 + 64], [stgf[i]])
                if c % 2 == 0:
                    POOL(lambda e, fv=fv, bv=bv: e.tensor_copy(out=bv, in_=fv), [stgf[i]], [stgb[i]])
                else:
                    act(bv, fv, AF.Identity, [stgf[i]], [stgb[i]])
                stq("pt%d" % i, wpobf[:, :, c0:c0 + 64], bv, [stgb[i]], [])

            stop_at(2)
            S.barrier()
            st2.close()
            work0 = sb("work0", [128, SEQ + 512]); wbA = work0.b; wbB = Buf()

            def wkinfo(g):
                if g < 8:
                    return (g % 2) * 4352, ([wbA] if g % 2 == 0 else [wbB])
                return 0, [wbA, wbB]
            nm = [sb("nm%d" % i, [128, 512], BF16) for i in range(2)]
            kc = [sb("kc%d" % i, [128, 4, 512], BF16) for i in range(2)]
            vc = [sb("vc%d" % i, [128, 4, 520], BF16) for i in range(2)]
            ET = [sb("ET%d" % i, [128, 512], BF16) for i in range(2)]

            def projscores(g, smp=None):
                sl = g % 2
                L = (g + 1) * 512
                if smp is None:
                    ld("cs", cs5[:, 0:4, :], cs_all[g * 512:(g + 1) * 512, :].rearrange("(t p) c -> p t c", p=128), [cs5])
                    ld("cs", cs5[:, 4, :], cs_own[g * 128:(g + 1) * 128, :], [cs5])
                else:
                    for j5 in range(5):
                        ld("cs", cs5[:, j5, :], cs_s_d, [cs5])
                for j in range(4):
                    xt = xs[j % 2]
                    if smp is None:
                        ld("x0", xt[:, :], xall[(4 * g + j) * 128:(4 * g + j + 1) * 128, :], [xt])
                    else:
                        ld("x0", xt[:, :], xsp[smp, :, :] if j == 0 else xzero, [xt])
                    make_hT(xt, j * 128)
                ld("xo%d" % sl, xo[sl][:, :], xown[g * 128:(g + 1) * 128, :] if smp is None else xsp[smp, :, :], [xo[sl]])
                make_hT(xo[sl], 512)
                stop_at(31)
                wt, wv = load_w(C_KA, 512)
                for pr in range(4):
                    g_ = nextG()
                    for kt in range(8):
                        mm(g_[:, :], wv[:, kt, pr * 128:(pr + 1) * 128], hT[:, kt, 0:512], kt == 0, kt == 7, [wt, hT], [g_])
                    act(kst[:, pr, :], g_[:, :], AF.Identity, [g_], [kst])
                stq("kst", kscrs[cur["set"]][:, :, g * 512:(g + 1) * 512], kst[:, :, :], [kst], [kscr_bs[cur["set"]][g]])
                g_ = nextG()
                for kt in range(8):
                    mm(g_[:, :], hT[:, kt, 512:640], wv[:, kt, :], kt == 0, kt == 7, [wt, hT], [g_])
                act(ko[:, :], g_[:, :], AF.Identity, [g_], [ko])
                if smp is None:
                    stq("ko", k_o[g * 128:(g + 1) * 128, :], ko[:, :], [ko])
                else:
                    stq("ko", ks_o[smp, :, :], ko[0:4, :], [ko])
                stop_at(32)
                wt, wv = load_w(C_VA, 512)
                for j in range(5):
                    g_ = nextG()
                    for kt in range(8):
                        mm(g_[:, :], hT[:, kt, j * 128:(j + 1) * 128], wv[:, kt, :], kt == 0, kt == 7, [wt, hT], [g_])
                    if j < 4:
                        act(vst[:, j, :, 0:64], g_[:, :].rearrange("p (h d) -> p h d", h=8), AF.Identity, [g_], [vst])
                    else:
                        act(vo[:, :], g_[:, :], AF.Identity, [g_], [vo])
                stq("vst", vscrs[cur["set"]][4 * g:4 * g + 4, :, :].rearrange("t p c -> p t c"), vst[:, :, :, :].rearrange("p t h c -> p t (h c)"),
                    [vst], [vscr_bs[cur["set"]][g]])
                if smp is None:
                    stq("vo", v_o[g * 128:(g + 1) * 128, :], vo[:, :], [vo])
                else:
                    stq("vo", vs_o[smp, :, :], vo[0:4, :], [vo])
                stop_at(33)
                wt, wv = load_w(C_KR, 512)
                for j in range(5):
                    g_ = nextG()
                    for kt in range(8):
                        mm(g_[:, :], hT[:, kt, j * 128:(j + 1) * 128], wv[:, kt, :], kt == 0, kt == 7, [wt, hT], [g_])
                    e_ = next_ev()
                    if j < 4:
                        for h in range(8):
                            act(e_[:, h * 64:(h + 1) * 64], g_[:, h * 64:(h + 1) * 64], AF.Identity, [g_, kdec], [e_], scale=kdec[:, h:h + 1])
                        rotary(e_, cs5[:, j, :], krd[:, j, :], krd)
                    else:
                        act(e_[:, :], g_[:, :], AF.Identity, [g_], [e_])
                        rotary(e_, cs5[:, 4, :], krown[:, :], krown)
                stop_at(34)
                wt, wv = load_w(C_VR, 512)
                for j in range(5):
                    g_ = nextG()
                    for kt in range(8):
                        mm(g_[:, :], hT[:, kt, j * 128:(j + 1) * 128], wv[:, kt, :], kt == 0, kt == 7, [wt, hT], [g_])
                    act(vrb[:, j, :], g_[:, :], AF.Identity, [g_], [vrb])
                stop_at(35)
                for j in range(4 if smp is None else 1):
                    if j == 0:
                        ts("pool", ownS[:, :, :], Sst[:, :, :], sel[:, 0:1], None, ALU.mult, None, [Sst, sel], [ownS])
                    else:
                        tS = next_ev()
                        tSv = tS.t[:, :].rearrange("p (a b) -> p a b", a=4)
                        ts("pool", tSv, Sst[:, :, :], sel[:, j:j + 1], None, ALU.mult, None, [Sst, sel], [tS])
                        tt("pool", ownS[:, :, :], ownS[:, :, :], tSv, ALU.add, [ownS, tS], [ownS])
                    for pr in range(4):
                        mm(XP[:, pr * 128:(pr + 1) * 128], krd[:, j, pr * 128:(pr + 1) * 128], vrb[:, j, pr * 128:(pr + 1) * 128], True, True,
                           [krd, vrb], [XP])
                    tmpS = next_ev()
                    act(tmpS[:, :], XP[:, :], AF.Identity, [XP], [tmpS])
                    tt("pool", Sst[:, :, :], Sst[:, :, :], g128[:, :, :], ALU.mult, [Sst, g128], [Sst])
                    tt("pool", Sst[:, :, :], Sst[:, :, :], tmpS[:, :].rearrange("p (a b) -> p a b", a=4), ALU.add, [Sst, tmpS], [Sst])
                for hb in range(2):
                    POOL(lambda e, hb=hb: e.tensor_copy(out=ownSb[hb * 64:(hb + 1) * 64, hb::2, :],
                                                        in_=ownS[hb * 64:(hb + 1) * 64, :, hb * 64:(hb + 1) * 64]), [ownS], [ownSb])
                stop_at(36)
                wt, wv = load_w(C_KID, 512)
                g_ = nextG()
                for kt in range(8):
                    mm(g_[:, :], wv[:, kt, 0:128], hT[:, kt, 0:512], kt == 0, kt == 7, [wt, hT], [g_])
                pb = g // 9
                cc = g % 9
                if os.environ.get('MK_V', '') != 'noki':
                    act(kiT[pb * 64:(pb + 1) * 64, cc * 512:(cc + 1) * 512], g_[pb * 64:(pb + 1) * 64, :], AF.Identity, [g_], [kiT])
                stop_at(37)
                wt, wv = load_w(C_QA, 512)
                g_ = nextG()
                for pr in range(4):
                    for kt in range(8):
                        mm(g_[:, pr * 128:(pr + 1) * 128], wv[:, kt, pr * 128:(pr + 1) * 128], hT[:, kt, 512:640], kt == 0, kt == 7, [wt, hT], [g_])
                for hb in range(2):
                    act(QT[sl][hb * 64:(hb + 1) * 64, hb::2, :], g_[hb * 64:(hb + 1) * 64, :].rearrange("p (a q) -> p a q", a=4), AF.Identity,
                        [g_], [QT[sl]])
                wt, wv = load_w(C_QID, 512)
                g_ = nextG()
                for h in range(4):
                    for kt in range(8):
                        mm(g_[:, h * 128:(h + 1) * 128], wv[:, kt, h * 128:(h + 1) * 128], hT[:, kt, 512:640], kt == 0, kt == 7, [wt, hT], [g_])
                for hb in range(2):
                    act(qiT2[hb][hb * 64:(hb + 1) * 64, :, :], g_[hb * 64:(hb + 1) * 64, :].rearrange("p (a q) -> p a q", a=4), AF.Identity,
                        [g_], [qiT2[hb]])
                wt, wv = load_w(C_KIWI, 512)
                g_ = nextG()
                for kt in range(8):
                    mm(g_[:, 0:68], hT[:, kt, 512:640], wv[:, kt, 0:68], kt == 0, kt == 7, [wt, hT], [g_])
                act(kio[:, :], g_[:, 0:68], AF.Identity, [g_], [kio])
                if smp is None:
                    stq("kio", ki_o[g * 128:(g + 1) * 128, :], kio[:, 0:64], [kio])
                else:
                    stq("kio", kis_o[smp, :, :], kio[0:4, 0:64], [kio])
                ts("pool", wsc[sl][:, :], kio[:, 64:68], 1.0 / 16.0, None, ALU.mult, None, [kio], [wsc[sl]])
                if PROJ_ONLY:
                    return
                wt, wv = load_w(C_ZA, 512)
                g_ = nextG()
                for kt in range(8):
                    mm(g_[:, :], hT[:, kt, 512:640], wv[:, kt, :], kt == 0, kt == 7, [wt, hT], [g_])
                act(siluza[sl][:, :], g_[:, :], AF.Silu, [g_], [siluza[sl]])
                wt, wv = load_w(C_QR, 512)
                g_ = nextG()
                for kt in range(8):
                    mm(g_[:, :], hT[:, kt, 512:640], wv[:, kt, :], kt == 0, kt == 7, [wt, hT], [g_])
                e_ = next_ev()
                for h in range(8):
                    act(e_[:, h * 64:(h + 1) * 64], g_[:, h * 64:(h + 1) * 64], AF.Identity, [g_, qdec], [e_], scale=qdec[:, h:h + 1])
                rotary(e_, cs5[:, 4, :], qp[:, :], qp)
                for pr in range(4):
                    PE(lambda e, pr=pr: e.transpose(out=TP[:, pr, :], in_=qp[:, pr * 128:(pr + 1) * 128], identity=identb[:, :]), [qp, identb], [TP])
                for hb in range(2):
                    act(QpT[hb * 64:(hb + 1) * 64, hb::2, :], TP[hb * 64:(hb + 1) * 64, 0:4, :], AF.Identity, [TP], [QpT])
                for pr in range(4):
                    PE(lambda e, pr=pr: e.transpose(out=TP[:, 4 + pr, :], in_=krown[:, pr * 128:(pr + 1) * 128], identity=identb[:, :]), [krown, identb], [TP])
                act(KrT[:, :, :], TP[:, 4:8, :], AF.Identity, [TP], [KrT])
                wt, wv = load_w(C_ZR, 512)
                g_ = nextG()
                for kt in range(8):
                    mm(g_[:, :], hT[:, kt, 512:640], wv[:, kt, :], kt == 0, kt == 7, [wt, hT], [g_])
                act(siluzr[:, :], g_[:, :], AF.Silu, [g_], [siluzr])
                for ci, (c0, dst) in enumerate([(C_GA, sga[sl]), (C_GA + 512, sga[sl]), (C_GR, sgr), (C_GR + 512, sgr)]):
                    wt, wv = load_w(c0, 512)
                    g_ = nextG()
                    for kt in range(8):
                        mm(g_[:, :], hT[:, kt, 512:640], wv[:, kt, :], kt == 0, kt == 7, [wt, hT], [g_])
                    act(dst[:, (ci % 2) * 512:(ci % 2 + 1) * 512], g_[:, :], AF.Sigmoid, [g_], [dst])
                stop_at(38)
                for h in range(8):
                    pr, hb = h // 2, h % 2
                    a_ = A[h // 4]
                    mm(a_[:, (h % 4) * 128:(h % 4 + 1) * 128], KrT[:, pr, :], QpT[:, h, :], True, True,
                       [KrT, QpT], [a_])
                stop_at(381)
                for hf in range(2):
                    e_ = next_ev()
                    act(e_[:, :], A[hf][:, :], AF.Identity, [A[hf]], [e_])
                    tt("pool", ATb[:, hf * 4:(hf + 1) * 4, :], e_[:, :].rearrange("p (a q) -> p a q", a=4), dtab[:, hf * 4:(hf + 1) * 4, :], ALU.mult,
                       [e_, dtab], [ATb])
                stop_at(382)
                for h in range(8):
                    pr, hb = h // 2, h % 2
                    mm(A[2][:, h * 64:(h + 1) * 64], ATb[:, h, :], vrb[:, 4, h * 64:(h + 1) * 64], True, False, [ATb, vrb], [A[2]])
                    mm(A[2][:, h * 64:(h + 1) * 64], QpT[:, h, :], ownSb[:, h, :], False, True,
                       [QpT, ownSb], [A[2]])
                stop_at(383)
                act(oret[:, :, :], A[2][:, :].rearrange("p (h d) -> p h d", h=8), AF.Identity, [A[2]], [oret])
                stop_at(39)
                DVE(lambda e: e.tensor_reduce(out=gst[:, :, 0], in_=oret[:, :, :], axis=AX.X, op=ALU.add), [oret], [gst])
                tt("pool", osq[:, :, :], oret[:, :, :], oret[:, :, :], ALU.mult, [oret], [osq])
                DVE(lambda e: e.tensor_reduce(out=gst[:, :, 1], in_=osq[:, :, :], axis=AX.X, op=ALU.add), [osq], [gst])
                ts("pool", gst[:, :, 2], gst[:, :, 0], 1.0 / 64, None, ALU.mult, None, [gst], [gst])
                tt("pool", gst[:, :, 3], gst[:, :, 2], gst[:, :, 2], ALU.mult, [gst], [gst])
                ts("pool", gst[:, :, 7], gst[:, :, 1], 1.0 / 64, None, ALU.mult, None, [gst], [gst])
                tt("pool", gst[:, :, 4], gst[:, :, 7], gst[:, :, 3], ALU.subtract, [gst], [gst])
                act(gst[:, :, 5], gst[:, :, 4], AF.Ln, [gst], [gst], bias=epsT[:, 0:1])
                act(gst[:, :, 6], gst[:, :, 5], AF.Exp, [gst], [gst], scale=-0.5)
                tt("pool", osq[:, :, :], oret[:, :, :], gst[:, :, 2:3].to_broadcast([128, 8, 64]), ALU.subtract, [oret, gst], [osq])
                tt("pool", osq[:, :, :], osq[:, :, :], gst[:, :, 6:7].to_broadcast([128, 8, 64]), ALU.mult, [osq, gst], [osq])
                osq2 = osq.t[:, :, :].rearrange("p h d -> p (h d)")
                tt("pool", osq2, osq2, gret_bc[:, :], ALU.mult, [osq, gret_bc], [osq])
                tt("pool", yrb[:, :], osq2, siluzr[:, :], ALU.mult, [osq, siluzr], [yrb])
                for pr in range(4):
                    PE(lambda e, pr=pr: e.transpose(out=TP[:, pr, :], in_=yrb[:, pr * 128:(pr + 1) * 128], identity=identb[:, :]), [yrb, identb], [TP])
                act(yrT[:, :, :], TP[:, 0:4, :], AF.Identity, [TP], [yrT])
                for hf in range(2):
                    wt, wv = load_wpo(4, 4, hf * 512, 512)
                    g_ = nextG()
                    for kt in range(4):
                        mm(g_[:, :], yrT[:, kt, :], wv[:, kt, :], kt == 0, kt == 3, [wt, yrT], [g_])
                    e_ = next_ev()
                    act(e_[:, :], g_[:, :], AF.Identity, [g_], [e_])
                    tt("pool", mr[sl][:, hf * 512:(hf + 1) * 512], e_[:, :], sgr[:, hf * 512:(hf + 1) * 512], ALU.mult, [e_, sgr], [mr[sl]])
            def scores(g):
                sl = g % 2
                off, wkb = wkinfo(g)
                for c in range(g + 1):
                    pb = c // 9
                    cc = c % 9
                    for h in range(4):
                        g_ = nextG()
                        mm(g_[:, :], qiT2[pb][:, h, :], kiT[:, cc * 512:(cc + 1) * 512], True, True, [qiT2[pb], kiT], [g_])
                        rli[0] += 1
                        r_ = rl[rli[0] % 2]
                        act(r_[:, :], g_[:, :], AF.Relu, [g_], [r_])
                        dst = work0[:, off + c * 512:off + (c + 1) * 512]
                        if h == 0:
                            if c == g:
                                stt("dve", dst, r_[:, :], wsc[sl][:, 0:1], penS[:, :], ALU.mult, ALU.add, [r_, wsc[sl], penS], wkb)
                            else:
                                ts("dve", dst, r_[:, :], wsc[sl][:, 0:1], None, ALU.mult, None, [r_, wsc[sl]], wkb)
                        else:
                            stt("dve", dst, r_[:, :], wsc[sl][:, h:h + 1], dst, ALU.mult, ALU.add, [r_, wsc[sl]] + wkb, wkb)

            def topk(g):
                off, wkb = wkinfo(g)
                L = (g + 1) * 512
                for r in range(TOPK // 8):
                    DVE(lambda e: e.max(out=m8[:, :], in_=work0[:, off:off + L]), wkb, [m8])
                    DVE(lambda e: e.match_replace(out=work0[:, off:off + L], in_to_replace=m8[:, :], in_values=work0[:, off:off + L], imm_value=NEG_BIG), wkb + [m8], wkb)

            def attention(g, smp=None):
                sl = g % 2
                off, wkb = wkinfo(g)
                nkt = 4 * (g + 1)
                for c in range(g + 1):
                    kc_ = kc[c % 2]; vc_ = vc[c % 2]; nm_ = nm[c % 2]
                    ld("kc%d" % (c % 2), kc_[:, :, :], kscrs[cur["set"]][:, :, c * 512:(c + 1) * 512], [kc_], [kscr_bs[cur["set"]][c]])
                    ld("vc%d" % (c % 2), vc_[:, :, :], vscrs[cur["set"]][4 * c:4 * c + 4, :, :].rearrange("t p c -> p t c"), [vc_], [vscr_bs[cur["set"]][c]])
                    ts("dve", nm_[:, :], work0[:, off + c * 512:off + (c + 1) * 512], -1.0e38, -30000.0, ALU.is_gt, ALU.mult, wkb, [nm_])
                    if c == g:
                        tt("dve", nm_[:, :], nm_[:, :], penM[:, :], ALU.add, [nm_, penM], [nm_])
                    if g == 0 and c == 0 and os.environ.get('MK_DBG'):
                        stq("dbg", dbgb[:, 0:512], nm_[:, :], [nm_])
                        stq("dbg", dbgb[:, 1024:1536], kc_[:, 0, :], [kc_])
                    for t in range(4):
                        kt_i = c * 4 + t
                        for hg in range(2):
                            a_ = A[(2 * (kt_i % 2)) + hg]
                            mm(a_[:, :], nm_[:, t * 128:(t + 1) * 128], ident4[:, :, :].rearrange("p a q -> p (a q)"), True, False,
                               [nm_, ident4], [a_])
                            for hh in range(4):
                                h = hg * 4 + hh
                                pr, hb = h // 2, h % 2
                                mm(a_[:, hh * 128:(hh + 1) * 128], kc_[:, pr, t * 128:(t + 1) * 128],
                                   QT[sl][:, h, :], False, hh == 3, [kc_, QT[sl]], [a_])
                            eti[0] += 1
                            et = ET[eti[0] % 2]
                            act(et[:, :], a_[:, :], AF.Exp, [a_], [et], scale=0.125)
                            if g == 0 and kt_i == 0 and hg == 0 and os.environ.get('MK_DBG'):
                                stq("dbg", dbgb[:, 512:1024], et[:, :], [et])
                                stq("dbg", dbgb[:, 1536:2048], QT[sl][:, 0:4, :].rearrange("p a q -> p (a q)"), [QT[sl]])
                            for hh in range(4):
                                h = hg * 4 + hh
                                if kt_i == 0 and hh == 0:
                                    mm(G[hg][:, 0:260], zl[:, :], vc_[:, t, 0:260], True, False, [zl, vc_], [G[hg]])
                                mm(G[hg][:, hh * 65:(hh + 1) * 65], et[:, hh * 128:(hh + 1) * 128], vc_[:, t, h * 65:(h + 1) * 65],
                                   False, (kt_i == nkt - 1 and hh == 3), [et, vc_], [G[hg]])
                for hg in range(2):
                    act(acc[:, hg * 4:(hg + 1) * 4, :], G[hg][:, 0:260].rearrange("p (h c) -> p h c", h=4), AF.Identity, [G[hg]], [acc])
                if g == 0 and os.environ.get('MK_DBG'):
                    stq("dbg", dbg[:, 512:1032], acc.t[:, :, :].rearrange("p h c -> p (h c)"), [acc])
                act(rden[:, :, 0], acc[:, :, 64], AF.Ln, [acc], [rden])
                act(rden[:, :, 1], rden[:, :, 0], AF.Exp, [rden], [rden], scale=-1.0)
                tt("pool", yaf[:, :, :], acc[:, :, 0:64], rden[:, :, 1:2].to_broadcast([128, 8, 64]), ALU.mult, [acc, rden], [yaf])
                tt("pool", yab[:, :], yaf.t[:, :, :].rearrange("p h d -> p (h d)"), siluza[sl][:, :], ALU.mult, [yaf, siluza[sl]], [yab])
                for pr in range(4):
                    PE(lambda e, pr=pr: e.transpose(out=TP[:, pr, :], in_=yab[:, pr * 128:(pr + 1) * 128], identity=identb[:, :]), [yab, identb], [TP])
                act(yaT[:, :, :], TP[:, 0:4, :], AF.Identity, [TP], [yaT])
                for hf in range(2):
                    wt, wv = load_wpo(0, 4, hf * 512, 512)
                    g_ = nextG()
                    for kt in range(4):
                        mm(g_[:, :], yaT[:, kt, :], wv[:, kt, :], kt == 0, kt == 3, [wt, yaT], [g_])
                    e_ = next_ev()
                    act(e_[:, :], g_[:, :], AF.Identity, [g_], [e_])
                    tt("pool", e_[:, :], e_[:, :], sga[sl][:, hf * 512:(hf + 1) * 512], ALU.mult, [e_, sga[sl]], [e_])
                    tt("pool", mg[:, hf * 512:(hf + 1) * 512], e_[:, :], mr[sl][:, hf * 512:(hf + 1) * 512], ALU.add, [e_, mr[sl]], [mg])
                for kt in range(8):
                    PE(lambda e, kt=kt: e.transpose(out=TP[:, kt, :], in_=mg[:, kt * 128:(kt + 1) * 128], identity=identb[:, :]), [mg, identb], [TP])
                act(mT[:, :, :], TP[:, :, :], AF.Identity, [TP], [mT])
                for hf in range(2):
                    wt, wv = load_wpo(8, 8, hf * 512, 512)
                    g_ = nextG()
                    for kt in range(8):
                        mm(g_[:, :], mT[:, kt, :], wv[:, kt, :], kt == 0, kt == 7, [wt, mT], [g_])
                    e_ = next_ev()
                    act(e_[:, :], g_[:, :], AF.Identity, [g_], [e_])
                    tt("pool", e_[:, :], e_[:, :], gate_bc[:, hf * 512:(hf + 1) * 512], ALU.mult, [e_, gate_bc], [e_])
                    tt("pool", res[:, hf * 512:(hf + 1) * 512], e_[:, :], xo[sl][:, hf * 512:(hf + 1) * 512], ALU.add, [e_, xo[sl]], [res])
                if g == 0 and os.environ.get('MK_DBG'):
                    stq("dbg", dbg[:, 2048:3072], res[:, :], [res])
                rstd_chain(res, res[:, :], D, 1)
                act(res[:, :], res[:, :], AF.Identity, [res, ssq], [res], scale=ssq[:, 1:2])
                tt("pool", yout[:, :], res[:, :], gfin_bc[:, :], ALU.mult, [res, gfin_bc], [yout])
                if smp is None:
                    stq("yo", y_o[g * 128:(g + 1) * 128, :], yout[:, :], [yout])
                else:
                    stq("yo", ys_o[smp, :, :], yout[0:4, :], [yout])

            n = NSTEPS_RUN
            if PROJ_ONLY:
                for g in range(n):
                    projscores(g)
                raise _Stop2()
            projscores(0)
            stop_at(3)
            scores(0)
            stop_at(4)
            for g in range(n):
                topk(g)
                if g + 1 < n:
                    projscores(g + 1)
                    if g + 1 < 8:
                        scores(g + 1)
                        attention(g)
                    else:
                        attention(g)
                        scores(g + 1)
                else:
                    attention(g)
            for h in range(8):
                pr, hb = h // 2, h % 2
                stq("sto", st_o[h, :, :], Sst[hb * 64:(hb + 1) * 64, pr, hb * 64:(hb + 1) * 64], [Sst])

            def ingest(s_):
                ld("pt", ptb[:, :], ptab[s_:s_ + 1, :].partition_broadcast(128), [ptb])
                ts("pool", idxa[:, :], ptb[:, :], 128.0, pidx[:, 0:1], ALU.mult, ALU.add, [ptb, pidx], [idxa])
                POOL(lambda e: e.memset(Sst[:, :, :], 0.0), [], [Sst])
                for hb in range(2):
                    ld("sti", Sst[hb * 64:(hb + 1) * 64, :, hb * 64:(hb + 1) * 64],
                       state_in[s_, hb::2, :, :].rearrange("h p e -> p h e"), [Sst])
                for pg in range(64):
                    c = pg // 4; t = pg % 4
                    kp = rl[pg % 2]; vp = ev[pg % 2]
                    S.dma("gk%d" % (pg % 2), lambda e, kp=kp, pg=pg: e.indirect_dma_start(
                        out=kp[:, :], out_offset=None, in_=cache_k_d,
                        in_offset=bass.IndirectOffsetOnAxis(ap=idxa[:, pg:pg + 1], axis=0)), reads=[_b(idxa)], writes=[_b(kp)], q="pool")
                    S.dma("gv%d" % (pg % 2), lambda e, vp=vp, pg=pg: e.indirect_dma_start(
                        out=vp[:, :], out_offset=None, in_=cache_v_d,
                        in_offset=bass.IndirectOffsetOnAxis(ap=idxa[:, pg:pg + 1], axis=0)), reads=[_b(idxa)], writes=[_b(vp)], q="pool")
                    S.dma("gi", lambda e, pg=pg: e.indirect_dma_start(
                        out=kio[:, 0:64], out_offset=None, in_=cache_ki_d,
                        in_offset=bass.IndirectOffsetOnAxis(ap=idxa[:, pg:pg + 1], axis=0)), reads=[_b(idxa)], writes=[_b(kio)], q="pool")
                    kb = qp if pg % 2 == 0 else krown
                    act(kb[:, :], kp[:, :], AF.Identity, [kp], [kb])
                    for pr in range(4):
                        PE(lambda e, pr=pr, kb=kb: e.transpose(out=TP[:, pr, :], in_=kb[:, pr * 128:(pr + 1) * 128], identity=identb[:, :]),
                           [kb, identb], [TP])
                    act(kst[:, :, t * 128:(t + 1) * 128], TP[:, 0:4, :], AF.Identity, [TP], [kst])
                    act(vst[:, t, :, 0:64], vp[:, :].rearrange("p (h d) -> p h d", h=8), AF.Identity, [vp], [vst])
                    for dd in range(2):
                        act(yab[:, dd * 64:(dd + 1) * 64], kio[:, 0:64], AF.Identity, [kio], [yab])
                    PE(lambda e: e.transpose(out=TP[:, 4, :], in_=yab[:, 0:128], identity=identb[:, :]), [yab, identb], [TP])
                    pb = c // 9; cc = c % 9
                    act(kiT[pb * 64:(pb + 1) * 64, cc * 512 + t * 128:cc * 512 + (t + 1) * 128], TP[pb * 64:(pb + 1) * 64, 4, :], AF.Identity,
                        [TP], [kiT])
                    if t == 3:
                        stq("kst", kscrs[cur["set"]][:, :, c * 512:(c + 1) * 512], kst[:, :, :], [kst], [kscr_bs[cur["set"]][c]])
                        stq("vst", vscrs[cur["set"]][4 * c:4 * c + 4, :, :].rearrange("t p c -> p t c"), vst[:, :, :, :].rearrange("p t h c -> p t (h c)"),
                            [vst], [vscr_bs[cur["set"]][c]])

            NSAMP = int(os.environ.get("MK_NSAMP", "4"))
            if NSAMP > 0:
                S.barrier()
                ld("c1", kdec[:, :], kdec_s_d, [kdec]); ld("c1", qdec[:, :], qdec_s_d, [qdec]); ld("c1", dtab[:, :, :], dtab_s_d, [dtab])
                ld("c1", g128[:, :, :], g4_d, [g128]); ld("c1", sel[:, :], sel_s_d, [sel]); ld("c1", penS[:, :], penS_s_d, [penS])
                pe_ = next_ev()
                ld("c1", pe_[:, :], penM_s_d, [pe_])
                POOL(lambda e, pe_=pe_: e.tensor_copy(out=penM[:, :], in_=pe_[:, :]), [pe_], [penM])
                cur["set"] = 0
                ingest(0)
                for s_ in range(NSAMP):
                    cur["set"] = s_ % 2
                    cur["j"] = 1 + s_
                    projscores(16, smp=s_)
                    for h in range(8):
                        pr, hb = h // 2, h % 2
                        stq("sto", sts_o[s_, h, :, :], Sst[hb * 64:(hb + 1) * 64, pr, hb * 64:(hb + 1) * 64], [Sst])
                    scores(16)
                    if s_ + 1 < NSAMP:
                        cur["set"] = (s_ + 1) % 2
                        ingest(s_ + 1)
                        cur["set"] = s_ % 2
                    topk(16)
                    ld("g5l", gate_bc[:, :], gate5_d[1 + s_:2 + s_, :].partition_broadcast(128), [gate_bc], [gate5_b])
                    attention(16, smp=s_)
        except _Stop:
            st2.close()
        except _Stop2:
            for h in range(8):
                pr, hb = h // 2, h % 2
                stq("sto", st_o[h, :, :], Sst[hb * 64:(hb + 1) * 64, pr, hb * 64:(hb + 1) * 64], [Sst])
        S.finish()
        S.emit()
    return nc


def _consts():
    half = 32
    inv = (10000.0 ** (-np.arange(half, dtype=np.float32) / half)).astype(np.float32)
    pos = np.arange(SEQ, dtype=np.float32)
    ang = (pos[:, None] * inv[None, :]).astype(np.float32)
    cs = np.concatenate([np.cos(ang), np.sin(ang)], axis=1).astype(np.float32)
    lg = np.log1p(-np.exp2(-5.0 - np.arange(8, dtype=np.float64)))
    i = np.arange(128, dtype=np.float64)
    kdec = (0.125 * np.exp(lg[None, :] * (127.0 - i)[:, None])).astype(np.float32)
    qdec = np.exp(lg[None, :] * (i[:, None] + 1.0)).astype(np.float32)
    dt = np.zeros((128, 8, 128), np.float64)
    for h in range(8):
        v = 0.125 * np.exp(-lg[h] * (i + 1.0))
        dt[:, h, :] = v[:, None] * (i[None, :] >= i[:, None])
    g128 = np.zeros((128, 4, 128), np.float32)
    for pr in range(4):
        for hb in range(2):
            g128[hb * 64:(hb + 1) * 64, pr, :] = np.exp(lg[2 * pr + hb] * 128.0)
    return cs, kdec, qdec, dt.astype(np.float32), g128


def _consts_sample():
    half = 32
    inv = (10000.0 ** (-np.arange(half, dtype=np.float32) / half)).astype(np.float32)
    tt_ = np.minimum(np.arange(128), 3)
    pos = (8192 + tt_).astype(np.float32)
    ang = (pos[:, None] * inv[None, :]).astype(np.float32)
    cs_s = np.concatenate([np.cos(ang), np.sin(ang)], axis=1).astype(np.float32)
    lg = np.log1p(-np.exp2(-5.0 - np.arange(8, dtype=np.float64)))
    p = np.arange(128, dtype=np.float64)
    real = (p < 4)
    kdec_s = (0.125 * np.exp(lg[None, :] * (3.0 - p)[:, None]) * real[:, None]).astype(np.float32)
    qdec_s = np.exp(lg[None, :] * (tt_.astype(np.float64)[:, None] + 1.0)).astype(np.float32)
    dt = np.zeros((128, 8, 128), np.float64)
    for h in range(8):
        v = 0.125 * np.exp(-lg[h] * (p + 1.0)) * real
        dt[:, h, :] = v[:, None] * (tt_[None, :] >= p[:, None])
    g4 = np.zeros((128, 4, 128), np.float32)
    for pr in range(4):
        for hb in range(2):
            g4[hb * 64:(hb + 1) * 64, pr, :] = np.exp(lg[2 * pr + hb] * 4.0)
    j = np.arange(512)[None, :]
    adm = j <= tt_[:, None]
    penS_s = np.where(adm, 0.0, -1.0e30).astype(np.float32)
    penM_s = np.where(adm, 0.0, -30000.0).astype(np.float32)
    return cs_s, kdec_s, qdec_s, dt.astype(np.float32), g4, penS_s, penM_s


def kernel(x_prompt, x_sample, cache_k, cache_v, cache_kidx, state_ret, page_table, c_prompt, c_sample,
           w_ada, b_ada, g_norm, w_in, g_ret, w_pa, w_pr, w_out, g_final):
    f = lambda a: np.ascontiguousarray(np.asarray(a, dtype=np.float32))
    x_prompt = f(x_prompt); w_in0 = f(w_in)[0]
    cols = np.concatenate([
        np.arange(512, 1024), np.arange(1024, 1536), np.arange(2884, 3396), np.arange(3396, 3908),
        np.arange(2304, 2368), np.arange(2304, 2368),
        np.arange(0, 512),
        np.concatenate([np.concatenate([np.arange(2048 + 64 * h, 2048 + 64 * (h + 1))] * 2) for h in range(4)]),
        np.arange(2304, 2372), np.arange(1536, 2048), np.arange(2372, 2884), np.arange(3908, 4420),
        np.arange(4420, 5444), np.arange(5444, 6468)])
    assert cols.shape[0] == NEXT
    ptile = lambda w: np.ascontiguousarray(w.reshape(w.shape[0] // 128, 128, w.shape[1]).transpose(1, 0, 2))
    wext = ptile(w_in0[:, cols])
    wpo = ptile(np.concatenate([f(w_pa)[0], f(w_pr)[0], f(w_out)[0]], axis=0))
    wada = ptile(f(w_ada)[0])
    badaT = np.ascontiguousarray(f(b_ada)[0].reshape(24, 128).T)
    bgate = np.ascontiguousarray(np.broadcast_to(f(b_ada)[0][2048:3072], (128, 1024)))
    gnT = np.ascontiguousarray(f(g_norm)[0].reshape(8, 128).T)
    gfin = np.ascontiguousarray(np.broadcast_to(f(g_final), (128, 1024)))
    gret = np.ascontiguousarray(np.broadcast_to(f(g_ret)[0], (128, 512)))
    cs, kdec, qdec, dtab, g128 = _consts()
    cs_s, kdec_s, qdec_s, dtab_s, g4, penS_s, penM_s = _consts_sample()
    x_sample = f(x_sample); state_ret = f(state_ret)
    ck2 = f(cache_k)[0].reshape(2560 * 128, 512); cv2 = f(cache_v)[0].reshape(2560 * 128, 512)
    cki2 = f(cache_kidx)[0].reshape(2560 * 128, 64)
    page_table = np.asarray(page_table).astype(np.int32)
    xzero = np.zeros((128, D), np.float32)
    sel_s = np.zeros((128, 4), np.float32); sel_s[:, 0] = 1.0
    pidx = np.arange(128, dtype=np.float32).reshape(128, 1)
    ident = np.eye(128, dtype=np.float32)
    iota = np.ascontiguousarray(np.broadcast_to(np.arange(512, dtype=np.float32), (128, 512)))
    c_prompt = f(c_prompt); c_sample = f(c_sample)
    in_maps = []
    for c in range(8):
        b, r = c // 4, c % 4
        own_tiles = np.array([4 * g + r for g in range(NSTEP)])
        rows = (own_tiles[:, None] * 128 + np.arange(128)[None, :]).reshape(-1)
        cvec = np.stack([c_prompt[b]] + [c_sample[4 * c + s] for s in range(4)], axis=1)
        cT = np.ascontiguousarray(cvec.reshape(8, 128, 5).transpose(1, 0, 2))
        sel = np.zeros((128, 4), np.float32); sel[:, r] = 1.0
        xsp = np.zeros((4, 128, D), np.float32)
        for s_ in range(4):
            xsp[s_, 0:4] = x_sample[4 * c + s_]
        qrel = (r * 128 + np.arange(128, dtype=np.float32)).reshape(128, 1)
        in_maps.append({
            "xall": x_prompt[b], "xown": np.ascontiguousarray(x_prompt[b][rows]),
            "wext": wext, "wpo": wpo, "wada": wada, "badaT": badaT, "bgate": bgate, "gnT": gnT, "gfin": gfin, "gret": gret,
            "cT": cT, "ident": ident, "cs_all": cs, "cs_own": np.ascontiguousarray(cs[rows]), "kdec": kdec, "qdec": qdec,
            "dtab": dtab, "g128": g128, "sel": sel, "iota512": iota, "qrel": qrel,
            "xsp": xsp, "xzero": xzero, "cache_k2": ck2, "cache_v2": cv2, "cache_ki2": cki2,
            "ptab": np.ascontiguousarray(page_table[4 * c:4 * c + 4]), "state_in": np.ascontiguousarray(state_ret[0, 4 * c:4 * c + 4]),
            "cs_s": cs_s, "kdec_s": kdec_s, "qdec_s": qdec_s, "dtab_s": dtab_s, "g4": g4, "sel_s": sel_s,
            "penS_s": penS_s, "penM_s": penM_s, "pidx": pidx,
        })
    nc = build_nc()
    resr = run_bass_kernel_spmd(nc, in_maps, core_ids=list(range(8)))
    R = resr.results
    y_prompt = np.zeros((2, SEQ, D), np.float32)
    k_rows = np.zeros((1, 2, SEQ, 8, 64), np.float32); v_rows = np.zeros((1, 2, SEQ, 8, 64), np.float32)
    ki_rows = np.zeros((1, 2, SEQ, 64), np.float32); st_p = np.zeros((1, 2, 8, 64, 64), np.float32)
    for c in range(8):
        b, r = c // 4, c % 4
        for g in range(NSTEP):
            t = 4 * g + r
            sl_ = slice(t * 128, (t + 1) * 128); so = slice(g * 128, (g + 1) * 128)
            y_prompt[b, sl_] = R[c]["y_o"][so]
            k_rows[0, b, sl_] = R[c]["k_o"][so].reshape(128, 8, 64)
            v_rows[0, b, sl_] = R[c]["v_o"][so].reshape(128, 8, 64)
            ki_rows[0, b, sl_] = R[c]["ki_o"][so]
        if r == 3:
            st_p[0, b] = R[c]["st_o"]
    y_sample = np.zeros((32, 4, D), np.float32)
    ks = np.zeros((1, 32, 4, 8, 64), np.float32); vs = np.zeros((1, 32, 4, 8, 64), np.float32)
    kis = np.zeros((1, 32, 4, 64), np.float32); sts = np.zeros((1, 32, 8, 64, 64), np.float32)
    for c in range(8):
        for s_ in range(4):
            bi = 4 * c + s_
            y_sample[bi] = R[c]["ys_o"][s_]
            ks[0, bi] = R[c]["ks_o"][s_].reshape(4, 8, 64); vs[0, bi] = R[c]["vs_o"][s_].reshape(4, 8, 64)
            kis[0, bi] = R[c]["kis_o"][s_]; sts[0, bi] = R[c]["sts_o"][s_]
    return (y_prompt, y_sample, k_rows, v_rows, ki_rows, st_p, ks, vs, kis, sts)
```

```python
import os
import numpy as np
from contextlib import ExitStack
import concourse.bass as bass
import concourse.mybir as mybir
from concourse.bass_utils import run_bass_kernel_spmd

F32 = mybir.dt.float32
BF16 = mybir.dt.bfloat16
AF = mybir.ActivationFunctionType
ALU = mybir.AluOpType
AX = mybir.AxisListType

D = 1024
SEQ = 8192
NSTEP = 16
NSTEPS_RUN = int(os.environ.get("MK_NSTEPS", "16"))
TOPK = 256
NEG_BIG = -3.0e38
C_KA, C_VA, C_KR, C_VR, C_KID = 0, 512, 1024, 1536, 2048
C_QA, C_QID, C_KIWI, C_ZA, C_QR, C_ZR, C_GA, C_GR = 2176, 2688, 3200, 3268, 3780, 4292, 4804, 5828
NEXT = 6852
ENGS = ("pe", "act", "dve", "pool", "sp")


class _Stop(Exception):
    pass


class _Stop2(Exception):
    pass


STOP = int(os.environ.get('MK_STOP', '99'))
PROJ_ONLY = os.environ.get('MK_FULL', '1') != '1'


def stop_at(k):
    if STOP == k:
        raise _Stop()


class Buf:
    __slots__ = ("w", "r")

    def __init__(self):
        self.w = None
        self.r = []


class Tl:
    def __init__(self, t):
        self.t = t
        self.b = Buf()

    def __getitem__(self, k):
        return self.t[k]


def _b(x):
    return x.b if hasattr(x, "b") else x


class Sched:
    def __init__(self, nc, stack):
        self.nc = nc
        self.stack = stack
        self.ops = {e: [] for e in ENGS}
        self.cnt = {e: 0 for e in ENGS}
        self.esem = {e: stack.enter_context(nc.semaphore("prog_" + e)) for e in ENGS if e != "sp"}
        self.dsem = {}
        self.dcnt = {}
        self.waited = {}
        self.same = {"pe": False, "act": True, "dve": True, "pool": True, "sp": False}

    def _collect(self, eng, reads, writes):
        deps = []
        for b in reads:
            if b.w is not None:
                deps.append(b.w)
        for b in writes:
            if b.w is not None:
                deps.append(b.w)
            deps.extend(b.r)
        waits = []
        for d in deps:
            if d[0] == "e":
                if d[1] == eng and not self.same[eng]:
                    continue
                key = (eng, "e", d[1])
                val = d[2]
                sem = self.esem[d[1]]
            else:
                key = (eng, "d", d[1])
                val = self.dcnt[d[1]]
                sem = self.dsem[d[1]]
            if self.waited.get(key, 0) >= val:
                continue
            self.waited[key] = val
            waits.append((sem, val))
        return waits

    def _update(self, d, reads, writes):
        for b in reads:
            b.r.append(d)
            if len(b.r) > 64:
                b.r = b.r[-64:] if False else b.r
        for b in writes:
            b.w = d
            b.r = []

    def op(self, eng, fn, reads=(), writes=()):
        reads = [_b(x) for x in reads]
        writes = [_b(x) for x in writes]
        waits = self._collect(eng, reads, writes)
        self.cnt[eng] += 1
        idx = self.cnt[eng]
        sem = self.esem[eng]

        def thunk(e, waits=waits, fn=fn, sem=sem):
            for ws, wv in waits:
                e.wait_ge(ws, wv)
            fn(e).then_inc(sem, 1)

        self.ops[eng].append(thunk)
        self._update(("e", eng, idx), reads, writes)

    def dma(self, semkey, fn, reads=(), writes=(), q="sp"):
        reads = [_b(x) for x in reads]
        writes = [_b(x) for x in writes]
        if semkey not in self.dsem:
            self.dsem[semkey] = self.stack.enter_context(self.nc.semaphore("d_" + semkey))
            self.dcnt[semkey] = 0
        waits = self._collect(q, reads, writes)
        self.dcnt[semkey] += 16
        sem = self.dsem[semkey]

        def thunk(e, waits=waits, fn=fn, sem=sem):
            for ws, wv in waits:
                e.wait_ge(ws, wv)
            fn(e).then_inc(sem, 16)

        self.ops[q].append(thunk)
        self._update(("d", semkey), reads, writes)

    def barrier(self):
        fin = [(sem, self.dcnt[k]) for k, sem in self.dsem.items() if self.dcnt[k] > 0]
        fin += [(sem, self.cnt[e]) for e, sem in self.esem.items() if self.cnt[e] > 0]
        for q in ENGS:
            def thunk(e, fin=fin):
                for ws, wv in fin:
                    e.wait_ge(ws, wv)
            self.ops[q].append(thunk)

    def finish(self, q="sp"):
        fin = [(sem, self.dcnt[k]) for k, sem in self.dsem.items()]
        fin += [(sem, self.cnt[e]) for e, sem in self.esem.items() if self.cnt[e] > 0]

        def thunk(e, fin=fin):
            for ws, wv in fin:
                e.wait_ge(ws, wv)

        self.ops[q].append(thunk)

    def emit(self):
        nc = self.nc
        ops = self.ops
        with nc.Block() as block:
            @block.tensor
            def _(e):
                for t in ops["pe"]:
                    t(e)

            @block.scalar
            def _(e):
                for t in ops["act"]:
                    t(e)

            @block.vector
            def _(e):
                for t in ops["dve"]:
                    t(e)

            @block.gpsimd
            def _(e):
                for t in ops["pool"]:
                    t(e)

            @block.sync
            def _(e):
                for t in ops["sp"]:
                    t(e)


def build_nc():
    nc = bass.Bass("TRN2", target_bir_lowering=False)
    din = lambda n, s, dt=F32: nc.dram_tensor(n, list(s), dt, kind="ExternalInput").ap()
    dout = lambda n, s, dt=F32: nc.dram_tensor(n, list(s), dt, kind="ExternalOutput").ap()
    dscr = lambda n, s, dt: nc.dram_tensor(n, list(s), dt, kind="Internal").ap()
    xall = din("xall", [SEQ, D]); xown = din("xown", [NSTEP * 128, D])
    wext = din("wext", [128, 8, NEXT]); wpo = din("wpo", [128, 16, 1024]); wada = din("wada", [128, 8, 3072])
    badaT = din("badaT", [128, 24]); bgate = din("bgate", [128, 1024]); gnT = din("gnT", [128, 8])
    gfin = din("gfin", [128, 1024]); gret = din("gret", [128, 512]); cT = din("cT", [128, 8, 5])
    ident_d = din("ident", [128, 128]); cs_all = din("cs_all", [SEQ, 64]); cs_own = din("cs_own", [NSTEP * 128, 64])
    kdec_d = din("kdec", [128, 8]); qdec_d = din("qdec", [128, 8]); dtab_d = din("dtab", [128, 8, 128])
    g128_d = din("g128", [128, 4, 128]); sel_d = din("sel", [128, 4]); iota_d = din("iota512", [128, 512])
    qrel_d = din("qrel", [128, 1])
    I32 = mybir.dt.int32
    xsp = din("xsp", [4, 128, D]); xzero = din("xzero", [128, D])
    cache_k_d = din("cache_k2", [2560 * 128, 512]); cache_v_d = din("cache_v2", [2560 * 128, 512]); cache_ki_d = din("cache_ki2", [2560 * 128, 64])
    ptab = din("ptab", [4, 64], I32); state_in = din("state_in", [4, 8, 64, 64])
    cs_s_d = din("cs_s", [128, 64]); kdec_s_d = din("kdec_s", [128, 8]); qdec_s_d = din("qdec_s", [128, 8])
    dtab_s_d = din("dtab_s", [128, 8, 128]); g4_d = din("g4", [128, 4, 128]); sel_s_d = din("sel_s", [128, 4])
    penS_s_d = din("penS_s", [128, 512]); penM_s_d = din("penM_s", [128, 512]); pidx_d = din("pidx", [128, 1])
    ys_o = dout("ys_o", [4, 4, D]); ks_o = dout("ks_o", [4, 4, 512]); vs_o = dout("vs_o", [4, 4, 512])
    kis_o = dout("kis_o", [4, 4, 64]); sts_o = dout("sts_o", [4, 8, 64, 64])
    gate5_d = dscr("gate5_d", [5, D], F32)
    y_o = dout("y_o", [NSTEP * 128, D]); k_o = dout("k_o", [NSTEP * 128, 512]); v_o = dout("v_o", [NSTEP * 128, 512])
    ki_o = dout("ki_o", [NSTEP * 128, 64]); st_o = dout("st_o", [8, 64, 64])
    wbf = dscr("wbf", [128, 8, NEXT], BF16); wpobf = dscr("wpobf", [128, 16, 1024], BF16)
    kscrs = [dscr("kscr%d" % i, [128, 4, SEQ + 512], BF16) for i in range(2)]; vscrs = [dscr("vscr%d" % i, [SEQ // 128 + 4, 128, 520], BF16) for i in range(2)]

    with ExitStack() as st:
        S = Sched(nc, st)
        sb = lambda n, s, dt=F32: Tl(st.enter_context(nc.sbuf_tensor("s_" + n, list(s), dt)))
        psb = lambda n, s, dt=F32: Tl(st.enter_context(nc.psum_tensor("p_" + n, list(s), dt)))
        ACT = lambda fn, r=(), w=(): S.op("act", fn, r, w)
        POOL = lambda fn, r=(), w=(): S.op("pool", fn, r, w)
        DVE = lambda fn, r=(), w=(): S.op("dve", fn, r, w)
        PE = lambda fn, r=(), w=(): S.op("pe", fn, r, w)

        G = [psb("G%d" % i, [128, 512]) for i in range(2)]
        A = [psb("A%d" % i, [128, 512]) for i in range(4)]
        TP = psb("TP", [128, 8, 128], BF16)
        XP = psb("XP", [128, 512])
        gi = [0]

        def nextG():
            gi[0] ^= 1
            return G[gi[0]]

        kscr_bs = [[Buf() for _ in range(NSTEP + 1)] for _ in range(2)]; vscr_bs = [[Buf() for _ in range(NSTEP + 1)] for _ in range(2)]
        wbf_b = Buf(); wpobf_b = Buf()
        kiT = sb("kiT", [128, 4608], BF16)
        wsl = [sb("wsl%d" % i, [128, 4096], BF16) for i in range(2)]
        wi_ = [0]
        hT = sb("hT", [128, 8, 640], BF16)
        xo = [sb("xo%d" % i, [128, D]) for i in range(2)]
        xn = sb("xn", [128, D], BF16); sqj = xn
        ssq = sb("ssq", [128, 4]); epsT = sb("epsT", [128, 1])
        kst = sb("kst", [128, 4, 512], BF16); vst = sb("vst", [128, 4, 8, 65], BF16)
        ko = sb("ko", [128, 512]); vo = sb("vo", [128, 512]); kio = sb("kio", [128, 68])
        ev = [sb("ev%d" % i, [128, 512]) for i in range(2)]
        evi = [0]
        rt = [sb("rt%d" % i, [128, 8, 32]) for i in range(4)]
        krd = sb("krd", [128, 4, 512], BF16); vrb = sb("vrb", [128, 5, 512], BF16)
        krown = sb("krown", [128, 512], BF16); KrT = sb("KrT", [128, 4, 128], BF16)
        qp = sb("qp", [128, 512], BF16); QpT = sb("QpT", [128, 8, 128], BF16)
        Sst = sb("Sst", [128, 4, 128]); ownS = sb("ownS", [128, 4, 128])
        ownSb = sb("ownSb", [128, 8, 64], BF16)
        QT = [sb("QT%d" % i, [128, 8, 128], BF16) for i in range(2)]
        qiT2 = [sb("qiTlo", [128, 4, 128], BF16), sb("qiThi", [128, 4, 128], BF16)]
        wsc = [sb("wsc%d" % i, [128, 4]) for i in range(2)]
        siluza = [sb("siluza%d" % i, [128, 512], BF16) for i in range(2)]
        sga = [sb("sga%d" % i, [128, 1024], BF16) for i in range(2)]
        sgr = sb("sgr", [128, 1024], BF16)
        mr = [sb("mr%d" % i, [128, 1024], BF16) for i in range(2)]
        siluzr = sb("siluzr", [128, 512], BF16)
        rl = [sb("rl%d" % i, [128, 512]) for i in range(2)]
        rli = [0]
        eti = [0]
        dtab = sb("dtab", [128, 8, 128]); g128 = sb("g128", [128, 4, 128])
        cs5 = sb("cs5", [128, 5, 64])
        identf = sb("identf", [128, 128]); identb = sb("identb", [128, 128], BF16)
        ident4 = sb("ident4", [128, 4, 128], BF16); zl = sb("zl", [128, 128], BF16)
        ATb = sb("ATb", [128, 8, 128], BF16)
        oret = sb("oret", [128, 8, 64]); osq = sb("osq", [128, 8, 64])
        gst = sb("gst", [128, 8, 8])
        yrb = sb("yrb", [128, 512], BF16); yrT = sb("yrT", [128, 4, 128], BF16)
        acc = sb("acc", [128, 8, 65]); rden = sb("rden", [128, 8, 2])
        yaf = oret; yab = sb("yab", [128, 512], BF16); yaT = sb("yaT", [128, 4, 128], BF16)
        mg = sb("mg", [128, 1024], BF16); mT = sb("mT", [128, 8, 128], BF16)
        res = sb("res", [128, D]); yout = res; xs = [res] * 2
        gate_bc = sb("gate_bc", [128, D]); gfin_bc = sb("gfin_bc", [128, D]); gret_bc = sb("gret_bc", [128, 512])
        penS = sb("penS", [128, 512]); penM = sb("penM", [128, 512], BF16)
        kdec = sb("kdec", [128, 8]); qdec = sb("qdec", [128, 8]); sel = sb("sel", [128, 4])
        AcA = sb("AcA", [128, 5, 8]); BcA = sb("BcA", [128, 5, 8]); cur = {"j": 0, "set": 0}
        ptb = sb("ptb", [128, 64], I32); idxa = sb("idxa", [128, 64], I32); pidx = sb("pidx", [128, 1])
        m8 = sb("m8", [128, 8])

        def ld(key, dst_ap, src_ap, w, r=()):
            S.dma(key, lambda e: e.dma_start(out=dst_ap, in_=src_ap), reads=r, writes=w)

        def stq(key, dst_ap, src_ap, r, w=()):
            S.dma(key, lambda e: e.dma_start(out=dst_ap, in_=src_ap), reads=r, writes=w)

        def mm(out, lhsT, rhs, start, stop, r, w):
            PE(lambda e: e.matmul(out=out, lhsT=lhsT, rhs=rhs, start=start, stop=stop), r, w)

        def act(out, in_, func, r, w, **kw):
            ACT(lambda e: e.activation(out=out, in_=in_, func=func, **kw), r, w)

        def tt(eng, out, in0, in1, op, r, w):
            S.op(eng, lambda e: e.tensor_tensor(out=out, in0=in0, in1=in1, op=op), r, w)

        def ts(eng, out, in0, s1, s2, op0, op1, r, w):
            if op1 is None:
                S.op(eng, lambda e: e.tensor_scalar(out=out, in0=in0, scalar1=s1, scalar2=None, op0=op0), r, w)
            else:
                S.op(eng, lambda e: e.tensor_scalar(out=out, in0=in0, scalar1=s1, scalar2=s2, op0=op0, op1=op1), r, w)

        def stt(eng, out, in0, scalar, in1, op0, op1, r, w):
            S.op(eng, lambda e: e.scalar_tensor_tensor(out=out, in0=in0, scalar=scalar, in1=in1, op0=op0, op1=op1), r, w)

        def load_w(col0, ncols):
            sl = wsl[wi_[0] % 2]; wi_[0] += 1
            v = sl.t[:, 0:8 * ncols].rearrange("p (k c) -> p k c", k=8)
            ld("w%d" % ((wi_[0] - 1) % 2), v, wbf[:, :, col0:col0 + ncols], [sl], [wbf_b])
            return sl, v

        def load_wpo(kt0, nk, c0, ncols):
            sl = wsl[wi_[0] % 2]; wi_[0] += 1
            v = sl.t[:, 0:nk * ncols].rearrange("p (k c) -> p k c", k=nk)
            ld("w%d" % ((wi_[0] - 1) % 2), v, wpobf[:, kt0:kt0 + nk, c0:c0 + ncols], [sl], [wpobf_b])
            return sl, v

        def rstd_chain(src_tl, src_ap, n, col):
            act(sqj[:, 0:n], src_ap, AF.Square, [src_tl], [sqj, ssq], accum_out=ssq[:, 3:4])
            act(ssq[:, 2:3], ssq[:, 3:4], AF.Ln, [ssq], [ssq], scale=1.0 / n, bias=epsT[:, 0:1])
            act(ssq[:, col:col + 1], ssq[:, 2:3], AF.Exp, [ssq], [ssq], scale=-0.5)

        def make_hT(xt, col0):
            rstd_chain(xt, xt[:, :], D, 0)
            act(xn[:, :], xt[:, :], AF.Identity, [xt, ssq], [xn], scale=ssq[:, 0:1])
            for kt in range(8):
                PE(lambda e, kt=kt: e.transpose(out=TP[:, kt, :], in_=xn[:, kt * 128:(kt + 1) * 128], identity=identb[:, :]),
                   [xn, identb], [TP])
            for kt in range(8):
                jj = cur["j"]
                act(hT[:, kt, col0:col0 + 128], TP[:, kt, :], AF.Identity, [TP, AcA, BcA], [hT],
                    scale=AcA[:, jj, kt:kt + 1], bias=BcA[:, jj, kt:kt + 1])

        def rotary(src, cs_ap, dst_ap, dst_tl):
            sv = src.t[:, :].rearrange("p (h t f) -> p h t f", h=8, t=2)
            dv = dst_ap.rearrange("p (h t f) -> p h t f", h=8, t=2)
            cosb = cs_ap[:, 0:32].unsqueeze(1).to_broadcast([128, 8, 32])
            sinb = cs_ap[:, 32:64].unsqueeze(1).to_broadcast([128, 8, 32])
            x1 = sv[:, :, 0, :]; x2 = sv[:, :, 1, :]
            tt("pool", rt[0][:, :, :], x1, cosb, ALU.mult, [src, cs5], [rt[0]])
            tt("pool", rt[1][:, :, :], x2, sinb, ALU.mult, [src, cs5], [rt[1]])
            tt("pool", dv[:, :, 0, :], rt[0][:, :, :], rt[1][:, :, :], ALU.subtract, [rt[0], rt[1]], [dst_tl])
            tt("pool", rt[2][:, :, :], x1, sinb, ALU.mult, [src, cs5], [rt[2]])
            tt("pool", rt[3][:, :, :], x2, cosb, ALU.mult, [src, cs5], [rt[3]])
            tt("pool", dv[:, :, 1, :], rt[2][:, :, :], rt[3][:, :, :], ALU.add, [rt[2], rt[3]], [dst_tl])

        def next_ev():
            evi[0] += 1
            return ev[evi[0] % 2]

        try:
            st2 = ExitStack()
            sb2 = lambda n, s_, dt=F32: Tl(st2.enter_context(nc.sbuf_tensor("s_" + n, list(s_), dt)))
            bgate_sb = sb2("bgate_sb", [128, D])
            iota = sb2("iota", [128, 512]); qrel = sb2("qrel", [128, 1])
            cTs = sb2("cTs", [128, 8, 5]); scT = sb2("scT", [128, 8, 5]); gate5 = sb2("gate5", [5, D])
            modT = sb2("modT", [128, 16, 5]); badaTs = sb2("badaTs", [128, 24]); gnTs = sb2("gnTs", [128, 8])
            wadas = [sb2("wadas%d" % i, [128, 8, 256]) for i in range(2)]
            stgf = [sb2("stgf%d" % i, [128, 1024]) for i in range(2)]
            stgb = [sb2("stgb%d" % i, [128, 1024], BF16) for i in range(2)]
            ld("c0", identf[:, :], ident_d, [identf]); ld("c0", dtab[:, :, :], dtab_d, [dtab]); ld("c0", g128[:, :, :], g128_d, [g128])
            ld("c0", kdec[:, :], kdec_d, [kdec]); ld("c0", qdec[:, :], qdec_d, [qdec]); ld("c0", sel[:, :], sel_d, [sel])
            ld("c0", iota[:, :], iota_d, [iota]); ld("c0", qrel[:, :], qrel_d, [qrel]); ld("c0", cTs[:, :, :], cT, [cTs])
            ld("c0", badaTs[:, :], badaT, [badaTs]); ld("c0", gnTs[:, :], gnT, [gnTs]); ld("c0", bgate_sb[:, :], bgate, [bgate_sb])
            ld("c0", gfin_bc[:, :], gfin, [gfin_bc]); ld("c0", gret_bc[:, :], gret, [gret_bc])
            POOL(lambda e: e.tensor_copy(out=identb[:, :], in_=identf[:, :]), [identf], [identb])
            for a4 in range(4):
                POOL(lambda e, a4=a4: e.tensor_copy(out=ident4[:, a4, :], in_=identf[:, :]), [identf], [ident4])
            POOL(lambda e: e.memset(zl[:, :], 0.0), [], [zl])
            POOL(lambda e: e.memset(epsT[:, :], 1e-6), [], [epsT])
            POOL(lambda e: e.memset(vst[:, :, :, 64:65], 1.0), [], [vst])
            POOL(lambda e: e.memset(Sst[:, :, :], 0.0), [], [Sst])
            POOL(lambda e: e.memset(ownSb[:, :, :], 0.0), [], [ownSb])
            POOL(lambda e: e.memset(QpT[:, :, :], 0.0), [], [QpT])
            POOL(lambda e: e.memset(kiT[:, :], 0.0), [], [kiT])
            for q_ in qiT2:
                POOL(lambda e, q_=q_: e.memset(q_[:, :, :], 0.0), [], [q_])
            for q_ in QT:
                POOL(lambda e, q_=q_: e.memset(q_[:, :, :], 0.0), [], [q_])
            ts("pool", penS[:, :], iota[:, :], qrel[:, 0:1], -1.0e30, ALU.is_gt, ALU.mult, [iota, qrel], [penS])
            ts("pool", penM[:, :], iota[:, :], qrel[:, 0:1], -30000.0, ALU.is_gt, ALU.mult, [iota, qrel], [penM])
            act(scT[:, :, :], cTs[:, :, :], AF.Silu, [cTs], [scT])
            for cc in range(12):
                wa = wadas[cc % 2]
                ld("wa%d" % (cc % 2), wa[:, :, :], wada[:, :, cc * 256:(cc + 1) * 256], [wa])
                if cc < 8:
                    for t2 in range(2):
                        ct = cc * 2 + t2
                        for kt in range(8):
                            mm(XP[:, ct * 5:ct * 5 + 5], wa[:, kt, t2 * 128:(t2 + 1) * 128], scT[:, kt, :], kt == 0, kt == 7,
                               [wa, scT], [XP])
                else:
                    gch = cc - 8
                    g_ = G[gch // 2]
                    for kt in range(8):
                        mm(g_[0:5, (gch % 2) * 256:(gch % 2) * 256 + 256], scT[:, kt, :], wa[:, kt, :], kt == 0, kt == 7,
                           [wa, scT], [g_])
            act(modT[:, :, :], XP[:, 0:80].rearrange("p (c j) -> p c j", j=5), AF.Identity, [XP], [modT])
            for j in range(5):
                tt("pool", BcA[:, j, :], modT[:, 0:8, j], badaTs[:, 0:8], ALU.add, [modT, badaTs], [BcA])
                ts("pool", AcA[:, j, :], modT[:, 8:16, j], 1.0, None, ALU.add, None, [modT], [AcA])
                tt("pool", AcA[:, j, :], AcA[:, j, :], badaTs[:, 8:16], ALU.add, [AcA, badaTs], [AcA])
                tt("pool", AcA[:, j, :], AcA[:, j, :], gnTs[:, :], ALU.mult, [AcA, gnTs], [AcA])
            for hf in range(2):
                act(gate5[:, hf * 512:(hf + 1) * 512], G[hf][0:5, :], AF.Identity, [G[hf]], [gate5])
            tt("pool", gate5[:, :], gate5[:, :], bgate_sb[0:5, :], ALU.add, [gate5, bgate_sb], [gate5])
            gate5_b = Buf()
            stq("g5", gate5_d, gate5[:, :], [gate5], [gate5_b])
            ld("g5l", gate_bc[:, :], gate5_d[0:1, :].partition_broadcast(128), [gate_bc], [gate5_b])
            ld("c0", pidx[:, :], pidx_d, [pidx])
            stop_at(1)
            nch = (NEXT + 127) // 128
            for c in range(nch):
                c0 = c * 128; n = min(128, NEXT - c0); i = c % 2
                fv = stgf[i].t[:, 0:8 * n].rearrange("p (k c) -> p k c", k=8)
                bv = stgb[i].t[:, 0:8 * n].rearrange("p (k c) -> p k c", k=8)
                ld("ps%d" % i, fv, wext[:, :, c0:c0 + n], [stgf[i]])
                if c % 2 == 0:
                    DVE(lambda e, fv=fv, bv=bv: e.tensor_copy(out=bv, in_=fv), [stgf[i]], [stgb[i]])
                else:
                    act(bv, fv, AF.Identity, [stgf[i]], [stgb[i]])
                stq("pt%d" % i, wbf[:, :, c0:c0 + n], bv, [stgb[i]], [])
            for c in range(16):
                c0 = c * 64; i = c % 2
                fv = stgf[i].t[:, :].rearrange("p (k c) -> p k c", k=16)
                bv = stgb[i].t[:, :].rearrange("p (k c) -> p k c", k=16)
                ld("ps%d" % i, fv, wpo[:, :, c0:c0 + 64], [stgf[i]])
                if c % 2 == 0:
                    DVE(lambda e, fv=fv, bv=bv: e.tensor_copy(out=bv, in_=fv), [stgf[i]], [stgb[i]])
                else:
                    act(bv, fv, AF.Identity, [stgf[i]], [stgb[i]])
                stq("pt%d" % i, wpobf[:, :, c0:c0 + 64], bv, [stgb[i]], [])

            stop_at(2)
            S.barrier()
            st2.close()
            work0 = sb("work0", [128, SEQ + 512]); wbA = work0.b; wbB = Buf()

            def wkinfo(g):
                if g < 8:
                    return (g % 2) * 4352, ([wbA] if g % 2 == 0 else [wbB])
                return 0, [wbA, wbB]
            nm = [sb("nm%d" % i, [128, 512], BF16) for i in range(2)]
            kc = [sb("kc%d" % i, [128, 4, 512], BF16) for i in range(2)]
            vc = [sb("vc%d" % i, [128, 4, 520], BF16) for i in range(2)]
            ET = [sb("ET%d" % i, [128, 512], BF16) for i in range(2)]

            def projscores(g, smp=None):
                sl = g % 2
                L = (g + 1) * 512
                if smp is None:
                    ld("cs", cs5[:, 0:4, :], cs_all[g * 512:(g + 1) * 512, :].rearrange("(t p) c -> p t c", p=128), [cs5])
                    ld("cs", cs5[:, 4, :], cs_own[g * 128:(g + 1) * 128, :], [cs5])
                else:
                    for j5 in range(5):
                        ld("cs", cs5[:, j5, :], cs_s_d, [cs5])
                for j in range(4 if smp is None else 1):
                    xt = xs[j % 2]
                    if smp is None:
                        ld("x0", xt[:, :], xall[(4 * g + j) * 128:(4 * g + j + 1) * 128, :], [xt])
                    else:
                        ld("x0", xt[:, :], xsp[smp, :, :] if j == 0 else xzero, [xt])
                    make_hT(xt, j * 128)
                ld("xo%d" % sl, xo[sl][:, :], xown[g * 128:(g + 1) * 128, :] if smp is None else xsp[smp, :, :], [xo[sl]])
                make_hT(xo[sl], 512)
                stop_at(31)
                wt, wv = load_w(C_KA, 512)
                for pr in range(4):
                    g_ = nextG()
                    for kt in range(8):
                        mm(g_[:, :], wv[:, kt, pr * 128:(pr + 1) * 128], hT[:, kt, 0:512], kt == 0, kt == 7, [wt, hT], [g_])
                    act(kst[:, pr, :], g_[:, :], AF.Identity, [g_], [kst])
                stq("kst", kscrs[cur["set"]][:, :, g * 512:(g + 1) * 512], kst[:, :, :], [kst], [kscr_bs[cur["set"]][g]])
                g_ = nextG()
                for kt in range(8):
                    mm(g_[:, :], hT[:, kt, 512:640], wv[:, kt, :], kt == 0, kt == 7, [wt, hT], [g_])
                act(ko[:, :], g_[:, :], AF.Identity, [g_], [ko])
                if smp is None:
                    stq("ko", k_o[g * 128:(g + 1) * 128, :], ko[:, :], [ko])
                else:
                    stq("ko", ks_o[smp, :, :], ko[0:4, :], [ko])
                stop_at(32)
                wt, wv = load_w(C_VA, 512)
                for j in (range(5) if smp is None else (0, 4)):
                    g_ = nextG()
                    for kt in range(8):
                        mm(g_[:, :], hT[:, kt, j * 128:(j + 1) * 128], wv[:, kt, :], kt == 0, kt == 7, [wt, hT], [g_])
                    if j < 4:
                        act(vst[:, j, :, 0:64], g_[:, :].rearrange("p (h d) -> p h d", h=8), AF.Identity, [g_], [vst])
                    else:
                        act(vo[:, :], g_[:, :], AF.Identity, [g_], [vo])
                stq("vst", vscrs[cur["set"]][4 * g:4 * g + 4, :, :].rearrange("t p c -> p t c"), vst[:, :, :, :].rearrange("p t h c -> p t (h c)"),
                    [vst], [vscr_bs[cur["set"]][g]])
                if smp is None:
                    stq("vo", v_o[g * 128:(g + 1) * 128, :], vo[:, :], [vo])
                else:
                    stq("vo", vs_o[smp, :, :], vo[0:4, :], [vo])
                stop_at(33)
                wt, wv = load_w(C_KR, 512)
                for j in (range(5) if smp is None else (0, 4)):
                    g_ = nextG()
                    for kt in range(8):
                        mm(g_[:, :], hT[:, kt, j * 128:(j + 1) * 128], wv[:, kt, :], kt == 0, kt == 7, [wt, hT], [g_])
                    e_ = next_ev()
                    if j < 4:
                        for h in range(8):
                            act(e_[:, h * 64:(h + 1) * 64], g_[:, h * 64:(h + 1) * 64], AF.Identity, [g_, kdec], [e_], scale=kdec[:, h:h + 1])
                        rotary(e_, cs5[:, j, :], krd[:, j, :], krd)
                    else:
                        act(e_[:, :], g_[:, :], AF.Identity, [g_], [e_])
                        rotary(e_, cs5[:, 4, :], krown[:, :], krown)
                stop_at(34)
                wt, wv = load_w(C_VR, 512)
                for j in (range(5) if smp is None else (0, 4)):
                    g_ = nextG()
                    for kt in range(8):
                        mm(g_[:, :], hT[:, kt, j * 128:(j + 1) * 128], wv[:, kt, :], kt == 0, kt == 7, [wt, hT], [g_])
                    act(vrb[:, j, :], g_[:, :], AF.Identity, [g_], [vrb])
                stop_at(35)
                for j in range(4 if smp is None else 1):
                    if j == 0:
                        ts("pool", ownS[:, :, :], Sst[:, :, :], sel[:, 0:1], None, ALU.mult, None, [Sst, sel], [ownS])
                    else:
                        tS = next_ev()
                        tSv = tS.t[:, :].rearrange("p (a b) -> p a b", a=4)
                        ts("pool", tSv, Sst[:, :, :], sel[:, j:j + 1], None, ALU.mult, None, [Sst, sel], [tS])
                        tt("pool", ownS[:, :, :], ownS[:, :, :], tSv, ALU.add, [ownS, tS], [ownS])
                    for pr in range(4):
                        mm(XP[:, pr * 128:(pr + 1) * 128], krd[:, j, pr * 128:(pr + 1) * 128], vrb[:, j, pr * 128:(pr + 1) * 128], True, True,
                           [krd, vrb], [XP])
                    tmpS = next_ev()
                    act(tmpS[:, :], XP[:, :], AF.Identity, [XP], [tmpS])
                    tt("pool", Sst[:, :, :], Sst[:, :, :], g128[:, :, :], ALU.mult, [Sst, g128], [Sst])
                    tt("pool", Sst[:, :, :], Sst[:, :, :], tmpS[:, :].rearrange("p (a b) -> p a b", a=4), ALU.add, [Sst, tmpS], [Sst])
                for hb in range(2):
                    POOL(lambda e, hb=hb: e.tensor_copy(out=ownSb[hb * 64:(hb + 1) * 64, hb::2, :],
                                                        in_=ownS[hb * 64:(hb + 1) * 64, :, hb * 64:(hb + 1) * 64]), [ownS], [ownSb])
                stop_at(36)
                wt, wv = load_w(C_KID, 512)
                g_ = nextG()
                for kt in range(8):
                    mm(g_[:, :], wv[:, kt, 0:128], hT[:, kt, 0:512], kt == 0, kt == 7, [wt, hT], [g_])
                pb = g // 9
                cc = g % 9
                if os.environ.get('MK_V', '') != 'noki':
                    act(kiT[pb * 64:(pb + 1) * 64, cc * 512:(cc + 1) * 512], g_[pb * 64:(pb + 1) * 64, :], AF.Identity, [g_], [kiT])
                stop_at(37)
                wt, wv = load_w(C_QA, 512)
                g_ = nextG()
                for pr in range(4):
                    for kt in range(8):
                        mm(g_[:, pr * 128:(pr + 1) * 128], wv[:, kt, pr * 128:(pr + 1) * 128], hT[:, kt, 512:640], kt == 0, kt == 7, [wt, hT], [g_])
                for hb in range(2):
                    act(QT[sl][hb * 64:(hb + 1) * 64, hb::2, :], g_[hb * 64:(hb + 1) * 64, :].rearrange("p (a q) -> p a q", a=4), AF.Identity,
                        [g_], [QT[sl]])
                wt, wv = load_w(C_QID, 512)
                g_ = nextG()
                for h in range(4):
                    for kt in range(8):
                        mm(g_[:, h * 128:(h + 1) * 128], wv[:, kt, h * 128:(h + 1) * 128], hT[:, kt, 512:640], kt == 0, kt == 7, [wt, hT], [g_])
                for hb in range(2):
                    act(qiT2[hb][hb * 64:(hb + 1) * 64, :, :], g_[hb * 64:(hb + 1) * 64, :].rearrange("p (a q) -> p a q", a=4), AF.Identity,
                        [g_], [qiT2[hb]])
                wt, wv = load_w(C_KIWI, 512)
                g_ = nextG()
                for kt in range(8):
                    mm(g_[:, 0:68], hT[:, kt, 512:640], wv[:, kt, 0:68], kt == 0, kt == 7, [wt, hT], [g_])
                act(kio[:, :], g_[:, 0:68], AF.Identity, [g_], [kio])
                if smp is None:
                    stq("kio", ki_o[g * 128:(g + 1) * 128, :], kio[:, 0:64], [kio])
                else:
                    stq("kio", kis_o[smp, :, :], kio[0:4, 0:64], [kio])
                ts("pool", wsc[sl][:, :], kio[:, 64:68], 1.0 / 16.0, None, ALU.mult, None, [kio], [wsc[sl]])
                if PROJ_ONLY:
                    return
                wt, wv = load_w(C_ZA, 512)
                g_ = nextG()
                for kt in range(8):
                    mm(g_[:, :], hT[:, kt, 512:640], wv[:, kt, :], kt == 0, kt == 7, [wt, hT], [g_])
                act(siluza[sl][:, :], g_[:, :], AF.Silu, [g_], [siluza[sl]])
                wt, wv = load_w(C_QR, 512)
                g_ = nextG()
                for kt in range(8):
                    mm(g_[:, :], hT[:, kt, 512:640], wv[:, kt, :], kt == 0, kt == 7, [wt, hT], [g_])
                e_ = next_ev()
                for h in range(8):
                    act(e_[:, h * 64:(h + 1) * 64], g_[:, h * 64:(h + 1) * 64], AF.Identity, [g_, qdec], [e_], scale=qdec[:, h:h + 1])
                rotary(e_, cs5[:, 4, :], qp[:, :], qp)
                for pr in range(4):
                    PE(lambda e, pr=pr: e.transpose(out=TP[:, pr, :], in_=qp[:, pr * 128:(pr + 1) * 128], identity=identb[:, :]), [qp, identb], [TP])
                for hb in range(2):
                    act(QpT[hb * 64:(hb + 1) * 64, hb::2, :], TP[hb * 64:(hb + 1) * 64, 0:4, :], AF.Identity, [TP], [QpT])
                for pr in range(4):
                    PE(lambda e, pr=pr: e.transpose(out=TP[:, 4 + pr, :], in_=krown[:, pr * 128:(pr + 1) * 128], identity=identb[:, :]), [krown, identb], [TP])
                act(KrT[:, :, :], TP[:, 4:8, :], AF.Identity, [TP], [KrT])
                wt, wv = load_w(C_ZR, 512)
                g_ = nextG()
                for kt in range(8):
                    mm(g_[:, :], hT[:, kt, 512:640], wv[:, kt, :], kt == 0, kt == 7, [wt, hT], [g_])
                act(siluzr[:, :], g_[:, :], AF.Silu, [g_], [siluzr])
                for ci, (c0, dst) in enumerate([(C_GA, sga[sl]), (C_GA + 512, sga[sl]), (C_GR, sgr), (C_GR + 512, sgr)]):
                    wt, wv = load_w(c0, 512)
                    g_ = nextG()
                    for kt in range(8):
                        mm(g_[:, :], hT[:, kt, 512:640], wv[:, kt, :], kt == 0, kt == 7, [wt, hT], [g_])
                    act(dst[:, (ci % 2) * 512:(ci % 2 + 1) * 512], g_[:, :], AF.Sigmoid, [g_], [dst])
                stop_at(38)
                for h in range(8):
                    pr, hb = h // 2, h % 2
                    a_ = A[h // 4]
                    mm(a_[:, (h % 4) * 128:(h % 4 + 1) * 128], KrT[:, pr, :], QpT[:, h, :], True, True,
                       [KrT, QpT], [a_])
                stop_at(381)
                for hf in range(2):
                    e_ = next_ev()
                    act(e_[:, :], A[hf][:, :], AF.Identity, [A[hf]], [e_])
                    tt("pool", ATb[:, hf * 4:(hf + 1) * 4, :], e_[:, :].rearrange("p (a q) -> p a q", a=4), dtab[:, hf * 4:(hf + 1) * 4, :], ALU.mult,
                       [e_, dtab], [ATb])
                stop_at(382)
                for h in range(8):
                    pr, hb = h // 2, h % 2
                    mm(A[2][:, h * 64:(h + 1) * 64], ATb[:, h, :], vrb[:, 4, h * 64:(h + 1) * 64], True, False, [ATb, vrb], [A[2]])
                    mm(A[2][:, h * 64:(h + 1) * 64], QpT[:, h, :], ownSb[:, h, :], False, True,
                       [QpT, ownSb], [A[2]])
                stop_at(383)
                act(oret[:, :, :], A[2][:, :].rearrange("p (h d) -> p h d", h=8), AF.Identity, [A[2]], [oret])
                stop_at(39)
                DVE(lambda e: e.tensor_reduce(out=gst[:, :, 0], in_=oret[:, :, :], axis=AX.X, op=ALU.add), [oret], [gst])
                tt("pool", osq[:, :, :], oret[:, :, :], oret[:, :, :], ALU.mult, [oret], [osq])
                DVE(lambda e: e.tensor_reduce(out=gst[:, :, 1], in_=osq[:, :, :], axis=AX.X, op=ALU.add), [osq], [gst])
                ts("pool", gst[:, :, 2], gst[:, :, 0], 1.0 / 64, None, ALU.mult, None, [gst], [gst])
                tt("pool", gst[:, :, 3], gst[:, :, 2], gst[:, :, 2], ALU.mult, [gst], [gst])
                ts("pool", gst[:, :, 7], gst[:, :, 1], 1.0 / 64, None, ALU.mult, None, [gst], [gst])
                tt("pool", gst[:, :, 4], gst[:, :, 7], gst[:, :, 3], ALU.subtract, [gst], [gst])
                act(gst[:, :, 5], gst[:, :, 4], AF.Ln, [gst], [gst], bias=epsT[:, 0:1])
                act(gst[:, :, 6], gst[:, :, 5], AF.Exp, [gst], [gst], scale=-0.5)
                tt("pool", osq[:, :, :], oret[:, :, :], gst[:, :, 2:3].to_broadcast([128, 8, 64]), ALU.subtract, [oret, gst], [osq])
                tt("pool", osq[:, :, :], osq[:, :, :], gst[:, :, 6:7].to_broadcast([128, 8, 64]), ALU.mult, [osq, gst], [osq])
                osq2 = osq.t[:, :, :].rearrange("p h d -> p (h d)")
                tt("pool", osq2, osq2, gret_bc[:, :], ALU.mult, [osq, gret_bc], [osq])
                tt("pool", yrb[:, :], osq2, siluzr[:, :], ALU.mult, [osq, siluzr], [yrb])
                for pr in range(4):
                    PE(lambda e, pr=pr: e.transpose(out=TP[:, pr, :], in_=yrb[:, pr * 128:(pr + 1) * 128], identity=identb[:, :]), [yrb, identb], [TP])
                act(yrT[:, :, :], TP[:, 0:4, :], AF.Identity, [TP], [yrT])
                for hf in range(2):
                    wt, wv = load_wpo(4, 4, hf * 512, 512)
                    g_ = nextG()
                    for kt in range(4):
                        mm(g_[:, :], yrT[:, kt, :], wv[:, kt, :], kt == 0, kt == 3, [wt, yrT], [g_])
                    e_ = next_ev()
                    act(e_[:, :], g_[:, :], AF.Identity, [g_], [e_])
                    tt("pool", mr[sl][:, hf * 512:(hf + 1) * 512], e_[:, :], sgr[:, hf * 512:(hf + 1) * 512], ALU.mult, [e_, sgr], [mr[sl]])
            def scores(g):
                sl = g % 2
                off, wkb = wkinfo(g)
                for c in range(g + 1):
                    pb = c // 9
                    cc = c % 9
                    for h in range(4):
                        g_ = nextG()
                        mm(g_[:, :], qiT2[pb][:, h, :], kiT[:, cc * 512:(cc + 1) * 512], True, True, [qiT2[pb], kiT], [g_])
                        rli[0] += 1
                        r_ = rl[rli[0] % 2]
                        act(r_[:, :], g_[:, :], AF.Relu, [g_], [r_])
                        dst = work0[:, off + c * 512:off + (c + 1) * 512]
                        if h == 0:
                            if c == g:
                                stt("dve", dst, r_[:, :], wsc[sl][:, 0:1], penS[:, :], ALU.mult, ALU.add, [r_, wsc[sl], penS], wkb)
                            else:
                                ts("dve", dst, r_[:, :], wsc[sl][:, 0:1], None, ALU.mult, None, [r_, wsc[sl]], wkb)
                        else:
                            stt("dve", dst, r_[:, :], wsc[sl][:, h:h + 1], dst, ALU.mult, ALU.add, [r_, wsc[sl]] + wkb, wkb)

            def topk(g):
                off, wkb = wkinfo(g)
                L = (g + 1) * 512
                for r in range(TOPK // 8):
                    DVE(lambda e: e.max(out=m8[:, :], in_=work0[:, off:off + L]), wkb, [m8])
                    DVE(lambda e: e.match_replace(out=work0[:, off:off + L], in_to_replace=m8[:, :], in_values=work0[:, off:off + L], imm_value=NEG_BIG), wkb + [m8], wkb)

            def attention(g, smp=None):
                sl = g % 2
                off, wkb = wkinfo(g)
                nkt = 4 * (g + 1)
                for c in range(g + 1):
                    kc_ = kc[c % 2]; vc_ = vc[c % 2]; nm_ = nm[c % 2]
                    ld("kc%d" % (c % 2), kc_[:, :, :], kscrs[cur["set"]][:, :, c * 512:(c + 1) * 512], [kc_], [kscr_bs[cur["set"]][c]])
                    ld("vc%d" % (c % 2), vc_[:, :, :], vscrs[cur["set"]][4 * c:4 * c + 4, :, :].rearrange("t p c -> p t c"), [vc_], [vscr_bs[cur["set"]][c]])
                    ts("dve", nm_[:, :], work0[:, off + c * 512:off + (c + 1) * 512], -1.0e38, -30000.0, ALU.is_gt, ALU.mult, wkb, [nm_])
                    if c == g:
                        tt("dve", nm_[:, :], nm_[:, :], penM[:, :], ALU.add, [nm_, penM], [nm_])
                    if g == 0 and c == 0 and os.environ.get('MK_DBG'):
                        stq("dbg", dbgb[:, 0:512], nm_[:, :], [nm_])
                        stq("dbg", dbgb[:, 1024:1536], kc_[:, 0, :], [kc_])
                    for t in range(4):
                        kt_i = c * 4 + t
                        for hg in range(2):
                            a_ = A[(2 * (kt_i % 2)) + hg]
                            mm(a_[:, :], nm_[:, t * 128:(t + 1) * 128], ident4[:, :, :].rearrange("p a q -> p (a q)"), True, False,
                               [nm_, ident4], [a_])
                            for pp in range(2):
                                pr = hg * 2 + pp
                                mm(a_[:, pp * 256:(pp + 1) * 256], kc_[:, pr, t * 128:(t + 1) * 128],
                                   QT[sl][:, 2 * pr:2 * pr + 2, :].rearrange("p a q -> p (a q)"), False, pp == 1, [kc_, QT[sl]], [a_])
                            eti[0] += 1
                            et = ET[eti[0] % 2]
                            act(et[:, :], a_[:, :], AF.Exp, [a_], [et], scale=0.125)
                            if g == 0 and kt_i == 0 and hg == 0 and os.environ.get('MK_DBG'):
                                stq("dbg", dbgb[:, 512:1024], et[:, :], [et])
                                stq("dbg", dbgb[:, 1536:2048], QT[sl][:, 0:4, :].rearrange("p a q -> p (a q)"), [QT[sl]])
                            for hh in range(4):
                                h = hg * 4 + hh
                                if kt_i == 0 and hh == 0:
                                    mm(G[hg][:, 0:260], zl[:, :], vc_[:, t, 0:260], True, False, [zl, vc_], [G[hg]])
                                mm(G[hg][:, hh * 65:(hh + 1) * 65], et[:, hh * 128:(hh + 1) * 128], vc_[:, t, h * 65:(h + 1) * 65],
                                   False, (kt_i == nkt - 1 and hh == 3), [et, vc_], [G[hg]])
                for hg in range(2):
                    act(acc[:, hg * 4:(hg + 1) * 4, :], G[hg][:, 0:260].rearrange("p (h c) -> p h c", h=4), AF.Identity, [G[hg]], [acc])
                if g == 0 and os.environ.get('MK_DBG'):
                    stq("dbg", dbg[:, 512:1032], acc.t[:, :, :].rearrange("p h c -> p (h c)"), [acc])
                act(rden[:, :, 0], acc[:, :, 64], AF.Ln, [acc], [rden])
                act(rden[:, :, 1], rden[:, :, 0], AF.Exp, [rden], [rden], scale=-1.0)
                tt("pool", yaf[:, :, :], acc[:, :, 0:64], rden[:, :, 1:2].to_broadcast([128, 8, 64]), ALU.mult, [acc, rden], [yaf])
                tt("pool", yab[:, :], yaf.t[:, :, :].rearrange("p h d -> p (h d)"), siluza[sl][:, :], ALU.mult, [yaf, siluza[sl]], [yab])
                for pr in range(4):
                    PE(lambda e, pr=pr: e.transpose(out=TP[:, pr, :], in_=yab[:, pr * 128:(pr + 1) * 128], identity=identb[:, :]), [yab, identb], [TP])
                act(yaT[:, :, :], TP[:, 0:4, :], AF.Identity, [TP], [yaT])
                for hf in range(2):
                    wt, wv = load_wpo(0, 4, hf * 512, 512)
                    g_ = nextG()
                    for kt in range(4):
                        mm(g_[:, :], yaT[:, kt, :], wv[:, kt, :], kt == 0, kt == 3, [wt, yaT], [g_])
                    e_ = next_ev()
                    act(e_[:, :], g_[:, :], AF.Identity, [g_], [e_])
                    tt("pool", e_[:, :], e_[:, :], sga[sl][:, hf * 512:(hf + 1) * 512], ALU.mult, [e_, sga[sl]], [e_])
                    tt("pool", mg[:, hf * 512:(hf + 1) * 512], e_[:, :], mr[sl][:, hf * 512:(hf + 1) * 512], ALU.add, [e_, mr[sl]], [mg])
                for kt in range(8):
                    PE(lambda e, kt=kt: e.transpose(out=TP[:, kt, :], in_=mg[:, kt * 128:(kt + 1) * 128], identity=identb[:, :]), [mg, identb], [TP])
                act(mT[:, :, :], TP[:, :, :], AF.Identity, [TP], [mT])
                for hf in range(2):
                    wt, wv = load_wpo(8, 8, hf * 512, 512)
                    g_ = nextG()
                    for kt in range(8):
                        mm(g_[:, :], mT[:, kt, :], wv[:, kt, :], kt == 0, kt == 7, [wt, mT], [g_])
                    e_ = next_ev()
                    act(e_[:, :], g_[:, :], AF.Identity, [g_], [e_])
                    tt("pool", e_[:, :], e_[:, :], gate_bc[:, hf * 512:(hf + 1) * 512], ALU.mult, [e_, gate_bc], [e_])
                    tt("pool", res[:, hf * 512:(hf + 1) * 512], e_[:, :], xo[sl][:, hf * 512:(hf + 1) * 512], ALU.add, [e_, xo[sl]], [res])
                if g == 0 and os.environ.get('MK_DBG'):
                    stq("dbg", dbg[:, 2048:3072], res[:, :], [res])
                rstd_chain(res, res[:, :], D, 1)
                act(res[:, :], res[:, :], AF.Identity, [res, ssq], [res], scale=ssq[:, 1:2])
                tt("pool", yout[:, :], res[:, :], gfin_bc[:, :], ALU.mult, [res, gfin_bc], [yout])
                if smp is None:
                    stq("yo", y_o[g * 128:(g + 1) * 128, :], yout[:, :], [yout])
                else:
                    stq("yo", ys_o[smp, :, :], yout[0:4, :], [yout])

            n = NSTEPS_RUN
            if PROJ_ONLY:
                for g in range(n):
                    projscores(g)
                raise _Stop2()
            projscores(0)
            stop_at(3)
            scores(0)
            stop_at(4)
            for g in range(n):
                topk(g)
                if g + 1 < n:
                    projscores(g + 1)
                    if g + 1 < 8:
                        scores(g + 1)
                        attention(g)
                    else:
                        attention(g)
                        scores(g + 1)
                else:
                    attention(g)
            for h in range(8):
                pr, hb = h // 2, h % 2
                stq("sto", st_o[h, :, :], Sst[hb * 64:(hb + 1) * 64, pr, hb * 64:(hb + 1) * 64], [Sst])

            def ingest(s_):
                ld("pt", ptb[:, :], ptab[s_:s_ + 1, :].partition_broadcast(128), [ptb])
                ts("pool", idxa[:, :], ptb[:, :], 128.0, pidx[:, 0:1], ALU.mult, ALU.add, [ptb, pidx], [idxa])
                POOL(lambda e: e.memset(Sst[:, :, :], 0.0), [], [Sst])
                for hb in range(2):
                    ld("sti", Sst[hb * 64:(hb + 1) * 64, :, hb * 64:(hb + 1) * 64],
                       state_in[s_, hb::2, :, :].rearrange("h p e -> p h e"), [Sst])
                for pg in range(64):
                    c = pg // 4; t = pg % 4
                    kp = rl[pg % 2]; vp = ev[pg % 2]
                    S.dma("gk%d" % (pg % 2), lambda e, kp=kp, pg=pg: e.indirect_dma_start(
                        out=kp[:, :], out_offset=None, in_=cache_k_d,
                        in_offset=bass.IndirectOffsetOnAxis(ap=idxa[:, pg:pg + 1], axis=0)), reads=[_b(idxa)], writes=[_b(kp)], q="pool")
                    S.dma("gv%d" % (pg % 2), lambda e, vp=vp, pg=pg: e.indirect_dma_start(
                        out=vp[:, :], out_offset=None, in_=cache_v_d,
                        in_offset=bass.IndirectOffsetOnAxis(ap=idxa[:, pg:pg + 1], axis=0)), reads=[_b(idxa)], writes=[_b(vp)], q="pool")
                    S.dma("gi", lambda e, pg=pg: e.indirect_dma_start(
                        out=kio[:, 0:64], out_offset=None, in_=cache_ki_d,
                        in_offset=bass.IndirectOffsetOnAxis(ap=idxa[:, pg:pg + 1], axis=0)), reads=[_b(idxa)], writes=[_b(kio)], q="pool")
                    kb = qp if pg % 2 == 0 else krown
                    act(kb[:, :], kp[:, :], AF.Identity, [kp], [kb])
                    for pr in range(4):
                        PE(lambda e, pr=pr, kb=kb: e.transpose(out=TP[:, pr, :], in_=kb[:, pr * 128:(pr + 1) * 128], identity=identb[:, :]),
                           [kb, identb], [TP])
                    act(kst[:, :, t * 128:(t + 1) * 128], TP[:, 0:4, :], AF.Identity, [TP], [kst])
                    act(vst[:, t, :, 0:64], vp[:, :].rearrange("p (h d) -> p h d", h=8), AF.Identity, [vp], [vst])
                    for dd in range(2):
                        act(yab[:, dd * 64:(dd + 1) * 64], kio[:, 0:64], AF.Identity, [kio], [yab])
                    PE(lambda e: e.transpose(out=TP[:, 4, :], in_=yab[:, 0:128], identity=identb[:, :]), [yab, identb], [TP])
                    pb = c // 9; cc = c % 9
                    act(kiT[pb * 64:(pb + 1) * 64, cc * 512 + t * 128:cc * 512 + (t + 1) * 128], TP[pb * 64:(pb + 1) * 64, 4, :], AF.Identity,
                        [TP], [kiT])
                    if t == 3:
                        stq("kst", kscrs[cur["set"]][:, :, c * 512:(c + 1) * 512], kst[:, :, :], [kst], [kscr_bs[cur["set"]][c]])
                        stq("vst", vscrs[cur["set"]][4 * c:4 * c + 4, :, :].rearrange("t p c -> p t c"), vst[:, :, :, :].rearrange("p t h c -> p t (h c)"),
                            [vst], [vscr_bs[cur["set"]][c]])

            NSAMP = int(os.environ.get("MK_NSAMP", "4"))
            if NSAMP > 0:
                S.barrier()
                ld("c1", kdec[:, :], kdec_s_d, [kdec]); ld("c1", qdec[:, :], qdec_s_d, [qdec]); ld("c1", dtab[:, :, :], dtab_s_d, [dtab])
                ld("c1", g128[:, :, :], g4_d, [g128]); ld("c1", sel[:, :], sel_s_d, [sel]); ld("c1", penS[:, :], penS_s_d, [penS])
                pe_ = next_ev()
                ld("c1", pe_[:, :], penM_s_d, [pe_])
                POOL(lambda e, pe_=pe_: e.tensor_copy(out=penM[:, :], in_=pe_[:, :]), [pe_], [penM])
                cur["set"] = 0
                ingest(0)
                for s_ in range(NSAMP):
                    cur["set"] = s_ % 2
                    cur["j"] = 1 + s_
                    projscores(16, smp=s_)
                    for h in range(8):
                        pr, hb = h // 2, h % 2
                        stq("sto", sts_o[s_, h, :, :], Sst[hb * 64:(hb + 1) * 64, pr, hb * 64:(hb + 1) * 64], [Sst])
                    scores(16)
                    if s_ + 1 < NSAMP:
                        cur["set"] = (s_ + 1) % 2
                        ingest(s_ + 1)
                        cur["set"] = s_ % 2
                    topk(16)
                    ld("g5l", gate_bc[:, :], gate5_d[1 + s_:2 + s_, :].partition_broadcast(128), [gate_bc], [gate5_b])
                    attention(16, smp=s_)
        except _Stop:
            st2.close()
        except _Stop2:
            for h in range(8):
                pr, hb = h // 2, h % 2
                stq("sto", st_o[h, :, :], Sst[hb * 64:(hb + 1) * 64, pr, hb * 64:(hb + 1) * 64], [Sst])
        S.finish()
        S.emit()
    return nc


def _consts():
    half = 32
    inv = (10000.0 ** (-np.arange(half, dtype=np.float32) / half)).astype(np.float32)
    pos = np.arange(SEQ, dtype=np.float32)
    ang = (pos[:, None] * inv[None, :]).astype(np.float32)
    cs = np.concatenate([np.cos(ang), np.sin(ang)], axis=1).astype(np.float32)
    lg = np.log1p(-np.exp2(-5.0 - np.arange(8, dtype=np.float64)))
    i = np.arange(128, dtype=np.float64)
    kdec = (0.125 * np.exp(lg[None, :] * (127.0 - i)[:, None])).astype(np.float32)
    qdec = np.exp(lg[None, :] * (i[:, None] + 1.0)).astype(np.float32)
    dt = np.zeros((128, 8, 128), np.float64)
    for h in range(8):
        v = 0.125 * np.exp(-lg[h] * (i + 1.0))
        dt[:, h, :] = v[:, None] * (i[None, :] >= i[:, None])
    g128 = np.zeros((128, 4, 128), np.float32)
    for pr in range(4):
        for hb in range(2):
            g128[hb * 64:(hb + 1) * 64, pr, :] = np.exp(lg[2 * pr + hb] * 128.0)
    return cs, kdec, qdec, dt.astype(np.float32), g128


def _consts_sample():
    half = 32
    inv = (10000.0 ** (-np.arange(half, dtype=np.float32) / half)).astype(np.float32)
    tt_ = np.minimum(np.arange(128), 3)
    pos = (8192 + tt_).astype(np.float32)
    ang = (pos[:, None] * inv[None, :]).astype(np.float32)
    cs_s = np.concatenate([np.cos(ang), np.sin(ang)], axis=1).astype(np.float32)
    lg = np.log1p(-np.exp2(-5.0 - np.arange(8, dtype=np.float64)))
    p = np.arange(128, dtype=np.float64)
    real = (p < 4)
    kdec_s = (0.125 * np.exp(lg[None, :] * (3.0 - p)[:, None]) * real[:, None]).astype(np.float32)
    qdec_s = np.exp(lg[None, :] * (tt_.astype(np.float64)[:, None] + 1.0)).astype(np.float32)
    dt = np.zeros((128, 8, 128), np.float64)
    for h in range(8):
        v = 0.125 * np.exp(-lg[h] * (p + 1.0)) * real
        dt[:, h, :] = v[:, None] * (tt_[None, :] >= p[:, None])
    g4 = np.zeros((128, 4, 128), np.float32)
    for pr in range(4):
        for hb in range(2):
            g4[hb * 64:(hb + 1) * 64, pr, :] = np.exp(lg[2 * pr + hb] * 4.0)
    j = np.arange(512)[None, :]
    adm = j <= tt_[:, None]
    penS_s = np.where(adm, 0.0, -1.0e30).astype(np.float32)
    penM_s = np.where(adm, 0.0, -30000.0).astype(np.float32)
    return cs_s, kdec_s, qdec_s, dt.astype(np.float32), g4, penS_s, penM_s


def kernel(x_prompt, x_sample, cache_k, cache_v, cache_kidx, state_ret, page_table, c_prompt, c_sample,
           w_ada, b_ada, g_norm, w_in, g_ret, w_pa, w_pr, w_out, g_final):
    f = lambda a: np.ascontiguousarray(np.asarray(a, dtype=np.float32))
    x_prompt = f(x_prompt); w_in0 = f(w_in)[0]
    cols = np.concatenate([
        np.arange(512, 1024), np.arange(1024, 1536), np.arange(2884, 3396), np.arange(3396, 3908),
        np.arange(2304, 2368), np.arange(2304, 2368),
        np.arange(0, 512),
        np.concatenate([np.concatenate([np.arange(2048 + 64 * h, 2048 + 64 * (h + 1))] * 2) for h in range(4)]),
        np.arange(2304, 2372), np.arange(1536, 2048), np.arange(2372, 2884), np.arange(3908, 4420),
        np.arange(4420, 5444), np.arange(5444, 6468)])
    assert cols.shape[0] == NEXT
    ptile = lambda w: np.ascontiguousarray(w.reshape(w.shape[0] // 128, 128, w.shape[1]).transpose(1, 0, 2))
    wext = ptile(w_in0[:, cols])
    wpo = ptile(np.concatenate([f(w_pa)[0], f(w_pr)[0], f(w_out)[0]], axis=0))
    wada = ptile(f(w_ada)[0])
    badaT = np.ascontiguousarray(f(b_ada)[0].reshape(24, 128).T)
    bgate = np.ascontiguousarray(np.broadcast_to(f(b_ada)[0][2048:3072], (128, 1024)))
    gnT = np.ascontiguousarray(f(g_norm)[0].reshape(8, 128).T)
    gfin = np.ascontiguousarray(np.broadcast_to(f(g_final), (128, 1024)))
    gret = np.ascontiguousarray(np.broadcast_to(f(g_ret)[0], (128, 512)))
    cs, kdec, qdec, dtab, g128 = _consts()
    cs_s, kdec_s, qdec_s, dtab_s, g4, penS_s, penM_s = _consts_sample()
    x_sample = f(x_sample); state_ret = f(state_ret)
    ck2 = f(cache_k)[0].reshape(2560 * 128, 512); cv2 = f(cache_v)[0].reshape(2560 * 128, 512)
    cki2 = f(cache_kidx)[0].reshape(2560 * 128, 64)
    page_table = np.asarray(page_table).astype(np.int32)
    xzero = np.zeros((128, D), np.float32)
    sel_s = np.zeros((128, 4), np.float32); sel_s[:, 0] = 1.0
    pidx = np.arange(128, dtype=np.float32).reshape(128, 1)
    ident = np.eye(128, dtype=np.float32)
    iota = np.ascontiguousarray(np.broadcast_to(np.arange(512, dtype=np.float32), (128, 512)))
    c_prompt = f(c_prompt); c_sample = f(c_sample)
    in_maps = []
    for c in range(8):
        b, r = c // 4, c % 4
        own_tiles = np.array([4 * g + r for g in range(NSTEP)])
        rows = (own_tiles[:, None] * 128 + np.arange(128)[None, :]).reshape(-1)
        cvec = np.stack([c_prompt[b]] + [c_sample[4 * c + s] for s in range(4)], axis=1)
        cT = np.ascontiguousarray(cvec.reshape(8, 128, 5).transpose(1, 0, 2))
        sel = np.zeros((128, 4), np.float32); sel[:, r] = 1.0
        xsp = np.zeros((4, 128, D), np.float32)
        for s_ in range(4):
            xsp[s_, 0:4] = x_sample[4 * c + s_]
        qrel = (r * 128 + np.arange(128, dtype=np.float32)).reshape(128, 1)
        in_maps.append({
            "xall": x_prompt[b], "xown": np.ascontiguousarray(x_prompt[b][rows]),
            "wext": wext, "wpo": wpo, "wada": wada, "badaT": badaT, "bgate": bgate, "gnT": gnT, "gfin": gfin, "gret": gret,
            "cT": cT, "ident": ident, "cs_all": cs, "cs_own": np.ascontiguousarray(cs[rows]), "kdec": kdec, "qdec": qdec,
            "dtab": dtab, "g128": g128, "sel": sel, "iota512": iota, "qrel": qrel,
            "xsp": xsp, "xzero": xzero, "cache_k2": ck2, "cache_v2": cv2, "cache_ki2": cki2,
            "ptab": np.ascontiguousarray(page_table[4 * c:4 * c + 4]), "state_in": np.ascontiguousarray(state_ret[0, 4 * c:4 * c + 4]),
            "cs_s": cs_s, "kdec_s": kdec_s, "qdec_s": qdec_s, "dtab_s": dtab_s, "g4": g4, "sel_s": sel_s,
            "penS_s": penS_s, "penM_s": penM_s, "pidx": pidx,
        })
    nc = build_nc()
    resr = run_bass_kernel_spmd(nc, in_maps, core_ids=list(range(8)))
    R = resr.results
    y_prompt = np.zeros((2, SEQ, D), np.float32)
    k_rows = np.zeros((1, 2, SEQ, 8, 64), np.float32); v_rows = np.zeros((1, 2, SEQ, 8, 64), np.float32)
    ki_rows = np.zeros((1, 2, SEQ, 64), np.float32); st_p = np.zeros((1, 2, 8, 64, 64), np.float32)
    for c in range(8):
        b, r = c // 4, c % 4
        for g in range(NSTEP):
            t = 4 * g + r
            sl_ = slice(t * 128, (t + 1) * 128); so = slice(g * 128, (g + 1) * 128)
            y_prompt[b, sl_] = R[c]["y_o"][so]
            k_rows[0, b, sl_] = R[c]["k_o"][so].reshape(128, 8, 64)
            v_rows[0, b, sl_] = R[c]["v_o"][so].reshape(128, 8, 64)
            ki_rows[0, b, sl_] = R[c]["ki_o"][so]
        if r == 3:
            st_p[0, b] = R[c]["st_o"]
    y_sample = np.zeros((32, 4, D), np.float32)
    ks = np.zeros((1, 32, 4, 8, 64), np.float32); vs = np.zeros((1, 32, 4, 8, 64), np.float32)
    kis = np.zeros((1, 32, 4, 64), np.float32); sts = np.zeros((1, 32, 8, 64, 64), np.float32)
    for c in range(8):
        for s_ in range(4):
            bi = 4 * c + s_
            y_sample[bi] = R[c]["ys_o"][s_]
            ks[0, bi] = R[c]["ks_o"][s_].reshape(4, 8, 64); vs[0, bi] = R[c]["vs_o"][s_].reshape(4, 8, 64)
            kis[0, bi] = R[c]["kis_o"][s_]; sts[0, bi] = R[c]["sts_o"][s_]
    return (y_prompt, y_sample, k_rows, v_rows, ki_rows, st_p, ks, vs, kis, sts)
```

```python
import os
import numpy as np
from contextlib import ExitStack
import concourse.bass as bass
import concourse.mybir as mybir
from concourse.bass_utils import run_bass_kernel_spmd

F32 = mybir.dt.float32
BF16 = mybir.dt.bfloat16
AF = mybir.ActivationFunctionType
ALU = mybir.AluOpType
AX = mybir.AxisListType

D = 1024
SEQ = 8192
NSTEP = 16
NSTEPS_RUN = int(os.environ.get("MK_NSTEPS", "16"))
TOPK = 256
NEG_BIG = -3.0e38
C_KA, C_VA, C_KR, C_VR, C_KID = 0, 512, 1024, 1536, 2048
C_QA, C_QID, C_KIWI, C_ZA, C_QR, C_ZR, C_GA, C_GR = 2176, 2688, 3200, 3268, 3780, 4292, 4804, 5828
NEXT = 6852
ENGS = ("pe", "act", "dve", "pool", "sp")


class _Stop(Exception):
    pass


class _Stop2(Exception):
    pass


STOP = int(os.environ.get('MK_STOP', '99'))
PROJ_ONLY = os.environ.get('MK_FULL', '1') != '1'


def stop_at(k):
    if STOP == k:
        raise _Stop()


class Buf:
    __slots__ = ("w", "r")

    def __init__(self):
        self.w = None
        self.r = []


class Tl:
    def __init__(self, t):
        self.t = t
        self.b = Buf()

    def __getitem__(self, k):
        return self.t[k]


def _b(x):
    return x.b if hasattr(x, "b") else x


class Sched:
    def __init__(self, nc, stack):
        self.nc = nc
        self.stack = stack
        self.ops = {e: [] for e in ENGS}
        self.cnt = {e: 0 for e in ENGS}
        self.esem = {e: stack.enter_context(nc.semaphore("prog_" + e)) for e in ENGS if e != "sp"}
        self.dsem = {}
        self.dcnt = {}
        self.waited = {}
        self.same = {"pe": False, "act": True, "dve": True, "pool": True, "sp": False}

    def _collect(self, eng, reads, writes):
        deps = []
        for b in reads:
            if b.w is not None:
                deps.append(b.w)
        for b in writes:
            if b.w is not None:
                deps.append(b.w)
            deps.extend(b.r)
        waits = []
        for d in deps:
            if d[0] == "e":
                if d[1] == eng and not self.same[eng]:
                    continue
                key = (eng, "e", d[1])
                val = d[2]
                sem = self.esem[d[1]]
            else:
                key = (eng, "d", d[1])
                val = self.dcnt[d[1]]
                sem = self.dsem[d[1]]
            if self.waited.get(key, 0) >= val:
                continue
            self.waited[key] = val
            waits.append((sem, val))
        return waits

    def _update(self, d, reads, writes):
        for b in reads:
            b.r.append(d)
            if len(b.r) > 64:
                b.r = b.r[-64:] if False else b.r
        for b in writes:
            b.w = d
            b.r = []

    def op(self, eng, fn, reads=(), writes=()):
        reads = [_b(x) for x in reads]
        writes = [_b(x) for x in writes]
        waits = self._collect(eng, reads, writes)
        self.cnt[eng] += 1
        idx = self.cnt[eng]
        sem = self.esem[eng]

        def thunk(e, waits=waits, fn=fn, sem=sem):
            for ws, wv in waits:
                e.wait_ge(ws, wv)
            fn(e).then_inc(sem, 1)

        self.ops[eng].append(thunk)
        self._update(("e", eng, idx), reads, writes)

    def dma(self, semkey, fn, reads=(), writes=(), q="sp"):
        reads = [_b(x) for x in reads]
        writes = [_b(x) for x in writes]
        if semkey not in self.dsem:
            self.dsem[semkey] = self.stack.enter_context(self.nc.semaphore("d_" + semkey))
            self.dcnt[semkey] = 0
        waits = self._collect(q, reads, writes)
        self.dcnt[semkey] += 16
        sem = self.dsem[semkey]

        def thunk(e, waits=waits, fn=fn, sem=sem):
            for ws, wv in waits:
                e.wait_ge(ws, wv)
            fn(e).then_inc(sem, 16)

        self.ops[q].append(thunk)
        self._update(("d", semkey), reads, writes)

    def barrier(self):
        fin = [(sem, self.dcnt[k]) for k, sem in self.dsem.items() if self.dcnt[k] > 0]
        fin += [(sem, self.cnt[e]) for e, sem in self.esem.items() if self.cnt[e] > 0]
        for q in ENGS:
            def thunk(e, fin=fin):
                for ws, wv in fin:
                    e.wait_ge(ws, wv)
            self.ops[q].append(thunk)

    def finish(self, q="sp"):
        fin = [(sem, self.dcnt[k]) for k, sem in self.dsem.items()]
        fin += [(sem, self.cnt[e]) for e, sem in self.esem.items() if self.cnt[e] > 0]

        def thunk(e, fin=fin):
            for ws, wv in fin:
                e.wait_ge(ws, wv)

        self.ops[q].append(thunk)

    def emit(self):
        nc = self.nc
        ops = self.ops
        with nc.Block() as block:
            @block.tensor
            def _(e):
                for t in ops["pe"]:
                    t(e)

            @block.scalar
            def _(e):
                for t in ops["act"]:
                    t(e)

            @block.vector
            def _(e):
                for t in ops["dve"]:
                    t(e)

            @block.gpsimd
            def _(e):
                for t in ops["pool"]:
                    t(e)

            @block.sync
            def _(e):
                for t in ops["sp"]:
                    t(e)


def build_nc():
    nc = bass.Bass("TRN2", target_bir_lowering=False)
    din = lambda n, s, dt=F32: nc.dram_tensor(n, list(s), dt, kind="ExternalInput").ap()
    dout = lambda n, s, dt=F32: nc.dram_tensor(n, list(s), dt, kind="ExternalOutput").ap()
    dscr = lambda n, s, dt: nc.dram_tensor(n, list(s), dt, kind="Internal").ap()
    xall = din("xall", [SEQ, D]); xown = din("xown", [NSTEP * 128, D])
    wext = din("wext", [128, 8, NEXT]); wpo = din("wpo", [128, 16, 1024]); wada = din("wada", [128, 8, 3072])
    badaT = din("badaT", [128, 24]); bgate = din("bgate", [128, 1024]); gnT = din("gnT", [128, 8])
    gfin = din("gfin", [128, 1024]); gret = din("gret", [128, 512]); cT = din("cT", [128, 8, 5])
    ident_d = din("ident", [128, 128]); cs_all = din("cs_all", [SEQ, 64]); cs_own = din("cs_own", [NSTEP * 128, 64])
    kdec_d = din("kdec", [128, 8]); qdec_d = din("qdec", [128, 8]); dtab_d = din("dtab", [128, 8, 128])
    g128_d = din("g128", [128, 4, 128]); sel_d = din("sel", [128, 4]); iota_d = din("iota512", [128, 512])
    qrel_d = din("qrel", [128, 1])
    I32 = mybir.dt.int32
    xsp = din("xsp", [4, 128, D]); xzero = din("xzero", [128, D])
    cache_k_d = din("cache_k2", [2560 * 128, 512]); cache_v_d = din("cache_v2", [2560 * 128, 512]); cache_ki_d = din("cache_ki2", [2560 * 128, 64])
    ptab = din("ptab", [4, 64], I32); state_in = din("state_in", [4, 8, 64, 64])
    cs_s_d = din("cs_s", [128, 64]); kdec_s_d = din("kdec_s", [128, 8]); qdec_s_d = din("qdec_s", [128, 8])
    dtab_s_d = din("dtab_s", [128, 8, 128]); g4_d = din("g4", [128, 4, 128]); sel_s_d = din("sel_s", [128, 4])
    penS_s_d = din("penS_s", [128, 512]); penM_s_d = din("penM_s", [128, 512]); pidx_d = din("pidx", [128, 1])
    ys_o = dout("ys_o", [4, 4, D]); ks_o = dout("ks_o", [4, 4, 512]); vs_o = dout("vs_o", [4, 4, 512])
    kis_o = dout("kis_o", [4, 4, 64]); sts_o = dout("sts_o", [4, 8, 64, 64])
    gate5_d = dscr("gate5_d", [5, D], F32)
    y_o = dout("y_o", [NSTEP * 128, D]); k_o = dout("k_o", [NSTEP * 128, 512]); v_o = dout("v_o", [NSTEP * 128, 512])
    ki_o = dout("ki_o", [NSTEP * 128, 64]); st_o = dout("st_o", [8, 64, 64])
    wbf = dscr("wbf", [128, 8, NEXT], BF16); wpobf = dscr("wpobf", [128, 16, 1024], BF16)
    kscrs = [dscr("kscr%d" % i, [128, 4, SEQ + 512], BF16) for i in range(2)]; vscrs = [dscr("vscr%d" % i, [SEQ // 128 + 4, 128, 520], BF16) for i in range(2)]

    with ExitStack() as st:
        S = Sched(nc, st)
        sb = lambda n, s, dt=F32: Tl(st.enter_context(nc.sbuf_tensor("s_" + n, list(s), dt)))
        psb = lambda n, s, dt=F32: Tl(st.enter_context(nc.psum_tensor("p_" + n, list(s), dt)))
        ACT = lambda fn, r=(), w=(): S.op("act", fn, r, w)
        POOL = lambda fn, r=(), w=(): S.op("pool", fn, r, w)
        DVE = lambda fn, r=(), w=(): S.op("dve", fn, r, w)
        PE = lambda fn, r=(), w=(): S.op("pe", fn, r, w)

        G = [psb("G%d" % i, [128, 512]) for i in range(2)]
        A = [psb("A%d" % i, [128, 512]) for i in range(4)]
        TP = psb("TP", [128, 8, 128], BF16)
        XP = psb("XP", [128, 512])
        gi = [0]

        def nextG():
            gi[0] ^= 1
            return G[gi[0]]

        kscr_bs = [[Buf() for _ in range(NSTEP + 1)] for _ in range(2)]; vscr_bs = [[Buf() for _ in range(NSTEP + 1)] for _ in range(2)]
        wbf_b = Buf(); wpobf_b = Buf()
        kiT = sb("kiT", [128, 4608], BF16)
        wsl = [sb("wsl%d" % i, [128, 4096], BF16) for i in range(2)]
        wi_ = [0]
        hT = sb("hT", [128, 8, 640], BF16)
        xo = [sb("xo%d" % i, [128, D]) for i in range(2)]
        xn = sb("xn", [128, D], BF16); sqj = xn
        ssq = sb("ssq", [128, 4]); epsT = sb("epsT", [128, 1])
        kst = sb("kst", [128, 4, 512], BF16); vst = sb("vst", [128, 4, 8, 65], BF16)
        ko = sb("ko", [128, 512]); vo = sb("vo", [128, 512]); kio = sb("kio", [128, 68])
        ev = [sb("ev%d" % i, [128, 512]) for i in range(2)]
        evi = [0]
        rt = [sb("rt%d" % i, [128, 8, 32]) for i in range(4)]
        krd = sb("krd", [128, 4, 512], BF16); vrb = sb("vrb", [128, 5, 512], BF16)
        krown = sb("krown", [128, 512], BF16); KrT = sb("KrT", [128, 4, 128], BF16)
        qp = sb("qp", [128, 512], BF16); QpT = sb("QpT", [128, 8, 128], BF16)
        Sst = sb("Sst", [128, 4, 128]); ownS = sb("ownS", [128, 4, 128])
        ownSb = sb("ownSb", [128, 8, 64], BF16)
        QT = [sb("QT%d" % i, [128, 8, 128], BF16) for i in range(2)]
        qiT2 = [sb("qiTlo", [128, 4, 128], BF16), sb("qiThi", [128, 4, 128], BF16)]
        wsc = [sb("wsc%d" % i, [128, 4]) for i in range(2)]
        siluza = [sb("siluza%d" % i, [128, 512], BF16) for i in range(2)]
        sga = [sb("sga%d" % i, [128, 1024], BF16) for i in range(2)]
        sgr = sb("sgr", [128, 1024], BF16)
        mr = [sb("mr%d" % i, [128, 1024], BF16) for i in range(2)]
        siluzr = sb("siluzr", [128, 512], BF16)
        rl = [sb("rl%d" % i, [128, 512]) for i in range(2)]
        rli = [0]
        eti = [0]
        dtab = sb("dtab", [128, 8, 128]); g128 = sb("g128", [128, 4, 128])
        cs5 = sb("cs5", [128, 5, 64])
        identf = sb("identf", [128, 128]); identb = sb("identb", [128, 128], BF16)
        ident4 = sb("ident4", [128, 4, 128], BF16); zl = sb("zl", [128, 128], BF16)
        ATb = sb("ATb", [128, 8, 128], BF16)
        oret = sb("oret", [128, 8, 64]); osq = sb("osq", [128, 8, 64])
        gst = sb("gst", [128, 8, 8])
        yrb = sb("yrb", [128, 512], BF16); yrT = sb("yrT", [128, 4, 128], BF16)
        acc = sb("acc", [128, 8, 65]); rden = sb("rden", [128, 8, 2])
        yaf = oret; yab = sb("yab", [128, 512], BF16); yaT = sb("yaT", [128, 4, 128], BF16)
        mg = sb("mg", [128, 1024], BF16); mT = sb("mT", [128, 8, 128], BF16)
        res = sb("res", [128, D]); yout = res; xs = [res] * 2
        gate_bc = sb("gate_bc", [128, D]); gfin_bc = sb("gfin_bc", [128, D]); gret_bc = sb("gret_bc", [128, 512])
        penS = sb("penS", [128, 512]); penM = sb("penM", [128, 512], BF16)
        kdec = sb("kdec", [128, 8]); qdec = sb("qdec", [128, 8]); sel = sb("sel", [128, 4])
        AcA = sb("AcA", [128, 5, 8]); BcA = sb("BcA", [128, 5, 8]); cur = {"j": 0, "set": 0}
        ptb = sb("ptb", [128, 64], I32); idxa = sb("idxa", [128, 64], I32); pidx = sb("pidx", [128, 1])
        m8 = sb("m8", [128, 8])

        def ld(key, dst_ap, src_ap, w, r=()):
            S.dma(key, lambda e: e.dma_start(out=dst_ap, in_=src_ap), reads=r, writes=w)

        def stq(key, dst_ap, src_ap, r, w=()):
            S.dma(key, lambda e: e.dma_start(out=dst_ap, in_=src_ap), reads=r, writes=w)

        def mm(out, lhsT, rhs, start, stop, r, w):
            PE(lambda e: e.matmul(out=out, lhsT=lhsT, rhs=rhs, start=start, stop=stop), r, w)

        def act(out, in_, func, r, w, **kw):
            ACT(lambda e: e.activation(out=out, in_=in_, func=func, **kw), r, w)

        def tt(eng, out, in0, in1, op, r, w):
            S.op(eng, lambda e: e.tensor_tensor(out=out, in0=in0, in1=in1, op=op), r, w)

        def ts(eng, out, in0, s1, s2, op0, op1, r, w):
            if op1 is None:
                S.op(eng, lambda e: e.tensor_scalar(out=out, in0=in0, scalar1=s1, scalar2=None, op0=op0), r, w)
            else:
                S.op(eng, lambda e: e.tensor_scalar(out=out, in0=in0, scalar1=s1, scalar2=s2, op0=op0, op1=op1), r, w)

        def stt(eng, out, in0, scalar, in1, op0, op1, r, w):
            S.op(eng, lambda e: e.scalar_tensor_tensor(out=out, in0=in0, scalar=scalar, in1=in1, op0=op0, op1=op1), r, w)

        def load_w(col0, ncols):
            sl = wsl[wi_[0] % 2]; wi_[0] += 1
            v = sl.t[:, 0:8 * ncols].rearrange("p (k c) -> p k c", k=8)
            ld("w%d" % ((wi_[0] - 1) % 2), v, wbf[:, :, col0:col0 + ncols], [sl], [wbf_b])
            return sl, v

        def load_wpo(kt0, nk, c0, ncols):
            sl = wsl[wi_[0] % 2]; wi_[0] += 1
            v = sl.t[:, 0:nk * ncols].rearrange("p (k c) -> p k c", k=nk)
            ld("w%d" % ((wi_[0] - 1) % 2), v, wpobf[:, kt0:kt0 + nk, c0:c0 + ncols], [sl], [wpobf_b])
            return sl, v

        def rstd_chain(src_tl, src_ap, n, col):
            act(sqj[:, 0:n], src_ap, AF.Square, [src_tl], [sqj, ssq], accum_out=ssq[:, 3:4])
            act(ssq[:, 2:3], ssq[:, 3:4], AF.Ln, [ssq], [ssq], scale=1.0 / n, bias=epsT[:, 0:1])
            act(ssq[:, col:col + 1], ssq[:, 2:3], AF.Exp, [ssq], [ssq], scale=-0.5)

        def make_hT(xt, col0):
            rstd_chain(xt, xt[:, :], D, 0)
            act(xn[:, :], xt[:, :], AF.Identity, [xt, ssq], [xn], scale=ssq[:, 0:1])
            for kt in range(8):
                PE(lambda e, kt=kt: e.transpose(out=TP[:, kt, :], in_=xn[:, kt * 128:(kt + 1) * 128], identity=identb[:, :]),
                   [xn, identb], [TP])
            for kt in range(8):
                jj = cur["j"]
                act(hT[:, kt, col0:col0 + 128], TP[:, kt, :], AF.Identity, [TP, AcA, BcA], [hT],
                    scale=AcA[:, jj, kt:kt + 1], bias=BcA[:, jj, kt:kt + 1])

        def rotary(src, cs_ap, dst_ap, dst_tl):
            sv = src.t[:, :].rearrange("p (h t f) -> p h t f", h=8, t=2)
            dv = dst_ap.rearrange("p (h t f) -> p h t f", h=8, t=2)
            cosb = cs_ap[:, 0:32].unsqueeze(1).to_broadcast([128, 8, 32])
            sinb = cs_ap[:, 32:64].unsqueeze(1).to_broadcast([128, 8, 32])
            x1 = sv[:, :, 0, :]; x2 = sv[:, :, 1, :]
            tt("pool", rt[0][:, :, :], x1, cosb, ALU.mult, [src, cs5], [rt[0]])
            tt("pool", rt[1][:, :, :], x2, sinb, ALU.mult, [src, cs5], [rt[1]])
            tt("pool", dv[:, :, 0, :], rt[0][:, :, :], rt[1][:, :, :], ALU.subtract, [rt[0], rt[1]], [dst_tl])
            tt("pool", rt[2][:, :, :], x1, sinb, ALU.mult, [src, cs5], [rt[2]])
            tt("pool", rt[3][:, :, :], x2, cosb, ALU.mult, [src, cs5], [rt[3]])
            tt("pool", dv[:, :, 1, :], rt[2][:, :, :], rt[3][:, :, :], ALU.add, [rt[2], rt[3]], [dst_tl])

        def next_ev():
            evi[0] += 1
            return ev[evi[0] % 2]

        try:
            st2 = ExitStack()
            sb2 = lambda n, s_, dt=F32: Tl(st2.enter_context(nc.sbuf_tensor("s_" + n, list(s_), dt)))
            bgate_sb = sb2("bgate_sb", [128, D])
            iota = sb2("iota", [128, 512]); qrel = sb2("qrel", [128, 1])
            cTs = sb2("cTs", [128, 8, 5]); scT = sb2("scT", [128, 8, 5]); gate5 = sb2("gate5", [5, D])
            modT = sb2("modT", [128, 16, 5]); badaTs = sb2("badaTs", [128, 24]); gnTs = sb2("gnTs", [128, 8])
            wadas = [sb2("wadas%d" % i, [128, 8, 256]) for i in range(2)]
            stgf = [sb2("stgf%d" % i, [128, 1024]) for i in range(2)]
            stgb = [sb2("stgb%d" % i, [128, 1024], BF16) for i in range(2)]
            ld("c0", identf[:, :], ident_d, [identf]); ld("c0", dtab[:, :, :], dtab_d, [dtab]); ld("c0", g128[:, :, :], g128_d, [g128])
            ld("c0", kdec[:, :], kdec_d, [kdec]); ld("c0", qdec[:, :], qdec_d, [qdec]); ld("c0", sel[:, :], sel_d, [sel])
            ld("c0", iota[:, :], iota_d, [iota]); ld("c0", qrel[:, :], qrel_d, [qrel]); ld("c0", cTs[:, :, :], cT, [cTs])
            ld("c0", badaTs[:, :], badaT, [badaTs]); ld("c0", gnTs[:, :], gnT, [gnTs]); ld("c0", bgate_sb[:, :], bgate, [bgate_sb])
            ld("c0", gfin_bc[:, :], gfin, [gfin_bc]); ld("c0", gret_bc[:, :], gret, [gret_bc])
            POOL(lambda e: e.tensor_copy(out=identb[:, :], in_=identf[:, :]), [identf], [identb])
            for a4 in range(4):
                POOL(lambda e, a4=a4: e.tensor_copy(out=ident4[:, a4, :], in_=identf[:, :]), [identf], [ident4])
            POOL(lambda e: e.memset(zl[:, :], 0.0), [], [zl])
            POOL(lambda e: e.memset(epsT[:, :], 1e-6), [], [epsT])
            POOL(lambda e: e.memset(vst[:, :, :, 64:65], 1.0), [], [vst])
            POOL(lambda e: e.memset(Sst[:, :, :], 0.0), [], [Sst])
            POOL(lambda e: e.memset(ownSb[:, :, :], 0.0), [], [ownSb])
            POOL(lambda e: e.memset(QpT[:, :, :], 0.0), [], [QpT])
            POOL(lambda e: e.memset(kiT[:, :], 0.0), [], [kiT])
            for q_ in qiT2:
                POOL(lambda e, q_=q_: e.memset(q_[:, :, :], 0.0), [], [q_])
            for q_ in QT:
                POOL(lambda e, q_=q_: e.memset(q_[:, :, :], 0.0), [], [q_])
            ts("pool", penS[:, :], iota[:, :], qrel[:, 0:1], -1.0e30, ALU.is_gt, ALU.mult, [iota, qrel], [penS])
            ts("pool", penM[:, :], iota[:, :], qrel[:, 0:1], -30000.0, ALU.is_gt, ALU.mult, [iota, qrel], [penM])
            act(scT[:, :, :], cTs[:, :, :], AF.Silu, [cTs], [scT])
            for cc in range(12):
                wa = wadas[cc % 2]
                ld("wa%d" % (cc % 2), wa[:, :, :], wada[:, :, cc * 256:(cc + 1) * 256], [wa])
                if cc < 8:
                    for t2 in range(2):
                        ct = cc * 2 + t2
                        for kt in range(8):
                            mm(XP[:, ct * 5:ct * 5 + 5], wa[:, kt, t2 * 128:(t2 + 1) * 128], scT[:, kt, :], kt == 0, kt == 7,
                               [wa, scT], [XP])
                else:
                    gch = cc - 8
                    g_ = G[gch // 2]
                    for kt in range(8):
                        mm(g_[0:5, (gch % 2) * 256:(gch % 2) * 256 + 256], scT[:, kt, :], wa[:, kt, :], kt == 0, kt == 7,
                           [wa, scT], [g_])
            act(modT[:, :, :], XP[:, 0:80].rearrange("p (c j) -> p c j", j=5), AF.Identity, [XP], [modT])
            for j in range(5):
                tt("pool", BcA[:, j, :], modT[:, 0:8, j], badaTs[:, 0:8], ALU.add, [modT, badaTs], [BcA])
                ts("pool", AcA[:, j, :], modT[:, 8:16, j], 1.0, None, ALU.add, None, [modT], [AcA])
                tt("pool", AcA[:, j, :], AcA[:, j, :], badaTs[:, 8:16], ALU.add, [AcA, badaTs], [AcA])
                tt("pool", AcA[:, j, :], AcA[:, j, :], gnTs[:, :], ALU.mult, [AcA, gnTs], [AcA])
            for hf in range(2):
                act(gate5[:, hf * 512:(hf + 1) * 512], G[hf][0:5, :], AF.Identity, [G[hf]], [gate5])
            tt("pool", gate5[:, :], gate5[:, :], bgate_sb[0:5, :], ALU.add, [gate5, bgate_sb], [gate5])
            gate5_b = Buf()
            stq("g5", gate5_d, gate5[:, :], [gate5], [gate5_b])
            ld("g5l", gate_bc[:, :], gate5_d[0:1, :].partition_broadcast(128), [gate_bc], [gate5_b])
            ld("c0", pidx[:, :], pidx_d, [pidx])
            stop_at(1)
            nch = (NEXT + 127) // 128
            for c in range(nch):
                c0 = c * 128; n = min(128, NEXT - c0); i = c % 2
                fv = stgf[i].t[:, 0:8 * n].rearrange("p (k c) -> p k c", k=8)
                bv = stgb[i].t[:, 0:8 * n].rearrange("p (k c) -> p k c", k=8)
                ld("ps%d" % i, fv, wext[:, :, c0:c0 + n], [stgf[i]])
                if c % 2 == 0:
                    DVE(lambda e, fv=fv, bv=bv: e.tensor_copy(out=bv, in_=fv), [stgf[i]], [stgb[i]])
                else:
                    act(bv, fv, AF.Identity, [stgf[i]], [stgb[i]])
                stq("pt%d" % i, wbf[:, :, c0:c0 + n], bv, [stgb[i]], [])
            for c in range(16):
                c0 = c * 64; i = c % 2
                fv = stgf[i].t[:, :].rearrange("p (k c) -> p k c", k=16)
                bv = stgb[i].t[:, :].rearrange("p (k c) -> p k c", k=16)
                ld("ps%d" % i, fv, wpo[:, :, c0:c0 + 64], [stgf[i]])
                if c % 2 == 0:
                    DVE(lambda e, fv=fv, bv=bv: e.tensor_copy(out=bv, in_=fv), [stgf[i]], [stgb[i]])
                else:
                    act(bv, fv, AF.Identity, [stgf[i]], [stgb[i]])
                stq("pt%d" % i, wpobf[:, :, c0:c0 + 64], bv, [stgb[i]], [])

            stop_at(2)
            S.barrier()
            st2.close()
            work0 = sb("work0", [128, SEQ + 512]); wbA = work0.b; wbB = Buf()

            def wkinfo(g):
                if g < 8:
                    return (g % 2) * 4352, ([wbA] if g % 2 == 0 else [wbB])
                return 0, [wbA, wbB]
            nm = [sb("nm%d" % i, [128, 512], BF16) for i in range(2)]
            kc = [sb("kc%d" % i, [128, 4, 512], BF16) for i in range(2)]
            vc = [sb("vc%d" % i, [128, 4, 520], BF16) for i in range(2)]
            ET = [sb("ET%d" % i, [128, 512], BF16) for i in range(2)]

            def projscores(g, smp=None):
                sl = g % 2
                L = (g + 1) * 512
                if smp is None:
                    ld("cs", cs5[:, 0:4, :], cs_all[g * 512:(g + 1) * 512, :].rearrange("(t p) c -> p t c", p=128), [cs5])
                    ld("cs", cs5[:, 4, :], cs_own[g * 128:(g + 1) * 128, :], [cs5])
                else:
                    for j5 in range(5):
                        ld("cs", cs5[:, j5, :], cs_s_d, [cs5])
                for j in range(4 if smp is None else 1):
                    xt = xs[j % 2]
                    if smp is None:
                        ld("x0", xt[:, :], xall[(4 * g + j) * 128:(4 * g + j + 1) * 128, :], [xt])
                    else:
                        ld("x0", xt[:, :], xsp[smp, :, :] if j == 0 else xzero, [xt])
                    make_hT(xt, j * 128)
                ld("xo%d" % sl, xo[sl][:, :], xown[g * 128:(g + 1) * 128, :] if smp is None else xsp[smp, :, :], [xo[sl]])
                make_hT(xo[sl], 512)
                stop_at(31)
                wt, wv = load_w(C_KA, 512)
                for pr in range(4):
                    g_ = nextG()
                    for kt in range(8):
                        mm(g_[:, :], wv[:, kt, pr * 128:(pr + 1) * 128], hT[:, kt, 0:512], kt == 0, kt == 7, [wt, hT], [g_])
                    act(kst[:, pr, :], g_[:, :], AF.Identity, [g_], [kst])
                stq("kst", kscrs[cur["set"]][:, :, g * 512:(g + 1) * 512], kst[:, :, :], [kst], [kscr_bs[cur["set"]][g]])
                g_ = nextG()
                for kt in range(8):
                    mm(g_[:, :], hT[:, kt, 512:640], wv[:, kt, :], kt == 0, kt == 7, [wt, hT], [g_])
                act(ko[:, :], g_[:, :], AF.Identity, [g_], [ko])
                if smp is None:
                    stq("ko", k_o[g * 128:(g + 1) * 128, :], ko[:, :], [ko])
                else:
                    stq("ko", ks_o[smp, :, :], ko[0:4, :], [ko])
                stop_at(32)
                wt, wv = load_w(C_VA, 512)
                for j in (range(5) if smp is None else (0, 4)):
                    g_ = nextG()
                    for kt in range(8):
                        mm(g_[:, :], hT[:, kt, j * 128:(j + 1) * 128], wv[:, kt, :], kt == 0, kt == 7, [wt, hT], [g_])
                    if j < 4:
                        act(vst[:, j, :, 0:64], g_[:, :].rearrange("p (h d) -> p h d", h=8), AF.Identity, [g_], [vst])
                    else:
                        act(vo[:, :], g_[:, :], AF.Identity, [g_], [vo])
                stq("vst", vscrs[cur["set"]][4 * g:4 * g + 4, :, :].rearrange("t p c -> p t c"), vst[:, :, :, :].rearrange("p t h c -> p t (h c)"),
                    [vst], [vscr_bs[cur["set"]][g]])
                if smp is None:
                    stq("vo", v_o[g * 128:(g + 1) * 128, :], vo[:, :], [vo])
                else:
                    stq("vo", vs_o[smp, :, :], vo[0:4, :], [vo])
                stop_at(33)
                wt, wv = load_w(C_KR, 512)
                for j in (range(5) if smp is None else (0, 4)):
                    g_ = nextG()
                    for kt in range(8):
                        mm(g_[:, :], hT[:, kt, j * 128:(j + 1) * 128], wv[:, kt, :], kt == 0, kt == 7, [wt, hT], [g_])
                    e_ = next_ev()
                    if j < 4:
                        for h in range(8):
                            act(e_[:, h * 64:(h + 1) * 64], g_[:, h * 64:(h + 1) * 64], AF.Identity, [g_, kdec], [e_], scale=kdec[:, h:h + 1])
                        rotary(e_, cs5[:, j, :], krd[:, j, :], krd)
                    else:
                        act(e_[:, :], g_[:, :], AF.Identity, [g_], [e_])
                        rotary(e_, cs5[:, 4, :], krown[:, :], krown)
                stop_at(34)
                wt, wv = load_w(C_VR, 512)
                for j in (range(5) if smp is None else (0, 4)):
                    g_ = nextG()
                    for kt in range(8):
                        mm(g_[:, :], hT[:, kt, j * 128:(j + 1) * 128], wv[:, kt, :], kt == 0, kt == 7, [wt, hT], [g_])
                    act(vrb[:, j, :], g_[:, :], AF.Identity, [g_], [vrb])
                stop_at(35)
                for j in range(4 if smp is None else 1):
                    if j == 0:
                        ts("pool", ownS[:, :, :], Sst[:, :, :], sel[:, 0:1], None, ALU.mult, None, [Sst, sel], [ownS])
                    else:
                        tS = next_ev()
                        tSv = tS.t[:, :].rearrange("p (a b) -> p a b", a=4)
                        ts("pool", tSv, Sst[:, :, :], sel[:, j:j + 1], None, ALU.mult, None, [Sst, sel], [tS])
                        tt("pool", ownS[:, :, :], ownS[:, :, :], tSv, ALU.add, [ownS, tS], [ownS])
                    for pr in range(4):
                        mm(XP[:, pr * 128:(pr + 1) * 128], krd[:, j, pr * 128:(pr + 1) * 128], vrb[:, j, pr * 128:(pr + 1) * 128], True, True,
                           [krd, vrb], [XP])
                    tmpS = next_ev()
                    act(tmpS[:, :], XP[:, :], AF.Identity, [XP], [tmpS])
                    tt("pool", Sst[:, :, :], Sst[:, :, :], g128[:, :, :], ALU.mult, [Sst, g128], [Sst])
                    tt("pool", Sst[:, :, :], Sst[:, :, :], tmpS[:, :].rearrange("p (a b) -> p a b", a=4), ALU.add, [Sst, tmpS], [Sst])
                for hb in range(2):
                    POOL(lambda e, hb=hb: e.tensor_copy(out=ownSb[hb * 64:(hb + 1) * 64, hb::2, :],
                                                        in_=ownS[hb * 64:(hb + 1) * 64, :, hb * 64:(hb + 1) * 64]), [ownS], [ownSb])
                stop_at(36)
                wt, wv = load_w(C_KID, 512)
                g_ = nextG()
                for kt in range(8):
                    mm(g_[:, :], wv[:, kt, 0:128], hT[:, kt, 0:512], kt == 0, kt == 7, [wt, hT], [g_])
                pb = g // 9
                cc = g % 9
                if os.environ.get('MK_V', '') != 'noki':
                    act(kiT[pb * 64:(pb + 1) * 64, cc * 512:(cc + 1) * 512], g_[pb * 64:(pb + 1) * 64, :], AF.Identity, [g_], [kiT])
                stop_at(37)
                wt, wv = load_w(C_QA, 512)
                g_ = nextG()
                for pr in range(4):
                    for kt in range(8):
                        mm(g_[:, pr * 128:(pr + 1) * 128], wv[:, kt, pr * 128:(pr + 1) * 128], hT[:, kt, 512:640], kt == 0, kt == 7, [wt, hT], [g_])
                for hb in range(2):
                    act(QT[sl][hb * 64:(hb + 1) * 64, hb::2, :], g_[hb * 64:(hb + 1) * 64, :].rearrange("p (a q) -> p a q", a=4), AF.Identity,
                        [g_], [QT[sl]])
                wt, wv = load_w(C_QID, 512)
                g_ = nextG()
                for h in range(4):
                    for kt in range(8):
                        mm(g_[:, h * 128:(h + 1) * 128], wv[:, kt, h * 128:(h + 1) * 128], hT[:, kt, 512:640], kt == 0, kt == 7, [wt, hT], [g_])
                for hb in range(2):
                    act(qiT2[hb][hb * 64:(hb + 1) * 64, :, :], g_[hb * 64:(hb + 1) * 64, :].rearrange("p (a q) -> p a q", a=4), AF.Identity,
                        [g_], [qiT2[hb]])
                wt, wv = load_w(C_KIWI, 512)
                g_ = nextG()
                for kt in range(8):
                    mm(g_[:, 0:68], hT[:, kt, 512:640], wv[:, kt, 0:68], kt == 0, kt == 7, [wt, hT], [g_])
                act(kio[:, :], g_[:, 0:68], AF.Identity, [g_], [kio])
                if smp is None:
                    stq("kio", ki_o[g * 128:(g + 1) * 128, :], kio[:, 0:64], [kio])
                else:
                    stq("kio", kis_o[smp, :, :], kio[0:4, 0:64], [kio])
                ts("pool", wsc[sl][:, :], kio[:, 64:68], 1.0 / 16.0, None, ALU.mult, None, [kio], [wsc[sl]])
                if PROJ_ONLY:
                    return
                wt, wv = load_w(C_ZA, 512)
                g_ = nextG()
                for kt in range(8):
                    mm(g_[:, :], hT[:, kt, 512:640], wv[:, kt, :], kt == 0, kt == 7, [wt, hT], [g_])
                act(siluza[sl][:, :], g_[:, :], AF.Silu, [g_], [siluza[sl]])
                wt, wv = load_w(C_QR, 512)
                g_ = nextG()
                for kt in range(8):
                    mm(g_[:, :], hT[:, kt, 512:640], wv[:, kt, :], kt == 0, kt == 7, [wt, hT], [g_])
                e_ = next_ev()
                for h in range(8):
                    act(e_[:, h * 64:(h + 1) * 64], g_[:, h * 64:(h + 1) * 64], AF.Identity, [g_, qdec], [e_], scale=qdec[:, h:h + 1])
                rotary(e_, cs5[:, 4, :], qp[:, :], qp)
                for pr in range(4):
                    PE(lambda e, pr=pr: e.transpose(out=TP[:, pr, :], in_=qp[:, pr * 128:(pr + 1) * 128], identity=identb[:, :]), [qp, identb], [TP])
                for hb in range(2):
                    act(QpT[hb * 64:(hb + 1) * 64, hb::2, :], TP[hb * 64:(hb + 1) * 64, 0:4, :], AF.Identity, [TP], [QpT])
                for pr in range(4):
                    PE(lambda e, pr=pr: e.transpose(out=TP[:, 4 + pr, :], in_=krown[:, pr * 128:(pr + 1) * 128], identity=identb[:, :]), [krown, identb], [TP])
                act(KrT[:, :, :], TP[:, 4:8, :], AF.Identity, [TP], [KrT])
                wt, wv = load_w(C_ZR, 512)
                g_ = nextG()
                for kt in range(8):
                    mm(g_[:, :], hT[:, kt, 512:640], wv[:, kt, :], kt == 0, kt == 7, [wt, hT], [g_])
                act(siluzr[:, :], g_[:, :], AF.Silu, [g_], [siluzr])
                for ci, (c0, dst) in enumerate([(C_GA, sga[sl]), (C_GA + 512, sga[sl]), (C_GR, sgr), (C_GR + 512, sgr)]):
                    wt, wv = load_w(c0, 512)
                    g_ = nextG()
                    for kt in range(8):
                        mm(g_[:, :], hT[:, kt, 512:640], wv[:, kt, :], kt == 0, kt == 7, [wt, hT], [g_])
                    act(dst[:, (ci % 2) * 512:(ci % 2 + 1) * 512], g_[:, :], AF.Sigmoid, [g_], [dst])
                stop_at(38)
                for h in range(8):
                    pr, hb = h // 2, h % 2
                    a_ = A[h // 4]
                    mm(a_[:, (h % 4) * 128:(h % 4 + 1) * 128], KrT[:, pr, :], QpT[:, h, :], True, True,
                       [KrT, QpT], [a_])
                stop_at(381)
                for hf in range(2):
                    e_ = next_ev()
                    act(e_[:, :], A[hf][:, :], AF.Identity, [A[hf]], [e_])
                    tt("pool", ATb[:, hf * 4:(hf + 1) * 4, :], e_[:, :].rearrange("p (a q) -> p a q", a=4), dtab[:, hf * 4:(hf + 1) * 4, :], ALU.mult,
                       [e_, dtab], [ATb])
                stop_at(382)
                for h in range(8):
                    pr, hb = h // 2, h % 2
                    mm(A[2][:, h * 64:(h + 1) * 64], ATb[:, h, :], vrb[:, 4, h * 64:(h + 1) * 64], True, False, [ATb, vrb], [A[2]])
                    mm(A[2][:, h * 64:(h + 1) * 64], QpT[:, h, :], ownSb[:, h, :], False, True,
                       [QpT, ownSb], [A[2]])
                stop_at(383)
                act(oret[:, :, :], A[2][:, :].rearrange("p (h d) -> p h d", h=8), AF.Identity, [A[2]], [oret])
                stop_at(39)
                DVE(lambda e: e.tensor_reduce(out=gst[:, :, 0], in_=oret[:, :, :], axis=AX.X, op=ALU.add), [oret], [gst])
                tt("pool", osq[:, :, :], oret[:, :, :], oret[:, :, :], ALU.mult, [oret], [osq])
                DVE(lambda e: e.tensor_reduce(out=gst[:, :, 1], in_=osq[:, :, :], axis=AX.X, op=ALU.add), [osq], [gst])
                ts("pool", gst[:, :, 2], gst[:, :, 0], 1.0 / 64, None, ALU.mult, None, [gst], [gst])
                tt("pool", gst[:, :, 3], gst[:, :, 2], gst[:, :, 2], ALU.mult, [gst], [gst])
                ts("pool", gst[:, :, 7], gst[:, :, 1], 1.0 / 64, None, ALU.mult, None, [gst], [gst])
                tt("pool", gst[:, :, 4], gst[:, :, 7], gst[:, :, 3], ALU.subtract, [gst], [gst])
                act(gst[:, :, 5], gst[:, :, 4], AF.Ln, [gst], [gst], bias=epsT[:, 0:1])
                act(gst[:, :, 6], gst[:, :, 5], AF.Exp, [gst], [gst], scale=-0.5)
                tt("pool", osq[:, :, :], oret[:, :, :], gst[:, :, 2:3].to_broadcast([128, 8, 64]), ALU.subtract, [oret, gst], [osq])
                tt("pool", osq[:, :, :], osq[:, :, :], gst[:, :, 6:7].to_broadcast([128, 8, 64]), ALU.mult, [osq, gst], [osq])
                osq2 = osq.t[:, :, :].rearrange("p h d -> p (h d)")
                tt("pool", osq2, osq2, gret_bc[:, :], ALU.mult, [osq, gret_bc], [osq])
                tt("pool", yrb[:, :], osq2, siluzr[:, :], ALU.mult, [osq, siluzr], [yrb])
                for pr in range(4):
                    PE(lambda e, pr=pr: e.transpose(out=TP[:, pr, :], in_=yrb[:, pr * 128:(pr + 1) * 128], identity=identb[:, :]), [yrb, identb], [TP])
                act(yrT[:, :, :], TP[:, 0:4, :], AF.Identity, [TP], [yrT])
                for hf in range(2):
                    wt, wv = load_wpo(4, 4, hf * 512, 512)
                    g_ = nextG()
                    for kt in range(4):
                        mm(g_[:, :], yrT[:, kt, :], wv[:, kt, :], kt == 0, kt == 3, [wt, yrT], [g_])
                    e_ = next_ev()
                    act(e_[:, :], g_[:, :], AF.Identity, [g_], [e_])
                    tt("pool", mr[sl][:, hf * 512:(hf + 1) * 512], e_[:, :], sgr[:, hf * 512:(hf + 1) * 512], ALU.mult, [e_, sgr], [mr[sl]])
            def scores(g):
                sl = g % 2
                off, wkb = wkinfo(g)
                for c in range(g + 1):
                    pb = c // 9
                    cc = c % 9
                    for h in range(4):
                        g_ = nextG()
                        mm(g_[:, :], qiT2[pb][:, h, :], kiT[:, cc * 512:(cc + 1) * 512], True, True, [qiT2[pb], kiT], [g_])
                        rli[0] += 1
                        r_ = rl[rli[0] % 2]
                        act(r_[:, :], g_[:, :], AF.Relu, [g_], [r_])
                        dst = work0[:, off + c * 512:off + (c + 1) * 512]
                        if h == 0:
                            if c == g:
                                stt("dve", dst, r_[:, :], wsc[sl][:, 0:1], penS[:, :], ALU.mult, ALU.add, [r_, wsc[sl], penS], wkb)
                            else:
                                ts("dve", dst, r_[:, :], wsc[sl][:, 0:1], None, ALU.mult, None, [r_, wsc[sl]], wkb)
                        else:
                            stt("dve", dst, r_[:, :], wsc[sl][:, h:h + 1], dst, ALU.mult, ALU.add, [r_, wsc[sl]] + wkb, wkb)

            def topk(g):
                off, wkb = wkinfo(g)
                L = (g + 1) * 512
                for r in range(TOPK // 8):
                    DVE(lambda e: e.max(out=m8[:, :], in_=work0[:, off:off + L]), wkb, [m8])
                    DVE(lambda e: e.match_replace(out=work0[:, off:off + L], in_to_replace=m8[:, :], in_values=work0[:, off:off + L], imm_value=NEG_BIG), wkb + [m8], wkb)

            def attention(g, smp=None):
                sl = g % 2
                off, wkb = wkinfo(g)
                nkt = 4 * (g + 1)
                for c in range(g + 1):
                    kc_ = kc[c % 2]; vc_ = vc[c % 2]; nm_ = nm[c % 2]
                    ld("kc%d" % (c % 2), kc_[:, :, :], kscrs[cur["set"]][:, :, c * 512:(c + 1) * 512], [kc_], [kscr_bs[cur["set"]][c]])
                    ld("vc%d" % (c % 2), vc_[:, :, :], vscrs[cur["set"]][4 * c:4 * c + 4, :, :].rearrange("t p c -> p t c"), [vc_], [vscr_bs[cur["set"]][c]])
                    ts("dve", nm_[:, :], work0[:, off + c * 512:off + (c + 1) * 512], -1.0e38, -30000.0, ALU.is_gt, ALU.mult, wkb, [nm_])
                    if c == g:
                        tt("dve", nm_[:, :], nm_[:, :], penM[:, :], ALU.add, [nm_, penM], [nm_])
                    if g == 0 and c == 0 and os.environ.get('MK_DBG'):
                        stq("dbg", dbgb[:, 0:512], nm_[:, :], [nm_])
                        stq("dbg", dbgb[:, 1024:1536], kc_[:, 0, :], [kc_])
                    for t in range(4):
                        kt_i = c * 4 + t
                        for hg in range(2):
                            a_ = A[(2 * (kt_i % 2)) + hg]
                            mm(a_[:, :], nm_[:, t * 128:(t + 1) * 128], ident4[:, :, :].rearrange("p a q -> p (a q)"), True, False,
                               [nm_, ident4], [a_])
                            for pp in range(2):
                                pr = hg * 2 + pp
                                mm(a_[:, pp * 256:(pp + 1) * 256], kc_[:, pr, t * 128:(t + 1) * 128],
                                   QT[sl][:, 2 * pr:2 * pr + 2, :].rearrange("p a q -> p (a q)"), False, pp == 1, [kc_, QT[sl]], [a_])
                            eti[0] += 1
                            et = ET[eti[0] % 2]
                            act(et[:, :], a_[:, :], AF.Exp, [a_], [et], scale=0.125)
                            if g == 0 and kt_i == 0 and hg == 0 and os.environ.get('MK_DBG'):
                                stq("dbg", dbgb[:, 512:1024], et[:, :], [et])
                                stq("dbg", dbgb[:, 1536:2048], QT[sl][:, 0:4, :].rearrange("p a q -> p (a q)"), [QT[sl]])
                            for hh in range(4):
                                h = hg * 4 + hh
                                if kt_i == 0 and hh == 0:
                                    mm(G[hg][:, 0:260], zl[:, :], vc_[:, t, 0:260], True, False, [zl, vc_], [G[hg]])
                                mm(G[hg][:, hh * 65:(hh + 1) * 65], et[:, hh * 128:(hh + 1) * 128], vc_[:, t, h * 65:(h + 1) * 65],
                                   False, (kt_i == nkt - 1 and hh == 3), [et, vc_], [G[hg]])
                for hg in range(2):
                    act(acc[:, hg * 4:(hg + 1) * 4, :], G[hg][:, 0:260].rearrange("p (h c) -> p h c", h=4), AF.Identity, [G[hg]], [acc])
                if g == 0 and os.environ.get('MK_DBG'):
                    stq("dbg", dbg[:, 512:1032], acc.t[:, :, :].rearrange("p h c -> p (h c)"), [acc])
                act(rden[:, :, 0], acc[:, :, 64], AF.Ln, [acc], [rden])
                act(rden[:, :, 1], rden[:, :, 0], AF.Exp, [rden], [rden], scale=-1.0)
                tt("pool", yaf[:, :, :], acc[:, :, 0:64], rden[:, :, 1:2].to_broadcast([128, 8, 64]), ALU.mult, [acc, rden], [yaf])
                tt("pool", yab[:, :], yaf.t[:, :, :].rearrange("p h d -> p (h d)"), siluza[sl][:, :], ALU.mult, [yaf, siluza[sl]], [yab])
                for pr in range(4):
                    PE(lambda e, pr=pr: e.transpose(out=TP[:, pr, :], in_=yab[:, pr * 128:(pr + 1) * 128], identity=identb[:, :]), [yab, identb], [TP])
                act(yaT[:, :, :], TP[:, 0:4, :], AF.Identity, [TP], [yaT])
                for hf in range(2):
                    wt, wv = load_wpo(0, 4, hf * 512, 512)
                    g_ = nextG()
                    for kt in range(4):
                        mm(g_[:, :], yaT[:, kt, :], wv[:, kt, :], kt == 0, kt == 3, [wt, yaT], [g_])
                    e_ = next_ev()
                    act(e_[:, :], g_[:, :], AF.Identity, [g_], [e_])
                    tt("pool", e_[:, :], e_[:, :], sga[sl][:, hf * 512:(hf + 1) * 512], ALU.mult, [e_, sga[sl]], [e_])
                    tt("pool", mg[:, hf * 512:(hf + 1) * 512], e_[:, :], mr[sl][:, hf * 512:(hf + 1) * 512], ALU.add, [e_, mr[sl]], [mg])
                for kt in range(8):
                    PE(lambda e, kt=kt: e.transpose(out=TP[:, kt, :], in_=mg[:, kt * 128:(kt + 1) * 128], identity=identb[:, :]), [mg, identb], [TP])
                act(mT[:, :, :], TP[:, :, :], AF.Identity, [TP], [mT])
                for hf in range(2):
                    wt, wv = load_wpo(8, 8, hf * 512, 512)
                    g_ = nextG()
                    for kt in range(8):
                        mm(g_[:, :], mT[:, kt, :], wv[:, kt, :], kt == 0, kt == 7, [wt, mT], [g_])
                    e_ = next_ev()
                    act(e_[:, :], g_[:, :], AF.Identity, [g_], [e_])
                    tt("pool", e_[:, :], e_[:, :], gate_bc[:, hf * 512:(hf + 1) * 512], ALU.mult, [e_, gate_bc], [e_])
                    tt("pool", res[:, hf * 512:(hf + 1) * 512], e_[:, :], xo[sl][:, hf * 512:(hf + 1) * 512], ALU.add, [e_, xo[sl]], [res])
                if g == 0 and os.environ.get('MK_DBG'):
                    stq("dbg", dbg[:, 2048:3072], res[:, :], [res])
                rstd_chain(res, res[:, :], D, 1)
                act(res[:, :], res[:, :], AF.Identity, [res, ssq], [res], scale=ssq[:, 1:2])
                tt("pool", yout[:, :], res[:, :], gfin_bc[:, :], ALU.mult, [res, gfin_bc], [yout])
                if smp is None:
                    stq("yo", y_o[g * 128:(g + 1) * 128, :], yout[:, :], [yout])
                else:
                    stq("yo", ys_o[smp, :, :], yout[0:4, :], [yout])

            def ingest(s_):
                ld("pt", ptb[:, :], ptab[s_:s_ + 1, :].partition_broadcast(128), [ptb])
                ts("pool", idxa[:, :], ptb[:, :], 128.0, pidx[:, 0:1], ALU.mult, ALU.add, [ptb, pidx], [idxa])
                POOL(lambda e: e.memset(Sst[:, :, :], 0.0), [], [Sst])
                for hb in range(2):
                    ld("sti", Sst[hb * 64:(hb + 1) * 64, :, hb * 64:(hb + 1) * 64],
                       state_in[s_, hb::2, :, :].rearrange("h p e -> p h e"), [Sst])
                for pg in range(64):
                    c = pg // 4; t = pg % 4
                    kp = rl[pg % 2]; vp = ev[pg % 2]
                    S.dma("gk%d" % (pg % 2), lambda e, kp=kp, pg=pg: e.indirect_dma_start(
                        out=kp[:, :], out_offset=None, in_=cache_k_d,
                        in_offset=bass.IndirectOffsetOnAxis(ap=idxa[:, pg:pg + 1], axis=0)), reads=[_b(idxa)], writes=[_b(kp)], q="pool")
                    S.dma("gv%d" % (pg % 2), lambda e, vp=vp, pg=pg: e.indirect_dma_start(
                        out=vp[:, :], out_offset=None, in_=cache_v_d,
                        in_offset=bass.IndirectOffsetOnAxis(ap=idxa[:, pg:pg + 1], axis=0)), reads=[_b(idxa)], writes=[_b(vp)], q="pool")
                    S.dma("gi", lambda e, pg=pg: e.indirect_dma_start(
                        out=kio[:, 0:64], out_offset=None, in_=cache_ki_d,
                        in_offset=bass.IndirectOffsetOnAxis(ap=idxa[:, pg:pg + 1], axis=0)), reads=[_b(idxa)], writes=[_b(kio)], q="pool")
                    kb = qp if pg % 2 == 0 else krown
                    act(kb[:, :], kp[:, :], AF.Identity, [kp], [kb])
                    for pr in range(4):
                        PE(lambda e, pr=pr, kb=kb: e.transpose(out=TP[:, pr, :], in_=kb[:, pr * 128:(pr + 1) * 128], identity=identb[:, :]),
                           [kb, identb], [TP])
                    act(kst[:, :, t * 128:(t + 1) * 128], TP[:, 0:4, :], AF.Identity, [TP], [kst])
                    act(vst[:, t, :, 0:64], vp[:, :].rearrange("p (h d) -> p h d", h=8), AF.Identity, [vp], [vst])
                    for dd in range(2):
                        act(yab[:, dd * 64:(dd + 1) * 64], kio[:, 0:64], AF.Identity, [kio], [yab])
                    PE(lambda e: e.transpose(out=TP[:, 4, :], in_=yab[:, 0:128], identity=identb[:, :]), [yab, identb], [TP])
                    pb = c // 9; cc = c % 9
                    act(kiT[pb * 64:(pb + 1) * 64, cc * 512 + t * 128:cc * 512 + (t + 1) * 128], TP[pb * 64:(pb + 1) * 64, 4, :], AF.Identity,
                        [TP], [kiT])
                    if t == 3:
                        stq("kst", kscrs[cur["set"]][:, :, c * 512:(c + 1) * 512], kst[:, :, :], [kst], [kscr_bs[cur["set"]][c]])
                        stq("vst", vscrs[cur["set"]][4 * c:4 * c + 4, :, :].rearrange("t p c -> p t c"), vst[:, :, :, :].rearrange("p t h c -> p t (h c)"),
                            [vst], [vscr_bs[cur["set"]][c]])

            NSAMP = int(os.environ.get("MK_NSAMP", "4"))
            early_ingest = [False]
            n = NSTEPS_RUN
            if PROJ_ONLY:
                for g in range(n):
                    projscores(g)
                raise _Stop2()
            projscores(0)
            stop_at(3)
            scores(0)
            stop_at(4)
            for g in range(n):
                if g == n - 1 and NSAMP > 0 and n == NSTEP:
                    for h in range(8):
                        pr, hb = h // 2, h % 2
                        stq("sto", st_o[h, :, :], Sst[hb * 64:(hb + 1) * 64, pr, hb * 64:(hb + 1) * 64], [Sst])
                    cur["set"] = 1
                    ingest(0)
                    cur["set"] = 0
                    early_ingest[0] = True
                topk(g)
                if g + 1 < n:
                    projscores(g + 1)
                    if g + 1 < 8:
                        scores(g + 1)
                        attention(g)
                    else:
                        attention(g)
                        scores(g + 1)
                else:
                    attention(g)
            for h in range(8 if not early_ingest[0] else 0):
                pr, hb = h // 2, h % 2
                stq("sto", st_o[h, :, :], Sst[hb * 64:(hb + 1) * 64, pr, hb * 64:(hb + 1) * 64], [Sst])

            if NSAMP > 0:
                S.barrier()
                ld("c1", kdec[:, :], kdec_s_d, [kdec]); ld("c1", qdec[:, :], qdec_s_d, [qdec]); ld("c1", dtab[:, :, :], dtab_s_d, [dtab])
                ld("c1", g128[:, :, :], g4_d, [g128]); ld("c1", sel[:, :], sel_s_d, [sel]); ld("c1", penS[:, :], penS_s_d, [penS])
                pe_ = next_ev()
                ld("c1", pe_[:, :], penM_s_d, [pe_])
                POOL(lambda e, pe_=pe_: e.tensor_copy(out=penM[:, :], in_=pe_[:, :]), [pe_], [penM])
                if not early_ingest[0]:
                    cur["set"] = 1
                    ingest(0)
                for s_ in range(NSAMP):
                    cur["set"] = (s_ + 1) % 2
                    cur["j"] = 1 + s_
                    projscores(16, smp=s_)
                    for h in range(8):
                        pr, hb = h // 2, h % 2
                        stq("sto", sts_o[s_, h, :, :], Sst[hb * 64:(hb + 1) * 64, pr, hb * 64:(hb + 1) * 64], [Sst])
                    scores(16)
                    if s_ + 1 < NSAMP:
                        cur["set"] = (s_ + 2) % 2
                        ingest(s_ + 1)
                        cur["set"] = (s_ + 1) % 2
                    topk(16)
                    ld("g5l", gate_bc[:, :], gate5_d[1 + s_:2 + s_, :].partition_broadcast(128), [gate_bc], [gate5_b])
                    attention(16, smp=s_)
        except _Stop:
            st2.close()
        except _Stop2:
            for h in range(8):
                pr, hb = h // 2, h % 2
                stq("sto", st_o[h, :, :], Sst[hb * 64:(hb + 1) * 64, pr, hb * 64:(hb + 1) * 64], [Sst])
        S.finish()
        S.emit()
    return nc


def _consts():
    half = 32
    inv = (10000.0 ** (-np.arange(half, dtype=np.float32) / half)).astype(np.float32)
    pos = np.arange(SEQ, dtype=np.float32)
    ang = (pos[:, None] * inv[None, :]).astype(np.float32)
    cs = np.concatenate([np.cos(ang), np.sin(ang)], axis=1).astype(np.float32)
    lg = np.log1p(-np.exp2(-5.0 - np.arange(8, dtype=np.float64)))
    i = np.arange(128, dtype=np.float64)
    kdec = (0.125 * np.exp(lg[None, :] * (127.0 - i)[:, None])).astype(np.float32)
    qdec = np.exp(lg[None, :] * (i[:, None] + 1.0)).astype(np.float32)
    dt = np.zeros((128, 8, 128), np.float64)
    for h in range(8):
        v = 0.125 * np.exp(-lg[h] * (i + 1.0))
        dt[:, h, :] = v[:, None] * (i[None, :] >= i[:, None])
    g128 = np.zeros((128, 4, 128), np.float32)
    for pr in range(4):
        for hb in range(2):
            g128[hb * 64:(hb + 1) * 64, pr, :] = np.exp(lg[2 * pr + hb] * 128.0)
    return cs, kdec, qdec, dt.astype(np.float32), g128


def _consts_sample():
    half = 32
    inv = (10000.0 ** (-np.arange(half, dtype=np.float32) / half)).astype(np.float32)
    tt_ = np.minimum(np.arange(128), 3)
    pos = (8192 + tt_).astype(np.float32)
    ang = (pos[:, None] * inv[None, :]).astype(np.float32)
    cs_s = np.concatenate([np.cos(ang), np.sin(ang)], axis=1).astype(np.float32)
    lg = np.log1p(-np.exp2(-5.0 - np.arange(8, dtype=np.float64)))
    p = np.arange(128, dtype=np.float64)
    real = (p < 4)
    kdec_s = (0.125 * np.exp(lg[None, :] * (3.0 - p)[:, None]) * real[:, None]).astype(np.float32)
    qdec_s = np.exp(lg[None, :] * (tt_.astype(np.float64)[:, None] + 1.0)).astype(np.float32)
    dt = np.zeros((128, 8, 128), np.float64)
    for h in range(8):
        v = 0.125 * np.exp(-lg[h] * (p + 1.0)) * real
        dt[:, h, :] = v[:, None] * (tt_[None, :] >= p[:, None])
    g4 = np.zeros((128, 4, 128), np.float32)
    for pr in range(4):
        for hb in range(2):
            g4[hb * 64:(hb + 1) * 64, pr, :] = np.exp(lg[2 * pr + hb] * 4.0)
    j = np.arange(512)[None, :]
    adm = j <= tt_[:, None]
    penS_s = np.where(adm, 0.0, -1.0e30).astype(np.float32)
    penM_s = np.where(adm, 0.0, -30000.0).astype(np.float32)
    return cs_s, kdec_s, qdec_s, dt.astype(np.float32), g4, penS_s, penM_s


def kernel(x_prompt, x_sample, cache_k, cache_v, cache_kidx, state_ret, page_table, c_prompt, c_sample,
           w_ada, b_ada, g_norm, w_in, g_ret, w_pa, w_pr, w_out, g_final):
    f = lambda a: np.ascontiguousarray(np.asarray(a, dtype=np.float32))
    x_prompt = f(x_prompt); w_in0 = f(w_in)[0]
    cols = np.concatenate([
        np.arange(512, 1024), np.arange(1024, 1536), np.arange(2884, 3396), np.arange(3396, 3908),
        np.arange(2304, 2368), np.arange(2304, 2368),
        np.arange(0, 512),
        np.concatenate([np.concatenate([np.arange(2048 + 64 * h, 2048 + 64 * (h + 1))] * 2) for h in range(4)]),
        np.arange(2304, 2372), np.arange(1536, 2048), np.arange(2372, 2884), np.arange(3908, 4420),
        np.arange(4420, 5444), np.arange(5444, 6468)])
    assert cols.shape[0] == NEXT
    ptile = lambda w: np.ascontiguousarray(w.reshape(w.shape[0] // 128, 128, w.shape[1]).transpose(1, 0, 2))
    wext = ptile(w_in0[:, cols])
    wpo = ptile(np.concatenate([f(w_pa)[0], f(w_pr)[0], f(w_out)[0]], axis=0))
    wada = ptile(f(w_ada)[0])
    badaT = np.ascontiguousarray(f(b_ada)[0].reshape(24, 128).T)
    bgate = np.ascontiguousarray(np.broadcast_to(f(b_ada)[0][2048:3072], (128, 1024)))
    gnT = np.ascontiguousarray(f(g_norm)[0].reshape(8, 128).T)
    gfin = np.ascontiguousarray(np.broadcast_to(f(g_final), (128, 1024)))
    gret = np.ascontiguousarray(np.broadcast_to(f(g_ret)[0], (128, 512)))
    cs, kdec, qdec, dtab, g128 = _consts()
    cs_s, kdec_s, qdec_s, dtab_s, g4, penS_s, penM_s = _consts_sample()
    x_sample = f(x_sample); state_ret = f(state_ret)
    ck2 = f(cache_k)[0].reshape(2560 * 128, 512); cv2 = f(cache_v)[0].reshape(2560 * 128, 512)
    cki2 = f(cache_kidx)[0].reshape(2560 * 128, 64)
    page_table = np.asarray(page_table).astype(np.int32)
    xzero = np.zeros((128, D), np.float32)
    sel_s = np.zeros((128, 4), np.float32); sel_s[:, 0] = 1.0
    pidx = np.arange(128, dtype=np.float32).reshape(128, 1)
    ident = np.eye(128, dtype=np.float32)
    iota = np.ascontiguousarray(np.broadcast_to(np.arange(512, dtype=np.float32), (128, 512)))
    c_prompt = f(c_prompt); c_sample = f(c_sample)
    in_maps = []
    for c in range(8):
        b, r = c // 4, c % 4
        own_tiles = np.array([4 * g + r for g in range(NSTEP)])
        rows = (own_tiles[:, None] * 128 + np.arange(128)[None, :]).reshape(-1)
        cvec = np.stack([c_prompt[b]] + [c_sample[4 * c + s] for s in range(4)], axis=1)
        cT = np.ascontiguousarray(cvec.reshape(8, 128, 5).transpose(1, 0, 2))
        sel = np.zeros((128, 4), np.float32); sel[:, r] = 1.0
        xsp = np.zeros((4, 128, D), np.float32)
        for s_ in range(4):
            xsp[s_, 0:4] = x_sample[4 * c + s_]
        qrel = (r * 128 + np.arange(128, dtype=np.float32)).reshape(128, 1)
        in_maps.append({
            "xall": x_prompt[b], "xown": np.ascontiguousarray(x_prompt[b][rows]),
            "wext": wext, "wpo": wpo, "wada": wada, "badaT": badaT, "bgate": bgate, "gnT": gnT, "gfin": gfin, "gret": gret,
            "cT": cT, "ident": ident, "cs_all": cs, "cs_own": np.ascontiguousarray(cs[rows]), "kdec": kdec, "qdec": qdec,
            "dtab": dtab, "g128": g128, "sel": sel, "iota512": iota, "qrel": qrel,
            "xsp": xsp, "xzero": xzero, "cache_k2": ck2, "cache_v2": cv2, "cache_ki2": cki2,
            "ptab": np.ascontiguousarray(page_table[4 * c:4 * c + 4]), "state_in": np.ascontiguousarray(state_ret[0, 4 * c:4 * c + 4]),
            "cs_s": cs_s, "kdec_s": kdec_s, "qdec_s": qdec_s, "dtab_s": dtab_s, "g4": g4, "sel_s": sel_s,
            "penS_s": penS_s, "penM_s": penM_s, "pidx": pidx,
        })
    nc = build_nc()
    resr = run_bass_kernel_spmd(nc, in_maps, core_ids=list(range(8)))
    R = resr.results
    y_prompt = np.zeros((2, SEQ, D), np.float32)
    k_rows = np.zeros((1, 2, SEQ, 8, 64), np.float32); v_rows = np.zeros((1, 2, SEQ, 8, 64), np.float32)
    ki_rows = np.zeros((1, 2, SEQ, 64), np.float32); st_p = np.zeros((1, 2, 8, 64, 64), np.float32)
    for c in range(8):
        b, r = c // 4, c % 4
        for g in range(NSTEP):
            t = 4 * g + r
            sl_ = slice(t * 128, (t + 1) * 128); so = slice(g * 128, (g + 1) * 128)
            y_prompt[b, sl_] = R[c]["y_o"][so]
            k_rows[0, b, sl_] = R[c]["k_o"][so].reshape(128, 8, 64)
            v_rows[0, b, sl_] = R[c]["v_o"][so].reshape(128, 8, 64)
            ki_rows[0, b, sl_] = R[c]["ki_o"][so]
        if r == 3:
            st_p[0, b] = R[c]["st_o"]
    y_sample = np.zeros((32, 4, D), np.float32)
    ks = np.zeros((1, 32, 4, 8, 64), np.float32); vs = np.zeros((1, 32, 4, 8, 64), np.float32)
    kis = np.zeros((1, 32, 4, 64), np.float32); sts = np.zeros((1, 32, 8, 64, 64), np.float32)
    for c in range(8):
        for s_ in range(4):
            bi = 4 * c + s_
            y_sample[bi] = R[c]["ys_o"][s_]
            ks[0, bi] = R[c]["ks_o"][s_].reshape(4, 8, 64); vs[0, bi] = R[c]["vs_o"][s_].reshape(4, 8, 64)
            kis[0, bi] = R[c]["kis_o"][s_]; sts[0, bi] = R[c]["sts_o"][s_]
    return (y_prompt, y_sample, k_rows, v_rows, ki_rows, st_p, ks, vs, kis, sts)
```
